# Optimizing a Trainium2 kernel written in Bass

```python
import math
import jax
import jax.numpy as jnp
from jax import lax
import numpy as np

D_MODEL = 1024
BATCH = 4
SEQ = 8192
DEPTH = 4

N_EVEN = (DEPTH + 1) // 2
N_ODD = DEPTH // 2

DIFF_HEADS = 4
DIFF_QK_DIM = 64
DIFF_V_DIM = 2 * DIFF_QK_DIM
ATTN_Q_BLOCK = 128
MOBA_HEADS = 4
MOBA_HEAD_DIM = 128
MOBA_BLOCK = 256
MOBA_TOPK = 3
MOBA_Q_CHUNK = 64
DIFF_QK_W = DIFF_HEADS * 2 * DIFF_QK_DIM
DIFF_V_W = DIFF_HEADS * DIFF_V_DIM
MOBA_W = MOBA_HEADS * MOBA_HEAD_DIM
HY_IN_W = 2 * DIFF_QK_W + DIFF_V_W + 3 * MOBA_W
HY_MIX_W = DIFF_V_W + MOBA_W
GLA_HEADS = 4
GLA_K_W = D_MODEL // 2
GLA_V_W = D_MODEL
GLA_DK = GLA_K_W // GLA_HEADS
GLA_DV = GLA_V_W // GLA_HEADS
GLA_GATE_RANK = 16
GLA_GATE_NORM = 16.0
GLA_CHUNK = 64
GLA_IN_W = 2 * GLA_K_W + 2 * GLA_V_W + GLA_GATE_RANK
D_FF = 4 * D_MODEL
ROPE_THETA = 10000.0
LN_EPS = 1e-5
RMS_EPS = 1e-5
DEEPNORM_ALPHA = (2 * DEPTH) ** 0.25
DEEPNORM_BETA = (8 * DEPTH) ** -0.25

kernel_name = "hybrid_diff_moba_gla_deepnorm"


def layer_norm(x, g, b):
    xf = x.astype(jnp.float32)
    mu = jnp.mean(xf, axis=-1, keepdims=True)
    var = jnp.mean(jnp.square(xf - mu), axis=-1, keepdims=True)
    return (((xf - mu) * lax.rsqrt(var + LN_EPS)) * g + b).astype(x.dtype)


def rms_norm(x, g):
    xf = x.astype(jnp.float32)
    y = xf * lax.rsqrt(jnp.mean(jnp.square(xf), axis=-1, keepdims=True) + RMS_EPS)
    return (y * g).astype(x.dtype)


def rope_tables(seq, dim):
    inv = 1.0 / (ROPE_THETA ** (jnp.arange(0, dim, 2, dtype=jnp.float32) / dim))
    ang = jnp.arange(seq, dtype=jnp.float32)[:, None] * inv[None, :]
    return jnp.cos(ang), jnp.sin(ang)


def apply_rope(x, cos, sin):
    x1, x2 = jnp.split(x, 2, axis=-1)
    c = cos.astype(x.dtype)
    s = sin.astype(x.dtype)
    return jnp.concatenate([x1 * c - x2 * s, x2 * c + x1 * s], axis=-1)


def diff_attention(q, k, v, lam, subln, lambda_init):
    Bn, H, _, S, dq = q.shape
    scale = dq ** -0.5
    lf = lam.astype(jnp.float32)
    lam_full = jnp.exp(jnp.sum(lf[0] * lf[1])) - jnp.exp(jnp.sum(lf[2] * lf[3])) + lambda_init
    nq = S // ATTN_Q_BLOCK
    qb = q.reshape(Bn, H, 2, nq, ATTN_Q_BLOCK, dq).transpose(3, 0, 1, 2, 4, 5)
    kpos = jnp.arange(S)

    def block(args):
        qi, i = args
        s = jnp.einsum('bhmqd,bhmkd->bhmqk', qi, k).astype(jnp.float32) * scale
        qpos = i * ATTN_Q_BLOCK + jnp.arange(ATTN_Q_BLOCK)
        s = jnp.where(kpos[None, :] <= qpos[:, None], s, -jnp.inf)
        p = jax.nn.softmax(s, axis=-1)
        w = p[:, :, 0] - lam_full * p[:, :, 1]
        return jnp.einsum('bhqk,bhkd->bhqd', w.astype(v.dtype), v)

    o = lax.map(block, (qb, jnp.arange(nq)))
    o = o.transpose(1, 2, 0, 3, 4).reshape(Bn, H, S, v.shape[-1])
    return rms_norm(o, subln) * (1.0 - lambda_init)


def moba_attention(q, k, v):
    Bn, H, S, d = q.shape
    scale = d ** -0.5
    s_pad = -(-S // MOBA_BLOCK) * MOBA_BLOCK
    pad = ((0, 0), (0, 0), (0, s_pad - S), (0, 0))
    q, k, v = jnp.pad(q, pad), jnp.pad(k, pad), jnp.pad(v, pad)
    nb = s_pad // MOBA_BLOCK
    topk = min(MOBA_TOPK, nb)
    kb = k.reshape(Bn, H, nb, MOBA_BLOCK, d)
    vb = v.reshape(Bn, H, nb, MOBA_BLOCK, d)
    kmean = jnp.mean(kb, axis=3)
    nc = s_pad // MOBA_Q_CHUNK
    qc = q.reshape(Bn, H, nc, MOBA_Q_CHUNK, d).transpose(2, 0, 1, 3, 4)
    gather = jax.vmap(jax.vmap(lambda blocks, idx: blocks[idx]))
    blk_ids = jnp.arange(nb)

    def chunk(args):
        qi, c = args
        start = c * MOBA_Q_CHUNK
        own = start // MOBA_BLOCK
        qpos = start + jnp.arange(MOBA_Q_CHUNK)
        gate = jnp.einsum('bhqd,bhnd->bhqn', qi, kmean).astype(jnp.float32)
        gate = jnp.where(blk_ids < own, gate, -jnp.inf)
        _, sel = lax.top_k(gate, topk)
        sel_valid = sel < own
        ks = gather(kb, sel)
        vs = gather(vb, sel)
        s_sel = jnp.einsum('bhqd,bhqnld->bhqnl', qi, ks).astype(jnp.float32) * scale
        s_sel = jnp.where(sel_valid[..., None], s_sel, -jnp.inf).reshape(Bn, H, MOBA_Q_CHUNK, topk * MOBA_BLOCK)
        k_own = lax.dynamic_index_in_dim(kb, own, axis=2, keepdims=False)
        v_own = lax.dynamic_index_in_dim(vb, own, axis=2, keepdims=False)
        s_own = jnp.einsum('bhqd,bhld->bhql', qi, k_own).astype(jnp.float32) * scale
        kpos = own * MOBA_BLOCK + jnp.arange(MOBA_BLOCK)
        s_own = jnp.where(kpos[None, :] <= qpos[:, None], s_own, -jnp.inf)
        p = jax.nn.softmax(jnp.concatenate([s_sel, s_own], axis=-1), axis=-1).astype(v.dtype)
        p_sel = p[..., :topk * MOBA_BLOCK].reshape(Bn, H, MOBA_Q_CHUNK, topk, MOBA_BLOCK)
        p_own = p[..., topk * MOBA_BLOCK:]
        return (jnp.einsum('bhqnl,bhqnld->bhqd', p_sel, vs)
                + jnp.einsum('bhql,bhld->bhqd', p_own, v_own))

    o = lax.map(chunk, (qc, jnp.arange(nc)))
    o = o.transpose(1, 2, 0, 3, 4).reshape(Bn, H, s_pad, d)
    return o[:, :, :S]


def gla_chunked(q, k, v, g):
    dtype = v.dtype
    Bn, H, S, dk = q.shape
    dv = v.shape[-1]
    L = GLA_CHUNK
    nc = S // L
    f32 = jnp.float32

    def to_chunks(t):
        return t.astype(f32).reshape(Bn, H, nc, L, t.shape[-1]).transpose(2, 0, 1, 3, 4)

    qc, kc, vc, gc = to_chunks(q * dk ** -0.5), to_chunks(k), to_chunks(v), to_chunks(g)
    causal = jnp.tril(jnp.ones((L, L), dtype=bool))[None, None, :, :, None]

    def step(state, inp):
        qi, ki, vi, gi = inp
        b = jnp.cumsum(gi, axis=2)
        o_inter = jnp.einsum('bhlk,bhkv->bhlv', qi * jnp.exp(b), state)
        rel = jnp.where(causal, b[:, :, :, None, :] - b[:, :, None, :, :], -jnp.inf)
        att = jnp.einsum('bhik,bhijk,bhjk->bhij', qi, jnp.exp(rel), ki)
        o_intra = jnp.einsum('bhij,bhjv->bhiv', att, vi)
        b_last = b[:, :, -1:, :]
        state = (jnp.exp(b_last[:, :, 0, :, None]) * state
                 + jnp.einsum('bhjk,bhjv->bhkv', ki * jnp.exp(b_last - b), vi))
        return state, o_inter + o_intra

    state0 = jnp.zeros((Bn, H, dk, dv), f32)
    _, o = lax.scan(step, state0, (qc, kc, vc, gc))
    return o.transpose(1, 2, 0, 3, 4).reshape(Bn, H, S, dv).astype(dtype)


def split_heads(t, n, d):
    Bn, S, _ = t.shape
    return t.reshape(Bn, S, n, d).transpose(0, 2, 1, 3)


def merge_heads(t):
    Bn, H, S, d = t.shape
    return t.transpose(0, 2, 1, 3).reshape(Bn, S, H * d)


def diff_moba_mixer(x, w_in, lam, subln, w_out, lambda_init, rope_d, rope_m):
    Bn, S, _ = x.shape
    h = x @ w_in
    o1 = DIFF_QK_W
    o2 = o1 + DIFF_QK_W
    o3 = o2 + DIFF_V_W
    o4 = o3 + MOBA_W
    o5 = o4 + MOBA_W

    def two_maps(t):
        return t.reshape(Bn, S, DIFF_HEADS, 2, DIFF_QK_DIM).transpose(0, 2, 3, 1, 4)

    dq = apply_rope(two_maps(h[..., :o1]), *rope_d)
    dk = apply_rope(two_maps(h[..., o1:o2]), *rope_d)
    dv = split_heads(h[..., o2:o3], DIFF_HEADS, DIFF_V_DIM)
    mq = apply_rope(split_heads(h[..., o3:o4], MOBA_HEADS, MOBA_HEAD_DIM), *rope_m)
    mk = apply_rope(split_heads(h[..., o4:o5], MOBA_HEADS, MOBA_HEAD_DIM), *rope_m)
    mv = split_heads(h[..., o5:], MOBA_HEADS, MOBA_HEAD_DIM)
    a = diff_attention(dq, dk, dv, lam, subln, lambda_init)
    b = moba_attention(mq, mk, mv)
    return jnp.concatenate([merge_heads(a), merge_heads(b)], axis=-1) @ w_out


def gla_mixer(x, w_in, w_gate_up, b_gate, norm_g, w_out):
    h = x @ w_in
    o1 = GLA_K_W
    o2 = o1 + GLA_K_W
    o3 = o2 + GLA_V_W
    o4 = o3 + GLA_V_W
    q = split_heads(h[..., :o1], GLA_HEADS, GLA_DK)
    k = split_heads(h[..., o1:o2], GLA_HEADS, GLA_DK)
    v = split_heads(h[..., o2:o3], GLA_HEADS, GLA_DV)
    r = h[..., o3:o4]
    g = jax.nn.log_sigmoid((h[..., o4:] @ w_gate_up + b_gate).astype(jnp.float32)) / GLA_GATE_NORM
    g = split_heads(g, GLA_HEADS, GLA_DK)
    o = rms_norm(gla_chunked(q, k, v, g), norm_g)
    return (merge_heads(o) * jax.nn.silu(r)) @ w_out


def sqrelu_mlp(x, w1, w2):
    return jnp.square(jax.nn.relu(x @ w1)) @ w2


def setup_inputs(seed: int = 0) -> dict:
    key = jax.random.key(seed)
    ks = jax.random.split(key, 17)

    def normal(k, shape, scale):
        return jax.random.normal(k, shape, jnp.float32) * scale

    return {
        'x': normal(ks[0], (BATCH, SEQ, D_MODEL), 1.0),
        'hy_w_in': normal(ks[1], (N_EVEN, D_MODEL, HY_IN_W), D_MODEL ** -0.5),
        'diff_lambda': normal(ks[2], (N_EVEN, 4, DIFF_QK_DIM), 0.1),
        'diff_subln': 1.0 + normal(ks[3], (N_EVEN, DIFF_V_DIM), 0.02),
        'hy_w_out': normal(ks[4], (N_EVEN, HY_MIX_W, D_MODEL), HY_MIX_W ** -0.5 * DEEPNORM_BETA),
        'gla_w_in': normal(ks[5], (N_ODD, D_MODEL, GLA_IN_W), D_MODEL ** -0.5),
        'gla_w_gate_up': normal(ks[6], (N_ODD, GLA_GATE_RANK, GLA_K_W), GLA_GATE_RANK ** -0.5),
        'gla_b_gate': normal(ks[7], (N_ODD, GLA_K_W), 0.02),
        'gla_norm': 1.0 + normal(ks[8], (N_ODD, GLA_DV), 0.02),
        'gla_w_out': normal(ks[9], (N_ODD, GLA_V_W, D_MODEL), GLA_V_W ** -0.5 * DEEPNORM_BETA),
        'ln_mix_g': 1.0 + normal(ks[10], (DEPTH, D_MODEL), 0.02),
        'ln_mix_b': normal(ks[11], (DEPTH, D_MODEL), 0.02),
        'ffn_w1': normal(ks[12], (DEPTH, D_MODEL, D_FF), D_MODEL ** -0.5),
        'ffn_w2': normal(ks[13], (DEPTH, D_FF, D_MODEL), D_FF ** -0.5 * DEEPNORM_BETA),
        'ln_ffn_g': 1.0 + normal(ks[14], (DEPTH, D_MODEL), 0.02),
        'ln_ffn_b': normal(ks[15], (DEPTH, D_MODEL), 0.02),
    }


def reference(x, hy_w_in, diff_lambda, diff_subln, hy_w_out, gla_w_in, gla_w_gate_up,
              gla_b_gate, gla_norm, gla_w_out, ln_mix_g, ln_mix_b, ffn_w1, ffn_w2,
              ln_ffn_g, ln_ffn_b):
    S = x.shape[1]
    rope_d = rope_tables(S, DIFF_QK_DIM)
    rope_m = rope_tables(S, MOBA_HEAD_DIM)
    for l in range(DEPTH):
        if l % 2 == 0:
            e = l // 2
            lambda_init = 0.8 - 0.6 * math.exp(-0.3 * l)
            mix = diff_moba_mixer(x, hy_w_in[e], diff_lambda[e], diff_subln[e], hy_w_out[e],
                                  lambda_init, rope_d, rope_m)
        else:
            o = l // 2
            mix = gla_mixer(x, gla_w_in[o], gla_w_gate_up[o], gla_b_gate[o], gla_norm[o], gla_w_out[o])
        x = layer_norm(DEEPNORM_ALPHA * x + mix, ln_mix_g[l], ln_mix_b[l])
        x = layer_norm(DEEPNORM_ALPHA * x + sqrelu_mlp(x, ffn_w1[l], ffn_w2[l]), ln_ffn_g[l], ln_ffn_b[l])
    return x
```

```python
import contextlib
import numpy as np
import concourse.bass as bass
import concourse.mybir as mybir
from concourse.bass_utils import run_bass_kernel_spmd

F32 = mybir.dt.float32
BF16 = mybir.dt.bfloat16
AF = mybir.ActivationFunctionType
ALU = mybir.AluOpType
AX = mybir.AxisListType

SEM_LIMIT = 30000


class Ent:
    __slots__ = ("stream", "seq", "flag", "hw", "val", "n", "item")

    def __init__(self, stream, seq, n):
        self.stream = stream
        self.seq = seq
        self.flag = False
        self.hw = None
        self.val = 0
        self.n = n
        self.item = None


class Stream:
    def __init__(self, name, inorder):
        self.name = name
        self.inorder = inorder
        self.ents = []

    def new(self, n):
        e = Ent(self, len(self.ents) + 1, n)
        self.ents.append(e)
        return e


class Item:
    __slots__ = ("waits", "fn", "ent")

    def __init__(self, waits, fn, ent):
        self.waits = waits
        self.fn = fn
        self.ent = ent


class Buf:
    def __init__(self, name=""):
        self.name = name
        self.w = {}
        self.r = {}
        self.ds = None


def _merge(dst, src):
    for k, e in src.items():
        o = dst.get(k)
        if o is None or o.seq < e.seq:
            dst[k] = e


class FW:
    def __init__(self, nc, stack):
        self.nc = nc
        self.stack = stack
        self.engs = ["sp", "pe", "act", "dve", "pool"]
        self.q = {k: [] for k in self.engs}
        self.es = {k: Stream(k, True) for k in self.engs}
        self.seen = {k: {} for k in self.engs}
        self.dstreams = []
        self.free_streams = []
        self.live_streams = []
        self.nsem = 0
        self.alloc_stack = stack

    def sb(self, name, shape, dtype):
        self.ntens = getattr(self, "ntens", 0) + 1
        name = f"{name}_{self.ntens}"
        return self.alloc_stack.enter_context(self.nc.sbuf_tensor(name, list(shape), dtype))

    def ps(self, name, shape, dtype=F32):
        self.ntens = getattr(self, "ntens", 0) + 1
        name = f"{name}_{self.ntens}"
        return self.alloc_stack.enter_context(self.nc.psum_tensor(name, list(shape), dtype))

    def dstream(self, name):
        s = Stream(name, False)
        self.dstreams.append(s)
        return s

    def _hw(self, name):
        self.nsem += 1
        return self.stack.enter_context(self.nc.semaphore(f"s{self.nsem}_{name}"))

    def _waits(self, eng, reads, writes):
        raw = {}
        for b in reads:
            _merge(raw, b.w)
        oth = {}
        for b in writes:
            _merge(oth, b.w)
            _merge(oth, b.r)
        own = self.es[eng]
        need = dict(raw)
        for k, e in oth.items():
            if e.stream is own and eng == "pe":
                continue
            o = need.get(k)
            if o is None or o.seq < e.seq:
                need[k] = e
        waits = []
        seen = self.seen[eng]
        for k, e in need.items():
            if seen.get(k, 0) >= e.seq:
                continue
            seen[k] = e.seq
            e.flag = True
            waits.append(e)
        return waits

    def _commit(self, ent, reads, writes):
        k = id(ent.stream)
        for b in writes:
            b.w = {k: ent}
            b.r = {}
        for b in reads:
            o = b.r.get(k)
            if o is None or o.seq < ent.seq:
                b.r[k] = ent

    def op(self, eng, fn, reads=(), writes=()):
        waits = self._waits(eng, reads, writes)
        ent = self.es[eng].new(1)
        it = Item(waits, fn, ent)
        ent.item = it
        self.q[eng].append(it)
        self._commit(ent, reads, writes)
        return ent

    def dma(self, sbuf, out, in_, reads=(), writes=(), queue="sp", **kw):
        stream = getattr(sbuf, "ds", None)
        if stream is None:
            stream = sbuf.ds = self._take_stream("d" + sbuf.name)
        waits = self._waits(queue, reads, writes)
        ent = stream.new(16)
        ent.flag = True
        it = Item(waits, lambda e: e.dma_start(out=out, in_=in_, **kw), ent)
        ent.item = it
        self.q[queue].append(it)
        self._commit(ent, reads, writes)
        return ent

    def _take_stream(self, name):
        if self.free_streams:
            st = self.free_streams.pop()
        else:
            st = self.dstream(name)
        self.live_streams.append(st)
        return st

    def cc(self, buf, kind, in_ap, out_ap, groups, reads=(), writes=()):
        stream = getattr(buf, "ds", None)
        if stream is None:
            stream = buf.ds = self._take_stream("cc" + buf.name)
        waits = self._waits("pool", reads, writes)
        ent = stream.new(1)
        ent.flag = True
        it = Item(waits, lambda e: e.collective_compute(kind, op=ALU.bypass, replica_groups=groups,
                                                        ins=[in_ap.opt()], outs=[out_ap.opt()]), ent)
        ent.item = it
        self.q["pool"].append(it)
        self._commit(ent, reads, writes)
        return ent

    def barrier(self):
        toks = []
        for k in self.engs:
            if self.es[k].ents:
                toks.append(self.es[k].ents[-1])
        for st in self.live_streams:
            if st.ents:
                toks.append(st.ents[-1])
        for eng in self.engs:
            waits = []
            seen = self.seen[eng]
            for e in toks:
                if e.stream is self.es[eng]:
                    continue
                k = id(e.stream)
                if seen.get(k, 0) >= e.seq:
                    continue
                seen[k] = e.seq
                e.flag = True
                waits.append(e)
            self.q[eng].append(Item(waits, None, None))
        self.free_streams.extend(self.live_streams)
        self.live_streams = []

    @contextlib.contextmanager
    def stage(self):
        outer = self.alloc_stack
        with contextlib.ExitStack() as sub:
            self.alloc_stack = sub
            try:
                yield
            finally:
                self.alloc_stack = outer
            self.barrier()

    def wait_all(self, eng, bufs):
        waits = self._waits(eng, bufs, ())
        self.q[eng].append(Item(waits, None, None))

    def emit(self):
        for s in list(self.es.values()) + self.dstreams:
            hw = None
            val = 0
            prev = None
            for e in s.ents:
                if not e.flag:
                    continue
                if hw is None or val + e.n > SEM_LIMIT:
                    if hw is not None and not s.inorder:
                        e.item.waits.append(prev)
                    hw = self._hw(s.name)
                    val = 0
                val += e.n
                e.hw = hw
                e.val = val
                prev = e
        nc = self.nc

        def mk(name):
            items = self.q[name]

            def body(e):
                for it in items:
                    for w in it.waits:
                        e.wait_ge(w.hw, w.val)
                    if it.fn is not None:
                        ins = it.fn(e)
                        if it.ent.flag:
                            ins.then_inc(it.ent.hw, it.ent.n)

            return body

        with nc.Block() as block:
            block.sync(mk("sp"))
            block.tensor(mk("pe"))
            block.scalar(mk("act"))
            block.vector(mk("dve"))
            block.gpsimd(mk("pool"))


import math


D = 1024
ALPHA = 8.0 ** 0.25
NEG = -30000.0


class KB:
    def __init__(self, nc, stack):
        self.nc = nc
        self.fw = FW(nc, stack)
        self.nps = 0

    def sb(self, name, shape, dt):
        return self.fw.sb(name, shape, dt)

    def bank(self, dt=F32):
        self.nps += 1
        n = 512 if dt == F32 else 1024
        return self.fw.ps(f"ps{self.nps}", [128, n], dt), Buf(f"ps{self.nps}")

    def mm(self, out, lhsT, rhs, start, stop, reads, writes):
        return self.fw.op("pe", lambda e: e.matmul(out, lhsT=lhsT, rhs=rhs, start=start, stop=stop,
                                                   skip_group_check=True), reads, writes)

    def tr(self, out, in_, ident, reads, writes):
        return self.fw.op("pe", lambda e: e.transpose(out=out, in_=in_, identity=ident), reads, writes)

    def act(self, out, in_, func, reads, writes, **kw):
        return self.fw.op("act", lambda e: e.activation(out=out, in_=in_, func=func, **kw), reads, writes)

    def tt(self, eng, out, in0, in1, op, reads, writes):
        return self.fw.op(eng, lambda e: e.tensor_tensor(out=out, in0=in0, in1=in1, op=op), reads, writes)

    def ts(self, eng, out, in0, s1, s2, op0, op1, reads, writes):
        if op1 is None:
            return self.fw.op(eng, lambda e: e.tensor_scalar(out=out, in0=in0, scalar1=s1, scalar2=None, op0=op0),
                              reads, writes)
        return self.fw.op(eng, lambda e: e.tensor_scalar(out=out, in0=in0, scalar1=s1, scalar2=s2, op0=op0, op1=op1),
                          reads, writes)

    def stt(self, eng, out, in0, scalar, in1, op0, op1, reads, writes):
        return self.fw.op(eng, lambda e: e.scalar_tensor_tensor(out=out, in0=in0, scalar=scalar, in1=in1,
                                                                op0=op0, op1=op1), reads, writes)

    def cp(self, eng, out, in_, reads, writes):
        if eng == "act":
            return self.fw.op("act", lambda e: e.copy(out=out, in_=in_), reads, writes)
        return self.fw.op(eng, lambda e: e.tensor_copy(out=out, in_=in_), reads, writes)

    def memset(self, eng, ap, val, writes):
        return self.fw.op(eng, lambda e: e.memset(ap, val), (), writes)

    def asel(self, out, in_, pattern, cmp, fill, base, cm, bufs):
        return self.fw.op("pool", lambda e: e.affine_select(out=out, in_=in_, pattern=pattern, compare_op=cmp,
                                                            fill=fill, base=base, channel_multiplier=cm), bufs, bufs)

    def consts(self, moba=False):
        c = {}
        B = Buf("consts")
        c["buf"] = B
        ident = self.sb("ident", [128, 128], BF16)
        self.memset("pool", ident[:], 1.0, [B])
        self.asel(ident[:], ident[:], [[-1, 128]], ALU.is_equal, 0.0, 0, 1, [B])
        c["ident"] = ident
        negm = self.sb("negm", [128, 128], BF16)
        self.memset("pool", negm[:], 0.0, [B])
        self.asel(negm[:], negm[:], [[1, 128]], ALU.is_ge, NEG, 0, -1, [B])
        c["negm"] = negm
        tri = self.sb("tri", [128, 128], F32)
        self.memset("pool", tri[:], 1.0, [B])
        self.asel(tri[:], tri[:], [[1, 128]], ALU.is_ge, 0.0, 0, -1, [B])
        c["tri"] = tri
        ui = self.sb("uincl", [128, 128], F32)
        self.memset("pool", ui[:], -1.0 / 16.0, [B])
        self.asel(ui[:], ui[:], [[1, 128]], ALU.is_ge, 0.0, 0, -1, [B])
        c["uincl"] = ui
        us = self.sb("ustr", [128, 128], F32)
        self.memset("pool", us[:], -1.0 / 16.0, [B])
        self.asel(us[:], us[:], [[-1, 128]], ALU.is_gt, 0.0, 0, 1, [B])
        c["ustr"] = us
        ones_f = self.sb("ones_f", [128, 128], F32)
        self.memset("pool", ones_f[:], 1.0, [B])
        c["ones_f"] = ones_f
        ones_b = self.sb("ones_b", [128, 128], BF16)
        self.memset("pool", ones_b[:], 1.0, [B])
        c["ones_b"] = ones_b
        e01 = self.sb("e01", [128, 4], BF16)
        self.memset("pool", e01[:], 0.0, [B])
        self.memset("pool", e01[:, 0:1], 1.0, [B])
        self.memset("pool", e01[:, 3:4], 1.0, [B])
        c["e01"] = e01
        e01f = self.sb("e01f", [128, 4], F32)
        self.memset("pool", e01f[:], 0.0, [B])
        self.memset("pool", e01f[:, 0:1], 1.0, [B])
        self.memset("pool", e01f[:, 3:4], 1.0, [B])
        c["e01f"] = e01f
        sel2 = self.sb("sel2", [2, 256], F32)
        self.memset("pool", sel2[:], 1.0, [B])
        self.asel(sel2[:, 0:128], sel2[:, 0:128], [[0, 128]], ALU.is_equal, 0.0, 0, 1, [B])
        self.asel(sel2[:, 128:256], sel2[:, 128:256], [[0, 128]], ALU.is_equal, 0.0, -1, 1, [B])
        c["sel2"] = sel2
        if not moba:
            self.c = c
            return c
        esel = self.sb("esel", [32, 32 * 128], BF16)
        self.memset("pool", esel[:], NEG, [B])
        ev = esel[:].rearrange("p (n k) -> p n k", k=128)
        self.asel(ev, ev, [[-1, 32], [0, 128]], ALU.is_equal, 0.0, 0, 1, [B])
        c["esel"] = esel
        self.c = c
        return c


def stage_row(kb, T, ho, xres, wout, w1, w2, lnp, xout, xTout, dbg=None, ho_sel=None, w_bf16=False):
    fw = kb.fw
    c = kb.c
    CB = c["buf"]
    wo_sb = kb.sb("wo_sb", [128, 8, 1024], BF16)
    WO = Buf("wo")
    ln_sb = kb.sb("ln_sb", [128, 4, 1024], F32)
    LNB = Buf("ln")
    wq_ = "sp" if w_bf16 else "pool"
    fw.dma(WO, wo_sb[:], wout.rearrange("(c p) n -> p c n", p=128), writes=[WO], queue=wq_)
    fw.dma(LNB, ln_sb[:], lnp, writes=[LNB])
    NW = 2
    w1b = [kb.sb(f"w1b{i}", [128, 8, 1024], BF16) for i in range(NW)]
    w2b = [kb.sb(f"w2b{i}", [128, 8, 1024], BF16) for i in range(NW)]
    W1B = [Buf() for _ in range(NW)]
    W2B = [Buf() for _ in range(NW)]
    hoc = kb.sb("hoc", [128, 8, 512], BF16)
    HOC = Buf("hoc")
    if ho_sel is not None:
        hoa = hoc
        hob = kb.sb("hob", [128, 8, 512], BF16)
        sel_sb = kb.sb("sel_sb", [128, 2], F32)
        HOA, HOBB, SELB = HOC, Buf("hob"), Buf("selb")
        fw.dma(SELB, sel_sb[:], ho_sel[1], writes=[SELB])
    y = kb.sb("y", [128, 4, 1024], F32)
    Y = [Buf(f"y{i}") for i in range(4)]
    acc = kb.sb("acc", [128, 4, 1024], F32)
    ACC = [Buf(f"acc{i}") for i in range(4)]
    xb = kb.sb("xb", [128, 1024], BF16)
    XB = Buf()
    x1T = kb.sb("x1T", [128, 8, 512], BF16)
    X1T = Buf("x1T")
    xTo, XTO = x1T, X1T
    hsq = [kb.sb(f"hsq{i}", [128, 8, 512], BF16) for i in range(2)]
    HSQ = [[Buf() for _ in range(8)] for _ in range(2)]
    rl = [kb.sb(f"rl{i}", [128, 512], F32) for i in range(2)]
    RL = [Buf() for _ in range(2)]
    st6 = kb.sb("st6", [128, 2, 6], F32)
    mv = kb.sb("mv", [128, 2], F32)
    rstd = kb.sb("rstd", [128, 2], F32)
    STB = Buf()
    G = [kb.bank() for _ in range(4)]
    TR = [kb.bank(BF16) for _ in range(2)]
    OUTS = []

    def nout():
        b = Buf("o")
        OUTS.append(b)
        return b
    gi = [0]

    def nextG():
        g = G[gi[0] % 4]
        gi[0] += 1
        return g

    ti = [0]

    def layer_norm(buf_ap, BUFS, j, gidx):
        v = buf_ap[:, j, :]
        for hh in range(2):
            fw.op("dve", lambda e, hh=hh: e.bn_stats(out=st6[:, hh, :], in_=buf_ap[:, j, hh * 512:(hh + 1) * 512]),
                  [BUFS[j]], [STB])
        fw.op("dve", lambda e: e.bn_aggr(out=mv[:], in_=st6[:].rearrange("p a b -> p (a b)")), [STB], [STB])
        kb.act(rstd[:, 0:1], mv[:, 1:2], AF.Sqrt, [STB], [STB], bias=1e-5, scale=1.0)
        fw.op("dve", lambda e: e.reciprocal(out=rstd[:, 1:2], in_=rstd[:, 0:1]), [STB], [STB])
        kb.ts("dve", v, v, mv[:, 0:1], rstd[:, 1:2], ALU.subtract, ALU.mult, [STB, BUFS[j]], [BUFS[j]])
        kb.tt("pool", v, v, ln_sb[:, gidx, :], ALU.mult, [BUFS[j], LNB], [BUFS[j]])
        kb.tt("pool", v, v, ln_sb[:, gidx + 1, :], ALU.add, [BUFS[j], LNB], [BUFS[j]])

    def to_T(src_ap, SRC, j, dstT, DST):
        kb.cp("act", xb[:], src_ap[:, j, :], [SRC[j]], [XB])
        trp, TRB = TR[ti[0] % 2]
        ti[0] += 1
        for k in range(8):
            kb.tr(trp[:, k * 128:(k + 1) * 128], xb[:, k * 128:(k + 1) * 128], c["ident"][:], [XB, CB], [TRB])
        kb.cp("dve", dstT[:, :, j * 128:(j + 1) * 128], trp[:].rearrange("p (k t) -> p k t", t=128), [TRB], [DST])

    nst = T // 512
    for st in range(nst):
        t0 = st * 512
        if ho_sel is None:
            fw.dma(HOC, hoc[:], ho.rearrange("c p t -> p c t")[:, :, t0:t0 + 512], writes=[HOC])
        else:
            hosrc, Thalf = ho_sel[0], ho_sel[2]
            fw.dma(HOA, hoa[:], hosrc(t0), writes=[HOA])
            fw.dma(HOBB, hob[:], hosrc(Thalf + t0), writes=[HOBB])
            kb.act(hoa[:], hoa[:], AF.Copy, [HOA, SELB], [HOA], scale=sel_sb[:, 0:1])
            kb.stt("dve", hoa[:], hob[:], sel_sb[:, 1:2], hoa[:], ALU.mult, ALU.add, [HOBB, HOA, SELB], [HOA])
        fw.dma(Y[0], y[:], xres[t0:t0 + 512, :].rearrange("(j p) d -> p j d", p=128), writes=Y)
        for j in range(4):
            for nb in range(2):
                g, GB = nextG()
                for cc in range(8):
                    kb.mm(g[:], hoc[:, cc, j * 128:(j + 1) * 128], wo_sb[:, cc, nb * 512:(nb + 1) * 512],
                          cc == 0, cc == 7, [HOC, WO], [GB])
                kb.stt("dve", y[:, j, nb * 512:(nb + 1) * 512], y[:, j, nb * 512:(nb + 1) * 512], ALPHA, g[:],
                       ALU.mult, ALU.add, [Y[j], GB], [Y[j]])
            layer_norm(y, Y, j, 0)
            to_T(y, Y, j, x1T, X1T)
        if dbg is not None:
            fw.dma(Y[0], dbg[0], y[:], reads=Y, writes=[nout()])
            fw.dma(X1T, dbg[1], x1T[:], reads=[X1T], writes=[nout()])
        for fb in range(4):
            wi = (st * 4 + fb) % NW
            fw.dma(W1B[wi], w1b[wi][:], w1.rearrange("(k p) f -> p k f", p=128)[:, :, fb * 1024:(fb + 1) * 1024],
                   writes=[W1B[wi]], queue=wq_)
            fw.dma(W2B[wi], w2b[wi][:], w2[fb * 1024:(fb + 1) * 1024, :].rearrange("(c p) d -> p c d", p=128),
                   writes=[W2B[wi]], queue=wq_)
            hi = fb % 2
            for fc in range(8):
                g, GB = nextG()
                for k in range(8):
                    kb.mm(g[:], w1b[wi][:, k, fc * 128:(fc + 1) * 128], x1T[:, k, :], k == 0, k == 7,
                          [W1B[wi], X1T], [GB])
                ri = fc % 2
                kb.act(rl[ri][:], g[:], AF.Relu, [GB], [RL[ri]])
                kb.tt("pool", hsq[hi][:, fc, :], rl[ri][:], rl[ri][:], ALU.mult, [RL[ri]], [HSQ[hi][fc]])
            for j in range(4):
                for nb in range(2):
                    g, GB = nextG()
                    for fc in range(8):
                        kb.mm(g[:], hsq[hi][:, fc, j * 128:(j + 1) * 128], w2b[wi][:, fc, nb * 512:(nb + 1) * 512],
                              fc == 0, fc == 7, [HSQ[hi][fc], W2B[wi]], [GB])
                    a = acc[:, j, nb * 512:(nb + 1) * 512]
                    if fb == 0:
                        kb.cp("dve", a, g[:], [GB], [ACC[j]])
                    else:
                        kb.tt("dve", a, a, g[:], ALU.add, [GB, ACC[j]], [ACC[j]])
        if dbg is not None:
            fw.dma(ACC[0], dbg[2], acc[:], reads=ACC, writes=[nout()])
            fw.dma(HSQ[1][0], dbg[3], hsq[1][:], reads=HSQ[1], writes=[nout()])
        for j in range(4):
            kb.stt("dve", acc[:, j, :], y[:, j, :], ALPHA, acc[:, j, :], ALU.mult, ALU.add, [Y[j], ACC[j]], [ACC[j]])
            layer_norm(acc, ACC, j, 2)
            to_T(acc, ACC, j, xTo, XTO)
        fw.dma(ACC[0], xout[t0:t0 + 512, :].rearrange("(j p) d -> p j d", p=128), acc[:], reads=ACC, writes=[nout()])
        fw.dma(XTO, xTout(t0), xTo[:], reads=[XTO], writes=[nout()])
    return OUTS


def stage_even(kb, S, xsrc, wfm, wv, ropes, lam128, gsub, hodst, lambda_init, hook=None):
    fw = kb.fw
    c = kb.c
    CB = c["buf"]
    nkt = S // 128
    nqb = S // 512
    nblk = S // 256
    QT = [kb.sb(f"QT{i}", [128, S], BF16) for i in range(2)]
    KT = [kb.sb(f"KT{i}", [128, S], BF16) for i in range(2)]
    V = [kb.sb(f"V{i}", [128, nkt, 128], BF16) for i in range(2)]
    QTB = [Buf() for _ in range(2)]
    KTB = [Buf() for _ in range(2)]
    VB = [Buf() for _ in range(2)]
    wfm_sb = kb.sb("wfm_sb", [128, 8, 1024], BF16)
    WFM = Buf("wfm")
    wv_sb = kb.sb("wv_sb", [128, 8, 256], BF16)
    WV = Buf("wv")
    xc = [kb.sb(f"xc{i}", [128, 8, 512], BF16) for i in range(2)]
    XC = [Buf(f"xc{i}") for i in range(2)]
    rp = [kb.sb(f"rp{i}", [128, 2, 512], F32) for i in range(2)]
    RP = [Buf(f"rp{i}") for i in range(2)]
    F = [kb.sb(f"F{i}", [128, 512], F32) for i in range(4)]
    FBUF = [Buf() for _ in range(4)]
    PT = [kb.sb(f"PT{i}", [128, 512], BF16) for i in range(4)]
    PTB = [Buf() for _ in range(4)]
    obf = [kb.sb(f"obf{i}", [128, 512], BF16) for i in range(2)]
    OBF = [Buf(f"obf{i}") for i in range(2)]
    rr = kb.sb("rr", [2, 512], F32)
    RR = Buf()
    accs = [[kb.sb(f"accs{p}_{i}", [128, 512], F32) for i in range(4)] for p in range(2)]
    ACCB = [[Buf() for _ in range(4)] for _ in range(2)]
    Os = [[kb.sb(f"Os{p}_{i}", [128, 512], F32) for i in range(2)] for p in range(2)]
    OSB = [[Buf() for _ in range(2)] for _ in range(2)]
    lam_sb = kb.sb("lam_sb", [128, 256], F32)
    gs_sb = kb.sb("gs_sb", [128, 1], F32)
    sm = kb.sb("sm_e", [128, 8], F32)
    LAM = Buf("lam")
    kmf = kb.sb("kmf", [128, 32], F32)
    kmb = [kb.sb(f"kmb{i}", [128, 32], BF16) for i in range(2)]
    KMB = [Buf() for _ in range(2)]
    Gs = kb.sb("Gs", [128, 32], F32)
    top8 = kb.sb("top8", [128, 8], F32)
    nots = kb.sb("nots", [128, 32], BF16)
    GSB = Buf()
    biasT = kb.sb("biasT", [32, 512], BF16)
    BIAS = Buf()
    SBK = [kb.bank() for _ in range(4)]
    O1, O1B = kb.bank()
    O2, O2B = kb.bank()
    SUMP, SUMB = kb.bank()
    FBK, FBB = kb.bank()
    OUTS = []

    fw.dma(LAM, lam_sb[:], lam128, writes=[LAM])
    fw.dma(LAM, gs_sb[:], gsub, writes=[LAM])
    kb.tt("dve", F[0][:, 0:64], lam_sb[:, 0:64], lam_sb[:, 64:128], ALU.mult, [LAM], [FBUF[0]])
    kb.tt("dve", F[0][:, 64:128], lam_sb[:, 128:192], lam_sb[:, 192:256], ALU.mult, [LAM], [FBUF[0]])
    fw.op("dve", lambda e: e.reduce_sum(out=sm[:, 0:1], in_=F[0][:, 0:64], axis=AX.X), [FBUF[0]], [LAM])
    fw.op("dve", lambda e: e.reduce_sum(out=sm[:, 1:2], in_=F[0][:, 64:128], axis=AX.X), [FBUF[0]], [LAM])
    kb.act(sm[:, 2:4], sm[:, 0:2], AF.Exp, [LAM], [LAM])
    kb.stt("dve", sm[:, 4:5], sm[:, 3:4], -float(lambda_init), sm[:, 2:3], ALU.add, ALU.subtract, [LAM], [LAM])

    def inproj(typ):
        fw.dma(WFM, wfm_sb[:], wfm[typ].rearrange("(k p) n -> p k n", p=128), writes=[WFM], queue="pool")
        fw.dma(WV, wv_sb[:], wv[typ].rearrange("(k p) n -> p k n", p=128), writes=[WV], queue="pool")
        gi = 0
        for cch in range(S // 512):
            t0 = cch * 512
            xi = cch % 2
            src, x_bf16 = xsrc(t0)
            fw.dma(XC[xi], xc[xi][:], src, writes=[XC[xi]], queue=("sp" if x_bf16 else "pool"))
            fw.dma(RP[xi], rp[xi][:], ropes[typ].rearrange("a p t -> p a t")[:, :, t0:t0 + 512], writes=[RP[xi]])
            for g in range(2):
                for hd in range(2):
                    dst, DB = (QT[hd], QTB[hd]) if g == 0 else (KT[hd], KTB[hd])
                    po, POB = SBK[gi % 4]
                    pp, PPB = SBK[(gi + 1) % 4]
                    gi += 2
                    fo = g * 4 + hd
                    fp = g * 4 + 2 + hd
                    for k in range(8):
                        kb.mm(po[:], wfm_sb[:, k, fo * 128:(fo + 1) * 128], xc[xi][:, k, :], k == 0, k == 7,
                              [WFM, XC[xi]], [POB])
                    for k in range(8):
                        kb.mm(pp[:], wfm_sb[:, k, fp * 128:(fp + 1) * 128], xc[xi][:, k, :], k == 0, k == 7,
                              [WFM, XC[xi]], [PPB])
                    fa = (g * 2 + hd) % 2 * 2
                    kb.tt("dve", F[fa][:], po[:], rp[xi][:, 0, :], ALU.mult, [POB, RP[xi]], [FBUF[fa]])
                    kb.tt("dve", F[fa + 1][:], pp[:], rp[xi][:, 1, :], ALU.mult, [PPB, RP[xi]], [FBUF[fa + 1]])
                    kb.tt("pool", dst[:, t0:t0 + 512], F[fa][:], F[fa + 1][:], ALU.add, [FBUF[fa], FBUF[fa + 1]], [DB])
            for sub in range(4):
                pv, PVB = SBK[gi % 4]
                gi += 1
                for k in range(8):
                    kb.mm(pv[:, 0:256], xc[xi][:, k, sub * 128:(sub + 1) * 128], wv_sb[:, k, :], k == 0, k == 7,
                          [WV, XC[xi]], [PVB])
                for hd in range(2):
                    kb.cp("act", V[hd][:, cch * 4 + sub, :], pv[:, hd * 128:(hd + 1) * 128], [PVB], [VB[hd]])

    def attention(typ, hd):
        nmap = 2 if typ == 0 else 1
        scale = 64.0 ** -0.5 if typ == 0 else 128.0 ** -0.5
        och = typ * 2 + hd
        for qb in range(nqb):
            Q0 = qb * 512
            if typ == 1:
                for j in range(4):
                    q0 = Q0 + j * 128
                    ob = q0 // 256
                    kb.memset("pool", nots[:], 0.0, [GSB])
                    if ob > 0:
                        kb.memset("pool", Gs[:], -1e30, [GSB])
                        gp, GPB = SBK[j % 4]
                        kb.mm(gp[:, 0:32], QT[hd][:, q0:q0 + 128], kmb[hd][:, 0:32], True, True,
                              [QTB[hd], KMB[hd]], [GPB])
                        kb.cp("dve", Gs[:, 0:ob], gp[:, 0:ob], [GPB, GSB], [GSB])
                        fw.op("dve", lambda e: e.max(out=top8[:], in_=Gs[:]), [GSB], [GSB])
                        kb.ts("dve", nots[:, 0:ob], Gs[:, 0:ob], top8[:, 2:3], None, ALU.is_lt, None, [GSB], [GSB])
                    kb.mm(FBK[0:32, j * 128:(j + 1) * 128], nots[:, 0:32], c["ident"][:], True, True, [GSB, CB], [FBB])
                kb.cp("act", biasT[:], FBK[0:32, 0:512], [FBB], [BIAS])
            items = [(kt, m) for kt in range((Q0 + 512) // 128) for m in range(nmap)]
            n = len(items)
            OB_ = [(O1, O1B), (O2, O2B)]
            last_kt = (Q0 + 512) // 128 - 1

            def issueS(i):
                kt, m = items[i]
                K0 = kt * 128
                o = max(0, K0 - Q0)
                diag = K0 >= Q0
                sp_, SPB = SBK[i % 4]
                if typ == 0:
                    kb.mm(sp_[:, o:512], KT[hd][m * 64:(m + 1) * 64, K0:K0 + 128],
                          QT[hd][m * 64:(m + 1) * 64, Q0 + o:Q0 + 512], True, not diag, [KTB[hd], QTB[hd]], [SPB])
                else:
                    kb.mm(sp_[:, o:512], KT[hd][:, K0:K0 + 128], QT[hd][:, Q0 + o:Q0 + 512], True, False,
                          [KTB[hd], QTB[hd]], [SPB])
                    nb_ = K0 // 256
                    kb.mm(sp_[:, o:512], c["esel"][0:32, nb_ * 128:(nb_ + 1) * 128], biasT[0:32, o:512], False,
                          not diag, [CB, BIAS], [SPB])
                if diag:
                    kb.mm(sp_[:, o:o + 128], c["ident"][:], c["negm"][:], False, True, [CB], [SPB])
                kb.act(PT[i % 4][:, o:512], sp_[:, o:512], AF.Exp, [SPB], [PTB[i % 4]], scale=scale)

            def issuePV(i):
                kt, m = items[i]
                K0 = kt * 128
                o = max(0, K0 - Q0)
                Op, OpB = OB_[m]
                kb.mm(Op[:, o:512], V[hd][:, kt, :], PT[i % 4][:, o:512], kt == 0, kt == last_kt,
                      [VB[hd], PTB[i % 4]], [OpB])
                eng = "pool" if i % 3 == 2 else "dve"
                ai = (1 if eng == "pool" else 0) * 2 + m
                A_, AB_ = accs[qcount[0] % 2], ACCB[qcount[0] % 2]
                if not acc_used[ai]:
                    acc_used[ai] = True
                    if o > 0:
                        kb.memset(eng, A_[ai][:, 0:o], 0.0, [AB_[ai]])
                    kb.cp(eng, A_[ai][:, o:512], PT[i % 4][:, o:512], [PTB[i % 4]], [AB_[ai]])
                else:
                    kb.tt(eng, A_[ai][:, o:512], A_[ai][:, o:512], PT[i % 4][:, o:512], ALU.add,
                          [PTB[i % 4], AB_[ai]], [AB_[ai]])

            acc_used = [False] * 4
            if typ == 0:
                for g in range(n // 2 + 1):
                    if g < n // 2:
                        issueS(2 * g)
                        issueS(2 * g + 1)
                    if g >= 1:
                        issuePV(2 * g - 2)
                        issuePV(2 * g - 1)
                    if g % 3 == 2:
                        defer_tick()
            else:
                LA = 2
                for i in range(n + LA):
                    if i < n:
                        issueS(i)
                    if i - LA >= 0:
                        issuePV(i - LA)
                    if i % 4 == 3:
                        defer_tick()
            flush()
            par = qcount[0] % 2
            qcount[0] += 1
            nr = 2 if typ == 0 else 1
            kb.cp("act", Os[par][0][:], O1[:], [O1B], [OSB[par][0]])
            if typ == 0:
                kb.cp("dve", Os[par][1][:], O2[:], [O2B], [OSB[par][1]])
            used = [ai for ai in range(4) if acc_used[ai]]
            pending.extend(make_steps(typ, och, Q0, par, nr, used, qb % 2))

    def make_steps(typ, och, Q0, par, nr, used, oi):
        A = accs[par]
        AB = ACCB[par]
        O1s, O2s = Os[par][0], Os[par][1]
        O1sB, O2sB = OSB[par][0], OSB[par][1]
        st = []

        def s_sum():
            for ui, ai in enumerate(used):
                m_ = ai % 2
                lhs = c["e01f"][:, 2 * m_:2 * m_ + 2] if typ == 0 else c["ones_f"][:, 0:1]
                kb.mm(SUMP[0:nr, :], lhs, A[ai][:], ui == 0, ui == len(used) - 1, [CB, AB[ai]], [SUMB])
        st.append(s_sum)

        def s_rcp():
            kb.act(rr[0:nr, :], SUMP[0:nr, :], AF.Ln, [SUMB], [RR])
            kb.act(rr[0:nr, :], rr[0:nr, :], AF.Exp, [RR], [RR], scale=-1.0)
        st.append(s_rcp)

        def s_out():
            ob_ = Buf("o")
            OUTS.append(ob_)
            fw.dma(OBF[oi], hodst(och, Q0), obf[oi][:], reads=[OBF[oi]], writes=[ob_])

        if typ == 1:
            def s_b():
                kb.mm(FBK[:], c["ones_f"][0:1, :], rr[0:1, :], True, True, [CB, RR], [FBB])
                kb.cp("act", F[0][:], FBK[:], [FBB], [FBUF[0]])
            st.append(s_b)

            def s_m():
                kb.tt("dve", obf[oi][:], O1s[:], F[0][:], ALU.mult, [O1sB, FBUF[0]], [OBF[oi]])
                s_out()
            st.append(s_m)
            return st

        def s1():
            kb.mm(FBK[:], c["sel2"][0:2, 0:128], rr[0:2, :], True, True, [CB, RR], [FBB])
            kb.cp("act", F[0][:], FBK[:], [FBB], [FBUF[0]])
        st.append(s1)

        def s2():
            kb.tt("dve", F[1][:], O1s[:], F[0][:], ALU.mult, [O1sB, FBUF[0]], [FBUF[1]])
            kb.mm(FBK[:], c["sel2"][0:2, 128:256], rr[0:2, :], True, True, [CB, RR], [FBB])
            kb.cp("act", F[0][:], FBK[:], [FBB], [FBUF[0]])
        st.append(s2)

        def s3():
            kb.tt("dve", F[2][:], O2s[:], F[0][:], ALU.mult, [O2sB, FBUF[0]], [FBUF[2]])
            kb.stt("dve", F[3][:], F[2][:], sm[:, 4:5], F[1][:], ALU.mult, ALU.add, [FBUF[2], FBUF[1], LAM],
                   [FBUF[3]])
            kb.tt("dve", F[1][:], F[3][:], F[3][:], ALU.mult, [FBUF[3]], [FBUF[1]])
        st.append(s3)

        def s4():
            kb.mm(FBK[0:1, :], c["ones_f"][:, 0:1], F[1][:], True, True, [CB, FBUF[1]], [FBB])
            kb.act(rr[0:1, :], FBK[0:1, :], AF.Ln, [FBB], [RR], bias=1e-5, scale=1.0 / 128.0)
            kb.act(rr[0:1, :], rr[0:1, :], AF.Exp, [RR], [RR], scale=-0.5)
        st.append(s4)

        def s5():
            kb.mm(FBK[:], c["ones_f"][0:1, :], rr[0:1, :], True, True, [CB, RR], [FBB])
            kb.stt("dve", F[2][:], F[3][:], gs_sb[:, 0:1], FBK[:], ALU.mult, ALU.mult, [FBUF[3], LAM, FBB],
                   [FBUF[2]])
            kb.act(obf[oi][:], F[2][:], AF.Copy, [FBUF[2]], [OBF[oi]], scale=float(1.0 - lambda_init))
            s_out()
        st.append(s5)
        return st

    pending = []
    qcount = [0]

    def flush():
        while pending:
            pending.pop(0)()

    def defer_tick():
        if pending:
            pending.pop(0)()

    KMF = Buf()
    for typ in range(2):
        inproj(typ)
        if typ == 0 and hook is not None:
            hook()
        if typ == 1:
            for hd in range(2):
                kb.memset("pool", kmf[:], 0.0, [KMF])
                fw.op("dve", lambda e, hd=hd: e.reduce_sum(out=kmf[:, 0:nblk],
                                                           in_=KT[hd][:].rearrange("p (n l) -> p n l", l=256),
                                                           axis=AX.X), [KTB[hd]], [KMF])
                kb.ts("dve", kmb[hd][:], kmf[:], 1.0 / 256.0, None, ALU.mult, None, [KMF], [KMB[hd]])
        for hd in range(2):
            attention(typ, hd)
            flush()
    return OUTS


def stage_gla(kb, S, xsrc, wq, wlr, wtm, wgu, bg, gn128, hodst4, hook=None):
    fw = kb.fw
    c = kb.c
    CB = c["buf"]
    wq_sb = kb.sb("wq_sb", [128, 8, 512], BF16)
    wlr_sb = kb.sb("wlr_sb", [128, 8, 16], BF16)
    wtm_sb = kb.sb("wtm_sb", [128, 8, 1280], BF16)
    wgu_sb = kb.sb("wgu_sb", [16, 256], BF16)
    bg_sb = kb.sb("bg_sb", [1, 256], BF16)
    gn_sb = kb.sb("gn_sb", [128, 256], F32)
    WB = Buf("glaw")
    fw.dma(WB, wq_sb[:], wq.rearrange("(k p) n -> p k n", p=128), writes=[WB], queue="pool")
    fw.dma(WB, wlr_sb[:], wlr.rearrange("(k p) n -> p k n", p=128), writes=[WB], queue="pool")
    fw.dma(WB, wtm_sb[:], wtm.rearrange("(k p) n -> p k n", p=128), writes=[WB], queue="pool")
    fw.dma(WB, wgu_sb[:], wgu, writes=[WB], queue="pool")
    fw.dma(WB, bg_sb[:], bg, writes=[WB], queue="pool")
    fw.dma(WB, gn_sb[:], gn128, writes=[WB])
    if hook is not None:
        hook()
    xc = [kb.sb(f"gxc{i}", [128, 8, 512], BF16) for i in range(2)]
    XC = [Buf(f"gxc{i}") for i in range(2)]
    qk = kb.sb("qk", [128, 4, 512], F32)
    QK = Buf()
    lrT = kb.sb("lrT", [16, 512], BF16)
    LRT = Buf()
    vb = kb.sb("vb", [128, 512], BF16)
    VBB = Buf()
    sr = kb.sb("sr", [128, 512], F32)
    SRB = Buf()
    ee = kb.sb("ee", [128, 256], F32)
    sp_ = kb.sb("spl", [128, 256], F32)
    SPB = Buf()
    E3 = kb.sb("E3", [128, 256], F32)
    E3B = Buf()
    khat = kb.sb("khat", [128, 256], BF16)
    KHB = Buf()
    E1 = kb.sb("E1", [128, 128], F32)
    E2 = kb.sb("E2", [128, 128], F32)
    EB = Buf()
    dec = kb.sb("dec", [128, 2], F32)
    DECB = [Buf() for _ in range(2)]
    qtl = [kb.sb(f"qtl{i}", [128, 128], BF16) for i in range(2)]
    ktl = [kb.sb(f"ktl{i}", [128, 128], BF16) for i in range(2)]
    QTL = [Buf() for _ in range(2)]
    KTL = [Buf() for _ in range(2)]
    attm = [kb.sb(f"attm{i}", [128, 128], BF16) for i in range(2)]
    ATM = [Buf() for _ in range(2)]
    Sst = [kb.sb(f"Sst{i}", [128, 256], F32) for i in range(2)]
    Sbf = [kb.sb(f"Sbf{i}", [128, 256], BF16) for i in range(2)]
    SST = [Buf() for _ in range(2)]
    SBF = [Buf() for _ in range(2)]
    junk = kb.sb("junk", [128, 256], F32)
    ssq = kb.sb("ssq", [128, 4], F32)
    SSQ = Buf()
    og = kb.sb("og", [128, 256], F32)
    OGB = Buf()
    ogb = kb.sb("ogb", [128, 256], BF16)
    OGBB = Buf()
    hoc = [kb.sb(f"ghoc{i}", [128, 4, 512], BF16) for i in range(2)]
    HOCB = [Buf(f"ghoc{i}") for i in range(2)]
    G = [kb.bank() for _ in range(3)]
    BBK, BBB = kb.bank()
    ATK, ATB = kb.bank()
    OK_, OKB = kb.bank()
    DSK, DSB = kb.bank()
    TRK, TRB = kb.bank(BF16)
    OUTS = []
    for hd in range(2):
        kb.memset("pool", Sst[hd][:], 0.0, [SST[hd]])
        kb.memset("pool", Sbf[hd][:], 0.0, [SBF[hd]])
    gi = [0]

    def nextG():
        g = G[gi[0] % 3]
        gi[0] += 1
        return g

    lnscale = math.log(128.0 ** -0.5)
    for cch in range(S // 512):
        t0 = cch * 512
        xi = cch % 2
        src, x_bf16 = xsrc(t0)
        fw.dma(XC[xi], xc[xi][:], src, writes=[XC[xi]], queue=("sp" if x_bf16 else "pool"))
        for ft in range(4):
            g, GB = nextG()
            for k in range(8):
                kb.mm(g[:], wq_sb[:, k, ft * 128:(ft + 1) * 128], xc[xi][:, k, :], k == 0, k == 7, [WB, XC[xi]], [GB])
            kb.cp("act", qk[:, ft, :], g[:], [GB], [QK])
        g, GB = nextG()
        for k in range(8):
            kb.mm(g[0:16, :], wlr_sb[:, k, :], xc[xi][:, k, :], k == 0, k == 7, [WB, XC[xi]], [GB])
        kb.cp("act", lrT[:], g[0:16, :], [GB], [LRT])
        hi = cch % 2
        for j in range(4):
            ts_ = slice(j * 128, (j + 1) * 128)
            gk, GKB = nextG()
            for k in range(8):
                kb.mm(gk[:, 0:256], xc[xi][:, k, ts_], wtm_sb[:, k, 0:256], k == 0, k == 7, [WB, XC[xi]], [GKB])
            kb.mm(gk[:, 256:512], lrT[0:16, ts_], wgu_sb[0:16, :], True, False, [LRT, WB], [GKB])
            kb.mm(gk[:, 256:512], c["ones_b"][0:1, 0:128], bg_sb[0:1, :], False, True, [CB, WB], [GKB])
            gv, GVB = nextG()
            for k in range(8):
                kb.mm(gv[:], xc[xi][:, k, ts_], wtm_sb[:, k, 256:768], k == 0, k == 7, [WB, XC[xi]], [GVB])
            kb.cp("act", vb[:], gv[:], [GVB], [VBB])
            gr, GRB = nextG()
            for k in range(8):
                kb.mm(gr[:], xc[xi][:, k, ts_], wtm_sb[:, k, 768:1280], k == 0, k == 7, [WB, XC[xi]], [GRB])
            kb.act(sr[:], gr[:], AF.Silu, [GRB], [SRB])
            kb.act(ee[:], gk[:, 256:512], AF.Exp, [GKB], [SPB], scale=-1.0)
            kb.act(sp_[:], ee[:], AF.Ln, [SPB], [SPB], bias=1.0, scale=1.0)
            for hd in range(2):
                kb.mm(BBK[:, hd * 128:(hd + 1) * 128], sp_[:, hd * 128:(hd + 1) * 128], c["uincl"][:], True, True,
                      [SPB, CB], [BBB])
            kb.mm(BBK[:, 256:512], c["ustr"][:], sp_[:], True, True, [SPB, CB], [BBB])
            kb.act(E3[:], BBK[:, 256:512], AF.Exp, [BBB], [E3B])
            kb.tt("dve", khat[:], gk[:, 0:256], E3[:], ALU.mult, [GKB, E3B], [KHB])
            for hd in range(2):
                bt = BBK[:, hd * 128:(hd + 1) * 128]
                kb.act(E1[:], bt, AF.Exp, [BBB], [EB], bias=lnscale, scale=1.0)
                kb.tt("dve", qtl[hd][:], qk[:, hd, ts_], E1[:], ALU.mult, [QK, EB], [QTL[hd]])
                kb.act(E2[:], bt, AF.Exp, [BBB, QTL[hd]], [EB], scale=-1.0)
                kb.tt("dve", ktl[hd][:], qk[:, 2 + hd, ts_], E2[:], ALU.mult, [QK, EB], [KTL[hd]])
                kb.act(dec[:, hd:hd + 1], BBK[:, hd * 128 + 127:hd * 128 + 128], AF.Exp, [BBB], [DECB[hd]])
                kb.mm(ATK[:, hd * 128:(hd + 1) * 128], ktl[hd][:], qtl[hd][:], True, True, [KTL[hd], QTL[hd]], [ATB])
                kb.tt("dve", attm[hd][:], ATK[:, hd * 128:(hd + 1) * 128], c["tri"][:], ALU.mult, [ATB, CB], [ATM[hd]])
                ov = OK_[:, hd * 256:(hd + 1) * 256]
                kb.mm(ov, attm[hd][:], vb[:, hd * 256:(hd + 1) * 256], True, False, [ATM[hd], VBB], [OKB])
                kb.mm(ov, qtl[hd][:], Sbf[hd][:], False, True, [QTL[hd], SBF[hd]], [OKB])
                dv = DSK[:, hd * 256:(hd + 1) * 256]
                kb.mm(dv, khat[:, hd * 128:(hd + 1) * 128], vb[:, hd * 256:(hd + 1) * 256], True, True, [KHB, VBB], [DSB])
                kb.stt("dve", Sst[hd][:], Sst[hd][:], dec[:, hd:hd + 1], dv, ALU.mult, ALU.add, [SST[hd], DECB[hd], DSB],
                       [SST[hd]])
                kb.cp("pool", Sbf[hd][:], Sst[hd][:], [SST[hd]], [SBF[hd]])
                kb.act(junk[:], ov, AF.Square, [OKB], [SSQ], accum_out=ssq[:, 0:1])
                kb.act(ssq[:, 1:2], ssq[:, 0:1], AF.Sqrt, [SSQ], [SSQ], bias=1e-5, scale=1.0 / 256.0)
                fw.op("dve", lambda e: e.reciprocal(out=ssq[:, 2:3], in_=ssq[:, 1:2]), [SSQ], [SSQ])
                kb.stt("dve", og[:], ov, ssq[:, 2:3], gn_sb[:], ALU.mult, ALU.mult, [OKB, SSQ, WB], [OGB])
                kb.tt("pool", ogb[:], og[:], sr[:, hd * 256:(hd + 1) * 256], ALU.mult, [OGB, SRB], [OGBB])
                for cc in range(2):
                    kb.tr(TRK[:, (hd * 2 + cc) * 128:(hd * 2 + cc + 1) * 128], ogb[:, cc * 128:(cc + 1) * 128],
                          c["ident"][:], [OGBB, CB], [TRB])
            kb.cp("act", hoc[hi][:, :, ts_], TRK[:, 0:512].rearrange("p (c t) -> p c t", t=128), [TRB], [HOCB[hi]])
        ob_ = Buf("o")
        OUTS.append(ob_)
        fw.dma(HOCB[hi], hodst4(t0), hoc[hi][:], reads=[HOCB[hi]], writes=[ob_])
    return OUTS


def stage_row2(kb, T, xres, wout, w1, w2, lnp, xout, xTout, hosrc, sel, Thalf):
    fw = kb.fw
    c = kb.c
    CB = c["buf"]
    wo_sb = kb.sb("wo_sb", [128, 8, 1024], BF16)
    WO = Buf("wo")
    ln_sb = kb.sb("ln_sb", [128, 4, 1024], F32)
    LNB = Buf("ln")
    sel_sb = kb.sb("sel_sb", [128, 2], F32)
    SELB = Buf("selb")
    fw.dma(WO, wo_sb[:], wout.rearrange("(c p) n -> p c n", p=128), writes=[WO])
    fw.dma(LNB, ln_sb[:], lnp, writes=[LNB])
    fw.dma(SELB, sel_sb[:], sel, writes=[SELB])
    w1b = [kb.sb(f"w1b{i}", [128, 8, 512], BF16) for i in range(2)]
    w2b = [kb.sb(f"w2b{i}", [128, 4, 1024], BF16) for i in range(2)]
    W1B = [Buf(f"w1b{i}") for i in range(2)]
    W2B = [Buf(f"w2b{i}") for i in range(2)]
    hoa = kb.sb("hoa", [128, 8, 512], BF16)
    hob = kb.sb("hob", [128, 8, 512], BF16)
    HOA, HOBB = Buf("hoa"), Buf("hob")
    y = [kb.sb(f"y{i}", [128, 4, 1024], F32) for i in range(2)]
    Y = [[Buf(f"y{i}_{j}") for j in range(4)] for i in range(2)]
    acc = kb.sb("acc", [128, 4, 1024], F32)
    ACC = [Buf(f"acc{j}") for j in range(4)]
    xb = kb.sb("xb", [128, 1024], BF16)
    XB = Buf()
    x1T = [kb.sb(f"x1T{i}", [128, 8, 512], BF16) for i in range(2)]
    X1T = [Buf(f"x1T{i}") for i in range(2)]
    xTo = kb.sb("xTo", [128, 8, 512], BF16)
    XTO = Buf("xTo")
    hsq = [kb.sb(f"hsq{i}", [128, 4, 512], BF16) for i in range(2)]
    HSQ = [[Buf() for _ in range(4)] for _ in range(2)]
    rl = [kb.sb(f"rl{i}", [128, 512], F32) for i in range(2)]
    RL = [Buf() for _ in range(2)]
    st6 = kb.sb("st6", [128, 2, 6], F32)
    mv = kb.sb("mv", [128, 2], F32)
    rstd = kb.sb("rstd", [128, 2], F32)
    STB = Buf()
    G = [kb.bank() for _ in range(4)]
    TR = [kb.bank(BF16) for _ in range(2)]
    OUTS = []
    gi = [0]
    ti = [0]
    wi_ = [0]

    def nout():
        b = Buf("o")
        OUTS.append(b)
        return b

    def nextG():
        g = G[gi[0] % 4]
        gi[0] += 1
        return g

    def layer_norm(buf_ap, BUFS, j, gidx):
        v = buf_ap[:, j, :]
        for hh in range(2):
            fw.op("dve", lambda e, hh=hh: e.bn_stats(out=st6[:, hh, :], in_=buf_ap[:, j, hh * 512:(hh + 1) * 512]),
                  [BUFS[j]], [STB])
        fw.op("dve", lambda e: e.bn_aggr(out=mv[:], in_=st6[:].rearrange("p a b -> p (a b)")), [STB], [STB])
        kb.act(rstd[:, 0:1], mv[:, 1:2], AF.Sqrt, [STB], [STB], bias=1e-5, scale=1.0)
        fw.op("dve", lambda e: e.reciprocal(out=rstd[:, 1:2], in_=rstd[:, 0:1]), [STB], [STB])
        kb.ts("dve", v, v, mv[:, 0:1], rstd[:, 1:2], ALU.subtract, ALU.mult, [STB, BUFS[j]], [BUFS[j]])
        kb.tt("pool", v, v, ln_sb[:, gidx, :], ALU.mult, [BUFS[j], LNB], [BUFS[j]])
        kb.tt("pool", v, v, ln_sb[:, gidx + 1, :], ALU.add, [BUFS[j], LNB], [BUFS[j]])

    def to_T(src_ap, SRC, j, dstT, DST):
        kb.cp("act", xb[:], src_ap[:, j, :], [SRC[j]], [XB])
        trp, TRB = TR[ti[0] % 2]
        ti[0] += 1
        for k in range(8):
            kb.tr(trp[:, k * 128:(k + 1) * 128], xb[:, k * 128:(k + 1) * 128], c["ident"][:], [XB, CB], [TRB])
        kb.cp("dve", dstT[:, :, j * 128:(j + 1) * 128], trp[:].rearrange("p (k t) -> p k t", t=128), [TRB], [DST])

    def phaseA(st):
        p = st % 2
        t0 = st * 512
        fw.dma(HOA, hoa[:], hosrc(t0), writes=[HOA])
        fw.dma(HOBB, hob[:], hosrc(Thalf + t0), writes=[HOBB])
        fw.dma(Y[p][0], y[p][:], xres[t0:t0 + 512, :].rearrange("(j p) d -> p j d", p=128), writes=Y[p])
        kb.act(hoa[:], hoa[:], AF.Copy, [HOA, SELB], [HOA], scale=sel_sb[:, 0:1])
        kb.stt("dve", hoa[:], hob[:], sel_sb[:, 1:2], hoa[:], ALU.mult, ALU.add, [HOBB, HOA, SELB], [HOA])
        for j in range(4):
            for nb in range(2):
                g, GB = nextG()
                for cc in range(8):
                    kb.mm(g[:], hoa[:, cc, j * 128:(j + 1) * 128], wo_sb[:, cc, nb * 512:(nb + 1) * 512],
                          cc == 0, cc == 7, [HOA, WO], [GB])
                ys = y[p][:, j, nb * 512:(nb + 1) * 512]
                kb.stt("dve", ys, ys, ALPHA, g[:], ALU.mult, ALU.add, [Y[p][j], GB], [Y[p][j]])
            layer_norm(y[p], Y[p], j, 0)
            to_T(y[p], Y[p], j, x1T[p], X1T[p])

    def phaseF(st, fb):
        p = st % 2
        wi = wi_[0] % 2
        wi_[0] += 1
        fw.dma(W1B[wi], w1b[wi][:], w1.rearrange("(k p) f -> p k f", p=128)[:, :, fb * 512:(fb + 1) * 512],
               writes=[W1B[wi]])
        fw.dma(W2B[wi], w2b[wi][:], w2[fb * 512:(fb + 1) * 512, :].rearrange("(c p) d -> p c d", p=128),
               writes=[W2B[wi]])
        hi = fb % 2
        for fc in range(4):
            g, GB = nextG()
            for k in range(8):
                kb.mm(g[:], w1b[wi][:, k, fc * 128:(fc + 1) * 128], x1T[p][:, k, :], k == 0, k == 7,
                      [W1B[wi], X1T[p]], [GB])
            ri = fc % 2
            kb.act(rl[ri][:], g[:], AF.Relu, [GB], [RL[ri]])
            kb.tt("pool", hsq[hi][:, fc, :], rl[ri][:], rl[ri][:], ALU.mult, [RL[ri]], [HSQ[hi][fc]])
        for j in range(4):
            for nb in range(2):
                g, GB = nextG()
                for fc in range(4):
                    kb.mm(g[:], hsq[hi][:, fc, j * 128:(j + 1) * 128], w2b[wi][:, fc, nb * 512:(nb + 1) * 512],
                          fc == 0, fc == 3, [HSQ[hi][fc], W2B[wi]], [GB])
                a = acc[:, j, nb * 512:(nb + 1) * 512]
                if fb == 0:
                    kb.cp("dve", a, g[:], [GB], [ACC[j]])
                else:
                    kb.tt("dve", a, a, g[:], ALU.add, [GB, ACC[j]], [ACC[j]])

    def phaseB1(st):
        p = st % 2
        for j in range(4):
            kb.stt("dve", y[p][:, j, :], y[p][:, j, :], ALPHA, acc[:, j, :], ALU.mult, ALU.add,
                   [Y[p][j], ACC[j]], [Y[p][j]])

    def phaseB2(st):
        p = st % 2
        t0 = st * 512
        for j in range(4):
            layer_norm(y[p], Y[p], j, 2)
            to_T(y[p], Y[p], j, xTo, XTO)
        fw.dma(Y[p][0], xout[t0:t0 + 512, :].rearrange("(j p) d -> p j d", p=128), y[p][:], reads=Y[p], writes=[nout()])
        fw.dma(XTO, xTout(t0), xTo[:], reads=[XTO], writes=[nout()])

    nst = T // 512
    phaseA(0)
    for st in range(nst):
        for fb in range(8):
            phaseF(st, fb)
            if fb == 3 and st + 1 < nst:
                phaseA(st + 1)
            if fb == 0 and st > 0:
                phaseB2(st - 1)
        phaseB1(st)
    phaseB2(nst - 1)
    return OUTS


ROPE_THETA = 10000.0


def rope_table(S, dim, nrows):
    half = dim // 2
    inv = (1.0 / (ROPE_THETA ** (np.arange(0, dim, 2, dtype=np.float32) / np.float32(dim)))).astype(np.float32)
    ang = np.arange(S, dtype=np.float32)[None, :] * inv[:, None]
    cs = np.cos(ang).astype(np.float32)
    sn = np.sin(ang).astype(np.float32)
    out = np.zeros((2, nrows, S), np.float32)
    for r in range(nrows):
        i = (r % dim) % half
        out[0, r] = cs[i]
        out[1, r] = -sn[i] if (r % dim) < half else sn[i]
    return out


def perm_cols(w, dim):
    n = w.shape[-1] // dim
    w4 = w.reshape(w.shape[0], n, 2, dim // 2)
    return np.ascontiguousarray(w4[:, :, ::-1, :]).reshape(w.shape)


def even_inputs(inp, e, h, xT, S):
    w = inp["hy_w_in"][e]
    hs = slice(2 * h * 128, (2 * h + 2) * 128)
    def fm(q, k, dim):
        qc = q[:, hs]; kc = k[:, hs]
        return np.concatenate([qc, perm_cols(qc, dim), kc, perm_cols(kc, dim)], axis=1)
    wfm = np.stack([fm(w[:, 0:512], w[:, 512:1024], 64), fm(w[:, 1536:2048], w[:, 2048:2560], 128)])
    wv = np.stack([w[:, 1024:1536][:, hs], w[:, 2560:3072][:, hs]])
    ropes = np.stack([rope_table(S, 64, 128), rope_table(S, 128, 128)])
    lam128 = np.broadcast_to(inp["diff_lambda"][e].reshape(1, 256), (128, 256))
    gsub = inp["diff_subln"][e].reshape(128, 1)
    return dict(xT=(None if xT is None else np.ascontiguousarray(xT)), wfm=np.ascontiguousarray(wfm, dtype=np.float32),
                wv=np.ascontiguousarray(wv, dtype=np.float32), ropes=ropes,
                lam128=np.ascontiguousarray(lam128, dtype=np.float32), gsub=np.ascontiguousarray(gsub, dtype=np.float32))


def gla_inputs(inp, o, h, xT, S):
    w = inp["gla_w_in"][o]
    hk = slice(2 * h * 128, (2 * h + 2) * 128)
    hv = slice(2 * h * 256, (2 * h + 2) * 256)
    q = w[:, 0:512][:, hk]; k = w[:, 512:1024][:, hk]
    v = w[:, 1024:2048][:, hv]; r = w[:, 2048:3072][:, hv]
    wq = np.concatenate([q, k], axis=1)
    wtm = np.concatenate([k, v, r], axis=1)
    f = lambda a: np.ascontiguousarray(a, dtype=np.float32)
    return dict(xT=(None if xT is None else np.ascontiguousarray(xT)), wq=f(wq), wlr=f(w[:, 3072:3088]), wtm=f(wtm),
                wgu=f(inp["gla_w_gate_up"][o][:, hk]), bg=f(inp["gla_b_gate"][o][hk].reshape(1, 256)),
                gn128=f(np.broadcast_to(inp["gla_norm"][o].reshape(1, 256), (128, 256))))


def row_inputs(inp, l, ho, xres):
    if l % 2 == 0:
        wo = inp["hy_w_out"][l // 2]
        wo = np.concatenate([wo[0:256], wo[512:768], wo[256:512], wo[768:1024]], axis=0)
    else:
        wo = inp["gla_w_out"][l // 2]
    ln = np.stack([inp["ln_mix_g"][l], inp["ln_mix_b"][l], inp["ln_ffn_g"][l], inp["ln_ffn_b"][l]])
    f = lambda a: np.ascontiguousarray(a, dtype=np.float32)
    return dict(ho=(None if ho is None else np.ascontiguousarray(ho)), xres=(None if xres is None else f(xres)), wout=f(wo), w1=f(inp["ffn_w1"][l]), w2=f(inp["ffn_w2"][l]),
                lnp=f(np.broadcast_to(ln[None], (128, 4, 1024))))


def build_even(S, x_bf16, lambda_init):
    nc = bass.Bass("TRN2", target_bir_lowering=False)
    xT = nc.dram_tensor("xT", [1024, S], BF16 if x_bf16 else F32, kind="ExternalInput").ap()
    wfm = nc.dram_tensor("wfm", [2, 1024, 1024], F32, kind="ExternalInput").ap()
    wv = nc.dram_tensor("wv", [2, 1024, 256], F32, kind="ExternalInput").ap()
    ropes = nc.dram_tensor("ropes", [2, 2, 128, S], F32, kind="ExternalInput").ap()
    lam = nc.dram_tensor("lam128", [128, 256], F32, kind="ExternalInput").ap()
    gsub = nc.dram_tensor("gsub", [128, 1], F32, kind="ExternalInput").ap()
    hoT = nc.dram_tensor("hoT", [4, 128, S], BF16, kind="ExternalOutput").ap()
    with contextlib.ExitStack() as st:
        kb = KB(nc, st)
        kb.consts(moba=True)
        outs = stage_even(kb, S, lambda t0: (xT.rearrange("(k p) t -> p k t", p=128)[:, :, t0:t0 + 512], x_bf16), wfm, wv, ropes, lam, gsub, lambda och, Q0: hoT[och][:, Q0:Q0 + 512], lambda_init)
        kb.fw.wait_all("sp", outs)
        kb.fw.emit()
    return nc


def build_gla(S, x_bf16):
    nc = bass.Bass("TRN2", target_bir_lowering=False)
    xT = nc.dram_tensor("xT", [1024, S], BF16 if x_bf16 else F32, kind="ExternalInput").ap()
    wq = nc.dram_tensor("wq", [1024, 512], F32, kind="ExternalInput").ap()
    wlr = nc.dram_tensor("wlr", [1024, 16], F32, kind="ExternalInput").ap()
    wtm = nc.dram_tensor("wtm", [1024, 1280], F32, kind="ExternalInput").ap()
    wgu = nc.dram_tensor("wgu", [16, 256], F32, kind="ExternalInput").ap()
    bg = nc.dram_tensor("bg", [1, 256], F32, kind="ExternalInput").ap()
    gn = nc.dram_tensor("gn128", [128, 256], F32, kind="ExternalInput").ap()
    hoT = nc.dram_tensor("hoT", [4, 128, S], BF16, kind="ExternalOutput").ap()
    with contextlib.ExitStack() as st:
        kb = KB(nc, st)
        kb.consts()
        outs = stage_gla(kb, S, lambda t0: (xT.rearrange("(k p) t -> p k t", p=128)[:, :, t0:t0 + 512], x_bf16), wq, wlr, wtm, wgu, bg, gn, lambda t0: hoT.rearrange("c p t -> p c t")[:, :, t0:t0 + 512])
        kb.fw.wait_all("sp", outs)
        kb.fw.emit()
    return nc


def build_row(T):
    nc = bass.Bass("TRN2", target_bir_lowering=False)
    ho = nc.dram_tensor("ho", [8, 128, T], BF16, kind="ExternalInput").ap()
    xres = nc.dram_tensor("xres", [T, 1024], F32, kind="ExternalInput").ap()
    wout = nc.dram_tensor("wout", [1024, 1024], F32, kind="ExternalInput").ap()
    w1 = nc.dram_tensor("w1", [1024, 4096], F32, kind="ExternalInput").ap()
    w2 = nc.dram_tensor("w2", [4096, 1024], F32, kind="ExternalInput").ap()
    lnp = nc.dram_tensor("lnp", [128, 4, 1024], F32, kind="ExternalInput").ap()
    xout = nc.dram_tensor("xout", [T, 1024], F32, kind="ExternalOutput").ap()
    xTout = nc.dram_tensor("xTout", [1024, T], BF16, kind="ExternalOutput").ap()
    with contextlib.ExitStack() as st:
        kb = KB(nc, st)
        kb.consts()
        outs = stage_row(kb, T, ho, xres, wout, w1, w2, lnp, xout, lambda t0: xTout.rearrange("(k p) t -> p k t", p=128)[:, :, t0:t0 + 512])
        kb.fw.wait_all("sp", outs)
        kb.fw.emit()
    return nc


def kernel_unfused(**inputs):
    inp = {k: np.asarray(v) for k, v in inputs.items()}
    x = inp["x"]
    Bn, S, _ = x.shape
    T = S // 2
    depth = inp["ln_mix_g"].shape[0]
    ncore = 2 * Bn
    cores = list(range(ncore))
    xT = [np.ascontiguousarray(x[b].T) for b in range(Bn)]
    xres = [x[c // 2, (c % 2) * T:(c % 2 + 1) * T] for c in cores]
    row_nc = build_row(T)
    gla_nc = None
    for l in range(depth):
        x_bf16 = l > 0
        if l % 2 == 0:
            lam_init = 0.8 - 0.6 * math.exp(-0.3 * l)
            nc = build_even(S, x_bf16, lam_init)
            maps = [even_inputs(inp, l // 2, c % 2, xT[c // 2], S) for c in cores]
        else:
            if gla_nc is None:
                gla_nc = build_gla(S, x_bf16)
            nc = gla_nc
            maps = [gla_inputs(inp, l // 2, c % 2, xT[c // 2], S) for c in cores]
        res = run_bass_kernel_spmd(nc, maps, core_ids=cores)
        hoT = [res.results[c]["hoT"] for c in cores]
        maps = []
        for c in cores:
            b, h = c // 2, c % 2
            ho = np.concatenate([hoT[2 * b][:, :, h * T:(h + 1) * T], hoT[2 * b + 1][:, :, h * T:(h + 1) * T]], axis=0)
            maps.append(row_inputs(inp, l, ho, xres[c]))
        res = run_bass_kernel_spmd(row_nc, maps, core_ids=cores)
        xres = [res.results[c]["xout"] for c in cores]
        xT = [np.concatenate([res.results[2 * b]["xTout"], res.results[2 * b + 1]["xTout"]], axis=1) for b in range(Bn)]
    out = np.stack([np.concatenate([xres[2 * b], xres[2 * b + 1]], axis=0) for b in range(Bn)])
    return out.astype(np.float32)


import os
CC_COLS = int(os.environ.get("CC_COLS", "0"))


def cc_chunked(fw, name, src, dst, groups, rows, cols):
    if os.environ.get("NOCC"):
        return
    step = CC_COLS if CC_COLS else cols
    for i, c0 in enumerate(range(0, cols, step)):
        fw.cc(Buf(f"{name}_{i}"), "AllGather", src[:, c0:c0 + step], dst[:, c0:c0 + step], groups)


def build_fused(S, depth, ncore):
    T = S // 2
    nc = bass.Bass("TRN2", target_bir_lowering=False)

    def ext(name, shape, dt=F32):
        return nc.dram_tensor(name, list(shape), dt, kind="ExternalInput").ap()

    xT0 = ext("xT0", [1024, S])
    xres0 = ext("xres0", [T, 1024])
    sel = ext("sel", [128, 2])
    ropes = ext("ropes", [2, 2, 128, S])
    W = []
    for l in range(depth):
        d = {}
        if l % 2 == 0:
            d["wfm"] = ext(f"wfm{l}", [2, 1024, 1024])
            d["wv"] = ext(f"wv{l}", [2, 1024, 256])
            d["lam"] = ext(f"lam{l}", [128, 256])
            d["gsub"] = ext(f"gsub{l}", [128, 1])
        else:
            d["wq"] = ext(f"wq{l}", [1024, 512])
            d["wlr"] = ext(f"wlr{l}", [1024, 16])
            d["wtm"] = ext(f"wtm{l}", [1024, 1280])
            d["wgu"] = ext(f"wgu{l}", [16, 256])
            d["bg"] = ext(f"bg{l}", [1, 256])
            d["gn"] = ext(f"gn{l}", [128, 256])
        d["wout"] = ext(f"wout{l}", [1024, 1024])
        d["w1"] = ext(f"w1_{l}", [1024, 4096])
        d["w2"] = ext(f"w2_{l}", [4096, 1024])
        d["lnp"] = ext(f"lnp{l}", [128, 4, 1024])
        W.append(d)
    xout = nc.dram_tensor("xout", [T, 1024], F32, kind="ExternalOutput").ap()
    HC = 1024
    XC_ = 512
    hoT_own = [nc.dram_tensor(f"hoT_own{k}", [512, HC], BF16).ap() for k in range(S // HC)]
    ho_all = [nc.dram_tensor(f"ho_all{k}", [1024, HC], BF16).ap() for k in range(S // HC)]
    xT_own = [nc.dram_tensor(f"xT_own{k}", [1024, XC_], BF16).ap() for k in range(T // XC_)]
    xT_all = [nc.dram_tensor(f"xT_all{k}", [2048, XC_], BF16).ap() for k in range(T // XC_)]
    xres_i = [nc.dram_tensor(f"xres_i{i}", [T, 1024], F32).ap() for i in range(2)]
    groups = [[2 * i, 2 * i + 1] for i in range(ncore // 2)]
    WBF = [dict(wout=nc.dram_tensor(f"woutb{l}", [1024, 1024], BF16).ap(),
                w1=nc.dram_tensor(f"w1b_{l}", [1024, 4096], BF16).ap(),
                w2=nc.dram_tensor(f"w2b_{l}", [4096, 1024], BF16).ap()) for l in range(depth)]

    with contextlib.ExitStack() as st:
        kb = KB(nc, st)
        fw = kb.fw
        kb.consts(moba=True)
        for l in range(depth):
            d = W[l]
            if l == 0:
                def xsrc(t0):
                    return xT0.rearrange("(k p) t -> p k t", p=128)[:, :, t0:t0 + 512], False
            else:
                def xsrc(t0):
                    r, tl = t0 // T, t0 % T
                    return xT_all[tl // XC_][r * 1024:(r + 1) * 1024, :].rearrange("(k p) t -> p k t", p=128), True

            def hodst(och, Q0):
                return hoT_own[Q0 // HC][och * 128:(och + 1) * 128, Q0 % HC:Q0 % HC + 512]

            def hodst4(t0):
                return hoT_own[t0 // HC].rearrange("(c p) t -> p c t", p=128)[:, :, t0 % HC:t0 % HC + 512]

            def hosrc(tok0):
                return ho_all[tok0 // HC].rearrange("(c p) t -> p c t", p=128)[:, :, tok0 % HC:tok0 % HC + 512]

            def xTdst(t0):
                return xT_own[t0 // XC_].rearrange("(k p) t -> p k t", p=128)
            def hook(l=l, d=d):
                cb = Buf(f"wconv{l}")
                for nm in ("wout", "w1", "w2"):
                    src, dst = d[nm], WBF[l][nm]
                    for r0 in range(0, src.shape[0], 256):
                        fw.dma(cb, dst[r0:r0 + 256, :], src[r0:r0 + 256, :], queue="pool")

            with fw.stage():
                if l % 2 == 0:
                    lam_init = 0.8 - 0.6 * math.exp(-0.3 * l)
                    stage_even(kb, S, xsrc, d["wfm"], d["wv"], ropes, d["lam"], d["gsub"], hodst, lam_init, hook=hook)
                else:
                    stage_gla(kb, S, xsrc, d["wq"], d["wlr"], d["wtm"], d["wgu"], d["bg"], d["gn"], hodst4, hook=hook)
            with fw.stage():
                for k in range(S // HC):
                    fw.cc(Buf(f"ccA{l}_{k}"), "AllGather", hoT_own[k], ho_all[k], groups)
            xin = xres0 if l == 0 else xres_i[(l - 1) % 2]
            xo = xout if l == depth - 1 else xres_i[l % 2]
            with fw.stage():
                outs = stage_row2(kb, T, xin, WBF[l]["wout"], WBF[l]["w1"], WBF[l]["w2"], d["lnp"], xo, xTdst,
                                  hosrc, sel, T)
                if l == depth - 1:
                    fw.wait_all("sp", outs)
            if l < depth - 1:
                with fw.stage():
                    for k in range(T // XC_):
                        fw.cc(Buf(f"ccB{l}_{k}"), "AllGather", xT_own[k], xT_all[k], groups)
        fw.emit()
        print("instr counts", {k: len(v) for k, v in fw.q.items()}, "sems", fw.nsem, flush=True)
    return nc


def fused_inputs(inp, c, S, depth):
    b, h = c // 2, c % 2
    T = S // 2
    x = inp["x"]
    m = dict(xT0=np.ascontiguousarray(x[b, :S].T), xres0=np.ascontiguousarray(x[b, h * T:(h + 1) * T]),
             sel=np.ascontiguousarray(np.broadcast_to(np.eye(2, dtype=np.float32)[h][None], (128, 2))))
    for l in range(depth):
        if l % 2 == 0:
            e = even_inputs(inp, l // 2, h, None, S)
            m["ropes"] = e["ropes"]
            m[f"wfm{l}"], m[f"wv{l}"], m[f"lam{l}"], m[f"gsub{l}"] = e["wfm"], e["wv"], e["lam128"], e["gsub"]
        else:
            g = gla_inputs(inp, l // 2, h, None, S)
            for k in ("wq", "wlr", "wtm", "wgu", "bg"):
                m[f"{k}{l}"] = g[k]
            m[f"gn{l}"] = g["gn128"]
        r = row_inputs(inp, l, None, None)
        m[f"wout{l}"], m[f"w1_{l}"], m[f"w2_{l}"], m[f"lnp{l}"] = r["wout"], r["w1"], r["w2"], r["lnp"]
    return m


def kernel(**inputs):
    inp = {k: np.asarray(v) for k, v in inputs.items()}
    x = inp["x"]
    Bn, S, _ = x.shape
    depth = inp["ln_mix_g"].shape[0]
    ncore = 2 * Bn
    nc = build_fused(S, depth, ncore)
    maps = [fused_inputs(inp, c, S, depth) for c in range(ncore)]
    res = run_bass_kernel_spmd(nc, maps, core_ids=list(range(ncore)))
    out = np.stack([np.concatenate([res.results[2 * b]["xout"], res.results[2 * b + 1]["xout"]], axis=0)
                    for b in range(Bn)])
    return out.astype(np.float32)
```

```python
import contextlib
import numpy as np
import concourse.bass as bass
import concourse.mybir as mybir
from concourse.bass_utils import run_bass_kernel_spmd

F32 = mybir.dt.float32
BF16 = mybir.dt.bfloat16
AF = mybir.ActivationFunctionType
ALU = mybir.AluOpType
AX = mybir.AxisListType

SEM_LIMIT = 30000


class Ent:
    __slots__ = ("stream", "seq", "flag", "hw", "val", "n", "item")

    def __init__(self, stream, seq, n):
        self.stream = stream
        self.seq = seq
        self.flag = False
        self.hw = None
        self.val = 0
        self.n = n
        self.item = None


class Stream:
    def __init__(self, name, inorder):
        self.name = name
        self.inorder = inorder
        self.ents = []

    def new(self, n):
        e = Ent(self, len(self.ents) + 1, n)
        self.ents.append(e)
        return e


class Item:
    __slots__ = ("waits", "fn", "ent")

    def __init__(self, waits, fn, ent):
        self.waits = waits
        self.fn = fn
        self.ent = ent


class Buf:
    def __init__(self, name=""):
        self.name = name
        self.w = {}
        self.r = {}
        self.ds = None


def _merge(dst, src):
    for k, e in src.items():
        o = dst.get(k)
        if o is None or o.seq < e.seq:
            dst[k] = e


class FW:
    def __init__(self, nc, stack):
        self.nc = nc
        self.stack = stack
        self.engs = ["sp", "pe", "act", "dve", "pool"]
        self.q = {k: [] for k in self.engs}
        self.es = {k: Stream(k, True) for k in self.engs}
        self.seen = {k: {} for k in self.engs}
        self.dstreams = []
        self.free_streams = []
        self.live_streams = []
        self.nsem = 0
        self.alloc_stack = stack

    def sb(self, name, shape, dtype):
        self.ntens = getattr(self, "ntens", 0) + 1
        name = f"{name}_{self.ntens}"
        return self.alloc_stack.enter_context(self.nc.sbuf_tensor(name, list(shape), dtype))

    def ps(self, name, shape, dtype=F32):
        self.ntens = getattr(self, "ntens", 0) + 1
        name = f"{name}_{self.ntens}"
        return self.alloc_stack.enter_context(self.nc.psum_tensor(name, list(shape), dtype))

    def dstream(self, name):
        s = Stream(name, False)
        self.dstreams.append(s)
        return s

    def _hw(self, name):
        self.nsem += 1
        return self.stack.enter_context(self.nc.semaphore(f"s{self.nsem}_{name}"))

    def _waits(self, eng, reads, writes):
        raw = {}
        for b in reads:
            _merge(raw, b.w)
        oth = {}
        for b in writes:
            _merge(oth, b.w)
            _merge(oth, b.r)
        own = self.es[eng]
        need = dict(raw)
        for k, e in oth.items():
            if e.stream is own and eng == "pe":
                continue
            o = need.get(k)
            if o is None or o.seq < e.seq:
                need[k] = e
        waits = []
        seen = self.seen[eng]
        for k, e in need.items():
            if seen.get(k, 0) >= e.seq:
                continue
            seen[k] = e.seq
            e.flag = True
            waits.append(e)
        return waits

    def _commit(self, ent, reads, writes):
        k = id(ent.stream)
        for b in writes:
            b.w = {k: ent}
            b.r = {}
        for b in reads:
            o = b.r.get(k)
            if o is None or o.seq < ent.seq:
                b.r[k] = ent

    def op(self, eng, fn, reads=(), writes=()):
        waits = self._waits(eng, reads, writes)
        ent = self.es[eng].new(1)
        it = Item(waits, fn, ent)
        ent.item = it
        self.q[eng].append(it)
        self._commit(ent, reads, writes)
        return ent

    def dma(self, sbuf, out, in_, reads=(), writes=(), queue="sp", **kw):
        stream = getattr(sbuf, "ds", None)
        if stream is None:
            stream = sbuf.ds = self._take_stream("d" + sbuf.name)
        waits = self._waits(queue, reads, writes)
        ent = stream.new(16)
        ent.flag = True
        it = Item(waits, lambda e: e.dma_start(out=out, in_=in_, **kw), ent)
        ent.item = it
        self.q[queue].append(it)
        self._commit(ent, reads, writes)
        return ent

    def _take_stream(self, name):
        if self.free_streams:
            st = self.free_streams.pop()
        else:
            st = self.dstream(name)
        self.live_streams.append(st)
        return st

    def cc(self, buf, kind, in_ap, out_ap, groups, reads=(), writes=()):
        stream = getattr(buf, "ds", None)
        if stream is None:
            stream = buf.ds = self._take_stream("cc" + buf.name)
        waits = self._waits("pool", reads, writes)
        ent = stream.new(1)
        ent.flag = True
        it = Item(waits, lambda e: e.collective_compute(kind, op=ALU.bypass, replica_groups=groups,
                                                        ins=[in_ap.opt()], outs=[out_ap.opt()]), ent)
        ent.item = it
        self.q["pool"].append(it)
        self._commit(ent, reads, writes)
        return ent

    def barrier(self):
        toks = []
        for k in self.engs:
            if self.es[k].ents:
                toks.append(self.es[k].ents[-1])
        for st in self.live_streams:
            if st.ents:
                toks.append(st.ents[-1])
        for eng in self.engs:
            waits = []
            seen = self.seen[eng]
            for e in toks:
                if e.stream is self.es[eng]:
                    continue
                k = id(e.stream)
                if seen.get(k, 0) >= e.seq:
                    continue
                seen[k] = e.seq
                e.flag = True
                waits.append(e)
            self.q[eng].append(Item(waits, None, None))
        self.free_streams.extend(self.live_streams)
        self.live_streams = []

    @contextlib.contextmanager
    def stage(self):
        outer = self.alloc_stack
        with contextlib.ExitStack() as sub:
            self.alloc_stack = sub
            try:
                yield
            finally:
                self.alloc_stack = outer
            self.barrier()

    def wait_all(self, eng, bufs):
        waits = self._waits(eng, bufs, ())
        self.q[eng].append(Item(waits, None, None))

    def emit(self):
        for s in list(self.es.values()) + self.dstreams:
            hw = None
            val = 0
            prev = None
            for e in s.ents:
                if not e.flag:
                    continue
                if hw is None or val + e.n > SEM_LIMIT:
                    if hw is not None and not s.inorder:
                        e.item.waits.append(prev)
                    hw = self._hw(s.name)
                    val = 0
                val += e.n
                e.hw = hw
                e.val = val
                prev = e
        nc = self.nc

        def mk(name):
            items = self.q[name]

            def body(e):
                for it in items:
                    for w in it.waits:
                        e.wait_ge(w.hw, w.val)
                    if it.fn is not None:
                        ins = it.fn(e)
                        if it.ent.flag:
                            ins.then_inc(it.ent.hw, it.ent.n)

            return body

        with nc.Block() as block:
            block.sync(mk("sp"))
            block.tensor(mk("pe"))
            block.scalar(mk("act"))
            block.vector(mk("dve"))
            block.gpsimd(mk("pool"))


import math


D = 1024
ALPHA = 8.0 ** 0.25
NEG = -30000.0


class KB:
    def __init__(self, nc, stack):
        self.nc = nc
        self.fw = FW(nc, stack)
        self.nps = 0

    def sb(self, name, shape, dt):
        return self.fw.sb(name, shape, dt)

    def bank(self, dt=F32):
        self.nps += 1
        n = 512 if dt == F32 else 1024
        return self.fw.ps(f"ps{self.nps}", [128, n], dt), Buf(f"ps{self.nps}")

    def mm(self, out, lhsT, rhs, start, stop, reads, writes):
        return self.fw.op("pe", lambda e: e.matmul(out, lhsT=lhsT, rhs=rhs, start=start, stop=stop,
                                                   skip_group_check=True), reads, writes)

    def tr(self, out, in_, ident, reads, writes):
        return self.fw.op("pe", lambda e: e.transpose(out=out, in_=in_, identity=ident), reads, writes)

    def act(self, out, in_, func, reads, writes, **kw):
        return self.fw.op("act", lambda e: e.activation(out=out, in_=in_, func=func, **kw), reads, writes)

    def tt(self, eng, out, in0, in1, op, reads, writes):
        return self.fw.op(eng, lambda e: e.tensor_tensor(out=out, in0=in0, in1=in1, op=op), reads, writes)

    def ts(self, eng, out, in0, s1, s2, op0, op1, reads, writes):
        if op1 is None:
            return self.fw.op(eng, lambda e: e.tensor_scalar(out=out, in0=in0, scalar1=s1, scalar2=None, op0=op0),
                              reads, writes)
        return self.fw.op(eng, lambda e: e.tensor_scalar(out=out, in0=in0, scalar1=s1, scalar2=s2, op0=op0, op1=op1),
                          reads, writes)

    def stt(self, eng, out, in0, scalar, in1, op0, op1, reads, writes):
        return self.fw.op(eng, lambda e: e.scalar_tensor_tensor(out=out, in0=in0, scalar=scalar, in1=in1,
                                                                op0=op0, op1=op1), reads, writes)

    def cp(self, eng, out, in_, reads, writes):
        if eng == "act":
            return self.fw.op("act", lambda e: e.copy(out=out, in_=in_), reads, writes)
        return self.fw.op(eng, lambda e: e.tensor_copy(out=out, in_=in_), reads, writes)

    def memset(self, eng, ap, val, writes):
        return self.fw.op(eng, lambda e: e.memset(ap, val), (), writes)

    def asel(self, out, in_, pattern, cmp, fill, base, cm, bufs):
        return self.fw.op("pool", lambda e: e.affine_select(out=out, in_=in_, pattern=pattern, compare_op=cmp,
                                                            fill=fill, base=base, channel_multiplier=cm), bufs, bufs)

    def consts(self, moba=False):
        c = {}
        B = Buf("consts")
        c["buf"] = B
        ident = self.sb("ident", [128, 128], BF16)
        self.memset("pool", ident[:], 1.0, [B])
        self.asel(ident[:], ident[:], [[-1, 128]], ALU.is_equal, 0.0, 0, 1, [B])
        c["ident"] = ident
        negm = self.sb("negm", [128, 128], BF16)
        self.memset("pool", negm[:], 0.0, [B])
        self.asel(negm[:], negm[:], [[1, 128]], ALU.is_ge, NEG, 0, -1, [B])
        c["negm"] = negm
        tri = self.sb("tri", [128, 128], F32)
        self.memset("pool", tri[:], 1.0, [B])
        self.asel(tri[:], tri[:], [[1, 128]], ALU.is_ge, 0.0, 0, -1, [B])
        c["tri"] = tri
        ui = self.sb("uincl", [128, 128], F32)
        self.memset("pool", ui[:], -1.0 / 16.0, [B])
        self.asel(ui[:], ui[:], [[1, 128]], ALU.is_ge, 0.0, 0, -1, [B])
        c["uincl"] = ui
        us = self.sb("ustr", [128, 128], F32)
        self.memset("pool", us[:], -1.0 / 16.0, [B])
        self.asel(us[:], us[:], [[-1, 128]], ALU.is_gt, 0.0, 0, 1, [B])
        c["ustr"] = us
        ones_f = self.sb("ones_f", [128, 128], F32)
        self.memset("pool", ones_f[:], 1.0, [B])
        c["ones_f"] = ones_f
        ones_b = self.sb("ones_b", [128, 128], BF16)
        self.memset("pool", ones_b[:], 1.0, [B])
        c["ones_b"] = ones_b
        e01 = self.sb("e01", [128, 4], BF16)
        self.memset("pool", e01[:], 0.0, [B])
        self.memset("pool", e01[:, 0:1], 1.0, [B])
        self.memset("pool", e01[:, 3:4], 1.0, [B])
        c["e01"] = e01
        e01f = self.sb("e01f", [128, 4], F32)
        self.memset("pool", e01f[:], 0.0, [B])
        self.memset("pool", e01f[:, 0:1], 1.0, [B])
        self.memset("pool", e01f[:, 3:4], 1.0, [B])
        c["e01f"] = e01f
        sel2 = self.sb("sel2", [2, 256], F32)
        self.memset("pool", sel2[:], 1.0, [B])
        self.asel(sel2[:, 0:128], sel2[:, 0:128], [[0, 128]], ALU.is_equal, 0.0, 0, 1, [B])
        self.asel(sel2[:, 128:256], sel2[:, 128:256], [[0, 128]], ALU.is_equal, 0.0, -1, 1, [B])
        c["sel2"] = sel2
        if not moba:
            self.c = c
            return c
        esel = self.sb("esel", [32, 32 * 128], BF16)
        self.memset("pool", esel[:], NEG, [B])
        ev = esel[:].rearrange("p (n k) -> p n k", k=128)
        self.asel(ev, ev, [[-1, 32], [0, 128]], ALU.is_equal, 0.0, 0, 1, [B])
        c["esel"] = esel
        self.c = c
        return c


def stage_row(kb, T, ho, xres, wout, w1, w2, lnp, xout, xTout, dbg=None, ho_sel=None, w_bf16=False):
    fw = kb.fw
    c = kb.c
    CB = c["buf"]
    wo_sb = kb.sb("wo_sb", [128, 8, 1024], BF16)
    WO = Buf("wo")
    ln_sb = kb.sb("ln_sb", [128, 4, 1024], F32)
    LNB = Buf("ln")
    wq_ = "sp" if w_bf16 else "pool"
    fw.dma(WO, wo_sb[:], wout.rearrange("(c p) n -> p c n", p=128), writes=[WO], queue=wq_)
    fw.dma(LNB, ln_sb[:], lnp, writes=[LNB])
    NW = 2
    w1b = [kb.sb(f"w1b{i}", [128, 8, 1024], BF16) for i in range(NW)]
    w2b = [kb.sb(f"w2b{i}", [128, 8, 1024], BF16) for i in range(NW)]
    W1B = [Buf() for _ in range(NW)]
    W2B = [Buf() for _ in range(NW)]
    hoc = kb.sb("hoc", [128, 8, 512], BF16)
    HOC = Buf("hoc")
    if ho_sel is not None:
        hoa = hoc
        hob = kb.sb("hob", [128, 8, 512], BF16)
        sel_sb = kb.sb("sel_sb", [128, 2], F32)
        HOA, HOBB, SELB = HOC, Buf("hob"), Buf("selb")
        fw.dma(SELB, sel_sb[:], ho_sel[1], writes=[SELB])
    y = kb.sb("y", [128, 4, 1024], F32)
    Y = [Buf(f"y{i}") for i in range(4)]
    acc = kb.sb("acc", [128, 4, 1024], F32)
    ACC = [Buf(f"acc{i}") for i in range(4)]
    xb = kb.sb("xb", [128, 1024], BF16)
    XB = Buf()
    x1T = kb.sb("x1T", [128, 8, 512], BF16)
    X1T = Buf("x1T")
    xTo, XTO = x1T, X1T
    hsq = [kb.sb(f"hsq{i}", [128, 8, 512], BF16) for i in range(2)]
    HSQ = [[Buf() for _ in range(8)] for _ in range(2)]
    rl = [kb.sb(f"rl{i}", [128, 512], F32) for i in range(2)]
    RL = [Buf() for _ in range(2)]
    st6 = kb.sb("st6", [128, 2, 6], F32)
    mv = kb.sb("mv", [128, 2], F32)
    rstd = kb.sb("rstd", [128, 2], F32)
    STB = Buf()
    G = [kb.bank() for _ in range(4)]
    TR = [kb.bank(BF16) for _ in range(2)]
    OUTS = []

    def nout():
        b = Buf("o")
        OUTS.append(b)
        return b
    gi = [0]

    def nextG():
        g = G[gi[0] % 4]
        gi[0] += 1
        return g

    ti = [0]

    def layer_norm(buf_ap, BUFS, j, gidx):
        v = buf_ap[:, j, :]
        for hh in range(2):
            fw.op("dve", lambda e, hh=hh: e.bn_stats(out=st6[:, hh, :], in_=buf_ap[:, j, hh * 512:(hh + 1) * 512]),
                  [BUFS[j]], [STB])
        fw.op("dve", lambda e: e.bn_aggr(out=mv[:], in_=st6[:].rearrange("p a b -> p (a b)")), [STB], [STB])
        kb.act(rstd[:, 0:1], mv[:, 1:2], AF.Sqrt, [STB], [STB], bias=1e-5, scale=1.0)
        fw.op("dve", lambda e: e.reciprocal(out=rstd[:, 1:2], in_=rstd[:, 0:1]), [STB], [STB])
        kb.ts("dve", v, v, mv[:, 0:1], rstd[:, 1:2], ALU.subtract, ALU.mult, [STB, BUFS[j]], [BUFS[j]])
        kb.tt("pool", v, v, ln_sb[:, gidx, :], ALU.mult, [BUFS[j], LNB], [BUFS[j]])
        kb.tt("pool", v, v, ln_sb[:, gidx + 1, :], ALU.add, [BUFS[j], LNB], [BUFS[j]])

    def to_T(src_ap, SRC, j, dstT, DST):
        kb.cp("act", xb[:], src_ap[:, j, :], [SRC[j]], [XB])
        trp, TRB = TR[ti[0] % 2]
        ti[0] += 1
        for k in range(8):
            kb.tr(trp[:, k * 128:(k + 1) * 128], xb[:, k * 128:(k + 1) * 128], c["ident"][:], [XB, CB], [TRB])
        kb.cp("dve", dstT[:, :, j * 128:(j + 1) * 128], trp[:].rearrange("p (k t) -> p k t", t=128), [TRB], [DST])

    nst = T // 512
    for st in range(nst):
        t0 = st * 512
        if ho_sel is None:
            fw.dma(HOC, hoc[:], ho.rearrange("c p t -> p c t")[:, :, t0:t0 + 512], writes=[HOC])
        else:
            hosrc, Thalf = ho_sel[0], ho_sel[2]
            fw.dma(HOA, hoa[:], hosrc(t0), writes=[HOA])
            fw.dma(HOBB, hob[:], hosrc(Thalf + t0), writes=[HOBB])
            kb.act(hoa[:], hoa[:], AF.Copy, [HOA, SELB], [HOA], scale=sel_sb[:, 0:1])
            kb.stt("dve", hoa[:], hob[:], sel_sb[:, 1:2], hoa[:], ALU.mult, ALU.add, [HOBB, HOA, SELB], [HOA])
        fw.dma(Y[0], y[:], xres[t0:t0 + 512, :].rearrange("(j p) d -> p j d", p=128), writes=Y)
        for j in range(4):
            for nb in range(2):
                g, GB = nextG()
                for cc in range(8):
                    kb.mm(g[:], hoc[:, cc, j * 128:(j + 1) * 128], wo_sb[:, cc, nb * 512:(nb + 1) * 512],
                          cc == 0, cc == 7, [HOC, WO], [GB])
                kb.stt("dve", y[:, j, nb * 512:(nb + 1) * 512], y[:, j, nb * 512:(nb + 1) * 512], ALPHA, g[:],
                       ALU.mult, ALU.add, [Y[j], GB], [Y[j]])
            layer_norm(y, Y, j, 0)
            to_T(y, Y, j, x1T, X1T)
        if dbg is not None:
            fw.dma(Y[0], dbg[0], y[:], reads=Y, writes=[nout()])
            fw.dma(X1T, dbg[1], x1T[:], reads=[X1T], writes=[nout()])
        for fb in range(4):
            wi = (st * 4 + fb) % NW
            fw.dma(W1B[wi], w1b[wi][:], w1.rearrange("(k p) f -> p k f", p=128)[:, :, fb * 1024:(fb + 1) * 1024],
                   writes=[W1B[wi]], queue=wq_)
            fw.dma(W2B[wi], w2b[wi][:], w2[fb * 1024:(fb + 1) * 1024, :].rearrange("(c p) d -> p c d", p=128),
                   writes=[W2B[wi]], queue=wq_)
            hi = fb % 2
            for fc in range(8):
                g, GB = nextG()
                for k in range(8):
                    kb.mm(g[:], w1b[wi][:, k, fc * 128:(fc + 1) * 128], x1T[:, k, :], k == 0, k == 7,
                          [W1B[wi], X1T], [GB])
                ri = fc % 2
                kb.act(rl[ri][:], g[:], AF.Relu, [GB], [RL[ri]])
                kb.tt("pool", hsq[hi][:, fc, :], rl[ri][:], rl[ri][:], ALU.mult, [RL[ri]], [HSQ[hi][fc]])
            for j in range(4):
                for nb in range(2):
                    g, GB = nextG()
                    for fc in range(8):
                        kb.mm(g[:], hsq[hi][:, fc, j * 128:(j + 1) * 128], w2b[wi][:, fc, nb * 512:(nb + 1) * 512],
                              fc == 0, fc == 7, [HSQ[hi][fc], W2B[wi]], [GB])
                    a = acc[:, j, nb * 512:(nb + 1) * 512]
                    if fb == 0:
                        kb.cp("dve", a, g[:], [GB], [ACC[j]])
                    else:
                        kb.tt("dve", a, a, g[:], ALU.add, [GB, ACC[j]], [ACC[j]])
        if dbg is not None:
            fw.dma(ACC[0], dbg[2], acc[:], reads=ACC, writes=[nout()])
            fw.dma(HSQ[1][0], dbg[3], hsq[1][:], reads=HSQ[1], writes=[nout()])
        for j in range(4):
            kb.stt("dve", acc[:, j, :], y[:, j, :], ALPHA, acc[:, j, :], ALU.mult, ALU.add, [Y[j], ACC[j]], [ACC[j]])
            layer_norm(acc, ACC, j, 2)
            to_T(acc, ACC, j, xTo, XTO)
        fw.dma(ACC[0], xout[t0:t0 + 512, :].rearrange("(j p) d -> p j d", p=128), acc[:], reads=ACC, writes=[nout()])
        fw.dma(XTO, xTout(t0), xTo[:], reads=[XTO], writes=[nout()])
    return OUTS


def stage_even(kb, S, xsrc, wfm, wv, ropes, lam128, gsub, hodst, lambda_init, hook=None):
    fw = kb.fw
    c = kb.c
    CB = c["buf"]
    nkt = S // 128
    nqb = S // 512
    nblk = S // 256
    QT = [kb.sb(f"QT{i}", [128, S], BF16) for i in range(2)]
    KT = [kb.sb(f"KT{i}", [128, S], BF16) for i in range(2)]
    V = [kb.sb(f"V{i}", [128, nkt, 128], BF16) for i in range(2)]
    QTB = [Buf() for _ in range(2)]
    KTB = [Buf() for _ in range(2)]
    VB = [Buf() for _ in range(2)]
    wfm_sb = kb.sb("wfm_sb", [128, 8, 1024], BF16)
    WFM = Buf("wfm")
    wv_sb = kb.sb("wv_sb", [128, 8, 256], BF16)
    WV = Buf("wv")
    xc = [kb.sb(f"xc{i}", [128, 8, 512], BF16) for i in range(2)]
    XC = [Buf(f"xc{i}") for i in range(2)]
    rp = [kb.sb(f"rp{i}", [128, 2, 512], F32) for i in range(2)]
    RP = [Buf(f"rp{i}") for i in range(2)]
    F = [kb.sb(f"F{i}", [128, 512], F32) for i in range(4)]
    FBUF = [Buf() for _ in range(4)]
    PT = [kb.sb(f"PT{i}", [128, 512], BF16) for i in range(4)]
    PTB = [Buf() for _ in range(4)]
    obf = [kb.sb(f"obf{i}", [128, 512], BF16) for i in range(2)]
    OBF = [Buf(f"obf{i}") for i in range(2)]
    rr = kb.sb("rr", [2, 512], F32)
    RR = Buf()
    accs = [[kb.sb(f"accs{p}_{i}", [128, 512], F32) for i in range(4)] for p in range(2)]
    ACCB = [[Buf() for _ in range(4)] for _ in range(2)]
    Os = [[kb.sb(f"Os{p}_{i}", [128, 512], F32) for i in range(2)] for p in range(2)]
    OSB = [[Buf() for _ in range(2)] for _ in range(2)]
    lam_sb = kb.sb("lam_sb", [128, 256], F32)
    gs_sb = kb.sb("gs_sb", [128, 1], F32)
    sm = kb.sb("sm_e", [128, 8], F32)
    LAM = Buf("lam")
    kmf = kb.sb("kmf", [128, 32], F32)
    kmb = [kb.sb(f"kmb{i}", [128, 32], BF16) for i in range(2)]
    KMB = [Buf() for _ in range(2)]
    Gs = kb.sb("Gs", [128, 32], F32)
    top8 = kb.sb("top8", [128, 8], F32)
    nots = kb.sb("nots", [128, 32], BF16)
    GSB = Buf()
    biasT = kb.sb("biasT", [32, 512], BF16)
    BIAS = Buf()
    SBK = [kb.bank() for _ in range(4)]
    O1, O1B = kb.bank()
    O2, O2B = kb.bank()
    SUMP, SUMB = kb.bank()
    FBK, FBB = kb.bank()
    OUTS = []

    fw.dma(LAM, lam_sb[:], lam128, writes=[LAM])
    fw.dma(LAM, gs_sb[:], gsub, writes=[LAM])
    kb.tt("dve", F[0][:, 0:64], lam_sb[:, 0:64], lam_sb[:, 64:128], ALU.mult, [LAM], [FBUF[0]])
    kb.tt("dve", F[0][:, 64:128], lam_sb[:, 128:192], lam_sb[:, 192:256], ALU.mult, [LAM], [FBUF[0]])
    fw.op("dve", lambda e: e.reduce_sum(out=sm[:, 0:1], in_=F[0][:, 0:64], axis=AX.X), [FBUF[0]], [LAM])
    fw.op("dve", lambda e: e.reduce_sum(out=sm[:, 1:2], in_=F[0][:, 64:128], axis=AX.X), [FBUF[0]], [LAM])
    kb.act(sm[:, 2:4], sm[:, 0:2], AF.Exp, [LAM], [LAM])
    kb.stt("dve", sm[:, 4:5], sm[:, 3:4], -float(lambda_init), sm[:, 2:3], ALU.add, ALU.subtract, [LAM], [LAM])

    def inproj(typ):
        fw.dma(WFM, wfm_sb[:], wfm[typ].rearrange("(k p) n -> p k n", p=128), writes=[WFM], queue="pool")
        fw.dma(WV, wv_sb[:], wv[typ].rearrange("(k p) n -> p k n", p=128), writes=[WV], queue="pool")
        gi = 0
        for cch in range(S // 512):
            t0 = cch * 512
            xi = cch % 2
            src, x_bf16 = xsrc(t0)
            fw.dma(XC[xi], xc[xi][:], src, writes=[XC[xi]], queue=("sp" if x_bf16 else "pool"))
            fw.dma(RP[xi], rp[xi][:], ropes[typ].rearrange("a p t -> p a t")[:, :, t0:t0 + 512], writes=[RP[xi]])
            for g in range(2):
                for hd in range(2):
                    dst, DB = (QT[hd], QTB[hd]) if g == 0 else (KT[hd], KTB[hd])
                    po, POB = SBK[gi % 4]
                    pp, PPB = SBK[(gi + 1) % 4]
                    gi += 2
                    fo = g * 4 + hd
                    fp = g * 4 + 2 + hd
                    for k in range(8):
                        kb.mm(po[:], wfm_sb[:, k, fo * 128:(fo + 1) * 128], xc[xi][:, k, :], k == 0, k == 7,
                              [WFM, XC[xi]], [POB])
                    for k in range(8):
                        kb.mm(pp[:], wfm_sb[:, k, fp * 128:(fp + 1) * 128], xc[xi][:, k, :], k == 0, k == 7,
                              [WFM, XC[xi]], [PPB])
                    fa = (g * 2 + hd) % 2 * 2
                    kb.tt("dve", F[fa][:], po[:], rp[xi][:, 0, :], ALU.mult, [POB, RP[xi]], [FBUF[fa]])
                    kb.tt("dve", F[fa + 1][:], pp[:], rp[xi][:, 1, :], ALU.mult, [PPB, RP[xi]], [FBUF[fa + 1]])
                    kb.tt("pool", dst[:, t0:t0 + 512], F[fa][:], F[fa + 1][:], ALU.add, [FBUF[fa], FBUF[fa + 1]], [DB])
            for sub in range(4):
                pv, PVB = SBK[gi % 4]
                gi += 1
                for k in range(8):
                    kb.mm(pv[:, 0:256], xc[xi][:, k, sub * 128:(sub + 1) * 128], wv_sb[:, k, :], k == 0, k == 7,
                          [WV, XC[xi]], [PVB])
                for hd in range(2):
                    kb.cp("act", V[hd][:, cch * 4 + sub, :], pv[:, hd * 128:(hd + 1) * 128], [PVB], [VB[hd]])

    def attention(typ, hd):
        nmap = 2 if typ == 0 else 1
        scale = 64.0 ** -0.5 if typ == 0 else 128.0 ** -0.5
        och = typ * 2 + hd
        for qb in range(nqb):
            Q0 = qb * 512
            if typ == 1:
                for j in range(4):
                    q0 = Q0 + j * 128
                    ob = q0 // 256
                    kb.memset("pool", nots[:], 0.0, [GSB])
                    if ob > 0:
                        kb.memset("pool", Gs[:], -1e30, [GSB])
                        gp, GPB = SBK[j % 4]
                        kb.mm(gp[:, 0:32], QT[hd][:, q0:q0 + 128], kmb[hd][:, 0:32], True, True,
                              [QTB[hd], KMB[hd]], [GPB])
                        kb.cp("dve", Gs[:, 0:ob], gp[:, 0:ob], [GPB, GSB], [GSB])
                        fw.op("dve", lambda e: e.max(out=top8[:], in_=Gs[:]), [GSB], [GSB])
                        kb.ts("dve", nots[:, 0:ob], Gs[:, 0:ob], top8[:, 2:3], None, ALU.is_lt, None, [GSB], [GSB])
                    kb.mm(FBK[0:32, j * 128:(j + 1) * 128], nots[:, 0:32], c["ident"][:], True, True, [GSB, CB], [FBB])
                kb.cp("act", biasT[:], FBK[0:32, 0:512], [FBB], [BIAS])
            items = [(kt, m) for kt in range((Q0 + 512) // 128) for m in range(nmap)]
            n = len(items)
            OB_ = [(O1, O1B), (O2, O2B)]
            last_kt = (Q0 + 512) // 128 - 1

            def issueS(i):
                kt, m = items[i]
                K0 = kt * 128
                o = max(0, K0 - Q0)
                diag = K0 >= Q0
                sp_, SPB = SBK[i % 4]
                if typ == 0:
                    kb.mm(sp_[:, o:512], KT[hd][m * 64:(m + 1) * 64, K0:K0 + 128],
                          QT[hd][m * 64:(m + 1) * 64, Q0 + o:Q0 + 512], True, not diag, [KTB[hd], QTB[hd]], [SPB])
                else:
                    kb.mm(sp_[:, o:512], KT[hd][:, K0:K0 + 128], QT[hd][:, Q0 + o:Q0 + 512], True, False,
                          [KTB[hd], QTB[hd]], [SPB])
                    nb_ = K0 // 256
                    kb.mm(sp_[:, o:512], c["esel"][0:32, nb_ * 128:(nb_ + 1) * 128], biasT[0:32, o:512], False,
                          not diag, [CB, BIAS], [SPB])
                if diag:
                    kb.mm(sp_[:, o:o + 128], c["ident"][:], c["negm"][:], False, True, [CB], [SPB])
                kb.act(PT[i % 4][:, o:512], sp_[:, o:512], AF.Exp, [SPB], [PTB[i % 4]], scale=scale)

            def issuePV(i):
                kt, m = items[i]
                K0 = kt * 128
                o = max(0, K0 - Q0)
                Op, OpB = OB_[m]
                kb.mm(Op[:, o:512], V[hd][:, kt, :], PT[i % 4][:, o:512], kt == 0, kt == last_kt,
                      [VB[hd], PTB[i % 4]], [OpB])
                eng = "pool" if i % 3 == 2 else "dve"
                ai = (1 if eng == "pool" else 0) * 2 + m
                A_, AB_ = accs[qcount[0] % 2], ACCB[qcount[0] % 2]
                if not acc_used[ai]:
                    acc_used[ai] = True
                    if o > 0:
                        kb.memset(eng, A_[ai][:, 0:o], 0.0, [AB_[ai]])
                    kb.cp(eng, A_[ai][:, o:512], PT[i % 4][:, o:512], [PTB[i % 4]], [AB_[ai]])
                else:
                    kb.tt(eng, A_[ai][:, o:512], A_[ai][:, o:512], PT[i % 4][:, o:512], ALU.add,
                          [PTB[i % 4], AB_[ai]], [AB_[ai]])

            acc_used = [False] * 4
            if typ == 0:
                for g in range(n // 2 + 1):
                    if g < n // 2:
                        issueS(2 * g)
                        issueS(2 * g + 1)
                    if g >= 1:
                        issuePV(2 * g - 2)
                        issuePV(2 * g - 1)
                    if g % 3 == 2:
                        defer_tick()
            else:
                LA = 2
                for i in range(n + LA):
                    if i < n:
                        issueS(i)
                    if i - LA >= 0:
                        issuePV(i - LA)
                    if i % 4 == 3:
                        defer_tick()
            flush()
            par = qcount[0] % 2
            qcount[0] += 1
            nr = 2 if typ == 0 else 1
            kb.cp("act", Os[par][0][:], O1[:], [O1B], [OSB[par][0]])
            if typ == 0:
                kb.cp("dve", Os[par][1][:], O2[:], [O2B], [OSB[par][1]])
            used = [ai for ai in range(4) if acc_used[ai]]
            pending.extend(make_steps(typ, och, Q0, par, nr, used, qb % 2))

    def make_steps(typ, och, Q0, par, nr, used, oi):
        A = accs[par]
        AB = ACCB[par]
        O1s, O2s = Os[par][0], Os[par][1]
        O1sB, O2sB = OSB[par][0], OSB[par][1]
        st = []

        def s_sum():
            for ui, ai in enumerate(used):
                m_ = ai % 2
                lhs = c["e01f"][:, 2 * m_:2 * m_ + 2] if typ == 0 else c["ones_f"][:, 0:1]
                kb.mm(SUMP[0:nr, :], lhs, A[ai][:], ui == 0, ui == len(used) - 1, [CB, AB[ai]], [SUMB])
        st.append(s_sum)

        def s_rcp():
            kb.act(rr[0:nr, :], SUMP[0:nr, :], AF.Ln, [SUMB], [RR])
            kb.act(rr[0:nr, :], rr[0:nr, :], AF.Exp, [RR], [RR], scale=-1.0)
        st.append(s_rcp)

        def s_out():
            ob_ = Buf("o")
            OUTS.append(ob_)
            fw.dma(OBF[oi], hodst(och, Q0), obf[oi][:], reads=[OBF[oi]], writes=[ob_])

        if typ == 1:
            def s_b():
                kb.mm(FBK[:], c["ones_f"][0:1, :], rr[0:1, :], True, True, [CB, RR], [FBB])
                kb.cp("act", F[0][:], FBK[:], [FBB], [FBUF[0]])
            st.append(s_b)

            def s_m():
                kb.tt("dve", obf[oi][:], O1s[:], F[0][:], ALU.mult, [O1sB, FBUF[0]], [OBF[oi]])
                s_out()
            st.append(s_m)
            return st

        def s1():
            kb.mm(FBK[:], c["sel2"][0:2, 0:128], rr[0:2, :], True, True, [CB, RR], [FBB])
            kb.cp("act", F[0][:], FBK[:], [FBB], [FBUF[0]])
        st.append(s1)

        def s2():
            kb.tt("dve", F[1][:], O1s[:], F[0][:], ALU.mult, [O1sB, FBUF[0]], [FBUF[1]])
            kb.mm(FBK[:], c["sel2"][0:2, 128:256], rr[0:2, :], True, True, [CB, RR], [FBB])
            kb.cp("act", F[0][:], FBK[:], [FBB], [FBUF[0]])
        st.append(s2)

        def s3():
            kb.tt("dve", F[2][:], O2s[:], F[0][:], ALU.mult, [O2sB, FBUF[0]], [FBUF[2]])
            kb.stt("dve", F[3][:], F[2][:], sm[:, 4:5], F[1][:], ALU.mult, ALU.add, [FBUF[2], FBUF[1], LAM],
                   [FBUF[3]])
            kb.tt("dve", F[1][:], F[3][:], F[3][:], ALU.mult, [FBUF[3]], [FBUF[1]])
        st.append(s3)

        def s4():
            kb.mm(FBK[0:1, :], c["ones_f"][:, 0:1], F[1][:], True, True, [CB, FBUF[1]], [FBB])
            kb.act(rr[0:1, :], FBK[0:1, :], AF.Ln, [FBB], [RR], bias=1e-5, scale=1.0 / 128.0)
            kb.act(rr[0:1, :], rr[0:1, :], AF.Exp, [RR], [RR], scale=-0.5)
        st.append(s4)

        def s5():
            kb.mm(FBK[:], c["ones_f"][0:1, :], rr[0:1, :], True, True, [CB, RR], [FBB])
            kb.stt("dve", F[2][:], F[3][:], gs_sb[:, 0:1], FBK[:], ALU.mult, ALU.mult, [FBUF[3], LAM, FBB],
                   [FBUF[2]])
            kb.act(obf[oi][:], F[2][:], AF.Copy, [FBUF[2]], [OBF[oi]], scale=float(1.0 - lambda_init))
            s_out()
        st.append(s5)
        return st

    pending = []
    qcount = [0]

    def flush():
        while pending:
            pending.pop(0)()

    def defer_tick():
        if pending:
            pending.pop(0)()

    KMF = Buf()
    for typ in range(2):
        inproj(typ)
        if typ == 0 and hook is not None:
            hook()
        if typ == 1:
            for hd in range(2):
                kb.memset("pool", kmf[:], 0.0, [KMF])
                fw.op("dve", lambda e, hd=hd: e.reduce_sum(out=kmf[:, 0:nblk],
                                                           in_=KT[hd][:].rearrange("p (n l) -> p n l", l=256),
                                                           axis=AX.X), [KTB[hd]], [KMF])
                kb.ts("dve", kmb[hd][:], kmf[:], 1.0 / 256.0, None, ALU.mult, None, [KMF], [KMB[hd]])
        for hd in range(2):
            attention(typ, hd)
            flush()
    return OUTS


def stage_gla(kb, S, xsrc, wq, wlr, wtm, wgu, bg, gn128, hodst4, hook=None):
    fw = kb.fw
    c = kb.c
    CB = c["buf"]
    wq_sb = kb.sb("wq_sb", [128, 8, 512], BF16)
    wlr_sb = kb.sb("wlr_sb", [128, 8, 16], BF16)
    wtm_sb = kb.sb("wtm_sb", [128, 8, 1280], BF16)
    wgu_sb = kb.sb("wgu_sb", [16, 256], BF16)
    bg_sb = kb.sb("bg_sb", [1, 256], BF16)
    gn_sb = kb.sb("gn_sb", [128, 256], F32)
    WB = Buf("glaw")
    fw.dma(WB, wq_sb[:], wq.rearrange("(k p) n -> p k n", p=128), writes=[WB], queue="pool")
    fw.dma(WB, wlr_sb[:], wlr.rearrange("(k p) n -> p k n", p=128), writes=[WB], queue="pool")
    fw.dma(WB, wtm_sb[:], wtm.rearrange("(k p) n -> p k n", p=128), writes=[WB], queue="pool")
    fw.dma(WB, wgu_sb[:], wgu, writes=[WB], queue="pool")
    fw.dma(WB, bg_sb[:], bg, writes=[WB], queue="pool")
    fw.dma(WB, gn_sb[:], gn128, writes=[WB])
    if hook is not None:
        hook()
    xc = [kb.sb(f"gxc{i}", [128, 8, 512], BF16) for i in range(2)]
    XC = [Buf(f"gxc{i}") for i in range(2)]
    qk = kb.sb("qk", [128, 4, 512], F32)
    QK = Buf()
    lrT = kb.sb("lrT", [16, 512], BF16)
    LRT = Buf()
    vb = kb.sb("vb", [128, 512], BF16)
    VBB = Buf()
    sr = kb.sb("sr", [128, 512], F32)
    SRB = Buf()
    ee = kb.sb("ee", [128, 256], F32)
    sp_ = kb.sb("spl", [128, 256], F32)
    SPB = Buf()
    E3 = kb.sb("E3", [128, 256], F32)
    E3B = Buf()
    khat = kb.sb("khat", [128, 256], BF16)
    KHB = Buf()
    E1 = kb.sb("E1", [128, 128], F32)
    E2 = kb.sb("E2", [128, 128], F32)
    EB = Buf()
    dec = kb.sb("dec", [128, 2], F32)
    DECB = [Buf() for _ in range(2)]
    qtl = [kb.sb(f"qtl{i}", [128, 128], BF16) for i in range(2)]
    ktl = [kb.sb(f"ktl{i}", [128, 128], BF16) for i in range(2)]
    QTL = [Buf() for _ in range(2)]
    KTL = [Buf() for _ in range(2)]
    attm = [kb.sb(f"attm{i}", [128, 128], BF16) for i in range(2)]
    ATM = [Buf() for _ in range(2)]
    Sst = [kb.sb(f"Sst{i}", [128, 256], F32) for i in range(2)]
    Sbf = [kb.sb(f"Sbf{i}", [128, 256], BF16) for i in range(2)]
    SST = [Buf() for _ in range(2)]
    SBF = [Buf() for _ in range(2)]
    junk = kb.sb("junk", [128, 256], F32)
    ssq = kb.sb("ssq", [128, 4], F32)
    SSQ = Buf()
    og = kb.sb("og", [128, 256], F32)
    OGB = Buf()
    ogb = kb.sb("ogb", [128, 256], BF16)
    OGBB = Buf()
    hoc = [kb.sb(f"ghoc{i}", [128, 4, 512], BF16) for i in range(2)]
    HOCB = [Buf(f"ghoc{i}") for i in range(2)]
    G = [kb.bank() for _ in range(3)]
    BBK, BBB = kb.bank()
    ATK, ATB = kb.bank()
    OK_, OKB = kb.bank()
    DSK, DSB = kb.bank()
    TRK, TRB = kb.bank(BF16)
    OUTS = []
    for hd in range(2):
        kb.memset("pool", Sst[hd][:], 0.0, [SST[hd]])
        kb.memset("pool", Sbf[hd][:], 0.0, [SBF[hd]])
    gi = [0]

    def nextG():
        g = G[gi[0] % 3]
        gi[0] += 1
        return g

    lnscale = math.log(128.0 ** -0.5)
    for cch in range(S // 512):
        t0 = cch * 512
        xi = cch % 2
        src, x_bf16 = xsrc(t0)
        fw.dma(XC[xi], xc[xi][:], src, writes=[XC[xi]], queue=("sp" if x_bf16 else "pool"))
        for ft in range(4):
            g, GB = nextG()
            for k in range(8):
                kb.mm(g[:], wq_sb[:, k, ft * 128:(ft + 1) * 128], xc[xi][:, k, :], k == 0, k == 7, [WB, XC[xi]], [GB])
            kb.cp("act", qk[:, ft, :], g[:], [GB], [QK])
        g, GB = nextG()
        for k in range(8):
            kb.mm(g[0:16, :], wlr_sb[:, k, :], xc[xi][:, k, :], k == 0, k == 7, [WB, XC[xi]], [GB])
        kb.cp("act", lrT[:], g[0:16, :], [GB], [LRT])
        hi = cch % 2
        for j in range(4):
            ts_ = slice(j * 128, (j + 1) * 128)
            gk, GKB = nextG()
            for k in range(8):
                kb.mm(gk[:, 0:256], xc[xi][:, k, ts_], wtm_sb[:, k, 0:256], k == 0, k == 7, [WB, XC[xi]], [GKB])
            kb.mm(gk[:, 256:512], lrT[0:16, ts_], wgu_sb[0:16, :], True, False, [LRT, WB], [GKB])
            kb.mm(gk[:, 256:512], c["ones_b"][0:1, 0:128], bg_sb[0:1, :], False, True, [CB, WB], [GKB])
            gv, GVB = nextG()
            for k in range(8):
                kb.mm(gv[:], xc[xi][:, k, ts_], wtm_sb[:, k, 256:768], k == 0, k == 7, [WB, XC[xi]], [GVB])
            kb.cp("act", vb[:], gv[:], [GVB], [VBB])
            gr, GRB = nextG()
            for k in range(8):
                kb.mm(gr[:], xc[xi][:, k, ts_], wtm_sb[:, k, 768:1280], k == 0, k == 7, [WB, XC[xi]], [GRB])
            kb.act(sr[:], gr[:], AF.Silu, [GRB], [SRB])
            kb.act(ee[:], gk[:, 256:512], AF.Exp, [GKB], [SPB], scale=-1.0)
            kb.act(sp_[:], ee[:], AF.Ln, [SPB], [SPB], bias=1.0, scale=1.0)
            for hd in range(2):
                kb.mm(BBK[:, hd * 128:(hd + 1) * 128], sp_[:, hd * 128:(hd + 1) * 128], c["uincl"][:], True, True,
                      [SPB, CB], [BBB])
            kb.mm(BBK[:, 256:512], c["ustr"][:], sp_[:], True, True, [SPB, CB], [BBB])
            kb.act(E3[:], BBK[:, 256:512], AF.Exp, [BBB], [E3B])
            kb.tt("dve", khat[:], gk[:, 0:256], E3[:], ALU.mult, [GKB, E3B], [KHB])
            for hd in range(2):
                bt = BBK[:, hd * 128:(hd + 1) * 128]
                kb.act(E1[:], bt, AF.Exp, [BBB], [EB], bias=lnscale, scale=1.0)
                kb.tt("dve", qtl[hd][:], qk[:, hd, ts_], E1[:], ALU.mult, [QK, EB], [QTL[hd]])
                kb.act(E2[:], bt, AF.Exp, [BBB, QTL[hd]], [EB], scale=-1.0)
                kb.tt("dve", ktl[hd][:], qk[:, 2 + hd, ts_], E2[:], ALU.mult, [QK, EB], [KTL[hd]])
                kb.act(dec[:, hd:hd + 1], BBK[:, hd * 128 + 127:hd * 128 + 128], AF.Exp, [BBB], [DECB[hd]])
                kb.mm(ATK[:, hd * 128:(hd + 1) * 128], ktl[hd][:], qtl[hd][:], True, True, [KTL[hd], QTL[hd]], [ATB])
                kb.tt("dve", attm[hd][:], ATK[:, hd * 128:(hd + 1) * 128], c["tri"][:], ALU.mult, [ATB, CB], [ATM[hd]])
                ov = OK_[:, hd * 256:(hd + 1) * 256]
                kb.mm(ov, attm[hd][:], vb[:, hd * 256:(hd + 1) * 256], True, False, [ATM[hd], VBB], [OKB])
                kb.mm(ov, qtl[hd][:], Sbf[hd][:], False, True, [QTL[hd], SBF[hd]], [OKB])
                dv = DSK[:, hd * 256:(hd + 1) * 256]
                kb.mm(dv, khat[:, hd * 128:(hd + 1) * 128], vb[:, hd * 256:(hd + 1) * 256], True, True, [KHB, VBB], [DSB])
                kb.stt("dve", Sst[hd][:], Sst[hd][:], dec[:, hd:hd + 1], dv, ALU.mult, ALU.add, [SST[hd], DECB[hd], DSB],
                       [SST[hd]])
                kb.cp("pool", Sbf[hd][:], Sst[hd][:], [SST[hd]], [SBF[hd]])
                kb.act(junk[:], ov, AF.Square, [OKB], [SSQ], accum_out=ssq[:, 0:1])
                kb.act(ssq[:, 1:2], ssq[:, 0:1], AF.Sqrt, [SSQ], [SSQ], bias=1e-5, scale=1.0 / 256.0)
                fw.op("dve", lambda e: e.reciprocal(out=ssq[:, 2:3], in_=ssq[:, 1:2]), [SSQ], [SSQ])
                kb.stt("dve", og[:], ov, ssq[:, 2:3], gn_sb[:], ALU.mult, ALU.mult, [OKB, SSQ, WB], [OGB])
                kb.tt("pool", ogb[:], og[:], sr[:, hd * 256:(hd + 1) * 256], ALU.mult, [OGB, SRB], [OGBB])
                for cc in range(2):
                    kb.tr(TRK[:, (hd * 2 + cc) * 128:(hd * 2 + cc + 1) * 128], ogb[:, cc * 128:(cc + 1) * 128],
                          c["ident"][:], [OGBB, CB], [TRB])
            kb.cp("act", hoc[hi][:, :, ts_], TRK[:, 0:512].rearrange("p (c t) -> p c t", t=128), [TRB], [HOCB[hi]])
        ob_ = Buf("o")
        OUTS.append(ob_)
        fw.dma(HOCB[hi], hodst4(t0), hoc[hi][:], reads=[HOCB[hi]], writes=[ob_])
    return OUTS


def stage_row2(kb, T, xres, wout, w1, w2, lnp, xout, xTout, hosrc, sel, Thalf):
    fw = kb.fw
    c = kb.c
    CB = c["buf"]
    wo_sb = kb.sb("wo_sb", [128, 8, 1024], BF16)
    WO = Buf("wo")
    ln_sb = kb.sb("ln_sb", [128, 4, 1024], F32)
    LNB = Buf("ln")
    sel_sb = kb.sb("sel_sb", [128, 2], F32)
    SELB = Buf("selb")
    fw.dma(WO, wo_sb[:], wout.rearrange("(c p) n -> p c n", p=128), writes=[WO])
    fw.dma(LNB, ln_sb[:], lnp, writes=[LNB])
    fw.dma(SELB, sel_sb[:], sel, writes=[SELB])
    w1b = [kb.sb(f"w1b{i}", [128, 8, 512], BF16) for i in range(2)]
    w2b = [kb.sb(f"w2b{i}", [128, 4, 1024], BF16) for i in range(2)]
    W1B = [Buf(f"w1b{i}") for i in range(2)]
    W2B = [Buf(f"w2b{i}") for i in range(2)]
    hoa = kb.sb("hoa", [128, 8, 512], BF16)
    hob = kb.sb("hob", [128, 8, 512], BF16)
    HOA, HOBB = Buf("hoa"), Buf("hob")
    y = [kb.sb(f"y{i}", [128, 4, 1024], F32) for i in range(2)]
    Y = [[Buf(f"y{i}_{j}") for j in range(4)] for i in range(2)]
    acc = kb.sb("acc", [128, 4, 1024], F32)
    ACC = [Buf(f"acc{j}") for j in range(4)]
    xb = kb.sb("xb", [128, 1024], BF16)
    XB = Buf()
    x1T = [kb.sb(f"x1T{i}", [128, 8, 512], BF16) for i in range(2)]
    X1T = [Buf(f"x1T{i}") for i in range(2)]
    xTo = kb.sb("xTo", [128, 8, 512], BF16)
    XTO = Buf("xTo")
    hsq = [kb.sb(f"hsq{i}", [128, 4, 512], BF16) for i in range(2)]
    HSQ = [[Buf() for _ in range(4)] for _ in range(2)]
    rl = [kb.sb(f"rl{i}", [128, 512], F32) for i in range(2)]
    RL = [Buf() for _ in range(2)]
    st6 = kb.sb("st6", [128, 2, 6], F32)
    mv = kb.sb("mv", [128, 2], F32)
    rstd = kb.sb("rstd", [128, 2], F32)
    STB = Buf()
    G = [kb.bank() for _ in range(4)]
    TR = [kb.bank(BF16) for _ in range(2)]
    OUTS = []
    gi = [0]
    ti = [0]
    wi_ = [0]

    def nout():
        b = Buf("o")
        OUTS.append(b)
        return b

    def nextG():
        g = G[gi[0] % 4]
        gi[0] += 1
        return g

    def layer_norm(buf_ap, BUFS, j, gidx):
        v = buf_ap[:, j, :]
        for hh in range(2):
            fw.op("dve", lambda e, hh=hh: e.bn_stats(out=st6[:, hh, :], in_=buf_ap[:, j, hh * 512:(hh + 1) * 512]),
                  [BUFS[j]], [STB])
        fw.op("dve", lambda e: e.bn_aggr(out=mv[:], in_=st6[:].rearrange("p a b -> p (a b)")), [STB], [STB])
        kb.act(rstd[:, 0:1], mv[:, 1:2], AF.Sqrt, [STB], [STB], bias=1e-5, scale=1.0)
        fw.op("dve", lambda e: e.reciprocal(out=rstd[:, 1:2], in_=rstd[:, 0:1]), [STB], [STB])
        kb.ts("dve", v, v, mv[:, 0:1], rstd[:, 1:2], ALU.subtract, ALU.mult, [STB, BUFS[j]], [BUFS[j]])
        kb.tt("pool", v, v, ln_sb[:, gidx, :], ALU.mult, [BUFS[j], LNB], [BUFS[j]])
        kb.tt("pool", v, v, ln_sb[:, gidx + 1, :], ALU.add, [BUFS[j], LNB], [BUFS[j]])

    def to_T(src_ap, SRC, j, dstT, DST):
        kb.cp("act", xb[:], src_ap[:, j, :], [SRC[j]], [XB])
        trp, TRB = TR[ti[0] % 2]
        ti[0] += 1
        for k in range(8):
            kb.tr(trp[:, k * 128:(k + 1) * 128], xb[:, k * 128:(k + 1) * 128], c["ident"][:], [XB, CB], [TRB])
        kb.cp("dve", dstT[:, :, j * 128:(j + 1) * 128], trp[:].rearrange("p (k t) -> p k t", t=128), [TRB], [DST])

    def A_load(st):
        p = st % 2
        t0 = st * 512
        fw.dma(HOA, hoa[:], hosrc(t0), writes=[HOA])
        fw.dma(HOBB, hob[:], hosrc(Thalf + t0), writes=[HOBB])
        fw.dma(Y[p][0], y[p][:], xres[t0:t0 + 512, :].rearrange("(j p) d -> p j d", p=128), writes=Y[p])

    def A_front(st):
        p = st % 2
        kb.act(hoa[:], hoa[:], AF.Copy, [HOA, SELB], [HOA], scale=sel_sb[:, 0:1])
        kb.stt("dve", hoa[:], hob[:], sel_sb[:, 1:2], hoa[:], ALU.mult, ALU.add, [HOBB, HOA, SELB], [HOA])
        for j in range(4):
            for nb in range(2):
                g, GB = nextG()
                for cc in range(8):
                    kb.mm(g[:], hoa[:, cc, j * 128:(j + 1) * 128], wo_sb[:, cc, nb * 512:(nb + 1) * 512],
                          cc == 0, cc == 7, [HOA, WO], [GB])
                ys = y[p][:, j, nb * 512:(nb + 1) * 512]
                kb.stt("dve", ys, ys, ALPHA, g[:], ALU.mult, ALU.add, [Y[p][j], GB], [Y[p][j]])
            layer_norm(y[p], Y[p], j, 0)

    def A_T(st, j):
        p = st % 2
        to_T(y[p], Y[p], j, x1T[p], X1T[p])

    def phaseF(st, fb):
        p = st % 2
        wi = wi_[0] % 2
        wi_[0] += 1
        fw.dma(W1B[wi], w1b[wi][:], w1.rearrange("(k p) f -> p k f", p=128)[:, :, fb * 512:(fb + 1) * 512],
               writes=[W1B[wi]])
        fw.dma(W2B[wi], w2b[wi][:], w2[fb * 512:(fb + 1) * 512, :].rearrange("(c p) d -> p c d", p=128),
               writes=[W2B[wi]])
        hi = fb % 2
        for fc in range(4):
            g, GB = nextG()
            for k in range(8):
                kb.mm(g[:], w1b[wi][:, k, fc * 128:(fc + 1) * 128], x1T[p][:, k, :], k == 0, k == 7,
                      [W1B[wi], X1T[p]], [GB])
            ri = fc % 2
            kb.act(rl[ri][:], g[:], AF.Relu, [GB], [RL[ri]])
            kb.tt("pool", hsq[hi][:, fc, :], rl[ri][:], rl[ri][:], ALU.mult, [RL[ri]], [HSQ[hi][fc]])
        for j in range(4):
            for nb in range(2):
                g, GB = nextG()
                for fc in range(4):
                    kb.mm(g[:], hsq[hi][:, fc, j * 128:(j + 1) * 128], w2b[wi][:, fc, nb * 512:(nb + 1) * 512],
                          fc == 0, fc == 3, [HSQ[hi][fc], W2B[wi]], [GB])
                a = acc[:, j, nb * 512:(nb + 1) * 512]
                if fb == 0:
                    kb.cp("dve", a, g[:], [GB], [ACC[j]])
                else:
                    kb.tt("dve", a, a, g[:], ALU.add, [GB, ACC[j]], [ACC[j]])

    def phaseB1(st):
        p = st % 2
        for j in range(4):
            kb.stt("dve", y[p][:, j, :], y[p][:, j, :], ALPHA, acc[:, j, :], ALU.mult, ALU.add,
                   [Y[p][j], ACC[j]], [Y[p][j]])

    def B_ln(st):
        p = st % 2
        for j in range(4):
            layer_norm(y[p], Y[p], j, 2)

    def B_T(st, j):
        p = st % 2
        to_T(y[p], Y[p], j, xTo, XTO)

    def B_store(st):
        p = st % 2
        t0 = st * 512
        fw.dma(Y[p][0], xout[t0:t0 + 512, :].rearrange("(j p) d -> p j d", p=128), y[p][:], reads=Y[p], writes=[nout()])
        fw.dma(XTO, xTout(t0), xTo[:], reads=[XTO], writes=[nout()])

    nst = T // 512
    A_load(0)
    A_front(0)
    for j in range(4):
        A_T(0, j)
    for st in range(nst):
        for fb in range(8):
            phaseF(st, fb)
            if st > 0:
                if fb == 0:
                    B_ln(st - 1)
                elif fb == 1:
                    B_T(st - 1, 0)
                    B_T(st - 1, 1)
                elif fb == 2:
                    B_T(st - 1, 2)
                    B_T(st - 1, 3)
                    B_store(st - 1)
            if st + 1 < nst:
                if fb == 2:
                    A_load(st + 1)
                elif fb == 3:
                    A_front(st + 1)
                elif fb >= 4:
                    A_T(st + 1, fb - 4)
        phaseB1(st)
    B_ln(nst - 1)
    for j in range(4):
        B_T(nst - 1, j)
    B_store(nst - 1)
    return OUTS


ROPE_THETA = 10000.0


def rope_table(S, dim, nrows):
    half = dim // 2
    inv = (1.0 / (ROPE_THETA ** (np.arange(0, dim, 2, dtype=np.float32) / np.float32(dim)))).astype(np.float32)
    ang = np.arange(S, dtype=np.float32)[None, :] * inv[:, None]
    cs = np.cos(ang).astype(np.float32)
    sn = np.sin(ang).astype(np.float32)
    out = np.zeros((2, nrows, S), np.float32)
    for r in range(nrows):
        i = (r % dim) % half
        out[0, r] = cs[i]
        out[1, r] = -sn[i] if (r % dim) < half else sn[i]
    return out


def perm_cols(w, dim):
    n = w.shape[-1] // dim
    w4 = w.reshape(w.shape[0], n, 2, dim // 2)
    return np.ascontiguousarray(w4[:, :, ::-1, :]).reshape(w.shape)


def even_inputs(inp, e, h, xT, S):
    w = inp["hy_w_in"][e]
    hs = slice(2 * h * 128, (2 * h + 2) * 128)
    def fm(q, k, dim):
        qc = q[:, hs]; kc = k[:, hs]
        return np.concatenate([qc, perm_cols(qc, dim), kc, perm_cols(kc, dim)], axis=1)
    wfm = np.stack([fm(w[:, 0:512], w[:, 512:1024], 64), fm(w[:, 1536:2048], w[:, 2048:2560], 128)])
    wv = np.stack([w[:, 1024:1536][:, hs], w[:, 2560:3072][:, hs]])
    ropes = np.stack([rope_table(S, 64, 128), rope_table(S, 128, 128)])
    lam128 = np.broadcast_to(inp["diff_lambda"][e].reshape(1, 256), (128, 256))
    gsub = inp["diff_subln"][e].reshape(128, 1)
    return dict(xT=(None if xT is None else np.ascontiguousarray(xT)), wfm=np.ascontiguousarray(wfm, dtype=np.float32),
                wv=np.ascontiguousarray(wv, dtype=np.float32), ropes=ropes,
                lam128=np.ascontiguousarray(lam128, dtype=np.float32), gsub=np.ascontiguousarray(gsub, dtype=np.float32))


def gla_inputs(inp, o, h, xT, S):
    w = inp["gla_w_in"][o]
    hk = slice(2 * h * 128, (2 * h + 2) * 128)
    hv = slice(2 * h * 256, (2 * h + 2) * 256)
    q = w[:, 0:512][:, hk]; k = w[:, 512:1024][:, hk]
    v = w[:, 1024:2048][:, hv]; r = w[:, 2048:3072][:, hv]
    wq = np.concatenate([q, k], axis=1)
    wtm = np.concatenate([k, v, r], axis=1)
    f = lambda a: np.ascontiguousarray(a, dtype=np.float32)
    return dict(xT=(None if xT is None else np.ascontiguousarray(xT)), wq=f(wq), wlr=f(w[:, 3072:3088]), wtm=f(wtm),
                wgu=f(inp["gla_w_gate_up"][o][:, hk]), bg=f(inp["gla_b_gate"][o][hk].reshape(1, 256)),
                gn128=f(np.broadcast_to(inp["gla_norm"][o].reshape(1, 256), (128, 256))))


def row_inputs(inp, l, ho, xres):
    if l % 2 == 0:
        wo = inp["hy_w_out"][l // 2]
        wo = np.concatenate([wo[0:256], wo[512:768], wo[256:512], wo[768:1024]], axis=0)
    else:
        wo = inp["gla_w_out"][l // 2]
    ln = np.stack([inp["ln_mix_g"][l], inp["ln_mix_b"][l], inp["ln_ffn_g"][l], inp["ln_ffn_b"][l]])
    f = lambda a: np.ascontiguousarray(a, dtype=np.float32)
    return dict(ho=(None if ho is None else np.ascontiguousarray(ho)), xres=(None if xres is None else f(xres)), wout=f(wo), w1=f(inp["ffn_w1"][l]), w2=f(inp["ffn_w2"][l]),
                lnp=f(np.broadcast_to(ln[None], (128, 4, 1024))))


def build_even(S, x_bf16, lambda_init):
    nc = bass.Bass("TRN2", target_bir_lowering=False)
    xT = nc.dram_tensor("xT", [1024, S], BF16 if x_bf16 else F32, kind="ExternalInput").ap()
    wfm = nc.dram_tensor("wfm", [2, 1024, 1024], F32, kind="ExternalInput").ap()
    wv = nc.dram_tensor("wv", [2, 1024, 256], F32, kind="ExternalInput").ap()
    ropes = nc.dram_tensor("ropes", [2, 2, 128, S], F32, kind="ExternalInput").ap()
    lam = nc.dram_tensor("lam128", [128, 256], F32, kind="ExternalInput").ap()
    gsub = nc.dram_tensor("gsub", [128, 1], F32, kind="ExternalInput").ap()
    hoT = nc.dram_tensor("hoT", [4, 128, S], BF16, kind="ExternalOutput").ap()
    with contextlib.ExitStack() as st:
        kb = KB(nc, st)
        kb.consts(moba=True)
        outs = stage_even(kb, S, lambda t0: (xT.rearrange("(k p) t -> p k t", p=128)[:, :, t0:t0 + 512], x_bf16), wfm, wv, ropes, lam, gsub, lambda och, Q0: hoT[och][:, Q0:Q0 + 512], lambda_init)
        kb.fw.wait_all("sp", outs)
        kb.fw.emit()
    return nc


def build_gla(S, x_bf16):
    nc = bass.Bass("TRN2", target_bir_lowering=False)
    xT = nc.dram_tensor("xT", [1024, S], BF16 if x_bf16 else F32, kind="ExternalInput").ap()
    wq = nc.dram_tensor("wq", [1024, 512], F32, kind="ExternalInput").ap()
    wlr = nc.dram_tensor("wlr", [1024, 16], F32, kind="ExternalInput").ap()
    wtm = nc.dram_tensor("wtm", [1024, 1280], F32, kind="ExternalInput").ap()
    wgu = nc.dram_tensor("wgu", [16, 256], F32, kind="ExternalInput").ap()
    bg = nc.dram_tensor("bg", [1, 256], F32, kind="ExternalInput").ap()
    gn = nc.dram_tensor("gn128", [128, 256], F32, kind="ExternalInput").ap()
    hoT = nc.dram_tensor("hoT", [4, 128, S], BF16, kind="ExternalOutput").ap()
    with contextlib.ExitStack() as st:
        kb = KB(nc, st)
        kb.consts()
        outs = stage_gla(kb, S, lambda t0: (xT.rearrange("(k p) t -> p k t", p=128)[:, :, t0:t0 + 512], x_bf16), wq, wlr, wtm, wgu, bg, gn, lambda t0: hoT.rearrange("c p t -> p c t")[:, :, t0:t0 + 512])
        kb.fw.wait_all("sp", outs)
        kb.fw.emit()
    return nc


def build_row(T):
    nc = bass.Bass("TRN2", target_bir_lowering=False)
    ho = nc.dram_tensor("ho", [8, 128, T], BF16, kind="ExternalInput").ap()
    xres = nc.dram_tensor("xres", [T, 1024], F32, kind="ExternalInput").ap()
    wout = nc.dram_tensor("wout", [1024, 1024], F32, kind="ExternalInput").ap()
    w1 = nc.dram_tensor("w1", [1024, 4096], F32, kind="ExternalInput").ap()
    w2 = nc.dram_tensor("w2", [4096, 1024], F32, kind="ExternalInput").ap()
    lnp = nc.dram_tensor("lnp", [128, 4, 1024], F32, kind="ExternalInput").ap()
    xout = nc.dram_tensor("xout", [T, 1024], F32, kind="ExternalOutput").ap()
    xTout = nc.dram_tensor("xTout", [1024, T], BF16, kind="ExternalOutput").ap()
    with contextlib.ExitStack() as st:
        kb = KB(nc, st)
        kb.consts()
        outs = stage_row(kb, T, ho, xres, wout, w1, w2, lnp, xout, lambda t0: xTout.rearrange("(k p) t -> p k t", p=128)[:, :, t0:t0 + 512])
        kb.fw.wait_all("sp", outs)
        kb.fw.emit()
    return nc


def kernel_unfused(**inputs):
    inp = {k: np.asarray(v) for k, v in inputs.items()}
    x = inp["x"]
    Bn, S, _ = x.shape
    T = S // 2
    depth = inp["ln_mix_g"].shape[0]
    ncore = 2 * Bn
    cores = list(range(ncore))
    xT = [np.ascontiguousarray(x[b].T) for b in range(Bn)]
    xres = [x[c // 2, (c % 2) * T:(c % 2 + 1) * T] for c in cores]
    row_nc = build_row(T)
    gla_nc = None
    for l in range(depth):
        x_bf16 = l > 0
        if l % 2 == 0:
            lam_init = 0.8 - 0.6 * math.exp(-0.3 * l)
            nc = build_even(S, x_bf16, lam_init)
            maps = [even_inputs(inp, l // 2, c % 2, xT[c // 2], S) for c in cores]
        else:
            if gla_nc is None:
                gla_nc = build_gla(S, x_bf16)
            nc = gla_nc
            maps = [gla_inputs(inp, l // 2, c % 2, xT[c // 2], S) for c in cores]
        res = run_bass_kernel_spmd(nc, maps, core_ids=cores)
        hoT = [res.results[c]["hoT"] for c in cores]
        maps = []
        for c in cores:
            b, h = c // 2, c % 2
            ho = np.concatenate([hoT[2 * b][:, :, h * T:(h + 1) * T], hoT[2 * b + 1][:, :, h * T:(h + 1) * T]], axis=0)
            maps.append(row_inputs(inp, l, ho, xres[c]))
        res = run_bass_kernel_spmd(row_nc, maps, core_ids=cores)
        xres = [res.results[c]["xout"] for c in cores]
        xT = [np.concatenate([res.results[2 * b]["xTout"], res.results[2 * b + 1]["xTout"]], axis=1) for b in range(Bn)]
    out = np.stack([np.concatenate([xres[2 * b], xres[2 * b + 1]], axis=0) for b in range(Bn)])
    return out.astype(np.float32)


import os
CC_COLS = int(os.environ.get("CC_COLS", "0"))


def cc_chunked(fw, name, src, dst, groups, rows, cols):
    if os.environ.get("NOCC"):
        return
    step = CC_COLS if CC_COLS else cols
    for i, c0 in enumerate(range(0, cols, step)):
        fw.cc(Buf(f"{name}_{i}"), "AllGather", src[:, c0:c0 + step], dst[:, c0:c0 + step], groups)


def build_fused(S, depth, ncore):
    T = S // 2
    nc = bass.Bass("TRN2", target_bir_lowering=False)

    def ext(name, shape, dt=F32):
        return nc.dram_tensor(name, list(shape), dt, kind="ExternalInput").ap()

    xT0 = ext("xT0", [1024, S])
    xres0 = ext("xres0", [T, 1024])
    sel = ext("sel", [128, 2])
    ropes = ext("ropes", [2, 2, 128, S])
    W = []
    for l in range(depth):
        d = {}
        if l % 2 == 0:
            d["wfm"] = ext(f"wfm{l}", [2, 1024, 1024])
            d["wv"] = ext(f"wv{l}", [2, 1024, 256])
            d["lam"] = ext(f"lam{l}", [128, 256])
            d["gsub"] = ext(f"gsub{l}", [128, 1])
        else:
            d["wq"] = ext(f"wq{l}", [1024, 512])
            d["wlr"] = ext(f"wlr{l}", [1024, 16])
            d["wtm"] = ext(f"wtm{l}", [1024, 1280])
            d["wgu"] = ext(f"wgu{l}", [16, 256])
            d["bg"] = ext(f"bg{l}", [1, 256])
            d["gn"] = ext(f"gn{l}", [128, 256])
        d["wout"] = ext(f"wout{l}", [1024, 1024])
        d["w1"] = ext(f"w1_{l}", [1024, 4096])
        d["w2"] = ext(f"w2_{l}", [4096, 1024])
        d["lnp"] = ext(f"lnp{l}", [128, 4, 1024])
        W.append(d)
    xout = nc.dram_tensor("xout", [T, 1024], F32, kind="ExternalOutput").ap()
    HC = 1024
    XC_ = 512
    hoT_own = [nc.dram_tensor(f"hoT_own{k}", [512, HC], BF16).ap() for k in range(S // HC)]
    ho_all = [nc.dram_tensor(f"ho_all{k}", [1024, HC], BF16).ap() for k in range(S // HC)]
    xT_own = [nc.dram_tensor(f"xT_own{k}", [1024, XC_], BF16).ap() for k in range(T // XC_)]
    xT_all = [nc.dram_tensor(f"xT_all{k}", [2048, XC_], BF16).ap() for k in range(T // XC_)]
    xres_i = [nc.dram_tensor(f"xres_i{i}", [T, 1024], F32).ap() for i in range(2)]
    groups = [[2 * i, 2 * i + 1] for i in range(ncore // 2)]
    WBF = [dict(wout=nc.dram_tensor(f"woutb{l}", [1024, 1024], BF16).ap(),
                w1=nc.dram_tensor(f"w1b_{l}", [1024, 4096], BF16).ap(),
                w2=nc.dram_tensor(f"w2b_{l}", [4096, 1024], BF16).ap()) for l in range(depth)]

    with contextlib.ExitStack() as st:
        kb = KB(nc, st)
        fw = kb.fw
        kb.consts(moba=True)
        for l in range(depth):
            d = W[l]
            if l == 0:
                def xsrc(t0):
                    return xT0.rearrange("(k p) t -> p k t", p=128)[:, :, t0:t0 + 512], False
            else:
                def xsrc(t0):
                    r, tl = t0 // T, t0 % T
                    return xT_all[tl // XC_][r * 1024:(r + 1) * 1024, :].rearrange("(k p) t -> p k t", p=128), True

            def hodst(och, Q0):
                return hoT_own[Q0 // HC][och * 128:(och + 1) * 128, Q0 % HC:Q0 % HC + 512]

            def hodst4(t0):
                return hoT_own[t0 // HC].rearrange("(c p) t -> p c t", p=128)[:, :, t0 % HC:t0 % HC + 512]

            def hosrc(tok0):
                return ho_all[tok0 // HC].rearrange("(c p) t -> p c t", p=128)[:, :, tok0 % HC:tok0 % HC + 512]

            def xTdst(t0):
                return xT_own[t0 // XC_].rearrange("(k p) t -> p k t", p=128)
            def hook(l=l, d=d):
                cb = Buf(f"wconv{l}")
                for nm in ("wout", "w1", "w2"):
                    src, dst = d[nm], WBF[l][nm]
                    for r0 in range(0, src.shape[0], 256):
                        fw.dma(cb, dst[r0:r0 + 256, :], src[r0:r0 + 256, :], queue="pool")

            with fw.stage():
                if l % 2 == 0:
                    lam_init = 0.8 - 0.6 * math.exp(-0.3 * l)
                    stage_even(kb, S, xsrc, d["wfm"], d["wv"], ropes, d["lam"], d["gsub"], hodst, lam_init, hook=hook)
                else:
                    stage_gla(kb, S, xsrc, d["wq"], d["wlr"], d["wtm"], d["wgu"], d["bg"], d["gn"], hodst4, hook=hook)
            with fw.stage():
                for k in range(S // HC):
                    fw.cc(Buf(f"ccA{l}_{k}"), "AllGather", hoT_own[k], ho_all[k], groups)
            xin = xres0 if l == 0 else xres_i[(l - 1) % 2]
            xo = xout if l == depth - 1 else xres_i[l % 2]
            with fw.stage():
                outs = stage_row2(kb, T, xin, WBF[l]["wout"], WBF[l]["w1"], WBF[l]["w2"], d["lnp"], xo, xTdst,
                                  hosrc, sel, T)
                if l == depth - 1:
                    fw.wait_all("sp", outs)
            if l < depth - 1:
                with fw.stage():
                    for k in range(T // XC_):
                        fw.cc(Buf(f"ccB{l}_{k}"), "AllGather", xT_own[k], xT_all[k], groups)
        fw.emit()
        print("instr counts", {k: len(v) for k, v in fw.q.items()}, "sems", fw.nsem, flush=True)
    return nc


def fused_inputs(inp, c, S, depth):
    b, h = c // 2, c % 2
    T = S // 2
    x = inp["x"]
    m = dict(xT0=np.ascontiguousarray(x[b, :S].T), xres0=np.ascontiguousarray(x[b, h * T:(h + 1) * T]),
             sel=np.ascontiguousarray(np.broadcast_to(np.eye(2, dtype=np.float32)[h][None], (128, 2))))
    for l in range(depth):
        if l % 2 == 0:
            e = even_inputs(inp, l // 2, h, None, S)
            m["ropes"] = e["ropes"]
            m[f"wfm{l}"], m[f"wv{l}"], m[f"lam{l}"], m[f"gsub{l}"] = e["wfm"], e["wv"], e["lam128"], e["gsub"]
        else:
            g = gla_inputs(inp, l // 2, h, None, S)
            for k in ("wq", "wlr", "wtm", "wgu", "bg"):
                m[f"{k}{l}"] = g[k]
            m[f"gn{l}"] = g["gn128"]
        r = row_inputs(inp, l, None, None)
        m[f"wout{l}"], m[f"w1_{l}"], m[f"w2_{l}"], m[f"lnp{l}"] = r["wout"], r["w1"], r["w2"], r["lnp"]
    return m


def kernel(**inputs):
    inp = {k: np.asarray(v) for k, v in inputs.items()}
    x = inp["x"]
    Bn, S, _ = x.shape
    depth = inp["ln_mix_g"].shape[0]
    ncore = 2 * Bn
    nc = build_fused(S, depth, ncore)
    maps = [fused_inputs(inp, c, S, depth) for c in range(ncore)]
    res = run_bass_kernel_spmd(nc, maps, core_ids=list(range(ncore)))
    out = np.stack([np.concatenate([res.results[2 * b]["xout"], res.results[2 * b + 1]["xout"]], axis=0)
                    for b in range(Bn)])
    return out.astype(np.float32)
```

```python
import contextlib
import numpy as np
import concourse.bass as bass
import concourse.mybir as mybir
from concourse.bass_utils import run_bass_kernel_spmd

F32 = mybir.dt.float32
BF16 = mybir.dt.bfloat16
AF = mybir.ActivationFunctionType
ALU = mybir.AluOpType
AX = mybir.AxisListType

SEM_LIMIT = 30000


class Ent:
    __slots__ = ("stream", "seq", "flag", "hw", "val", "n", "item")

    def __init__(self, stream, seq, n):
        self.stream = stream
        self.seq = seq
        self.flag = False
        self.hw = None
        self.val = 0
        self.n = n
        self.item = None


class Stream:
    def __init__(self, name, inorder):
        self.name = name
        self.inorder = inorder
        self.ents = []

    def new(self, n):
        e = Ent(self, len(self.ents) + 1, n)
        self.ents.append(e)
        return e


class Item:
    __slots__ = ("waits", "fn", "ent")

    def __init__(self, waits, fn, ent):
        self.waits = waits
        self.fn = fn
        self.ent = ent


class Buf:
    def __init__(self, name=""):
        self.name = name
        self.w = {}
        self.r = {}
        self.ds = None


def _merge(dst, src):
    for k, e in src.items():
        o = dst.get(k)
        if o is None or o.seq < e.seq:
            dst[k] = e


class FW:
    def __init__(self, nc, stack):
        self.nc = nc
        self.stack = stack
        self.engs = ["sp", "pe", "act", "dve", "pool"]
        self.q = {k: [] for k in self.engs}
        self.es = {k: Stream(k, True) for k in self.engs}
        self.seen = {k: {} for k in self.engs}
        self.dstreams = []
        self.free_streams = []
        self.live_streams = []
        self.nsem = 0
        self.alloc_stack = stack

    def sb(self, name, shape, dtype):
        self.ntens = getattr(self, "ntens", 0) + 1
        name = f"{name}_{self.ntens}"
        return self.alloc_stack.enter_context(self.nc.sbuf_tensor(name, list(shape), dtype))

    def ps(self, name, shape, dtype=F32):
        self.ntens = getattr(self, "ntens", 0) + 1
        name = f"{name}_{self.ntens}"
        return self.alloc_stack.enter_context(self.nc.psum_tensor(name, list(shape), dtype))

    def dstream(self, name):
        s = Stream(name, False)
        self.dstreams.append(s)
        return s

    def _hw(self, name):
        self.nsem += 1
        return self.stack.enter_context(self.nc.semaphore(f"s{self.nsem}_{name}"))

    def _waits(self, eng, reads, writes):
        raw = {}
        for b in reads:
            _merge(raw, b.w)
        oth = {}
        for b in writes:
            _merge(oth, b.w)
            _merge(oth, b.r)
        own = self.es[eng]
        need = dict(raw)
        for k, e in oth.items():
            if e.stream is own and eng == "pe":
                continue
            o = need.get(k)
            if o is None or o.seq < e.seq:
                need[k] = e
        waits = []
        seen = self.seen[eng]
        for k, e in need.items():
            if seen.get(k, 0) >= e.seq:
                continue
            seen[k] = e.seq
            e.flag = True
            waits.append(e)
        return waits

    def _commit(self, ent, reads, writes):
        k = id(ent.stream)
        for b in writes:
            b.w = {k: ent}
            b.r = {}
        for b in reads:
            o = b.r.get(k)
            if o is None or o.seq < ent.seq:
                b.r[k] = ent

    def op(self, eng, fn, reads=(), writes=()):
        waits = self._waits(eng, reads, writes)
        ent = self.es[eng].new(1)
        it = Item(waits, fn, ent)
        ent.item = it
        self.q[eng].append(it)
        self._commit(ent, reads, writes)
        return ent

    def dma(self, sbuf, out, in_, reads=(), writes=(), queue="sp", **kw):
        stream = getattr(sbuf, "ds", None)
        if stream is None:
            stream = sbuf.ds = self._take_stream("d" + sbuf.name)
        waits = self._waits(queue, reads, writes)
        ent = stream.new(16)
        ent.flag = True
        it = Item(waits, lambda e: e.dma_start(out=out, in_=in_, **kw), ent)
        ent.item = it
        self.q[queue].append(it)
        self._commit(ent, reads, writes)
        return ent

    def _take_stream(self, name):
        if self.free_streams:
            st = self.free_streams.pop()
        else:
            st = self.dstream(name)
        self.live_streams.append(st)
        return st

    def cc(self, buf, kind, in_ap, out_ap, groups, reads=(), writes=()):
        stream = getattr(buf, "ds", None)
        if stream is None:
            stream = buf.ds = self._take_stream("cc" + buf.name)
        waits = self._waits("pool", reads, writes)
        ent = stream.new(1)
        ent.flag = True
        it = Item(waits, lambda e: e.collective_compute(kind, op=ALU.bypass, replica_groups=groups,
                                                        ins=[in_ap.opt()], outs=[out_ap.opt()]), ent)
        ent.item = it
        self.q["pool"].append(it)
        self._commit(ent, reads, writes)
        return ent

    def barrier(self):
        toks = []
        for k in self.engs:
            if self.es[k].ents:
                toks.append(self.es[k].ents[-1])
        for st in self.live_streams:
            if st.ents:
                toks.append(st.ents[-1])
        for eng in self.engs:
            waits = []
            seen = self.seen[eng]
            for e in toks:
                if e.stream is self.es[eng]:
                    continue
                k = id(e.stream)
                if seen.get(k, 0) >= e.seq:
                    continue
                seen[k] = e.seq
                e.flag = True
                waits.append(e)
            self.q[eng].append(Item(waits, None, None))
        self.free_streams.extend(self.live_streams)
        self.live_streams = []

    @contextlib.contextmanager
    def stage(self):
        outer = self.alloc_stack
        with contextlib.ExitStack() as sub:
            self.alloc_stack = sub
            try:
                yield
            finally:
                self.alloc_stack = outer
            self.barrier()

    def wait_all(self, eng, bufs):
        waits = self._waits(eng, bufs, ())
        self.q[eng].append(Item(waits, None, None))

    def emit(self):
        for s in list(self.es.values()) + self.dstreams:
            hw = None
            val = 0
            prev = None
            for e in s.ents:
                if not e.flag:
                    continue
                if hw is None or val + e.n > SEM_LIMIT:
                    if hw is not None and not s.inorder:
                        e.item.waits.append(prev)
                    hw = self._hw(s.name)
                    val = 0
                val += e.n
                e.hw = hw
                e.val = val
                prev = e
        nc = self.nc

        def mk(name):
            items = self.q[name]

            def body(e):
                for it in items:
                    for w in it.waits:
                        e.wait_ge(w.hw, w.val)
                    if it.fn is not None:
                        ins = it.fn(e)
                        if it.ent.flag:
                            ins.then_inc(it.ent.hw, it.ent.n)

            return body

        with nc.Block() as block:
            block.sync(mk("sp"))
            block.tensor(mk("pe"))
            block.scalar(mk("act"))
            block.vector(mk("dve"))
            block.gpsimd(mk("pool"))


import math


D = 1024
ALPHA = 8.0 ** 0.25
NEG = -30000.0


class KB:
    def __init__(self, nc, stack):
        self.nc = nc
        self.fw = FW(nc, stack)
        self.nps = 0

    def sb(self, name, shape, dt):
        return self.fw.sb(name, shape, dt)

    def bank(self, dt=F32):
        self.nps += 1
        n = 512 if dt == F32 else 1024
        return self.fw.ps(f"ps{self.nps}", [128, n], dt), Buf(f"ps{self.nps}")

    def mm(self, out, lhsT, rhs, start, stop, reads, writes):
        return self.fw.op("pe", lambda e: e.matmul(out, lhsT=lhsT, rhs=rhs, start=start, stop=stop,
                                                   skip_group_check=True), reads, writes)

    def tr(self, out, in_, ident, reads, writes):
        return self.fw.op("pe", lambda e: e.transpose(out=out, in_=in_, identity=ident), reads, writes)

    def act(self, out, in_, func, reads, writes, **kw):
        return self.fw.op("act", lambda e: e.activation(out=out, in_=in_, func=func, **kw), reads, writes)

    def tt(self, eng, out, in0, in1, op, reads, writes):
        return self.fw.op(eng, lambda e: e.tensor_tensor(out=out, in0=in0, in1=in1, op=op), reads, writes)

    def ts(self, eng, out, in0, s1, s2, op0, op1, reads, writes):
        if op1 is None:
            return self.fw.op(eng, lambda e: e.tensor_scalar(out=out, in0=in0, scalar1=s1, scalar2=None, op0=op0),
                              reads, writes)
        return self.fw.op(eng, lambda e: e.tensor_scalar(out=out, in0=in0, scalar1=s1, scalar2=s2, op0=op0, op1=op1),
                          reads, writes)

    def stt(self, eng, out, in0, scalar, in1, op0, op1, reads, writes):
        return self.fw.op(eng, lambda e: e.scalar_tensor_tensor(out=out, in0=in0, scalar=scalar, in1=in1,
                                                                op0=op0, op1=op1), reads, writes)

    def cp(self, eng, out, in_, reads, writes):
        if eng == "act":
            return self.fw.op("act", lambda e: e.copy(out=out, in_=in_), reads, writes)
        return self.fw.op(eng, lambda e: e.tensor_copy(out=out, in_=in_), reads, writes)

    def memset(self, eng, ap, val, writes):
        return self.fw.op(eng, lambda e: e.memset(ap, val), (), writes)

    def asel(self, out, in_, pattern, cmp, fill, base, cm, bufs):
        return self.fw.op("pool", lambda e: e.affine_select(out=out, in_=in_, pattern=pattern, compare_op=cmp,
                                                            fill=fill, base=base, channel_multiplier=cm), bufs, bufs)

    def consts(self, moba=False):
        c = {}
        B = Buf("consts")
        c["buf"] = B
        ident = self.sb("ident", [128, 128], BF16)
        self.memset("pool", ident[:], 1.0, [B])
        self.asel(ident[:], ident[:], [[-1, 128]], ALU.is_equal, 0.0, 0, 1, [B])
        c["ident"] = ident
        negm = self.sb("negm", [128, 128], BF16)
        self.memset("pool", negm[:], 0.0, [B])
        self.asel(negm[:], negm[:], [[1, 128]], ALU.is_ge, NEG, 0, -1, [B])
        c["negm"] = negm
        tri = self.sb("tri", [128, 128], F32)
        self.memset("pool", tri[:], 1.0, [B])
        self.asel(tri[:], tri[:], [[1, 128]], ALU.is_ge, 0.0, 0, -1, [B])
        c["tri"] = tri
        ui = self.sb("uincl", [128, 128], F32)
        self.memset("pool", ui[:], -1.0 / 16.0, [B])
        self.asel(ui[:], ui[:], [[1, 128]], ALU.is_ge, 0.0, 0, -1, [B])
        c["uincl"] = ui
        us = self.sb("ustr", [128, 128], F32)
        self.memset("pool", us[:], -1.0 / 16.0, [B])
        self.asel(us[:], us[:], [[-1, 128]], ALU.is_gt, 0.0, 0, 1, [B])
        c["ustr"] = us
        ones_f = self.sb("ones_f", [128, 128], F32)
        self.memset("pool", ones_f[:], 1.0, [B])
        c["ones_f"] = ones_f
        ones_b = self.sb("ones_b", [128, 128], BF16)
        self.memset("pool", ones_b[:], 1.0, [B])
        c["ones_b"] = ones_b
        e01 = self.sb("e01", [128, 4], BF16)
        self.memset("pool", e01[:], 0.0, [B])
        self.memset("pool", e01[:, 0:1], 1.0, [B])
        self.memset("pool", e01[:, 3:4], 1.0, [B])
        c["e01"] = e01
        e01f = self.sb("e01f", [128, 4], F32)
        self.memset("pool", e01f[:], 0.0, [B])
        self.memset("pool", e01f[:, 0:1], 1.0, [B])
        self.memset("pool", e01f[:, 3:4], 1.0, [B])
        c["e01f"] = e01f
        sel2 = self.sb("sel2", [2, 256], F32)
        self.memset("pool", sel2[:], 1.0, [B])
        self.asel(sel2[:, 0:128], sel2[:, 0:128], [[0, 128]], ALU.is_equal, 0.0, 0, 1, [B])
        self.asel(sel2[:, 128:256], sel2[:, 128:256], [[0, 128]], ALU.is_equal, 0.0, -1, 1, [B])
        c["sel2"] = sel2
        if not moba:
            self.c = c
            return c
        esel = self.sb("esel", [32, 32 * 128], BF16)
        self.memset("pool", esel[:], NEG, [B])
        ev = esel[:].rearrange("p (n k) -> p n k", k=128)
        self.asel(ev, ev, [[-1, 32], [0, 128]], ALU.is_equal, 0.0, 0, 1, [B])
        c["esel"] = esel
        self.c = c
        return c


def stage_row(kb, T, ho, xres, wout, w1, w2, lnp, xout, xTout, dbg=None, ho_sel=None, w_bf16=False):
    fw = kb.fw
    c = kb.c
    CB = c["buf"]
    wo_sb = kb.sb("wo_sb", [128, 8, 1024], BF16)
    WO = Buf("wo")
    ln_sb = kb.sb("ln_sb", [128, 4, 1024], F32)
    LNB = Buf("ln")
    wq_ = "sp" if w_bf16 else "pool"
    fw.dma(WO, wo_sb[:], wout.rearrange("(c p) n -> p c n", p=128), writes=[WO], queue=wq_)
    fw.dma(LNB, ln_sb[:], lnp, writes=[LNB])
    NW = 2
    w1b = [kb.sb(f"w1b{i}", [128, 8, 1024], BF16) for i in range(NW)]
    w2b = [kb.sb(f"w2b{i}", [128, 8, 1024], BF16) for i in range(NW)]
    W1B = [Buf() for _ in range(NW)]
    W2B = [Buf() for _ in range(NW)]
    hoc = kb.sb("hoc", [128, 8, 512], BF16)
    HOC = Buf("hoc")
    if ho_sel is not None:
        hoa = hoc
        hob = kb.sb("hob", [128, 8, 512], BF16)
        sel_sb = kb.sb("sel_sb", [128, 2], F32)
        HOA, HOBB, SELB = HOC, Buf("hob"), Buf("selb")
        fw.dma(SELB, sel_sb[:], ho_sel[1], writes=[SELB])
    y = kb.sb("y", [128, 4, 1024], F32)
    Y = [Buf(f"y{i}") for i in range(4)]
    acc = kb.sb("acc", [128, 4, 1024], F32)
    ACC = [Buf(f"acc{i}") for i in range(4)]
    xb = kb.sb("xb", [128, 1024], BF16)
    XB = Buf()
    x1T = kb.sb("x1T", [128, 8, 512], BF16)
    X1T = Buf("x1T")
    xTo, XTO = x1T, X1T
    hsq = [kb.sb(f"hsq{i}", [128, 8, 512], BF16) for i in range(2)]
    HSQ = [[Buf() for _ in range(8)] for _ in range(2)]
    rl = [kb.sb(f"rl{i}", [128, 512], F32) for i in range(2)]
    RL = [Buf() for _ in range(2)]
    st6 = kb.sb("st6", [128, 2, 6], F32)
    mv = kb.sb("mv", [128, 2], F32)
    rstd = kb.sb("rstd", [128, 2], F32)
    STB = Buf()
    G = [kb.bank() for _ in range(4)]
    TR = [kb.bank(BF16) for _ in range(2)]
    OUTS = []

    def nout():
        b = Buf("o")
        OUTS.append(b)
        return b
    gi = [0]

    def nextG():
        g = G[gi[0] % 4]
        gi[0] += 1
        return g

    ti = [0]

    def layer_norm(buf_ap, BUFS, j, gidx):
        v = buf_ap[:, j, :]
        for hh in range(2):
            fw.op("dve", lambda e, hh=hh: e.bn_stats(out=st6[:, hh, :], in_=buf_ap[:, j, hh * 512:(hh + 1) * 512]),
                  [BUFS[j]], [STB])
        fw.op("dve", lambda e: e.bn_aggr(out=mv[:], in_=st6[:].rearrange("p a b -> p (a b)")), [STB], [STB])
        kb.act(rstd[:, 0:1], mv[:, 1:2], AF.Sqrt, [STB], [STB], bias=1e-5, scale=1.0)
        fw.op("dve", lambda e: e.reciprocal(out=rstd[:, 1:2], in_=rstd[:, 0:1]), [STB], [STB])
        kb.ts("dve", v, v, mv[:, 0:1], rstd[:, 1:2], ALU.subtract, ALU.mult, [STB, BUFS[j]], [BUFS[j]])
        kb.tt("pool", v, v, ln_sb[:, gidx, :], ALU.mult, [BUFS[j], LNB], [BUFS[j]])
        kb.tt("pool", v, v, ln_sb[:, gidx + 1, :], ALU.add, [BUFS[j], LNB], [BUFS[j]])

    def to_T(src_ap, SRC, j, dstT, DST):
        kb.cp("act", xb[:], src_ap[:, j, :], [SRC[j]], [XB])
        trp, TRB = TR[ti[0] % 2]
        ti[0] += 1
        for k in range(8):
            kb.tr(trp[:, k * 128:(k + 1) * 128], xb[:, k * 128:(k + 1) * 128], c["ident"][:], [XB, CB], [TRB])
        kb.cp("dve", dstT[:, :, j * 128:(j + 1) * 128], trp[:].rearrange("p (k t) -> p k t", t=128), [TRB], [DST])

    nst = T // 512
    for st in range(nst):
        t0 = st * 512
        if ho_sel is None:
            fw.dma(HOC, hoc[:], ho.rearrange("c p t -> p c t")[:, :, t0:t0 + 512], writes=[HOC])
        else:
            hosrc, Thalf = ho_sel[0], ho_sel[2]
            fw.dma(HOA, hoa[:], hosrc(t0), writes=[HOA])
            fw.dma(HOBB, hob[:], hosrc(Thalf + t0), writes=[HOBB])
            kb.act(hoa[:], hoa[:], AF.Copy, [HOA, SELB], [HOA], scale=sel_sb[:, 0:1])
            kb.stt("dve", hoa[:], hob[:], sel_sb[:, 1:2], hoa[:], ALU.mult, ALU.add, [HOBB, HOA, SELB], [HOA])
        fw.dma(Y[0], y[:], xres[t0:t0 + 512, :].rearrange("(j p) d -> p j d", p=128), writes=Y)
        for j in range(4):
            for nb in range(2):
                g, GB = nextG()
                for cc in range(8):
                    kb.mm(g[:], hoc[:, cc, j * 128:(j + 1) * 128], wo_sb[:, cc, nb * 512:(nb + 1) * 512],
                          cc == 0, cc == 7, [HOC, WO], [GB])
                kb.stt("dve", y[:, j, nb * 512:(nb + 1) * 512], y[:, j, nb * 512:(nb + 1) * 512], ALPHA, g[:],
                       ALU.mult, ALU.add, [Y[j], GB], [Y[j]])
            layer_norm(y, Y, j, 0)
            to_T(y, Y, j, x1T, X1T)
        if dbg is not None:
            fw.dma(Y[0], dbg[0], y[:], reads=Y, writes=[nout()])
            fw.dma(X1T, dbg[1], x1T[:], reads=[X1T], writes=[nout()])
        for fb in range(4):
            wi = (st * 4 + fb) % NW
            fw.dma(W1B[wi], w1b[wi][:], w1.rearrange("(k p) f -> p k f", p=128)[:, :, fb * 1024:(fb + 1) * 1024],
                   writes=[W1B[wi]], queue=wq_)
            fw.dma(W2B[wi], w2b[wi][:], w2[fb * 1024:(fb + 1) * 1024, :].rearrange("(c p) d -> p c d", p=128),
                   writes=[W2B[wi]], queue=wq_)
            hi = fb % 2
            for fc in range(8):
                g, GB = nextG()
                for k in range(8):
                    kb.mm(g[:], w1b[wi][:, k, fc * 128:(fc + 1) * 128], x1T[:, k, :], k == 0, k == 7,
                          [W1B[wi], X1T], [GB])
                ri = fc % 2
                kb.act(rl[ri][:], g[:], AF.Relu, [GB], [RL[ri]])
                kb.tt("pool", hsq[hi][:, fc, :], rl[ri][:], rl[ri][:], ALU.mult, [RL[ri]], [HSQ[hi][fc]])
            for j in range(4):
                for nb in range(2):
                    g, GB = nextG()
                    for fc in range(8):
                        kb.mm(g[:], hsq[hi][:, fc, j * 128:(j + 1) * 128], w2b[wi][:, fc, nb * 512:(nb + 1) * 512],
                              fc == 0, fc == 7, [HSQ[hi][fc], W2B[wi]], [GB])
                    a = acc[:, j, nb * 512:(nb + 1) * 512]
                    if fb == 0:
                        kb.cp("dve", a, g[:], [GB], [ACC[j]])
                    else:
                        kb.tt("dve", a, a, g[:], ALU.add, [GB, ACC[j]], [ACC[j]])
        if dbg is not None:
            fw.dma(ACC[0], dbg[2], acc[:], reads=ACC, writes=[nout()])
            fw.dma(HSQ[1][0], dbg[3], hsq[1][:], reads=HSQ[1], writes=[nout()])
        for j in range(4):
            kb.stt("dve", acc[:, j, :], y[:, j, :], ALPHA, acc[:, j, :], ALU.mult, ALU.add, [Y[j], ACC[j]], [ACC[j]])
            layer_norm(acc, ACC, j, 2)
            to_T(acc, ACC, j, xTo, XTO)
        fw.dma(ACC[0], xout[t0:t0 + 512, :].rearrange("(j p) d -> p j d", p=128), acc[:], reads=ACC, writes=[nout()])
        fw.dma(XTO, xTout(t0), xTo[:], reads=[XTO], writes=[nout()])
    return OUTS


def stage_even(kb, S, xsrc, wfm, wv, ropes, lam128, gsub, hodst, lambda_init, hook=None):
    fw = kb.fw
    c = kb.c
    CB = c["buf"]
    nkt = S // 128
    nqb = S // 512
    nblk = S // 256
    QT = [kb.sb(f"QT{i}", [128, S], BF16) for i in range(2)]
    KT = [kb.sb(f"KT{i}", [128, S], BF16) for i in range(2)]
    V = [kb.sb(f"V{i}", [128, nkt, 128], BF16) for i in range(2)]
    QTB = [Buf() for _ in range(2)]
    KTB = [Buf() for _ in range(2)]
    VB = [Buf() for _ in range(2)]
    wfm_sb = kb.sb("wfm_sb", [128, 8, 1024], BF16)
    WFM = Buf("wfm")
    wv_sb = kb.sb("wv_sb", [128, 8, 256], BF16)
    WV = Buf("wv")
    xc = [kb.sb(f"xc{i}", [128, 8, 512], BF16) for i in range(2)]
    XC = [Buf(f"xc{i}") for i in range(2)]
    rp = [kb.sb(f"rp{i}", [128, 2, 512], F32) for i in range(2)]
    RP = [Buf(f"rp{i}") for i in range(2)]
    F = [kb.sb(f"F{i}", [128, 512], F32) for i in range(4)]
    FBUF = [Buf() for _ in range(4)]
    PT = [kb.sb(f"PT{i}", [128, 512], BF16) for i in range(4)]
    PTB = [Buf() for _ in range(4)]
    obf = [kb.sb(f"obf{i}", [128, 512], BF16) for i in range(2)]
    OBF = [Buf(f"obf{i}") for i in range(2)]
    rr = kb.sb("rr", [2, 512], F32)
    RR = Buf()
    accs = [[kb.sb(f"accs{p}_{i}", [128, 512], F32) for i in range(4)] for p in range(2)]
    ACCB = [[Buf() for _ in range(4)] for _ in range(2)]
    Os = [[kb.sb(f"Os{p}_{i}", [128, 512], F32) for i in range(2)] for p in range(2)]
    OSB = [[Buf() for _ in range(2)] for _ in range(2)]
    lam_sb = kb.sb("lam_sb", [128, 256], F32)
    gs_sb = kb.sb("gs_sb", [128, 1], F32)
    sm = kb.sb("sm_e", [128, 8], F32)
    LAM = Buf("lam")
    kmf = kb.sb("kmf", [128, 32], F32)
    kmb = [kb.sb(f"kmb{i}", [128, 32], BF16) for i in range(2)]
    KMB = [Buf() for _ in range(2)]
    Gs = kb.sb("Gs", [128, 32], F32)
    top8 = kb.sb("top8", [128, 8], F32)
    nots = kb.sb("nots", [128, 32], BF16)
    GSB = Buf()
    biasT = kb.sb("biasT", [32, 512], BF16)
    BIAS = Buf()
    SBK = [kb.bank() for _ in range(4)]
    O1, O1B = kb.bank()
    O2, O2B = kb.bank()
    SUMP, SUMB = kb.bank()
    FBK, FBB = kb.bank()
    OUTS = []

    fw.dma(LAM, lam_sb[:], lam128, writes=[LAM])
    fw.dma(LAM, gs_sb[:], gsub, writes=[LAM])
    kb.tt("dve", F[0][:, 0:64], lam_sb[:, 0:64], lam_sb[:, 64:128], ALU.mult, [LAM], [FBUF[0]])
    kb.tt("dve", F[0][:, 64:128], lam_sb[:, 128:192], lam_sb[:, 192:256], ALU.mult, [LAM], [FBUF[0]])
    fw.op("dve", lambda e: e.reduce_sum(out=sm[:, 0:1], in_=F[0][:, 0:64], axis=AX.X), [FBUF[0]], [LAM])
    fw.op("dve", lambda e: e.reduce_sum(out=sm[:, 1:2], in_=F[0][:, 64:128], axis=AX.X), [FBUF[0]], [LAM])
    kb.act(sm[:, 2:4], sm[:, 0:2], AF.Exp, [LAM], [LAM])
    kb.stt("dve", sm[:, 4:5], sm[:, 3:4], -float(lambda_init), sm[:, 2:3], ALU.add, ALU.subtract, [LAM], [LAM])

    def inproj(typ):
        fw.dma(WFM, wfm_sb[:], wfm[typ].rearrange("(k p) n -> p k n", p=128), writes=[WFM], queue="pool")
        fw.dma(WV, wv_sb[:], wv[typ].rearrange("(k p) n -> p k n", p=128), writes=[WV], queue="pool")
        gi = 0
        for cch in range(S // 512):
            t0 = cch * 512
            xi = cch % 2
            src, x_bf16 = xsrc(t0)
            fw.dma(XC[xi], xc[xi][:], src, writes=[XC[xi]], queue=("sp" if x_bf16 else "pool"))
            fw.dma(RP[xi], rp[xi][:], ropes[typ].rearrange("a p t -> p a t")[:, :, t0:t0 + 512], writes=[RP[xi]])
            for g in range(2):
                for hd in range(2):
                    dst, DB = (QT[hd], QTB[hd]) if g == 0 else (KT[hd], KTB[hd])
                    po, POB = SBK[gi % 4]
                    pp, PPB = SBK[(gi + 1) % 4]
                    gi += 2
                    fo = g * 4 + hd
                    fp = g * 4 + 2 + hd
                    for k in range(8):
                        kb.mm(po[:], wfm_sb[:, k, fo * 128:(fo + 1) * 128], xc[xi][:, k, :], k == 0, k == 7,
                              [WFM, XC[xi]], [POB])
                    for k in range(8):
                        kb.mm(pp[:], wfm_sb[:, k, fp * 128:(fp + 1) * 128], xc[xi][:, k, :], k == 0, k == 7,
                              [WFM, XC[xi]], [PPB])
                    fa = (g * 2 + hd) % 2 * 2
                    kb.tt("dve", F[fa][:], po[:], rp[xi][:, 0, :], ALU.mult, [POB, RP[xi]], [FBUF[fa]])
                    kb.tt("dve", F[fa + 1][:], pp[:], rp[xi][:, 1, :], ALU.mult, [PPB, RP[xi]], [FBUF[fa + 1]])
                    kb.tt("pool", dst[:, t0:t0 + 512], F[fa][:], F[fa + 1][:], ALU.add, [FBUF[fa], FBUF[fa + 1]], [DB])
            for sub in range(4):
                pv, PVB = SBK[gi % 4]
                gi += 1
                for k in range(8):
                    kb.mm(pv[:, 0:256], xc[xi][:, k, sub * 128:(sub + 1) * 128], wv_sb[:, k, :], k == 0, k == 7,
                          [WV, XC[xi]], [PVB])
                for hd in range(2):
                    kb.cp("act", V[hd][:, cch * 4 + sub, :], pv[:, hd * 128:(hd + 1) * 128], [PVB], [VB[hd]])

    def attention(typ, hd):
        nmap = 2 if typ == 0 else 1
        scale = 64.0 ** -0.5 if typ == 0 else 128.0 ** -0.5
        och = typ * 2 + hd
        for qb in range(nqb):
            Q0 = qb * 512
            if typ == 1:
                for j in range(4):
                    q0 = Q0 + j * 128
                    ob = q0 // 256
                    kb.memset("pool", nots[:], 0.0, [GSB])
                    if ob > 0:
                        kb.memset("pool", Gs[:], -1e30, [GSB])
                        gp, GPB = SBK[j % 4]
                        kb.mm(gp[:, 0:32], QT[hd][:, q0:q0 + 128], kmb[hd][:, 0:32], True, True,
                              [QTB[hd], KMB[hd]], [GPB])
                        kb.cp("dve", Gs[:, 0:ob], gp[:, 0:ob], [GPB, GSB], [GSB])
                        fw.op("dve", lambda e: e.max(out=top8[:], in_=Gs[:]), [GSB], [GSB])
                        kb.ts("dve", nots[:, 0:ob], Gs[:, 0:ob], top8[:, 2:3], None, ALU.is_lt, None, [GSB], [GSB])
                    kb.mm(FBK[0:32, j * 128:(j + 1) * 128], nots[:, 0:32], c["ident"][:], True, True, [GSB, CB], [FBB])
                kb.cp("act", biasT[:], FBK[0:32, 0:512], [FBB], [BIAS])
            items = [(kt, m) for kt in range((Q0 + 512) // 128) for m in range(nmap)]
            n = len(items)
            OB_ = [(O1, O1B), (O2, O2B)]
            last_kt = (Q0 + 512) // 128 - 1

            def issueS(i):
                kt, m = items[i]
                K0 = kt * 128
                o = max(0, K0 - Q0)
                diag = K0 >= Q0
                sp_, SPB = SBK[i % 4]
                if typ == 0:
                    kb.mm(sp_[:, o:512], KT[hd][m * 64:(m + 1) * 64, K0:K0 + 128],
                          QT[hd][m * 64:(m + 1) * 64, Q0 + o:Q0 + 512], True, not diag, [KTB[hd], QTB[hd]], [SPB])
                else:
                    kb.mm(sp_[:, o:512], KT[hd][:, K0:K0 + 128], QT[hd][:, Q0 + o:Q0 + 512], True, False,
                          [KTB[hd], QTB[hd]], [SPB])
                    nb_ = K0 // 256
                    kb.mm(sp_[:, o:512], c["esel"][0:32, nb_ * 128:(nb_ + 1) * 128], biasT[0:32, o:512], False,
                          not diag, [CB, BIAS], [SPB])
                if diag:
                    kb.mm(sp_[:, o:o + 128], c["ident"][:], c["negm"][:], False, True, [CB], [SPB])
                kb.act(PT[i % 4][:, o:512], sp_[:, o:512], AF.Exp, [SPB], [PTB[i % 4]], scale=scale)

            def issuePV(i):
                kt, m = items[i]
                K0 = kt * 128
                o = max(0, K0 - Q0)
                Op, OpB = OB_[m]
                kb.mm(Op[:, o:512], V[hd][:, kt, :], PT[i % 4][:, o:512], kt == 0, kt == last_kt,
                      [VB[hd], PTB[i % 4]], [OpB])
                eng = "pool" if i % 3 == 2 else "dve"
                ai = (1 if eng == "pool" else 0) * 2 + m
                A_, AB_ = accs[qcount[0] % 2], ACCB[qcount[0] % 2]
                if not acc_used[ai]:
                    acc_used[ai] = True
                    if o > 0:
                        kb.memset(eng, A_[ai][:, 0:o], 0.0, [AB_[ai]])
                    kb.cp(eng, A_[ai][:, o:512], PT[i % 4][:, o:512], [PTB[i % 4]], [AB_[ai]])
                else:
                    kb.tt(eng, A_[ai][:, o:512], A_[ai][:, o:512], PT[i % 4][:, o:512], ALU.add,
                          [PTB[i % 4], AB_[ai]], [AB_[ai]])

            acc_used = [False] * 4
            if typ == 0:
                for g in range(n // 2 + 1):
                    if g < n // 2:
                        issueS(2 * g)
                        issueS(2 * g + 1)
                    if g >= 1:
                        issuePV(2 * g - 2)
                        issuePV(2 * g - 1)
                    if g % 3 == 2:
                        defer_tick()
            else:
                LA = 2
                for i in range(n + LA):
                    if i < n:
                        issueS(i)
                    if i - LA >= 0:
                        issuePV(i - LA)
                    if i % 4 == 3:
                        defer_tick()
            flush()
            par = qcount[0] % 2
            qcount[0] += 1
            nr = 2 if typ == 0 else 1
            kb.cp("act", Os[par][0][:], O1[:], [O1B], [OSB[par][0]])
            if typ == 0:
                kb.cp("dve", Os[par][1][:], O2[:], [O2B], [OSB[par][1]])
            used = [ai for ai in range(4) if acc_used[ai]]
            pending.extend(make_steps(typ, och, Q0, par, nr, used, qb % 2))

    def make_steps(typ, och, Q0, par, nr, used, oi):
        A = accs[par]
        AB = ACCB[par]
        O1s, O2s = Os[par][0], Os[par][1]
        O1sB, O2sB = OSB[par][0], OSB[par][1]
        st = []

        def s_sum():
            for ui, ai in enumerate(used):
                m_ = ai % 2
                lhs = c["e01f"][:, 2 * m_:2 * m_ + 2] if typ == 0 else c["ones_f"][:, 0:1]
                kb.mm(SUMP[0:nr, :], lhs, A[ai][:], ui == 0, ui == len(used) - 1, [CB, AB[ai]], [SUMB])
        st.append(s_sum)

        def s_rcp():
            kb.act(rr[0:nr, :], SUMP[0:nr, :], AF.Ln, [SUMB], [RR])
            kb.act(rr[0:nr, :], rr[0:nr, :], AF.Exp, [RR], [RR], scale=-1.0)
        st.append(s_rcp)

        def s_out():
            ob_ = Buf("o")
            OUTS.append(ob_)
            fw.dma(OBF[oi], hodst(och, Q0), obf[oi][:], reads=[OBF[oi]], writes=[ob_])

        if typ == 1:
            def s_b():
                kb.mm(FBK[:], c["ones_f"][0:1, :], rr[0:1, :], True, True, [CB, RR], [FBB])
                kb.cp("act", F[0][:], FBK[:], [FBB], [FBUF[0]])
            st.append(s_b)

            def s_m():
                kb.tt("dve", obf[oi][:], O1s[:], F[0][:], ALU.mult, [O1sB, FBUF[0]], [OBF[oi]])
                s_out()
            st.append(s_m)
            return st

        def s1():
            kb.mm(FBK[:], c["sel2"][0:2, 0:128], rr[0:2, :], True, True, [CB, RR], [FBB])
            kb.cp("act", F[0][:], FBK[:], [FBB], [FBUF[0]])
        st.append(s1)

        def s2():
            kb.tt("dve", F[1][:], O1s[:], F[0][:], ALU.mult, [O1sB, FBUF[0]], [FBUF[1]])
            kb.mm(FBK[:], c["sel2"][0:2, 128:256], rr[0:2, :], True, True, [CB, RR], [FBB])
            kb.cp("act", F[0][:], FBK[:], [FBB], [FBUF[0]])
        st.append(s2)

        def s3():
            kb.tt("dve", F[2][:], O2s[:], F[0][:], ALU.mult, [O2sB, FBUF[0]], [FBUF[2]])
            kb.stt("dve", F[3][:], F[2][:], sm[:, 4:5], F[1][:], ALU.mult, ALU.add, [FBUF[2], FBUF[1], LAM],
                   [FBUF[3]])
            kb.tt("dve", F[1][:], F[3][:], F[3][:], ALU.mult, [FBUF[3]], [FBUF[1]])
        st.append(s3)

        def s4():
            kb.mm(FBK[0:1, :], c["ones_f"][:, 0:1], F[1][:], True, True, [CB, FBUF[1]], [FBB])
            kb.act(rr[0:1, :], FBK[0:1, :], AF.Ln, [FBB], [RR], bias=1e-5, scale=1.0 / 128.0)
            kb.act(rr[0:1, :], rr[0:1, :], AF.Exp, [RR], [RR], scale=-0.5)
        st.append(s4)

        def s5():
            kb.mm(FBK[:], c["ones_f"][0:1, :], rr[0:1, :], True, True, [CB, RR], [FBB])
            kb.stt("dve", F[2][:], F[3][:], gs_sb[:, 0:1], FBK[:], ALU.mult, ALU.mult, [FBUF[3], LAM, FBB],
                   [FBUF[2]])
            kb.act(obf[oi][:], F[2][:], AF.Copy, [FBUF[2]], [OBF[oi]], scale=float(1.0 - lambda_init))
            s_out()
        st.append(s5)
        return st

    pending = []
    qcount = [0]

    def flush():
        while pending:
            pending.pop(0)()

    def defer_tick():
        if pending:
            pending.pop(0)()

    KMF = Buf()
    for typ in range(2):
        inproj(typ)
        if typ == 0 and hook is not None:
            hook()
        if typ == 1:
            for hd in range(2):
                kb.memset("pool", kmf[:], 0.0, [KMF])
                fw.op("dve", lambda e, hd=hd: e.reduce_sum(out=kmf[:, 0:nblk],
                                                           in_=KT[hd][:].rearrange("p (n l) -> p n l", l=256),
                                                           axis=AX.X), [KTB[hd]], [KMF])
                kb.ts("dve", kmb[hd][:], kmf[:], 1.0 / 256.0, None, ALU.mult, None, [KMF], [KMB[hd]])
        for hd in range(2):
            attention(typ, hd)
            flush()
    return OUTS


def stage_gla(kb, S, xsrc, wq, wlr, wtm, wgu, bg, gn128, hodst4, hook=None):
    fw = kb.fw
    c = kb.c
    CB = c["buf"]
    wq_sb = kb.sb("wq_sb", [128, 8, 512], BF16)
    wlr_sb = kb.sb("wlr_sb", [128, 8, 16], BF16)
    wtm_sb = kb.sb("wtm_sb", [128, 8, 1280], BF16)
    wgu_sb = kb.sb("wgu_sb", [16, 256], BF16)
    bg_sb = kb.sb("bg_sb", [1, 256], BF16)
    gn_sb = kb.sb("gn_sb", [128, 256], F32)
    WB = Buf("glaw")
    fw.dma(WB, wq_sb[:], wq.rearrange("(k p) n -> p k n", p=128), writes=[WB], queue="pool")
    fw.dma(WB, wlr_sb[:], wlr.rearrange("(k p) n -> p k n", p=128), writes=[WB], queue="pool")
    fw.dma(WB, wtm_sb[:], wtm.rearrange("(k p) n -> p k n", p=128), writes=[WB], queue="pool")
    fw.dma(WB, wgu_sb[:], wgu, writes=[WB], queue="pool")
    fw.dma(WB, bg_sb[:], bg, writes=[WB], queue="pool")
    fw.dma(WB, gn_sb[:], gn128, writes=[WB])
    if hook is not None:
        hook()
    xc = [kb.sb(f"gxc{i}", [128, 8, 512], BF16) for i in range(2)]
    XC = [Buf(f"gxc{i}") for i in range(2)]
    qk = kb.sb("qk", [128, 4, 512], F32)
    QK = Buf()
    lrT = kb.sb("lrT", [16, 512], BF16)
    LRT = Buf()
    vb = kb.sb("vb", [128, 512], BF16)
    VBB = Buf()
    sr = kb.sb("sr", [128, 512], F32)
    SRB = Buf()
    ee = kb.sb("ee", [128, 256], F32)
    sp_ = kb.sb("spl", [128, 256], F32)
    SPB = Buf()
    E3 = kb.sb("E3", [128, 256], F32)
    E3B = Buf()
    khat = kb.sb("khat", [128, 256], BF16)
    KHB = Buf()
    E1 = kb.sb("E1", [128, 128], F32)
    E2 = kb.sb("E2", [128, 128], F32)
    EB = Buf()
    dec = kb.sb("dec", [128, 2], F32)
    DECB = [Buf() for _ in range(2)]
    qtl = [kb.sb(f"qtl{i}", [128, 128], BF16) for i in range(2)]
    ktl = [kb.sb(f"ktl{i}", [128, 128], BF16) for i in range(2)]
    QTL = [Buf() for _ in range(2)]
    KTL = [Buf() for _ in range(2)]
    attm = [kb.sb(f"attm{i}", [128, 128], BF16) for i in range(2)]
    ATM = [Buf() for _ in range(2)]
    Sst = [kb.sb(f"Sst{i}", [128, 256], F32) for i in range(2)]
    Sbf = [kb.sb(f"Sbf{i}", [128, 256], BF16) for i in range(2)]
    SST = [Buf() for _ in range(2)]
    SBF = [Buf() for _ in range(2)]
    junk = kb.sb("junk", [128, 256], F32)
    ssq = kb.sb("ssq", [128, 4], F32)
    SSQ = Buf()
    og = kb.sb("og", [128, 256], F32)
    OGB = Buf()
    ogb = kb.sb("ogb", [128, 256], BF16)
    OGBB = Buf()
    hoc = [kb.sb(f"ghoc{i}", [128, 4, 512], BF16) for i in range(2)]
    HOCB = [Buf(f"ghoc{i}") for i in range(2)]
    G = [kb.bank() for _ in range(3)]
    BBK, BBB = kb.bank()
    ATK, ATB = kb.bank()
    OK_, OKB = kb.bank()
    DSK, DSB = kb.bank()
    TRK, TRB = kb.bank(BF16)
    OUTS = []
    for hd in range(2):
        kb.memset("pool", Sst[hd][:], 0.0, [SST[hd]])
        kb.memset("pool", Sbf[hd][:], 0.0, [SBF[hd]])
    gi = [0]

    def nextG():
        g = G[gi[0] % 3]
        gi[0] += 1
        return g

    lnscale = math.log(128.0 ** -0.5)
    qk2 = [qk, kb.sb("qk_b", [128, 4, 512], F32)]
    QK2 = [QK, Buf()]
    lrT2 = [lrT, kb.sb("lrT_b", [16, 512], BF16)]
    LRT2 = [LRT, Buf()]
    vb2 = [vb, kb.sb("vb_b", [128, 512], BF16)]
    VB2 = [VBB, Buf()]
    sr2 = [sr, kb.sb("sr_b", [128, 512], F32)]
    SR2 = [SRB, Buf()]
    khat2 = [khat, kb.sb("khat_b", [128, 256], BF16)]
    KH2 = [KHB, Buf()]
    qtl2 = [qtl, [kb.sb(f"qtl_b{i}", [128, 128], BF16) for i in range(2)]]
    QTL2 = [QTL, [Buf() for _ in range(2)]]
    attm2 = [attm, [kb.sb(f"attm_b{i}", [128, 128], BF16) for i in range(2)]]
    ATM2 = [ATM, [Buf() for _ in range(2)]]
    dec2 = [dec, kb.sb("dec_b", [128, 2], F32)]
    DEC2 = [DECB, [Buf() for _ in range(2)]]

    def prologue(cch):
        t0 = cch * 512
        xi = cch % 2
        src, x_bf16 = xsrc(t0)
        fw.dma(XC[xi], xc[xi][:], src, writes=[XC[xi]], queue=("sp" if x_bf16 else "pool"))
        for ft in range(4):
            g, GB = nextG()
            for k in range(8):
                kb.mm(g[:], wq_sb[:, k, ft * 128:(ft + 1) * 128], xc[xi][:, k, :], k == 0, k == 7, [WB, XC[xi]], [GB])
            kb.cp("act", qk2[xi][:, ft, :], g[:], [GB], [QK2[xi]])
        g, GB = nextG()
        for k in range(8):
            kb.mm(g[0:16, :], wlr_sb[:, k, :], xc[xi][:, k, :], k == 0, k == 7, [WB, XC[xi]], [GB])
        kb.cp("act", lrT2[xi][:], g[0:16, :], [GB], [LRT2[xi]])

    def front(cch, j):
        xi = cch % 2
        pj = (cch * 4 + j) % 2
        ts_ = slice(j * 128, (j + 1) * 128)
        gk, GKB = nextG()
        for k in range(8):
            kb.mm(gk[:, 0:256], xc[xi][:, k, ts_], wtm_sb[:, k, 0:256], k == 0, k == 7, [WB, XC[xi]], [GKB])
        kb.mm(gk[:, 256:512], lrT2[xi][0:16, ts_], wgu_sb[0:16, :], True, False, [LRT2[xi], WB], [GKB])
        kb.mm(gk[:, 256:512], c["ones_b"][0:1, 0:128], bg_sb[0:1, :], False, True, [CB, WB], [GKB])
        gv, GVB = nextG()
        for k in range(8):
            kb.mm(gv[:], xc[xi][:, k, ts_], wtm_sb[:, k, 256:768], k == 0, k == 7, [WB, XC[xi]], [GVB])
        kb.cp("act", vb2[pj][:], gv[:], [GVB], [VB2[pj]])
        gr, GRB = nextG()
        for k in range(8):
            kb.mm(gr[:], xc[xi][:, k, ts_], wtm_sb[:, k, 768:1280], k == 0, k == 7, [WB, XC[xi]], [GRB])
        kb.act(sr2[pj][:], gr[:], AF.Silu, [GRB], [SR2[pj]])
        kb.act(ee[:], gk[:, 256:512], AF.Exp, [GKB], [SPB], scale=-1.0)
        kb.act(sp_[:], ee[:], AF.Ln, [SPB], [SPB], bias=1.0, scale=1.0)
        for hd in range(2):
            kb.mm(BBK[:, hd * 128:(hd + 1) * 128], sp_[:, hd * 128:(hd + 1) * 128], c["uincl"][:], True, True,
                  [SPB, CB], [BBB])
        kb.mm(BBK[:, 256:512], c["ustr"][:], sp_[:], True, True, [SPB, CB], [BBB])
        kb.act(E3[:], BBK[:, 256:512], AF.Exp, [BBB], [E3B])
        kb.tt("dve", khat2[pj][:], gk[:, 0:256], E3[:], ALU.mult, [GKB, E3B], [KH2[pj]])
        for hd in range(2):
            bt = BBK[:, hd * 128:(hd + 1) * 128]
            kb.act(E1[:], bt, AF.Exp, [BBB], [EB], bias=lnscale, scale=1.0)
            kb.tt("dve", qtl2[pj][hd][:], qk2[xi][:, hd, ts_], E1[:], ALU.mult, [QK2[xi], EB], [QTL2[pj][hd]])
            kb.act(E2[:], bt, AF.Exp, [BBB, QTL2[pj][hd]], [EB], scale=-1.0)
            kb.tt("dve", ktl[hd][:], qk2[xi][:, 2 + hd, ts_], E2[:], ALU.mult, [QK2[xi], EB], [KTL[hd]])
            kb.act(dec2[pj][:, hd:hd + 1], BBK[:, hd * 128 + 127:hd * 128 + 128], AF.Exp, [BBB], [DEC2[pj][hd]])
            kb.mm(ATK[:, hd * 128:(hd + 1) * 128], ktl[hd][:], qtl2[pj][hd][:], True, True,
                  [KTL[hd], QTL2[pj][hd]], [ATB])
            kb.tt("dve", attm2[pj][hd][:], ATK[:, hd * 128:(hd + 1) * 128], c["tri"][:], ALU.mult, [ATB, CB],
                  [ATM2[pj][hd]])

    def back(cch, j):
        pj = (cch * 4 + j) % 2
        hi = cch % 2
        ts_ = slice(j * 128, (j + 1) * 128)
        for hd in range(2):
            ov = OK_[:, hd * 256:(hd + 1) * 256]
            vh = vb2[pj][:, hd * 256:(hd + 1) * 256]
            kb.mm(ov, attm2[pj][hd][:], vh, True, False, [ATM2[pj][hd], VB2[pj]], [OKB])
            kb.mm(ov, qtl2[pj][hd][:], Sbf[hd][:], False, True, [QTL2[pj][hd], SBF[hd]], [OKB])
            dv = DSK[:, hd * 256:(hd + 1) * 256]
            kb.mm(dv, khat2[pj][:, hd * 128:(hd + 1) * 128], vh, True, True, [KH2[pj], VB2[pj]], [DSB])
            kb.stt("dve", Sst[hd][:], Sst[hd][:], dec2[pj][:, hd:hd + 1], dv, ALU.mult, ALU.add,
                   [SST[hd], DEC2[pj][hd], DSB], [SST[hd]])
            kb.cp("pool", Sbf[hd][:], Sst[hd][:], [SST[hd]], [SBF[hd]])
            kb.act(junk[:], ov, AF.Square, [OKB], [SSQ], accum_out=ssq[:, 0:1])
            kb.act(ssq[:, 1:2], ssq[:, 0:1], AF.Ln, [SSQ], [SSQ], bias=1e-5, scale=1.0 / 256.0)
            kb.act(ssq[:, 2:3], ssq[:, 1:2], AF.Exp, [SSQ], [SSQ], scale=-0.5)
            kb.stt("dve", og[:], ov, ssq[:, 2:3], gn_sb[:], ALU.mult, ALU.mult, [OKB, SSQ, WB], [OGB])
            kb.tt("pool", ogb[:], og[:], sr2[pj][:, hd * 256:(hd + 1) * 256], ALU.mult, [OGB, SR2[pj]], [OGBB])
            for cc in range(2):
                kb.tr(TRK[:, (hd * 2 + cc) * 128:(hd * 2 + cc + 1) * 128], ogb[:, cc * 128:(cc + 1) * 128],
                      c["ident"][:], [OGBB, CB], [TRB])
        kb.cp("act", hoc[hi][:, :, ts_], TRK[:, 0:512].rearrange("p (c t) -> p c t", t=128), [TRB], [HOCB[hi]])
        if j == 3:
            ob_ = Buf("o")
            OUTS.append(ob_)
            fw.dma(HOCB[hi], hodst4(cch * 512), hoc[hi][:], reads=[HOCB[hi]], writes=[ob_])

    subs = [(cch, j) for cch in range(S // 512) for j in range(4)]
    prologue(0)
    front(0, 0)
    for idx, (cch, j) in enumerate(subs):
        if idx + 1 < len(subs):
            nc_, nj = subs[idx + 1]
            if nj == 0:
                prologue(nc_)
            front(nc_, nj)
        back(cch, j)
    return OUTS


def stage_row2(kb, T, xres, wout, w1, w2, lnp, xout, xTout, hosrc, sel, Thalf):
    fw = kb.fw
    c = kb.c
    CB = c["buf"]
    wo_sb = kb.sb("wo_sb", [128, 8, 1024], BF16)
    WO = Buf("wo")
    ln_sb = kb.sb("ln_sb", [128, 4, 1024], F32)
    LNB = Buf("ln")
    sel_sb = kb.sb("sel_sb", [128, 2], F32)
    SELB = Buf("selb")
    fw.dma(WO, wo_sb[:], wout.rearrange("(c p) n -> p c n", p=128), writes=[WO])
    fw.dma(LNB, ln_sb[:], lnp, writes=[LNB])
    fw.dma(SELB, sel_sb[:], sel, writes=[SELB])
    w1b = [kb.sb(f"w1b{i}", [128, 8, 512], BF16) for i in range(2)]
    w2b = [kb.sb(f"w2b{i}", [128, 4, 1024], BF16) for i in range(2)]
    W1B = [Buf(f"w1b{i}") for i in range(2)]
    W2B = [Buf(f"w2b{i}") for i in range(2)]
    hoa = kb.sb("hoa", [128, 8, 512], BF16)
    hob = kb.sb("hob", [128, 8, 512], BF16)
    HOA, HOBB = Buf("hoa"), Buf("hob")
    y = [kb.sb(f"y{i}", [128, 4, 1024], F32) for i in range(2)]
    Y = [[Buf(f"y{i}_{j}") for j in range(4)] for i in range(2)]
    acc = kb.sb("acc", [128, 4, 1024], F32)
    ACC = [Buf(f"acc{j}") for j in range(4)]
    xb = kb.sb("xb", [128, 1024], BF16)
    XB = Buf()
    x1T = [kb.sb(f"x1T{i}", [128, 8, 512], BF16) for i in range(2)]
    X1T = [Buf(f"x1T{i}") for i in range(2)]
    xTo = kb.sb("xTo", [128, 8, 512], BF16)
    XTO = Buf("xTo")
    hsq = [kb.sb(f"hsq{i}", [128, 4, 512], BF16) for i in range(2)]
    HSQ = [[Buf() for _ in range(4)] for _ in range(2)]
    rl = [kb.sb(f"rl{i}", [128, 512], F32) for i in range(2)]
    RL = [Buf() for _ in range(2)]
    st6 = kb.sb("st6", [128, 2, 6], F32)
    mv = kb.sb("mv", [128, 2], F32)
    rstd = kb.sb("rstd", [128, 2], F32)
    STB = Buf()
    G = [kb.bank() for _ in range(4)]
    TR = [kb.bank(BF16) for _ in range(2)]
    OUTS = []
    gi = [0]
    ti = [0]
    wi_ = [0]

    def nout():
        b = Buf("o")
        OUTS.append(b)
        return b

    def nextG():
        g = G[gi[0] % 4]
        gi[0] += 1
        return g

    def layer_norm(buf_ap, BUFS, j, gidx):
        v = buf_ap[:, j, :]
        for hh in range(2):
            fw.op("dve", lambda e, hh=hh: e.bn_stats(out=st6[:, hh, :], in_=buf_ap[:, j, hh * 512:(hh + 1) * 512]),
                  [BUFS[j]], [STB])
        fw.op("dve", lambda e: e.bn_aggr(out=mv[:], in_=st6[:].rearrange("p a b -> p (a b)")), [STB], [STB])
        kb.act(rstd[:, 0:1], mv[:, 1:2], AF.Sqrt, [STB], [STB], bias=1e-5, scale=1.0)
        fw.op("dve", lambda e: e.reciprocal(out=rstd[:, 1:2], in_=rstd[:, 0:1]), [STB], [STB])
        kb.ts("dve", v, v, mv[:, 0:1], rstd[:, 1:2], ALU.subtract, ALU.mult, [STB, BUFS[j]], [BUFS[j]])
        kb.tt("pool", v, v, ln_sb[:, gidx, :], ALU.mult, [BUFS[j], LNB], [BUFS[j]])
        kb.tt("pool", v, v, ln_sb[:, gidx + 1, :], ALU.add, [BUFS[j], LNB], [BUFS[j]])

    def to_T(src_ap, SRC, j, dstT, DST):
        kb.cp("act", xb[:], src_ap[:, j, :], [SRC[j]], [XB])
        trp, TRB = TR[ti[0] % 2]
        ti[0] += 1
        for k in range(8):
            kb.tr(trp[:, k * 128:(k + 1) * 128], xb[:, k * 128:(k + 1) * 128], c["ident"][:], [XB, CB], [TRB])
        kb.cp("dve", dstT[:, :, j * 128:(j + 1) * 128], trp[:].rearrange("p (k t) -> p k t", t=128), [TRB], [DST])

    def A_load(st):
        p = st % 2
        t0 = st * 512
        fw.dma(HOA, hoa[:], hosrc(t0), writes=[HOA])
        fw.dma(HOBB, hob[:], hosrc(Thalf + t0), writes=[HOBB])
        fw.dma(Y[p][0], y[p][:], xres[t0:t0 + 512, :].rearrange("(j p) d -> p j d", p=128), writes=Y[p])

    def A_front(st):
        p = st % 2
        kb.act(hoa[:], hoa[:], AF.Copy, [HOA, SELB], [HOA], scale=sel_sb[:, 0:1])
        kb.stt("dve", hoa[:], hob[:], sel_sb[:, 1:2], hoa[:], ALU.mult, ALU.add, [HOBB, HOA, SELB], [HOA])
        for j in range(4):
            for nb in range(2):
                g, GB = nextG()
                for cc in range(8):
                    kb.mm(g[:], hoa[:, cc, j * 128:(j + 1) * 128], wo_sb[:, cc, nb * 512:(nb + 1) * 512],
                          cc == 0, cc == 7, [HOA, WO], [GB])
                ys = y[p][:, j, nb * 512:(nb + 1) * 512]
                kb.stt("dve", ys, ys, ALPHA, g[:], ALU.mult, ALU.add, [Y[p][j], GB], [Y[p][j]])
            layer_norm(y[p], Y[p], j, 0)

    def A_T(st, j):
        p = st % 2
        to_T(y[p], Y[p], j, x1T[p], X1T[p])

    def phaseF(st, fb):
        p = st % 2
        wi = wi_[0] % 2
        wi_[0] += 1
        fw.dma(W1B[wi], w1b[wi][:], w1.rearrange("(k p) f -> p k f", p=128)[:, :, fb * 512:(fb + 1) * 512],
               writes=[W1B[wi]])
        fw.dma(W2B[wi], w2b[wi][:], w2[fb * 512:(fb + 1) * 512, :].rearrange("(c p) d -> p c d", p=128),
               writes=[W2B[wi]])
        hi = fb % 2
        for fc in range(4):
            g, GB = nextG()
            for k in range(8):
                kb.mm(g[:], w1b[wi][:, k, fc * 128:(fc + 1) * 128], x1T[p][:, k, :], k == 0, k == 7,
                      [W1B[wi], X1T[p]], [GB])
            ri = fc % 2
            kb.act(rl[ri][:], g[:], AF.Relu, [GB], [RL[ri]])
            kb.tt("pool", hsq[hi][:, fc, :], rl[ri][:], rl[ri][:], ALU.mult, [RL[ri]], [HSQ[hi][fc]])
        for j in range(4):
            for nb in range(2):
                g, GB = nextG()
                for fc in range(4):
                    kb.mm(g[:], hsq[hi][:, fc, j * 128:(j + 1) * 128], w2b[wi][:, fc, nb * 512:(nb + 1) * 512],
                          fc == 0, fc == 3, [HSQ[hi][fc], W2B[wi]], [GB])
                a = acc[:, j, nb * 512:(nb + 1) * 512]
                if fb == 0:
                    kb.cp("dve", a, g[:], [GB], [ACC[j]])
                else:
                    kb.tt("dve", a, a, g[:], ALU.add, [GB, ACC[j]], [ACC[j]])

    def phaseB1(st):
        p = st % 2
        for j in range(4):
            kb.stt("dve", y[p][:, j, :], y[p][:, j, :], ALPHA, acc[:, j, :], ALU.mult, ALU.add,
                   [Y[p][j], ACC[j]], [Y[p][j]])

    def B_ln(st):
        p = st % 2
        for j in range(4):
            layer_norm(y[p], Y[p], j, 2)

    def B_T(st, j):
        p = st % 2
        to_T(y[p], Y[p], j, xTo, XTO)

    def B_store(st):
        p = st % 2
        t0 = st * 512
        fw.dma(Y[p][0], xout[t0:t0 + 512, :].rearrange("(j p) d -> p j d", p=128), y[p][:], reads=Y[p], writes=[nout()])
        fw.dma(XTO, xTout(t0), xTo[:], reads=[XTO], writes=[nout()])

    nst = T // 512
    A_load(0)
    A_front(0)
    for j in range(4):
        A_T(0, j)
    for st in range(nst):
        for fb in range(8):
            phaseF(st, fb)
            if st > 0:
                if fb == 0:
                    B_ln(st - 1)
                elif fb == 1:
                    B_T(st - 1, 0)
                    B_T(st - 1, 1)
                elif fb == 2:
                    B_T(st - 1, 2)
                    B_T(st - 1, 3)
                    B_store(st - 1)
            if st + 1 < nst:
                if fb == 2:
                    A_load(st + 1)
                elif fb == 3:
                    A_front(st + 1)
                elif fb >= 4:
                    A_T(st + 1, fb - 4)
        phaseB1(st)
    B_ln(nst - 1)
    for j in range(4):
        B_T(nst - 1, j)
    B_store(nst - 1)
    return OUTS


ROPE_THETA = 10000.0


def rope_table(S, dim, nrows):
    half = dim // 2
    inv = (1.0 / (ROPE_THETA ** (np.arange(0, dim, 2, dtype=np.float32) / np.float32(dim)))).astype(np.float32)
    ang = np.arange(S, dtype=np.float32)[None, :] * inv[:, None]
    cs = np.cos(ang).astype(np.float32)
    sn = np.sin(ang).astype(np.float32)
    out = np.zeros((2, nrows, S), np.float32)
    for r in range(nrows):
        i = (r % dim) % half
        out[0, r] = cs[i]
        out[1, r] = -sn[i] if (r % dim) < half else sn[i]
    return out


def perm_cols(w, dim):
    n = w.shape[-1] // dim
    w4 = w.reshape(w.shape[0], n, 2, dim // 2)
    return np.ascontiguousarray(w4[:, :, ::-1, :]).reshape(w.shape)


def even_inputs(inp, e, h, xT, S):
    w = inp["hy_w_in"][e]
    hs = slice(2 * h * 128, (2 * h + 2) * 128)
    def fm(q, k, dim):
        qc = q[:, hs]; kc = k[:, hs]
        return np.concatenate([qc, perm_cols(qc, dim), kc, perm_cols(kc, dim)], axis=1)
    wfm = np.stack([fm(w[:, 0:512], w[:, 512:1024], 64), fm(w[:, 1536:2048], w[:, 2048:2560], 128)])
    wv = np.stack([w[:, 1024:1536][:, hs], w[:, 2560:3072][:, hs]])
    ropes = np.stack([rope_table(S, 64, 128), rope_table(S, 128, 128)])
    lam128 = np.broadcast_to(inp["diff_lambda"][e].reshape(1, 256), (128, 256))
    gsub = inp["diff_subln"][e].reshape(128, 1)
    return dict(xT=(None if xT is None else np.ascontiguousarray(xT)), wfm=np.ascontiguousarray(wfm, dtype=np.float32),
                wv=np.ascontiguousarray(wv, dtype=np.float32), ropes=ropes,
                lam128=np.ascontiguousarray(lam128, dtype=np.float32), gsub=np.ascontiguousarray(gsub, dtype=np.float32))


def gla_inputs(inp, o, h, xT, S):
    w = inp["gla_w_in"][o]
    hk = slice(2 * h * 128, (2 * h + 2) * 128)
    hv = slice(2 * h * 256, (2 * h + 2) * 256)
    q = w[:, 0:512][:, hk]; k = w[:, 512:1024][:, hk]
    v = w[:, 1024:2048][:, hv]; r = w[:, 2048:3072][:, hv]
    wq = np.concatenate([q, k], axis=1)
    wtm = np.concatenate([k, v, r], axis=1)
    f = lambda a: np.ascontiguousarray(a, dtype=np.float32)
    return dict(xT=(None if xT is None else np.ascontiguousarray(xT)), wq=f(wq), wlr=f(w[:, 3072:3088]), wtm=f(wtm),
                wgu=f(inp["gla_w_gate_up"][o][:, hk]), bg=f(inp["gla_b_gate"][o][hk].reshape(1, 256)),
                gn128=f(np.broadcast_to(inp["gla_norm"][o].reshape(1, 256), (128, 256))))


def row_inputs(inp, l, ho, xres):
    if l % 2 == 0:
        wo = inp["hy_w_out"][l // 2]
        wo = np.concatenate([wo[0:256], wo[512:768], wo[256:512], wo[768:1024]], axis=0)
    else:
        wo = inp["gla_w_out"][l // 2]
    ln = np.stack([inp["ln_mix_g"][l], inp["ln_mix_b"][l], inp["ln_ffn_g"][l], inp["ln_ffn_b"][l]])
    f = lambda a: np.ascontiguousarray(a, dtype=np.float32)
    return dict(ho=(None if ho is None else np.ascontiguousarray(ho)), xres=(None if xres is None else f(xres)), wout=f(wo), w1=f(inp["ffn_w1"][l]), w2=f(inp["ffn_w2"][l]),
                lnp=f(np.broadcast_to(ln[None], (128, 4, 1024))))


def build_even(S, x_bf16, lambda_init):
    nc = bass.Bass("TRN2", target_bir_lowering=False)
    xT = nc.dram_tensor("xT", [1024, S], BF16 if x_bf16 else F32, kind="ExternalInput").ap()
    wfm = nc.dram_tensor("wfm", [2, 1024, 1024], F32, kind="ExternalInput").ap()
    wv = nc.dram_tensor("wv", [2, 1024, 256], F32, kind="ExternalInput").ap()
    ropes = nc.dram_tensor("ropes", [2, 2, 128, S], F32, kind="ExternalInput").ap()
    lam = nc.dram_tensor("lam128", [128, 256], F32, kind="ExternalInput").ap()
    gsub = nc.dram_tensor("gsub", [128, 1], F32, kind="ExternalInput").ap()
    hoT = nc.dram_tensor("hoT", [4, 128, S], BF16, kind="ExternalOutput").ap()
    with contextlib.ExitStack() as st:
        kb = KB(nc, st)
        kb.consts(moba=True)
        outs = stage_even(kb, S, lambda t0: (xT.rearrange("(k p) t -> p k t", p=128)[:, :, t0:t0 + 512], x_bf16), wfm, wv, ropes, lam, gsub, lambda och, Q0: hoT[och][:, Q0:Q0 + 512], lambda_init)
        kb.fw.wait_all("sp", outs)
        kb.fw.emit()
    return nc


def build_gla(S, x_bf16):
    nc = bass.Bass("TRN2", target_bir_lowering=False)
    xT = nc.dram_tensor("xT", [1024, S], BF16 if x_bf16 else F32, kind="ExternalInput").ap()
    wq = nc.dram_tensor("wq", [1024, 512], F32, kind="ExternalInput").ap()
    wlr = nc.dram_tensor("wlr", [1024, 16], F32, kind="ExternalInput").ap()
    wtm = nc.dram_tensor("wtm", [1024, 1280], F32, kind="ExternalInput").ap()
    wgu = nc.dram_tensor("wgu", [16, 256], F32, kind="ExternalInput").ap()
    bg = nc.dram_tensor("bg", [1, 256], F32, kind="ExternalInput").ap()
    gn = nc.dram_tensor("gn128", [128, 256], F32, kind="ExternalInput").ap()
    hoT = nc.dram_tensor("hoT", [4, 128, S], BF16, kind="ExternalOutput").ap()
    with contextlib.ExitStack() as st:
        kb = KB(nc, st)
        kb.consts()
        outs = stage_gla(kb, S, lambda t0: (xT.rearrange("(k p) t -> p k t", p=128)[:, :, t0:t0 + 512], x_bf16), wq, wlr, wtm, wgu, bg, gn, lambda t0: hoT.rearrange("c p t -> p c t")[:, :, t0:t0 + 512])
        kb.fw.wait_all("sp", outs)
        kb.fw.emit()
    return nc


def build_row(T):
    nc = bass.Bass("TRN2", target_bir_lowering=False)
    ho = nc.dram_tensor("ho", [8, 128, T], BF16, kind="ExternalInput").ap()
    xres = nc.dram_tensor("xres", [T, 1024], F32, kind="ExternalInput").ap()
    wout = nc.dram_tensor("wout", [1024, 1024], F32, kind="ExternalInput").ap()
    w1 = nc.dram_tensor("w1", [1024, 4096], F32, kind="ExternalInput").ap()
    w2 = nc.dram_tensor("w2", [4096, 1024], F32, kind="ExternalInput").ap()
    lnp = nc.dram_tensor("lnp", [128, 4, 1024], F32, kind="ExternalInput").ap()
    xout = nc.dram_tensor("xout", [T, 1024], F32, kind="ExternalOutput").ap()
    xTout = nc.dram_tensor("xTout", [1024, T], BF16, kind="ExternalOutput").ap()
    with contextlib.ExitStack() as st:
        kb = KB(nc, st)
        kb.consts()
        outs = stage_row(kb, T, ho, xres, wout, w1, w2, lnp, xout, lambda t0: xTout.rearrange("(k p) t -> p k t", p=128)[:, :, t0:t0 + 512])
        kb.fw.wait_all("sp", outs)
        kb.fw.emit()
    return nc


def kernel_unfused(**inputs):
    inp = {k: np.asarray(v) for k, v in inputs.items()}
    x = inp["x"]
    Bn, S, _ = x.shape
    T = S // 2
    depth = inp["ln_mix_g"].shape[0]
    ncore = 2 * Bn
    cores = list(range(ncore))
    xT = [np.ascontiguousarray(x[b].T) for b in range(Bn)]
    xres = [x[c // 2, (c % 2) * T:(c % 2 + 1) * T] for c in cores]
    row_nc = build_row(T)
    gla_nc = None
    for l in range(depth):
        x_bf16 = l > 0
        if l % 2 == 0:
            lam_init = 0.8 - 0.6 * math.exp(-0.3 * l)
            nc = build_even(S, x_bf16, lam_init)
            maps = [even_inputs(inp, l // 2, c % 2, xT[c // 2], S) for c in cores]
        else:
            if gla_nc is None:
                gla_nc = build_gla(S, x_bf16)
            nc = gla_nc
            maps = [gla_inputs(inp, l // 2, c % 2, xT[c // 2], S) for c in cores]
        res = run_bass_kernel_spmd(nc, maps, core_ids=cores)
        hoT = [res.results[c]["hoT"] for c in cores]
        maps = []
        for c in cores:
            b, h = c // 2, c % 2
            ho = np.concatenate([hoT[2 * b][:, :, h * T:(h + 1) * T], hoT[2 * b + 1][:, :, h * T:(h + 1) * T]], axis=0)
            maps.append(row_inputs(inp, l, ho, xres[c]))
        res = run_bass_kernel_spmd(row_nc, maps, core_ids=cores)
        xres = [res.results[c]["xout"] for c in cores]
        xT = [np.concatenate([res.results[2 * b]["xTout"], res.results[2 * b + 1]["xTout"]], axis=1) for b in range(Bn)]
    out = np.stack([np.concatenate([xres[2 * b], xres[2 * b + 1]], axis=0) for b in range(Bn)])
    return out.astype(np.float32)


import os
CC_COLS = int(os.environ.get("CC_COLS", "0"))


def cc_chunked(fw, name, src, dst, groups, rows, cols):
    if os.environ.get("NOCC"):
        return
    step = CC_COLS if CC_COLS else cols
    for i, c0 in enumerate(range(0, cols, step)):
        fw.cc(Buf(f"{name}_{i}"), "AllGather", src[:, c0:c0 + step], dst[:, c0:c0 + step], groups)


def build_fused(S, depth, ncore):
    T = S // 2
    nc = bass.Bass("TRN2", target_bir_lowering=False)

    def ext(name, shape, dt=F32):
        return nc.dram_tensor(name, list(shape), dt, kind="ExternalInput").ap()

    xT0 = ext("xT0", [1024, S])
    xres0 = ext("xres0", [T, 1024])
    sel = ext("sel", [128, 2])
    ropes = ext("ropes", [2, 2, 128, S])
    W = []
    for l in range(depth):
        d = {}
        if l % 2 == 0:
            d["wfm"] = ext(f"wfm{l}", [2, 1024, 1024])
            d["wv"] = ext(f"wv{l}", [2, 1024, 256])
            d["lam"] = ext(f"lam{l}", [128, 256])
            d["gsub"] = ext(f"gsub{l}", [128, 1])
        else:
            d["wq"] = ext(f"wq{l}", [1024, 512])
            d["wlr"] = ext(f"wlr{l}", [1024, 16])
            d["wtm"] = ext(f"wtm{l}", [1024, 1280])
            d["wgu"] = ext(f"wgu{l}", [16, 256])
            d["bg"] = ext(f"bg{l}", [1, 256])
            d["gn"] = ext(f"gn{l}", [128, 256])
        d["wout"] = ext(f"wout{l}", [1024, 1024])
        d["w1"] = ext(f"w1_{l}", [1024, 4096])
        d["w2"] = ext(f"w2_{l}", [4096, 1024])
        d["lnp"] = ext(f"lnp{l}", [128, 4, 1024])
        W.append(d)
    xout = nc.dram_tensor("xout", [T, 1024], F32, kind="ExternalOutput").ap()
    HC = 1024
    XC_ = 512
    hoT_own = [nc.dram_tensor(f"hoT_own{k}", [512, HC], BF16).ap() for k in range(S // HC)]
    ho_all = [nc.dram_tensor(f"ho_all{k}", [1024, HC], BF16).ap() for k in range(S // HC)]
    xT_own = [nc.dram_tensor(f"xT_own{k}", [1024, XC_], BF16).ap() for k in range(T // XC_)]
    xT_all = [nc.dram_tensor(f"xT_all{k}", [2048, XC_], BF16).ap() for k in range(T // XC_)]
    xres_i = [nc.dram_tensor(f"xres_i{i}", [T, 1024], F32).ap() for i in range(2)]
    groups = [[2 * i, 2 * i + 1] for i in range(ncore // 2)]
    WBF = [dict(wout=nc.dram_tensor(f"woutb{l}", [1024, 1024], BF16).ap(),
                w1=nc.dram_tensor(f"w1b_{l}", [1024, 4096], BF16).ap(),
                w2=nc.dram_tensor(f"w2b_{l}", [4096, 1024], BF16).ap()) for l in range(depth)]

    with contextlib.ExitStack() as st:
        kb = KB(nc, st)
        fw = kb.fw
        kb.consts(moba=True)
        for l in range(depth):
            d = W[l]
            if l == 0:
                def xsrc(t0):
                    return xT0.rearrange("(k p) t -> p k t", p=128)[:, :, t0:t0 + 512], False
            else:
                def xsrc(t0):
                    r, tl = t0 // T, t0 % T
                    return xT_all[tl // XC_][r * 1024:(r + 1) * 1024, :].rearrange("(k p) t -> p k t", p=128), True

            def hodst(och, Q0):
                return hoT_own[Q0 // HC][och * 128:(och + 1) * 128, Q0 % HC:Q0 % HC + 512]

            def hodst4(t0):
                return hoT_own[t0 // HC].rearrange("(c p) t -> p c t", p=128)[:, :, t0 % HC:t0 % HC + 512]

            def hosrc(tok0):
                return ho_all[tok0 // HC].rearrange("(c p) t -> p c t", p=128)[:, :, tok0 % HC:tok0 % HC + 512]

            def xTdst(t0):
                return xT_own[t0 // XC_].rearrange("(k p) t -> p k t", p=128)
            def hook(l=l, d=d):
                cb = Buf(f"wconv{l}")
                for nm in ("wout", "w1", "w2"):
                    src, dst = d[nm], WBF[l][nm]
                    for r0 in range(0, src.shape[0], 256):
                        fw.dma(cb, dst[r0:r0 + 256, :], src[r0:r0 + 256, :], queue="pool")

            with fw.stage():
                if l % 2 == 0:
                    lam_init = 0.8 - 0.6 * math.exp(-0.3 * l)
                    stage_even(kb, S, xsrc, d["wfm"], d["wv"], ropes, d["lam"], d["gsub"], hodst, lam_init, hook=hook)
                else:
                    stage_gla(kb, S, xsrc, d["wq"], d["wlr"], d["wtm"], d["wgu"], d["bg"], d["gn"], hodst4, hook=hook)
            with fw.stage():
                for k in range(S // HC):
                    fw.cc(Buf(f"ccA{l}_{k}"), "AllGather", hoT_own[k], ho_all[k], groups)
            xin = xres0 if l == 0 else xres_i[(l - 1) % 2]
            xo = xout if l == depth - 1 else xres_i[l % 2]
            with fw.stage():
                outs = stage_row2(kb, T, xin, WBF[l]["wout"], WBF[l]["w1"], WBF[l]["w2"], d["lnp"], xo, xTdst,
                                  hosrc, sel, T)
                if l == depth - 1:
                    fw.wait_all("sp", outs)
            if l < depth - 1:
                with fw.stage():
                    for k in range(T // XC_):
                        fw.cc(Buf(f"ccB{l}_{k}"), "AllGather", xT_own[k], xT_all[k], groups)
        fw.emit()
        print("instr counts", {k: len(v) for k, v in fw.q.items()}, "sems", fw.nsem, flush=True)
    return nc


def fused_inputs(inp, c, S, depth):
    b, h = c // 2, c % 2
    T = S // 2
    x = inp["x"]
    m = dict(xT0=np.ascontiguousarray(x[b, :S].T), xres0=np.ascontiguousarray(x[b, h * T:(h + 1) * T]),
             sel=np.ascontiguousarray(np.broadcast_to(np.eye(2, dtype=np.float32)[h][None], (128, 2))))
    for l in range(depth):
        if l % 2 == 0:
            e = even_inputs(inp, l // 2, h, None, S)
            m["ropes"] = e["ropes"]
            m[f"wfm{l}"], m[f"wv{l}"], m[f"lam{l}"], m[f"gsub{l}"] = e["wfm"], e["wv"], e["lam128"], e["gsub"]
        else:
            g = gla_inputs(inp, l // 2, h, None, S)
            for k in ("wq", "wlr", "wtm", "wgu", "bg"):
                m[f"{k}{l}"] = g[k]
            m[f"gn{l}"] = g["gn128"]
        r = row_inputs(inp, l, None, None)
        m[f"wout{l}"], m[f"w1_{l}"], m[f"w2_{l}"], m[f"lnp{l}"] = r["wout"], r["w1"], r["w2"], r["lnp"]
    return m


def kernel(**inputs):
    inp = {k: np.asarray(v) for k, v in inputs.items()}
    x = inp["x"]
    Bn, S, _ = x.shape
    depth = inp["ln_mix_g"].shape[0]
    ncore = 2 * Bn
    nc = build_fused(S, depth, ncore)
    maps = [fused_inputs(inp, c, S, depth) for c in range(ncore)]
    res = run_bass_kernel_spmd(nc, maps, core_ids=list(range(ncore)))
    out = np.stack([np.concatenate([res.results[2 * b]["xout"], res.results[2 * b + 1]["xout"]], axis=0)
                    for b in range(Bn)])
    return out.astype(np.float32)
```

```python
import contextlib
import numpy as np
import concourse.bass as bass
import concourse.mybir as mybir
from concourse.bass_utils import run_bass_kernel_spmd

F32 = mybir.dt.float32
BF16 = mybir.dt.bfloat16
AF = mybir.ActivationFunctionType
ALU = mybir.AluOpType
AX = mybir.AxisListType

SEM_LIMIT = 30000


class Ent:
    __slots__ = ("stream", "seq", "flag", "hw", "val", "n", "item")

    def __init__(self, stream, seq, n):
        self.stream = stream
        self.seq = seq
        self.flag = False
        self.hw = None
        self.val = 0
        self.n = n
        self.item = None


class Stream:
    def __init__(self, name, inorder):
        self.name = name
        self.inorder = inorder
        self.ents = []

    def new(self, n):
        e = Ent(self, len(self.ents) + 1, n)
        self.ents.append(e)
        return e


class Item:
    __slots__ = ("waits", "fn", "ent")

    def __init__(self, waits, fn, ent):
        self.waits = waits
        self.fn = fn
        self.ent = ent


class Buf:
    def __init__(self, name=""):
        self.name = name
        self.w = {}
        self.r = {}
        self.ds = None


def _merge(dst, src):
    for k, e in src.items():
        o = dst.get(k)
        if o is None or o.seq < e.seq:
            dst[k] = e


class FW:
    def __init__(self, nc, stack):
        self.nc = nc
        self.stack = stack
        self.engs = ["sp", "pe", "act", "dve", "pool"]
        self.q = {k: [] for k in self.engs}
        self.es = {k: Stream(k, True) for k in self.engs}
        self.seen = {k: {} for k in self.engs}
        self.dstreams = []
        self.free_streams = []
        self.live_streams = []
        self.nsem = 0
        self.alloc_stack = stack

    def sb(self, name, shape, dtype):
        self.ntens = getattr(self, "ntens", 0) + 1
        name = f"{name}_{self.ntens}"
        return self.alloc_stack.enter_context(self.nc.sbuf_tensor(name, list(shape), dtype))

    def ps(self, name, shape, dtype=F32):
        self.ntens = getattr(self, "ntens", 0) + 1
        name = f"{name}_{self.ntens}"
        return self.alloc_stack.enter_context(self.nc.psum_tensor(name, list(shape), dtype))

    def dstream(self, name):
        s = Stream(name, False)
        self.dstreams.append(s)
        return s

    def _hw(self, name):
        self.nsem += 1
        return self.stack.enter_context(self.nc.semaphore(f"s{self.nsem}_{name}"))

    def _waits(self, eng, reads, writes):
        raw = {}
        for b in reads:
            _merge(raw, b.w)
        oth = {}
        for b in writes:
            _merge(oth, b.w)
            _merge(oth, b.r)
        own = self.es[eng]
        need = dict(raw)
        for k, e in oth.items():
            if e.stream is own and eng == "pe":
                continue
            o = need.get(k)
            if o is None or o.seq < e.seq:
                need[k] = e
        waits = []
        seen = self.seen[eng]
        for k, e in need.items():
            if seen.get(k, 0) >= e.seq:
                continue
            seen[k] = e.seq
            e.flag = True
            waits.append(e)
        return waits

    def _commit(self, ent, reads, writes):
        k = id(ent.stream)
        for b in writes:
            b.w = {k: ent}
            b.r = {}
        for b in reads:
            o = b.r.get(k)
            if o is None or o.seq < ent.seq:
                b.r[k] = ent

    def op(self, eng, fn, reads=(), writes=()):
        waits = self._waits(eng, reads, writes)
        ent = self.es[eng].new(1)
        it = Item(waits, fn, ent)
        ent.item = it
        self.q[eng].append(it)
        self._commit(ent, reads, writes)
        return ent

    def dma(self, sbuf, out, in_, reads=(), writes=(), queue="sp", **kw):
        stream = getattr(sbuf, "ds", None)
        if stream is None:
            stream = sbuf.ds = self._take_stream("d" + sbuf.name)
        waits = self._waits(queue, reads, writes)
        ent = stream.new(16)
        ent.flag = True
        it = Item(waits, lambda e: e.dma_start(out=out, in_=in_, **kw), ent)
        ent.item = it
        self.q[queue].append(it)
        self._commit(ent, reads, writes)
        return ent

    def _take_stream(self, name):
        if self.free_streams:
            st = self.free_streams.pop()
        else:
            st = self.dstream(name)
        self.live_streams.append(st)
        return st

    def cc(self, buf, kind, in_ap, out_ap, groups, reads=(), writes=()):
        stream = getattr(buf, "ds", None)
        if stream is None:
            stream = buf.ds = self._take_stream("cc" + buf.name)
        waits = self._waits("pool", reads, writes)
        ent = stream.new(1)
        ent.flag = True
        it = Item(waits, lambda e: e.collective_compute(kind, op=ALU.bypass, replica_groups=groups,
                                                        ins=[in_ap.opt()], outs=[out_ap.opt()]), ent)
        ent.item = it
        self.q["pool"].append(it)
        self._commit(ent, reads, writes)
        return ent

    def barrier(self):
        toks = []
        for k in self.engs:
            if self.es[k].ents:
                toks.append(self.es[k].ents[-1])
        for st in self.live_streams:
            if st.ents:
                toks.append(st.ents[-1])
        for eng in self.engs:
            waits = []
            seen = self.seen[eng]
            for e in toks:
                if e.stream is self.es[eng]:
                    continue
                k = id(e.stream)
                if seen.get(k, 0) >= e.seq:
                    continue
                seen[k] = e.seq
                e.flag = True
                waits.append(e)
            self.q[eng].append(Item(waits, None, None))
        self.free_streams.extend(self.live_streams)
        self.live_streams = []

    @contextlib.contextmanager
    def stage(self):
        outer = self.alloc_stack
        with contextlib.ExitStack() as sub:
            self.alloc_stack = sub
            try:
                yield
            finally:
                self.alloc_stack = outer
            self.barrier()

    def wait_all(self, eng, bufs):
        waits = self._waits(eng, bufs, ())
        self.q[eng].append(Item(waits, None, None))

    def emit(self):
        for s in list(self.es.values()) + self.dstreams:
            hw = None
            val = 0
            prev = None
            for e in s.ents:
                if not e.flag:
                    continue
                if hw is None or val + e.n > SEM_LIMIT:
                    if hw is not None and not s.inorder:
                        e.item.waits.append(prev)
                    hw = self._hw(s.name)
                    val = 0
                val += e.n
                e.hw = hw
                e.val = val
                prev = e
        nc = self.nc

        def mk(name):
            items = self.q[name]

            def body(e):
                for it in items:
                    for w in it.waits:
                        e.wait_ge(w.hw, w.val)
                    if it.fn is not None:
                        ins = it.fn(e)
                        if it.ent.flag:
                            ins.then_inc(it.ent.hw, it.ent.n)

            return body

        with nc.Block() as block:
            block.sync(mk("sp"))
            block.tensor(mk("pe"))
            block.scalar(mk("act"))
            block.vector(mk("dve"))
            block.gpsimd(mk("pool"))


import math


D = 1024
ALPHA = 8.0 ** 0.25
NEG = -30000.0


class KB:
    def __init__(self, nc, stack):
        self.nc = nc
        self.fw = FW(nc, stack)
        self.nps = 0

    def sb(self, name, shape, dt):
        return self.fw.sb(name, shape, dt)

    def bank(self, dt=F32):
        self.nps += 1
        n = 512 if dt == F32 else 1024
        return self.fw.ps(f"ps{self.nps}", [128, n], dt), Buf(f"ps{self.nps}")

    def mm(self, out, lhsT, rhs, start, stop, reads, writes):
        return self.fw.op("pe", lambda e: e.matmul(out, lhsT=lhsT, rhs=rhs, start=start, stop=stop,
                                                   skip_group_check=True), reads, writes)

    def tr(self, out, in_, ident, reads, writes):
        return self.fw.op("pe", lambda e: e.transpose(out=out, in_=in_, identity=ident), reads, writes)

    def act(self, out, in_, func, reads, writes, **kw):
        return self.fw.op("act", lambda e: e.activation(out=out, in_=in_, func=func, **kw), reads, writes)

    def tt(self, eng, out, in0, in1, op, reads, writes):
        return self.fw.op(eng, lambda e: e.tensor_tensor(out=out, in0=in0, in1=in1, op=op), reads, writes)

    def ts(self, eng, out, in0, s1, s2, op0, op1, reads, writes):
        if op1 is None:
            return self.fw.op(eng, lambda e: e.tensor_scalar(out=out, in0=in0, scalar1=s1, scalar2=None, op0=op0),
                              reads, writes)
        return self.fw.op(eng, lambda e: e.tensor_scalar(out=out, in0=in0, scalar1=s1, scalar2=s2, op0=op0, op1=op1),
                          reads, writes)

    def stt(self, eng, out, in0, scalar, in1, op0, op1, reads, writes):
        return self.fw.op(eng, lambda e: e.scalar_tensor_tensor(out=out, in0=in0, scalar=scalar, in1=in1,
                                                                op0=op0, op1=op1), reads, writes)

    def cp(self, eng, out, in_, reads, writes):
        if eng == "act":
            return self.fw.op("act", lambda e: e.copy(out=out, in_=in_), reads, writes)
        return self.fw.op(eng, lambda e: e.tensor_copy(out=out, in_=in_), reads, writes)

    def memset(self, eng, ap, val, writes):
        return self.fw.op(eng, lambda e: e.memset(ap, val), (), writes)

    def asel(self, out, in_, pattern, cmp, fill, base, cm, bufs):
        return self.fw.op("pool", lambda e: e.affine_select(out=out, in_=in_, pattern=pattern, compare_op=cmp,
                                                            fill=fill, base=base, channel_multiplier=cm), bufs, bufs)

    def consts(self, moba=False):
        c = {}
        B = Buf("consts")
        c["buf"] = B
        ident = self.sb("ident", [128, 128], BF16)
        self.memset("pool", ident[:], 1.0, [B])
        self.asel(ident[:], ident[:], [[-1, 128]], ALU.is_equal, 0.0, 0, 1, [B])
        c["ident"] = ident
        negm = self.sb("negm", [128, 128], BF16)
        self.memset("pool", negm[:], 0.0, [B])
        self.asel(negm[:], negm[:], [[1, 128]], ALU.is_ge, NEG, 0, -1, [B])
        c["negm"] = negm
        tri = self.sb("tri", [128, 128], F32)
        self.memset("pool", tri[:], 1.0, [B])
        self.asel(tri[:], tri[:], [[1, 128]], ALU.is_ge, 0.0, 0, -1, [B])
        c["tri"] = tri
        ui = self.sb("uincl", [128, 128], F32)
        self.memset("pool", ui[:], -1.0 / 16.0, [B])
        self.asel(ui[:], ui[:], [[1, 128]], ALU.is_ge, 0.0, 0, -1, [B])
        c["uincl"] = ui
        us = self.sb("ustr", [128, 128], F32)
        self.memset("pool", us[:], -1.0 / 16.0, [B])
        self.asel(us[:], us[:], [[-1, 128]], ALU.is_gt, 0.0, 0, 1, [B])
        c["ustr"] = us
        ones_f = self.sb("ones_f", [128, 128], F32)
        self.memset("pool", ones_f[:], 1.0, [B])
        c["ones_f"] = ones_f
        ones_b = self.sb("ones_b", [128, 128], BF16)
        self.memset("pool", ones_b[:], 1.0, [B])
        c["ones_b"] = ones_b
        e01 = self.sb("e01", [128, 4], BF16)
        self.memset("pool", e01[:], 0.0, [B])
        self.memset("pool", e01[:, 0:1], 1.0, [B])
        self.memset("pool", e01[:, 3:4], 1.0, [B])
        c["e01"] = e01
        e01f = self.sb("e01f", [128, 4], F32)
        self.memset("pool", e01f[:], 0.0, [B])
        self.memset("pool", e01f[:, 0:1], 1.0, [B])
        self.memset("pool", e01f[:, 3:4], 1.0, [B])
        c["e01f"] = e01f
        sel2 = self.sb("sel2", [2, 256], F32)
        self.memset("pool", sel2[:], 1.0, [B])
        self.asel(sel2[:, 0:128], sel2[:, 0:128], [[0, 128]], ALU.is_equal, 0.0, 0, 1, [B])
        self.asel(sel2[:, 128:256], sel2[:, 128:256], [[0, 128]], ALU.is_equal, 0.0, -1, 1, [B])
        c["sel2"] = sel2
        if not moba:
            self.c = c
            return c
        esel = self.sb("esel", [32, 32 * 128], BF16)
        self.memset("pool", esel[:], NEG, [B])
        ev = esel[:].rearrange("p (n k) -> p n k", k=128)
        self.asel(ev, ev, [[-1, 32], [0, 128]], ALU.is_equal, 0.0, 0, 1, [B])
        c["esel"] = esel
        self.c = c
        return c


def stage_row(kb, T, ho, xres, wout, w1, w2, lnp, xout, xTout, dbg=None, ho_sel=None, w_bf16=False):
    fw = kb.fw
    c = kb.c
    CB = c["buf"]
    wo_sb = kb.sb("wo_sb", [128, 8, 1024], BF16)
    WO = Buf("wo")
    ln_sb = kb.sb("ln_sb", [128, 4, 1024], F32)
    LNB = Buf("ln")
    wq_ = "sp" if w_bf16 else "pool"
    fw.dma(WO, wo_sb[:], wout.rearrange("(c p) n -> p c n", p=128), writes=[WO], queue=wq_)
    fw.dma(LNB, ln_sb[:], lnp, writes=[LNB])
    NW = 2
    w1b = [kb.sb(f"w1b{i}", [128, 8, 1024], BF16) for i in range(NW)]
    w2b = [kb.sb(f"w2b{i}", [128, 8, 1024], BF16) for i in range(NW)]
    W1B = [Buf() for _ in range(NW)]
    W2B = [Buf() for _ in range(NW)]
    hoc = kb.sb("hoc", [128, 8, 512], BF16)
    HOC = Buf("hoc")
    if ho_sel is not None:
        hoa = hoc
        hob = kb.sb("hob", [128, 8, 512], BF16)
        sel_sb = kb.sb("sel_sb", [128, 2], F32)
        HOA, HOBB, SELB = HOC, Buf("hob"), Buf("selb")
        fw.dma(SELB, sel_sb[:], ho_sel[1], writes=[SELB])
    y = kb.sb("y", [128, 4, 1024], F32)
    Y = [Buf(f"y{i}") for i in range(4)]
    acc = kb.sb("acc", [128, 4, 1024], F32)
    ACC = [Buf(f"acc{i}") for i in range(4)]
    xb = kb.sb("xb", [128, 1024], BF16)
    XB = Buf()
    x1T = kb.sb("x1T", [128, 8, 512], BF16)
    X1T = Buf("x1T")
    xTo, XTO = x1T, X1T
    hsq = [kb.sb(f"hsq{i}", [128, 8, 512], BF16) for i in range(2)]
    HSQ = [[Buf() for _ in range(8)] for _ in range(2)]
    rl = [kb.sb(f"rl{i}", [128, 512], F32) for i in range(2)]
    RL = [Buf() for _ in range(2)]
    st6 = kb.sb("st6", [128, 2, 6], F32)
    mv = kb.sb("mv", [128, 2], F32)
    rstd = kb.sb("rstd", [128, 2], F32)
    STB = Buf()
    G = [kb.bank() for _ in range(4)]
    TR = [kb.bank(BF16) for _ in range(2)]
    OUTS = []

    def nout():
        b = Buf("o")
        OUTS.append(b)
        return b
    gi = [0]

    def nextG():
        g = G[gi[0] % 4]
        gi[0] += 1
        return g

    ti = [0]

    def layer_norm(buf_ap, BUFS, j, gidx):
        v = buf_ap[:, j, :]
        for hh in range(2):
            fw.op("dve", lambda e, hh=hh: e.bn_stats(out=st6[:, hh, :], in_=buf_ap[:, j, hh * 512:(hh + 1) * 512]),
                  [BUFS[j]], [STB])
        fw.op("dve", lambda e: e.bn_aggr(out=mv[:], in_=st6[:].rearrange("p a b -> p (a b)")), [STB], [STB])
        kb.act(rstd[:, 0:1], mv[:, 1:2], AF.Sqrt, [STB], [STB], bias=1e-5, scale=1.0)
        fw.op("dve", lambda e: e.reciprocal(out=rstd[:, 1:2], in_=rstd[:, 0:1]), [STB], [STB])
        kb.ts("dve", v, v, mv[:, 0:1], rstd[:, 1:2], ALU.subtract, ALU.mult, [STB, BUFS[j]], [BUFS[j]])
        kb.tt("pool", v, v, ln_sb[:, gidx, :], ALU.mult, [BUFS[j], LNB], [BUFS[j]])
        kb.tt("pool", v, v, ln_sb[:, gidx + 1, :], ALU.add, [BUFS[j], LNB], [BUFS[j]])

    def to_T(src_ap, SRC, j, dstT, DST):
        kb.cp("act", xb[:], src_ap[:, j, :], [SRC[j]], [XB])
        trp, TRB = TR[ti[0] % 2]
        ti[0] += 1
        for k in range(8):
            kb.tr(trp[:, k * 128:(k + 1) * 128], xb[:, k * 128:(k + 1) * 128], c["ident"][:], [XB, CB], [TRB])
        kb.cp("dve", dstT[:, :, j * 128:(j + 1) * 128], trp[:].rearrange("p (k t) -> p k t", t=128), [TRB], [DST])

    nst = T // 512
    for st in range(nst):
        t0 = st * 512
        if ho_sel is None:
            fw.dma(HOC, hoc[:], ho.rearrange("c p t -> p c t")[:, :, t0:t0 + 512], writes=[HOC])
        else:
            hosrc, Thalf = ho_sel[0], ho_sel[2]
            fw.dma(HOA, hoa[:], hosrc(t0), writes=[HOA])
            fw.dma(HOBB, hob[:], hosrc(Thalf + t0), writes=[HOBB])
            kb.act(hoa[:], hoa[:], AF.Copy, [HOA, SELB], [HOA], scale=sel_sb[:, 0:1])
            kb.stt("dve", hoa[:], hob[:], sel_sb[:, 1:2], hoa[:], ALU.mult, ALU.add, [HOBB, HOA, SELB], [HOA])
        fw.dma(Y[0], y[:], xres[t0:t0 + 512, :].rearrange("(j p) d -> p j d", p=128), writes=Y)
        for j in range(4):
            for nb in range(2):
                g, GB = nextG()
                for cc in range(8):
                    kb.mm(g[:], hoc[:, cc, j * 128:(j + 1) * 128], wo_sb[:, cc, nb * 512:(nb + 1) * 512],
                          cc == 0, cc == 7, [HOC, WO], [GB])
                kb.stt("dve", y[:, j, nb * 512:(nb + 1) * 512], y[:, j, nb * 512:(nb + 1) * 512], ALPHA, g[:],
                       ALU.mult, ALU.add, [Y[j], GB], [Y[j]])
            layer_norm(y, Y, j, 0)
            to_T(y, Y, j, x1T, X1T)
        if dbg is not None:
            fw.dma(Y[0], dbg[0], y[:], reads=Y, writes=[nout()])
            fw.dma(X1T, dbg[1], x1T[:], reads=[X1T], writes=[nout()])
        for fb in range(4):
            wi = (st * 4 + fb) % NW
            fw.dma(W1B[wi], w1b[wi][:], w1.rearrange("(k p) f -> p k f", p=128)[:, :, fb * 1024:(fb + 1) * 1024],
                   writes=[W1B[wi]], queue=wq_)
            fw.dma(W2B[wi], w2b[wi][:], w2[fb * 1024:(fb + 1) * 1024, :].rearrange("(c p) d -> p c d", p=128),
                   writes=[W2B[wi]], queue=wq_)
            hi = fb % 2
            for fc in range(8):
                g, GB = nextG()
                for k in range(8):
                    kb.mm(g[:], w1b[wi][:, k, fc * 128:(fc + 1) * 128], x1T[:, k, :], k == 0, k == 7,
                          [W1B[wi], X1T], [GB])
                ri = fc % 2
                kb.act(rl[ri][:], g[:], AF.Relu, [GB], [RL[ri]])
                kb.tt("pool", hsq[hi][:, fc, :], rl[ri][:], rl[ri][:], ALU.mult, [RL[ri]], [HSQ[hi][fc]])
            for j in range(4):
                for nb in range(2):
                    g, GB = nextG()
                    for fc in range(8):
                        kb.mm(g[:], hsq[hi][:, fc, j * 128:(j + 1) * 128], w2b[wi][:, fc, nb * 512:(nb + 1) * 512],
                              fc == 0, fc == 7, [HSQ[hi][fc], W2B[wi]], [GB])
                    a = acc[:, j, nb * 512:(nb + 1) * 512]
                    if fb == 0:
                        kb.cp("dve", a, g[:], [GB], [ACC[j]])
                    else:
                        kb.tt("dve", a, a, g[:], ALU.add, [GB, ACC[j]], [ACC[j]])
        if dbg is not None:
            fw.dma(ACC[0], dbg[2], acc[:], reads=ACC, writes=[nout()])
            fw.dma(HSQ[1][0], dbg[3], hsq[1][:], reads=HSQ[1], writes=[nout()])
        for j in range(4):
            kb.stt("dve", acc[:, j, :], y[:, j, :], ALPHA, acc[:, j, :], ALU.mult, ALU.add, [Y[j], ACC[j]], [ACC[j]])
            layer_norm(acc, ACC, j, 2)
            to_T(acc, ACC, j, xTo, XTO)
        fw.dma(ACC[0], xout[t0:t0 + 512, :].rearrange("(j p) d -> p j d", p=128), acc[:], reads=ACC, writes=[nout()])
        fw.dma(XTO, xTout(t0), xTo[:], reads=[XTO], writes=[nout()])
    return OUTS


def stage_even(kb, S, xsrc, wfm, wv, ropes, lam128, gsub, hodst, lambda_init, hook=None):
    fw = kb.fw
    c = kb.c
    CB = c["buf"]
    nkt = S // 128
    nqb = S // 512
    nblk = S // 256
    QT = [kb.sb(f"QT{i}", [128, S], BF16) for i in range(2)]
    KT = [kb.sb(f"KT{i}", [128, S], BF16) for i in range(2)]
    V = [kb.sb(f"V{i}", [128, nkt, 128], BF16) for i in range(2)]
    QTB = [Buf() for _ in range(2)]
    KTB = [Buf() for _ in range(2)]
    VB = [Buf() for _ in range(2)]
    wfm_sb = kb.sb("wfm_sb", [128, 8, 1024], BF16)
    WFM = Buf("wfm")
    wv_sb = kb.sb("wv_sb", [128, 8, 256], BF16)
    WV = Buf("wv")
    xc = [kb.sb(f"xc{i}", [128, 8, 512], BF16) for i in range(2)]
    XC = [Buf(f"xc{i}") for i in range(2)]
    rp = [kb.sb(f"rp{i}", [128, 2, 512], F32) for i in range(2)]
    RP = [Buf(f"rp{i}") for i in range(2)]
    F = [kb.sb(f"F{i}", [128, 512], F32) for i in range(4)]
    FBUF = [Buf() for _ in range(4)]
    PT = [kb.sb(f"PT{i}", [128, 512], BF16) for i in range(4)]
    PTB = [Buf() for _ in range(4)]
    obf = [kb.sb(f"obf{i}", [128, 512], BF16) for i in range(2)]
    OBF = [Buf(f"obf{i}") for i in range(2)]
    rr = kb.sb("rr", [2, 512], F32)
    RR = Buf()
    accs = [[kb.sb(f"accs{p}_{i}", [128, 512], F32) for i in range(4)] for p in range(2)]
    ACCB = [[Buf() for _ in range(4)] for _ in range(2)]
    Os = [[kb.sb(f"Os{p}_{i}", [128, 512], F32) for i in range(2)] for p in range(2)]
    OSB = [[Buf() for _ in range(2)] for _ in range(2)]
    lam_sb = kb.sb("lam_sb", [128, 256], F32)
    gs_sb = kb.sb("gs_sb", [128, 1], F32)
    sm = kb.sb("sm_e", [128, 8], F32)
    LAM = Buf("lam")
    kmf = kb.sb("kmf", [128, 32], F32)
    kmb = [kb.sb(f"kmb{i}", [128, 32], BF16) for i in range(2)]
    KMB = [Buf() for _ in range(2)]
    Gs = kb.sb("Gs", [128, 32], F32)
    top8 = kb.sb("top8", [128, 8], F32)
    nots = kb.sb("nots", [128, 32], BF16)
    GSB = Buf()
    biasT = kb.sb("biasT", [32, 512], BF16)
    BIAS = Buf()
    SBK = [kb.bank() for _ in range(4)]
    O1, O1B = kb.bank()
    O2, O2B = kb.bank()
    SUMP, SUMB = kb.bank()
    FBK, FBB = kb.bank()
    OUTS = []

    fw.dma(LAM, lam_sb[:], lam128, writes=[LAM])
    fw.dma(LAM, gs_sb[:], gsub, writes=[LAM])
    kb.tt("dve", F[0][:, 0:64], lam_sb[:, 0:64], lam_sb[:, 64:128], ALU.mult, [LAM], [FBUF[0]])
    kb.tt("dve", F[0][:, 64:128], lam_sb[:, 128:192], lam_sb[:, 192:256], ALU.mult, [LAM], [FBUF[0]])
    fw.op("dve", lambda e: e.reduce_sum(out=sm[:, 0:1], in_=F[0][:, 0:64], axis=AX.X), [FBUF[0]], [LAM])
    fw.op("dve", lambda e: e.reduce_sum(out=sm[:, 1:2], in_=F[0][:, 64:128], axis=AX.X), [FBUF[0]], [LAM])
    kb.act(sm[:, 2:4], sm[:, 0:2], AF.Exp, [LAM], [LAM])
    kb.stt("dve", sm[:, 4:5], sm[:, 3:4], -float(lambda_init), sm[:, 2:3], ALU.add, ALU.subtract, [LAM], [LAM])

    def inproj(typ):
        fw.dma(WFM, wfm_sb[:], wfm[typ].rearrange("(k p) n -> p k n", p=128), writes=[WFM], queue="pool")
        fw.dma(WV, wv_sb[:], wv[typ].rearrange("(k p) n -> p k n", p=128), writes=[WV], queue="pool")
        gi = 0
        for cch in range(S // 512):
            t0 = cch * 512
            xi = cch % 2
            src, x_bf16 = xsrc(t0)
            fw.dma(XC[xi], xc[xi][:], src, writes=[XC[xi]], queue=("sp" if x_bf16 else "pool"))
            fw.dma(RP[xi], rp[xi][:], ropes[typ].rearrange("a p t -> p a t")[:, :, t0:t0 + 512], writes=[RP[xi]])
            for g in range(2):
                for hd in range(2):
                    dst, DB = (QT[hd], QTB[hd]) if g == 0 else (KT[hd], KTB[hd])
                    po, POB = SBK[gi % 4]
                    pp, PPB = SBK[(gi + 1) % 4]
                    gi += 2
                    fo = g * 4 + hd
                    fp = g * 4 + 2 + hd
                    for k in range(8):
                        kb.mm(po[:], wfm_sb[:, k, fo * 128:(fo + 1) * 128], xc[xi][:, k, :], k == 0, k == 7,
                              [WFM, XC[xi]], [POB])
                    for k in range(8):
                        kb.mm(pp[:], wfm_sb[:, k, fp * 128:(fp + 1) * 128], xc[xi][:, k, :], k == 0, k == 7,
                              [WFM, XC[xi]], [PPB])
                    fa = (g * 2 + hd) % 2 * 2
                    kb.tt("dve", F[fa][:], po[:], rp[xi][:, 0, :], ALU.mult, [POB, RP[xi]], [FBUF[fa]])
                    kb.tt("dve", F[fa + 1][:], pp[:], rp[xi][:, 1, :], ALU.mult, [PPB, RP[xi]], [FBUF[fa + 1]])
                    kb.tt("pool", dst[:, t0:t0 + 512], F[fa][:], F[fa + 1][:], ALU.add, [FBUF[fa], FBUF[fa + 1]], [DB])
            for sub in range(4):
                pv, PVB = SBK[gi % 4]
                gi += 1
                for k in range(8):
                    kb.mm(pv[:, 0:256], xc[xi][:, k, sub * 128:(sub + 1) * 128], wv_sb[:, k, :], k == 0, k == 7,
                          [WV, XC[xi]], [PVB])
                for hd in range(2):
                    kb.cp("act", V[hd][:, cch * 4 + sub, :], pv[:, hd * 128:(hd + 1) * 128], [PVB], [VB[hd]])

    def attention(typ, hd):
        nmap = 2 if typ == 0 else 1
        scale = 64.0 ** -0.5 if typ == 0 else 128.0 ** -0.5
        och = typ * 2 + hd
        for qb in range(nqb):
            Q0 = qb * 512
            if typ == 1:
                for j in range(4):
                    q0 = Q0 + j * 128
                    ob = q0 // 256
                    kb.memset("pool", nots[:], 0.0, [GSB])
                    if ob > 0:
                        kb.memset("pool", Gs[:], -1e30, [GSB])
                        gp, GPB = SBK[j % 4]
                        kb.mm(gp[:, 0:32], QT[hd][:, q0:q0 + 128], kmb[hd][:, 0:32], True, True,
                              [QTB[hd], KMB[hd]], [GPB])
                        kb.cp("dve", Gs[:, 0:ob], gp[:, 0:ob], [GPB, GSB], [GSB])
                        fw.op("dve", lambda e: e.max(out=top8[:], in_=Gs[:]), [GSB], [GSB])
                        kb.ts("dve", nots[:, 0:ob], Gs[:, 0:ob], top8[:, 2:3], None, ALU.is_lt, None, [GSB], [GSB])
                    kb.mm(FBK[0:32, j * 128:(j + 1) * 128], nots[:, 0:32], c["ident"][:], True, True, [GSB, CB], [FBB])
                kb.cp("act", biasT[:], FBK[0:32, 0:512], [FBB], [BIAS])
            items = [(kt, m) for kt in range((Q0 + 512) // 128) for m in range(nmap)]
            n = len(items)
            OB_ = [(O1, O1B), (O2, O2B)]
            last_kt = (Q0 + 512) // 128 - 1

            def issueS(i):
                kt, m = items[i]
                K0 = kt * 128
                o = max(0, K0 - Q0)
                diag = K0 >= Q0
                sp_, SPB = SBK[i % 4]
                if typ == 0:
                    kb.mm(sp_[:, o:512], KT[hd][m * 64:(m + 1) * 64, K0:K0 + 128],
                          QT[hd][m * 64:(m + 1) * 64, Q0 + o:Q0 + 512], True, not diag, [KTB[hd], QTB[hd]], [SPB])
                else:
                    kb.mm(sp_[:, o:512], KT[hd][:, K0:K0 + 128], QT[hd][:, Q0 + o:Q0 + 512], True, False,
                          [KTB[hd], QTB[hd]], [SPB])
                    nb_ = K0 // 256
                    kb.mm(sp_[:, o:512], c["esel"][0:32, nb_ * 128:(nb_ + 1) * 128], biasT[0:32, o:512], False,
                          not diag, [CB, BIAS], [SPB])
                if diag:
                    kb.mm(sp_[:, o:o + 128], c["ident"][:], c["negm"][:], False, True, [CB], [SPB])
                kb.act(PT[i % 4][:, o:512], sp_[:, o:512], AF.Exp, [SPB], [PTB[i % 4]], scale=scale)

            def issuePV(i):
                kt, m = items[i]
                K0 = kt * 128
                o = max(0, K0 - Q0)
                Op, OpB = OB_[m]
                kb.mm(Op[:, o:512], V[hd][:, kt, :], PT[i % 4][:, o:512], kt == 0, kt == last_kt,
                      [VB[hd], PTB[i % 4]], [OpB])
                eng = "pool" if i % 3 == 2 else "dve"
                ai = (1 if eng == "pool" else 0) * 2 + m
                A_, AB_ = accs[qcount[0] % 2], ACCB[qcount[0] % 2]
                if not acc_used[ai]:
                    acc_used[ai] = True
                    if o > 0:
                        kb.memset(eng, A_[ai][:, 0:o], 0.0, [AB_[ai]])
                    kb.cp(eng, A_[ai][:, o:512], PT[i % 4][:, o:512], [PTB[i % 4]], [AB_[ai]])
                else:
                    kb.tt(eng, A_[ai][:, o:512], A_[ai][:, o:512], PT[i % 4][:, o:512], ALU.add,
                          [PTB[i % 4], AB_[ai]], [AB_[ai]])

            acc_used = [False] * 4
            if typ == 0:
                for g in range(n // 2 + 1):
                    if g < n // 2:
                        issueS(2 * g)
                        issueS(2 * g + 1)
                    if g >= 1:
                        issuePV(2 * g - 2)
                        issuePV(2 * g - 1)
                    if g % 3 == 2:
                        defer_tick()
            else:
                LA = 2
                for i in range(n + LA):
                    if i < n:
                        issueS(i)
                    if i - LA >= 0:
                        issuePV(i - LA)
                    if i % 4 == 3:
                        defer_tick()
            flush()
            par = qcount[0] % 2
            qcount[0] += 1
            nr = 2 if typ == 0 else 1
            kb.cp("act", Os[par][0][:], O1[:], [O1B], [OSB[par][0]])
            if typ == 0:
                kb.cp("dve", Os[par][1][:], O2[:], [O2B], [OSB[par][1]])
            used = [ai for ai in range(4) if acc_used[ai]]
            pending.extend(make_steps(typ, och, Q0, par, nr, used, qb % 2))

    def make_steps(typ, och, Q0, par, nr, used, oi):
        A = accs[par]
        AB = ACCB[par]
        O1s, O2s = Os[par][0], Os[par][1]
        O1sB, O2sB = OSB[par][0], OSB[par][1]
        st = []

        def s_sum():
            for ui, ai in enumerate(used):
                m_ = ai % 2
                lhs = c["e01f"][:, 2 * m_:2 * m_ + 2] if typ == 0 else c["ones_f"][:, 0:1]
                kb.mm(SUMP[0:nr, :], lhs, A[ai][:], ui == 0, ui == len(used) - 1, [CB, AB[ai]], [SUMB])
        st.append(s_sum)

        def s_rcp():
            kb.act(rr[0:nr, :], SUMP[0:nr, :], AF.Ln, [SUMB], [RR])
            kb.act(rr[0:nr, :], rr[0:nr, :], AF.Exp, [RR], [RR], scale=-1.0)
        st.append(s_rcp)

        def s_out():
            ob_ = Buf("o")
            OUTS.append(ob_)
            fw.dma(OBF[oi], hodst(och, Q0), obf[oi][:], reads=[OBF[oi]], writes=[ob_])

        if typ == 1:
            def s_b():
                kb.mm(FBK[:], c["ones_f"][0:1, :], rr[0:1, :], True, True, [CB, RR], [FBB])
                kb.cp("act", F[0][:], FBK[:], [FBB], [FBUF[0]])
            st.append(s_b)

            def s_m():
                kb.tt("dve", obf[oi][:], O1s[:], F[0][:], ALU.mult, [O1sB, FBUF[0]], [OBF[oi]])
                s_out()
            st.append(s_m)
            return st

        def s1():
            kb.mm(FBK[:], c["sel2"][0:2, 0:128], rr[0:2, :], True, True, [CB, RR], [FBB])
            kb.cp("act", F[0][:], FBK[:], [FBB], [FBUF[0]])
        st.append(s1)

        def s2():
            kb.tt("dve", F[1][:], O1s[:], F[0][:], ALU.mult, [O1sB, FBUF[0]], [FBUF[1]])
            kb.mm(FBK[:], c["sel2"][0:2, 128:256], rr[0:2, :], True, True, [CB, RR], [FBB])
            kb.cp("act", F[0][:], FBK[:], [FBB], [FBUF[0]])
        st.append(s2)

        def s3():
            kb.tt("dve", F[2][:], O2s[:], F[0][:], ALU.mult, [O2sB, FBUF[0]], [FBUF[2]])
            kb.stt("dve", F[3][:], F[2][:], sm[:, 4:5], F[1][:], ALU.mult, ALU.add, [FBUF[2], FBUF[1], LAM],
                   [FBUF[3]])
            kb.tt("dve", F[1][:], F[3][:], F[3][:], ALU.mult, [FBUF[3]], [FBUF[1]])
        st.append(s3)

        def s4():
            kb.mm(FBK[0:1, :], c["ones_f"][:, 0:1], F[1][:], True, True, [CB, FBUF[1]], [FBB])
            kb.act(rr[0:1, :], FBK[0:1, :], AF.Ln, [FBB], [RR], bias=1e-5, scale=1.0 / 128.0)
            kb.act(rr[0:1, :], rr[0:1, :], AF.Exp, [RR], [RR], scale=-0.5)
        st.append(s4)

        def s5():
            kb.mm(FBK[:], c["ones_f"][0:1, :], rr[0:1, :], True, True, [CB, RR], [FBB])
            kb.stt("dve", F[2][:], F[3][:], gs_sb[:, 0:1], FBK[:], ALU.mult, ALU.mult, [FBUF[3], LAM, FBB],
                   [FBUF[2]])
            kb.act(obf[oi][:], F[2][:], AF.Copy, [FBUF[2]], [OBF[oi]], scale=float(1.0 - lambda_init))
            s_out()
        st.append(s5)
        return st

    pending = []
    qcount = [0]

    def flush():
        while pending:
            pending.pop(0)()

    def defer_tick():
        if pending:
            pending.pop(0)()

    KMF = Buf()
    for typ in range(2):
        inproj(typ)
        if typ == 0 and hook is not None:
            hook()
        if typ == 1:
            for hd in range(2):
                kb.memset("pool", kmf[:], 0.0, [KMF])
                fw.op("dve", lambda e, hd=hd: e.reduce_sum(out=kmf[:, 0:nblk],
                                                           in_=KT[hd][:].rearrange("p (n l) -> p n l", l=256),
                                                           axis=AX.X), [KTB[hd]], [KMF])
                kb.ts("dve", kmb[hd][:], kmf[:], 1.0 / 256.0, None, ALU.mult, None, [KMF], [KMB[hd]])
        for hd in range(2):
            attention(typ, hd)
            flush()
    return OUTS


def stage_gla(kb, S, xsrc, wq, wlr, wtm, wgu, bg, gn128, hodst4, hook=None):
    fw = kb.fw
    c = kb.c
    CB = c["buf"]
    wq_sb = kb.sb("wq_sb", [128, 8, 512], BF16)
    wlr_sb = kb.sb("wlr_sb", [128, 8, 16], BF16)
    wtm_sb = kb.sb("wtm_sb", [128, 8, 1280], BF16)
    wgu_sb = kb.sb("wgu_sb", [16, 256], BF16)
    bg_sb = kb.sb("bg_sb", [1, 256], BF16)
    gn_sb = kb.sb("gn_sb", [128, 256], F32)
    WB = Buf("glaw")
    fw.dma(WB, wq_sb[:], wq.rearrange("(k p) n -> p k n", p=128), writes=[WB], queue="pool")
    fw.dma(WB, wlr_sb[:], wlr.rearrange("(k p) n -> p k n", p=128), writes=[WB], queue="pool")
    fw.dma(WB, wtm_sb[:], wtm.rearrange("(k p) n -> p k n", p=128), writes=[WB], queue="pool")
    fw.dma(WB, wgu_sb[:], wgu, writes=[WB], queue="pool")
    fw.dma(WB, bg_sb[:], bg, writes=[WB], queue="pool")
    fw.dma(WB, gn_sb[:], gn128, writes=[WB])
    if hook is not None:
        hook()
    xc = [kb.sb(f"gxc{i}", [128, 8, 512], BF16) for i in range(2)]
    XC = [Buf(f"gxc{i}") for i in range(2)]
    qk = kb.sb("qk", [128, 4, 512], F32)
    QK = Buf()
    lrT = kb.sb("lrT", [16, 512], BF16)
    LRT = Buf()
    vb = kb.sb("vb", [128, 512], BF16)
    VBB = Buf()
    sr = kb.sb("sr", [128, 512], F32)
    SRB = Buf()
    ee = kb.sb("ee", [128, 256], F32)
    sp_ = kb.sb("spl", [128, 256], F32)
    SPB = Buf()
    E3 = kb.sb("E3", [128, 256], F32)
    E3B = Buf()
    khat = kb.sb("khat", [128, 256], BF16)
    KHB = Buf()
    E1 = kb.sb("E1", [128, 128], F32)
    E2 = kb.sb("E2", [128, 128], F32)
    EB = Buf()
    dec = kb.sb("dec", [128, 2], F32)
    DECB = [Buf() for _ in range(2)]
    qtl = [kb.sb(f"qtl{i}", [128, 128], BF16) for i in range(2)]
    ktl = [kb.sb(f"ktl{i}", [128, 128], BF16) for i in range(2)]
    QTL = [Buf() for _ in range(2)]
    KTL = [Buf() for _ in range(2)]
    attm = [kb.sb(f"attm{i}", [128, 128], BF16) for i in range(2)]
    ATM = [Buf() for _ in range(2)]
    Sst = [kb.sb(f"Sst{i}", [128, 256], F32) for i in range(2)]
    Sbf = [kb.sb(f"Sbf{i}", [128, 256], BF16) for i in range(2)]
    SST = [Buf() for _ in range(2)]
    SBF = [Buf() for _ in range(2)]
    junk = kb.sb("junk", [128, 256], F32)
    ssq = kb.sb("ssq", [128, 4], F32)
    SSQ = Buf()
    og = kb.sb("og", [128, 256], F32)
    OGB = Buf()
    ogb = kb.sb("ogb", [128, 256], BF16)
    OGBB = Buf()
    hoc = [kb.sb(f"ghoc{i}", [128, 4, 512], BF16) for i in range(2)]
    HOCB = [Buf(f"ghoc{i}") for i in range(2)]
    G = [kb.bank() for _ in range(3)]
    BBK, BBB = kb.bank()
    ATK, ATB = kb.bank()
    OK_, OKB = kb.bank()
    DSK, DSB = kb.bank()
    TRK, TRB = kb.bank(BF16)
    OUTS = []
    for hd in range(2):
        kb.memset("pool", Sst[hd][:], 0.0, [SST[hd]])
        kb.memset("pool", Sbf[hd][:], 0.0, [SBF[hd]])
    gi = [0]

    def nextG():
        g = G[gi[0] % 3]
        gi[0] += 1
        return g

    lnscale = math.log(128.0 ** -0.5)
    qk2 = [qk, kb.sb("qk_b", [128, 4, 512], F32)]
    QK2 = [QK, Buf()]
    lrT2 = [lrT, kb.sb("lrT_b", [16, 512], BF16)]
    LRT2 = [LRT, Buf()]
    vb2 = [vb, kb.sb("vb_b", [128, 512], BF16)]
    VB2 = [VBB, Buf()]
    sr2 = [sr, kb.sb("sr_b", [128, 512], F32)]
    SR2 = [SRB, Buf()]
    khat2 = [khat, kb.sb("khat_b", [128, 256], BF16)]
    KH2 = [KHB, Buf()]
    qtl2 = [qtl, [kb.sb(f"qtl_b{i}", [128, 128], BF16) for i in range(2)]]
    QTL2 = [QTL, [Buf() for _ in range(2)]]
    attm2 = [attm, [kb.sb(f"attm_b{i}", [128, 128], BF16) for i in range(2)]]
    ATM2 = [ATM, [Buf() for _ in range(2)]]
    dec2 = [dec, kb.sb("dec_b", [128, 2], F32)]
    DEC2 = [DECB, [Buf() for _ in range(2)]]

    E1h = [kb.sb(f"E1h{i}", [128, 128], F32) for i in range(2)]
    E2h = [kb.sb(f"E2h{i}", [128, 128], F32) for i in range(2)]
    EBh = [Buf() for _ in range(2)]
    ATBh = [Buf() for _ in range(2)]
    OKBh = [Buf() for _ in range(2)]
    DSBh = [Buf() for _ in range(2)]
    junkh = [kb.sb(f"junkh{i}", [128, 256], F32) for i in range(2)]
    ssqh = [kb.sb(f"ssqh{i}", [128, 4], F32) for i in range(2)]
    SSQh = [Buf() for _ in range(2)]
    ogh = [kb.sb(f"ogh{i}", [128, 256], F32) for i in range(2)]
    OGBh = [Buf() for _ in range(2)]
    ogbh = [kb.sb(f"ogbh{i}", [128, 256], BF16) for i in range(2)]
    OGBBh = [Buf() for _ in range(2)]

    def prologue(cch):
        t0 = cch * 512
        xi = cch % 2
        src, x_bf16 = xsrc(t0)
        fw.dma(XC[xi], xc[xi][:], src, writes=[XC[xi]], queue=("sp" if x_bf16 else "pool"))
        for ft in range(4):
            g, GB = nextG()
            for k in range(8):
                kb.mm(g[:], wq_sb[:, k, ft * 128:(ft + 1) * 128], xc[xi][:, k, :], k == 0, k == 7, [WB, XC[xi]], [GB])
            kb.cp("act", qk2[xi][:, ft, :], g[:], [GB], [QK2[xi]])
        g, GB = nextG()
        for k in range(8):
            kb.mm(g[0:16, :], wlr_sb[:, k, :], xc[xi][:, k, :], k == 0, k == 7, [WB, XC[xi]], [GB])
        kb.cp("act", lrT2[xi][:], g[0:16, :], [GB], [LRT2[xi]])

    def front(cch, j):
        xi = cch % 2
        pj = (cch * 4 + j) % 2
        ts_ = slice(j * 128, (j + 1) * 128)
        gk, GKB = nextG()
        for k in range(8):
            kb.mm(gk[:, 0:256], xc[xi][:, k, ts_], wtm_sb[:, k, 0:256], k == 0, k == 7, [WB, XC[xi]], [GKB])
        kb.mm(gk[:, 256:512], lrT2[xi][0:16, ts_], wgu_sb[0:16, :], True, False, [LRT2[xi], WB], [GKB])
        kb.mm(gk[:, 256:512], c["ones_b"][0:1, 0:128], bg_sb[0:1, :], False, True, [CB, WB], [GKB])
        gv, GVB = nextG()
        for k in range(8):
            kb.mm(gv[:], xc[xi][:, k, ts_], wtm_sb[:, k, 256:768], k == 0, k == 7, [WB, XC[xi]], [GVB])
        kb.cp("act", vb2[pj][:], gv[:], [GVB], [VB2[pj]])
        gr, GRB = nextG()
        for k in range(8):
            kb.mm(gr[:], xc[xi][:, k, ts_], wtm_sb[:, k, 768:1280], k == 0, k == 7, [WB, XC[xi]], [GRB])
        kb.act(sr2[pj][:], gr[:], AF.Silu, [GRB], [SR2[pj]])
        kb.act(ee[:], gk[:, 256:512], AF.Exp, [GKB], [SPB], scale=-1.0)
        kb.act(sp_[:], ee[:], AF.Ln, [SPB], [SPB], bias=1.0, scale=1.0)
        for hd in range(2):
            kb.mm(BBK[:, hd * 128:(hd + 1) * 128], sp_[:, hd * 128:(hd + 1) * 128], c["uincl"][:], True, True,
                  [SPB, CB], [BBB])
        kb.mm(BBK[:, 256:512], c["ustr"][:], sp_[:], True, True, [SPB, CB], [BBB])
        kb.act(E3[:], BBK[:, 256:512], AF.Exp, [BBB], [E3B])
        kb.tt("dve", khat2[pj][:], gk[:, 0:256], E3[:], ALU.mult, [GKB, E3B], [KH2[pj]])
        for hd in range(2):
            bt = BBK[:, hd * 128:(hd + 1) * 128]
            kb.act(E1h[hd][:], bt, AF.Exp, [BBB], [EBh[hd]], bias=lnscale, scale=1.0)
            kb.act(E2h[hd][:], bt, AF.Exp, [BBB], [EBh[hd]], scale=-1.0)
            kb.act(dec2[pj][:, hd:hd + 1], BBK[:, hd * 128 + 127:hd * 128 + 128], AF.Exp, [BBB], [DEC2[pj][hd]])
        for hd in range(2):
            kb.tt("dve", qtl2[pj][hd][:], qk2[xi][:, hd, ts_], E1h[hd][:], ALU.mult, [QK2[xi], EBh[hd]], [QTL2[pj][hd]])
            kb.tt("dve", ktl[hd][:], qk2[xi][:, 2 + hd, ts_], E2h[hd][:], ALU.mult, [QK2[xi], EBh[hd]], [KTL[hd]])
        for hd in range(2):
            kb.mm(ATK[:, hd * 128:(hd + 1) * 128], ktl[hd][:], qtl2[pj][hd][:], True, True,
                  [KTL[hd], QTL2[pj][hd]], [ATB])
        for hd in range(2):
            kb.tt("dve", attm2[pj][hd][:], ATK[:, hd * 128:(hd + 1) * 128], c["tri"][:], ALU.mult, [ATB, CB],
                  [ATM2[pj][hd]])

    def back(cch, j):
        pj = (cch * 4 + j) % 2
        hi = cch % 2
        ts_ = slice(j * 128, (j + 1) * 128)
        H = range(2)
        ov = [OK_[:, hd * 256:(hd + 1) * 256] for hd in H]
        vh = [vb2[pj][:, hd * 256:(hd + 1) * 256] for hd in H]
        dv = [DSK[:, hd * 256:(hd + 1) * 256] for hd in H]
        for hd in H:
            kb.mm(ov[hd], attm2[pj][hd][:], vh[hd], True, False, [ATM2[pj][hd], VB2[pj]], [OKB])
            kb.mm(ov[hd], qtl2[pj][hd][:], Sbf[hd][:], False, True, [QTL2[pj][hd], SBF[hd]], [OKB])
        for hd in H:
            kb.mm(dv[hd], khat2[pj][:, hd * 128:(hd + 1) * 128], vh[hd], True, True, [KH2[pj], VB2[pj]], [DSB])
        for hd in H:
            kb.stt("dve", Sst[hd][:], Sst[hd][:], dec2[pj][:, hd:hd + 1], dv[hd], ALU.mult, ALU.add,
                   [SST[hd], DEC2[pj][hd], DSB], [SST[hd]])
        for hd in H:
            kb.cp("pool", Sbf[hd][:], Sst[hd][:], [SST[hd]], [SBF[hd]])
        for hd in H:
            kb.act(junkh[hd][:], ov[hd], AF.Square, [OKB], [SSQh[hd]], accum_out=ssqh[hd][:, 0:1])
        for hd in H:
            kb.act(ssqh[hd][:, 1:2], ssqh[hd][:, 0:1], AF.Ln, [SSQh[hd]], [SSQh[hd]], bias=1e-5, scale=1.0 / 256.0)
        for hd in H:
            kb.act(ssqh[hd][:, 2:3], ssqh[hd][:, 1:2], AF.Exp, [SSQh[hd]], [SSQh[hd]], scale=-0.5)
        for hd in H:
            kb.stt("dve", ogh[hd][:], ov[hd], ssqh[hd][:, 2:3], gn_sb[:], ALU.mult, ALU.mult,
                   [OKB, SSQh[hd], WB], [OGBh[hd]])
        for hd in H:
            kb.tt("pool", ogbh[hd][:], ogh[hd][:], sr2[pj][:, hd * 256:(hd + 1) * 256], ALU.mult,
                  [OGBh[hd], SR2[pj]], [OGBBh[hd]])
        for hd in H:
            for cc in range(2):
                kb.tr(TRK[:, (hd * 2 + cc) * 128:(hd * 2 + cc + 1) * 128], ogbh[hd][:, cc * 128:(cc + 1) * 128],
                      c["ident"][:], [OGBBh[hd], CB], [TRB])
        kb.cp("act", hoc[hi][:, :, ts_], TRK[:, 0:512].rearrange("p (c t) -> p c t", t=128), [TRB], [HOCB[hi]])
        if j == 3:
            ob_ = Buf("o")
            OUTS.append(ob_)
            fw.dma(HOCB[hi], hodst4(cch * 512), hoc[hi][:], reads=[HOCB[hi]], writes=[ob_])

    subs = [(cch, j) for cch in range(S // 512) for j in range(4)]
    prologue(0)
    front(0, 0)
    for idx, (cch, j) in enumerate(subs):
        if idx + 1 < len(subs):
            nc_, nj = subs[idx + 1]
            if nj == 0:
                prologue(nc_)
            front(nc_, nj)
        back(cch, j)
    return OUTS


def stage_row2(kb, T, xres, wout, w1, w2, lnp, xout, xTout, hosrc, sel, Thalf):
    fw = kb.fw
    c = kb.c
    CB = c["buf"]
    wo_sb = kb.sb("wo_sb", [128, 8, 1024], BF16)
    WO = Buf("wo")
    ln_sb = kb.sb("ln_sb", [128, 4, 1024], F32)
    LNB = Buf("ln")
    sel_sb = kb.sb("sel_sb", [128, 2], F32)
    SELB = Buf("selb")
    fw.dma(WO, wo_sb[:], wout.rearrange("(c p) n -> p c n", p=128), writes=[WO])
    fw.dma(LNB, ln_sb[:], lnp, writes=[LNB])
    fw.dma(SELB, sel_sb[:], sel, writes=[SELB])
    w1b = [kb.sb(f"w1b{i}", [128, 8, 512], BF16) for i in range(2)]
    w2b = [kb.sb(f"w2b{i}", [128, 4, 1024], BF16) for i in range(2)]
    W1B = [Buf(f"w1b{i}") for i in range(2)]
    W2B = [Buf(f"w2b{i}") for i in range(2)]
    hoa = kb.sb("hoa", [128, 8, 512], BF16)
    hob = kb.sb("hob", [128, 8, 512], BF16)
    HOA, HOBB = Buf("hoa"), Buf("hob")
    y = [kb.sb(f"y{i}", [128, 4, 1024], F32) for i in range(2)]
    Y = [[Buf(f"y{i}_{j}") for j in range(4)] for i in range(2)]
    acc = kb.sb("acc", [128, 4, 1024], F32)
    ACC = [Buf(f"acc{j}") for j in range(4)]
    xb = kb.sb("xb", [128, 1024], BF16)
    XB = Buf()
    x1T = [kb.sb(f"x1T{i}", [128, 8, 512], BF16) for i in range(2)]
    X1T = [Buf(f"x1T{i}") for i in range(2)]
    xTo = kb.sb("xTo", [128, 8, 512], BF16)
    XTO = Buf("xTo")
    hsq = [kb.sb(f"hsq{i}", [128, 4, 512], BF16) for i in range(2)]
    HSQ = [[Buf() for _ in range(4)] for _ in range(2)]
    rl = [kb.sb(f"rl{i}", [128, 512], F32) for i in range(2)]
    RL = [Buf() for _ in range(2)]
    st6 = kb.sb("st6", [128, 2, 6], F32)
    mv = kb.sb("mv", [128, 2], F32)
    rstd = kb.sb("rstd", [128, 2], F32)
    STB = Buf()
    G = [kb.bank() for _ in range(4)]
    TR = [kb.bank(BF16) for _ in range(2)]
    OUTS = []
    gi = [0]
    ti = [0]
    wi_ = [0]

    def nout():
        b = Buf("o")
        OUTS.append(b)
        return b

    def nextG():
        g = G[gi[0] % 4]
        gi[0] += 1
        return g

    def layer_norm(buf_ap, BUFS, j, gidx):
        v = buf_ap[:, j, :]
        for hh in range(2):
            fw.op("dve", lambda e, hh=hh: e.bn_stats(out=st6[:, hh, :], in_=buf_ap[:, j, hh * 512:(hh + 1) * 512]),
                  [BUFS[j]], [STB])
        fw.op("dve", lambda e: e.bn_aggr(out=mv[:], in_=st6[:].rearrange("p a b -> p (a b)")), [STB], [STB])
        kb.act(rstd[:, 0:1], mv[:, 1:2], AF.Sqrt, [STB], [STB], bias=1e-5, scale=1.0)
        fw.op("dve", lambda e: e.reciprocal(out=rstd[:, 1:2], in_=rstd[:, 0:1]), [STB], [STB])
        kb.ts("dve", v, v, mv[:, 0:1], rstd[:, 1:2], ALU.subtract, ALU.mult, [STB, BUFS[j]], [BUFS[j]])
        kb.tt("pool", v, v, ln_sb[:, gidx, :], ALU.mult, [BUFS[j], LNB], [BUFS[j]])
        kb.tt("pool", v, v, ln_sb[:, gidx + 1, :], ALU.add, [BUFS[j], LNB], [BUFS[j]])

    def to_T(src_ap, SRC, j, dstT, DST):
        kb.cp("act", xb[:], src_ap[:, j, :], [SRC[j]], [XB])
        trp, TRB = TR[ti[0] % 2]
        ti[0] += 1
        for k in range(8):
            kb.tr(trp[:, k * 128:(k + 1) * 128], xb[:, k * 128:(k + 1) * 128], c["ident"][:], [XB, CB], [TRB])
        kb.cp("dve", dstT[:, :, j * 128:(j + 1) * 128], trp[:].rearrange("p (k t) -> p k t", t=128), [TRB], [DST])

    def A_load(st):
        p = st % 2
        t0 = st * 512
        fw.dma(HOA, hoa[:], hosrc(t0), writes=[HOA])
        fw.dma(HOBB, hob[:], hosrc(Thalf + t0), writes=[HOBB])
        fw.dma(Y[p][0], y[p][:], xres[t0:t0 + 512, :].rearrange("(j p) d -> p j d", p=128), writes=Y[p])

    def A_front(st):
        p = st % 2
        kb.act(hoa[:], hoa[:], AF.Copy, [HOA, SELB], [HOA], scale=sel_sb[:, 0:1])
        kb.stt("dve", hoa[:], hob[:], sel_sb[:, 1:2], hoa[:], ALU.mult, ALU.add, [HOBB, HOA, SELB], [HOA])
        for j in range(4):
            for nb in range(2):
                g, GB = nextG()
                for cc in range(8):
                    kb.mm(g[:], hoa[:, cc, j * 128:(j + 1) * 128], wo_sb[:, cc, nb * 512:(nb + 1) * 512],
                          cc == 0, cc == 7, [HOA, WO], [GB])
                ys = y[p][:, j, nb * 512:(nb + 1) * 512]
                kb.stt("dve", ys, ys, ALPHA, g[:], ALU.mult, ALU.add, [Y[p][j], GB], [Y[p][j]])
            layer_norm(y[p], Y[p], j, 0)

    def A_T(st, j):
        p = st % 2
        to_T(y[p], Y[p], j, x1T[p], X1T[p])

    def F1(st, fb):
        p = st % 2
        wi = fb % 2
        fw.dma(W1B[wi], w1b[wi][:], w1.rearrange("(k p) f -> p k f", p=128)[:, :, fb * 512:(fb + 1) * 512],
               writes=[W1B[wi]])
        fw.dma(W2B[wi], w2b[wi][:], w2[fb * 512:(fb + 1) * 512, :].rearrange("(c p) d -> p c d", p=128),
               writes=[W2B[wi]])
        hi = fb % 2
        for fc in range(4):
            g, GB = nextG()
            for k in range(8):
                kb.mm(g[:], w1b[wi][:, k, fc * 128:(fc + 1) * 128], x1T[p][:, k, :], k == 0, k == 7,
                      [W1B[wi], X1T[p]], [GB])
            ri = fc % 2
            kb.act(rl[ri][:], g[:], AF.Relu, [GB], [RL[ri]])
            kb.tt("pool", hsq[hi][:, fc, :], rl[ri][:], rl[ri][:], ALU.mult, [RL[ri]], [HSQ[hi][fc]])

    def F2(st, fb):
        wi = fb % 2
        hi = fb % 2
        for j in range(4):
            for nb in range(2):
                g, GB = nextG()
                for fc in range(4):
                    kb.mm(g[:], hsq[hi][:, fc, j * 128:(j + 1) * 128], w2b[wi][:, fc, nb * 512:(nb + 1) * 512],
                          fc == 0, fc == 3, [HSQ[hi][fc], W2B[wi]], [GB])
                a = acc[:, j, nb * 512:(nb + 1) * 512]
                if fb == 0:
                    kb.cp("dve", a, g[:], [GB], [ACC[j]])
                else:
                    kb.tt("dve", a, a, g[:], ALU.add, [GB, ACC[j]], [ACC[j]])

    def phaseB1(st):
        p = st % 2
        for j in range(4):
            kb.stt("dve", y[p][:, j, :], y[p][:, j, :], ALPHA, acc[:, j, :], ALU.mult, ALU.add,
                   [Y[p][j], ACC[j]], [Y[p][j]])

    def B_ln(st):
        p = st % 2
        for j in range(4):
            layer_norm(y[p], Y[p], j, 2)

    def B_T(st, j):
        p = st % 2
        to_T(y[p], Y[p], j, xTo, XTO)

    def B_store(st):
        p = st % 2
        t0 = st * 512
        fw.dma(Y[p][0], xout[t0:t0 + 512, :].rearrange("(j p) d -> p j d", p=128), y[p][:], reads=Y[p], writes=[nout()])
        fw.dma(XTO, xTout(t0), xTo[:], reads=[XTO], writes=[nout()])

    nst = T // 512
    A_load(0)
    A_front(0)
    for j in range(4):
        A_T(0, j)
    F1(0, 0)
    for st in range(nst):
        for fb in range(8):
            if fb + 1 < 8:
                F1(st, fb + 1)
            F2(st, fb)
            if st > 0:
                if fb == 0:
                    B_ln(st - 1)
                elif fb == 1:
                    B_T(st - 1, 0)
                    B_T(st - 1, 1)
                elif fb == 2:
                    B_T(st - 1, 2)
                    B_T(st - 1, 3)
                    B_store(st - 1)
            if st + 1 < nst:
                if fb == 2:
                    A_load(st + 1)
                elif fb == 3:
                    A_front(st + 1)
                elif fb >= 4:
                    A_T(st + 1, fb - 4)
                if fb == 7:
                    F1(st + 1, 0)
        phaseB1(st)
    B_ln(nst - 1)
    for j in range(4):
        B_T(nst - 1, j)
    B_store(nst - 1)
    return OUTS


ROPE_THETA = 10000.0


def rope_table(S, dim, nrows):
    half = dim // 2
    inv = (1.0 / (ROPE_THETA ** (np.arange(0, dim, 2, dtype=np.float32) / np.float32(dim)))).astype(np.float32)
    ang = np.arange(S, dtype=np.float32)[None, :] * inv[:, None]
    cs = np.cos(ang).astype(np.float32)
    sn = np.sin(ang).astype(np.float32)
    out = np.zeros((2, nrows, S), np.float32)
    for r in range(nrows):
        i = (r % dim) % half
        out[0, r] = cs[i]
        out[1, r] = -sn[i] if (r % dim) < half else sn[i]
    return out


def perm_cols(w, dim):
    n = w.shape[-1] // dim
    w4 = w.reshape(w.shape[0], n, 2, dim // 2)
    return np.ascontiguousarray(w4[:, :, ::-1, :]).reshape(w.shape)


def even_inputs(inp, e, h, xT, S):
    w = inp["hy_w_in"][e]
    hs = slice(2 * h * 128, (2 * h + 2) * 128)
    def fm(q, k, dim):
        qc = q[:, hs]; kc = k[:, hs]
        return np.concatenate([qc, perm_cols(qc, dim), kc, perm_cols(kc, dim)], axis=1)
    wfm = np.stack([fm(w[:, 0:512], w[:, 512:1024], 64), fm(w[:, 1536:2048], w[:, 2048:2560], 128)])
    wv = np.stack([w[:, 1024:1536][:, hs], w[:, 2560:3072][:, hs]])
    ropes = np.stack([rope_table(S, 64, 128), rope_table(S, 128, 128)])
    lam128 = np.broadcast_to(inp["diff_lambda"][e].reshape(1, 256), (128, 256))
    gsub = inp["diff_subln"][e].reshape(128, 1)
    return dict(xT=(None if xT is None else np.ascontiguousarray(xT)), wfm=np.ascontiguousarray(wfm, dtype=np.float32),
                wv=np.ascontiguousarray(wv, dtype=np.float32), ropes=ropes,
                lam128=np.ascontiguousarray(lam128, dtype=np.float32), gsub=np.ascontiguousarray(gsub, dtype=np.float32))


def gla_inputs(inp, o, h, xT, S):
    w = inp["gla_w_in"][o]
    hk = slice(2 * h * 128, (2 * h + 2) * 128)
    hv = slice(2 * h * 256, (2 * h + 2) * 256)
    q = w[:, 0:512][:, hk]; k = w[:, 512:1024][:, hk]
    v = w[:, 1024:2048][:, hv]; r = w[:, 2048:3072][:, hv]
    wq = np.concatenate([q, k], axis=1)
    wtm = np.concatenate([k, v, r], axis=1)
    f = lambda a: np.ascontiguousarray(a, dtype=np.float32)
    return dict(xT=(None if xT is None else np.ascontiguousarray(xT)), wq=f(wq), wlr=f(w[:, 3072:3088]), wtm=f(wtm),
                wgu=f(inp["gla_w_gate_up"][o][:, hk]), bg=f(inp["gla_b_gate"][o][hk].reshape(1, 256)),
                gn128=f(np.broadcast_to(inp["gla_norm"][o].reshape(1, 256), (128, 256))))


def row_inputs(inp, l, ho, xres):
    if l % 2 == 0:
        wo = inp["hy_w_out"][l // 2]
        wo = np.concatenate([wo[0:256], wo[512:768], wo[256:512], wo[768:1024]], axis=0)
    else:
        wo = inp["gla_w_out"][l // 2]
    ln = np.stack([inp["ln_mix_g"][l], inp["ln_mix_b"][l], inp["ln_ffn_g"][l], inp["ln_ffn_b"][l]])
    f = lambda a: np.ascontiguousarray(a, dtype=np.float32)
    return dict(ho=(None if ho is None else np.ascontiguousarray(ho)), xres=(None if xres is None else f(xres)), wout=f(wo), w1=f(inp["ffn_w1"][l]), w2=f(inp["ffn_w2"][l]),
                lnp=f(np.broadcast_to(ln[None], (128, 4, 1024))))


def build_even(S, x_bf16, lambda_init):
    nc = bass.Bass("TRN2", target_bir_lowering=False)
    xT = nc.dram_tensor("xT", [1024, S], BF16 if x_bf16 else F32, kind="ExternalInput").ap()
    wfm = nc.dram_tensor("wfm", [2, 1024, 1024], F32, kind="ExternalInput").ap()
    wv = nc.dram_tensor("wv", [2, 1024, 256], F32, kind="ExternalInput").ap()
    ropes = nc.dram_tensor("ropes", [2, 2, 128, S], F32, kind="ExternalInput").ap()
    lam = nc.dram_tensor("lam128", [128, 256], F32, kind="ExternalInput").ap()
    gsub = nc.dram_tensor("gsub", [128, 1], F32, kind="ExternalInput").ap()
    hoT = nc.dram_tensor("hoT", [4, 128, S], BF16, kind="ExternalOutput").ap()
    with contextlib.ExitStack() as st:
        kb = KB(nc, st)
        kb.consts(moba=True)
        outs = stage_even(kb, S, lambda t0: (xT.rearrange("(k p) t -> p k t", p=128)[:, :, t0:t0 + 512], x_bf16), wfm, wv, ropes, lam, gsub, lambda och, Q0: hoT[och][:, Q0:Q0 + 512], lambda_init)
        kb.fw.wait_all("sp", outs)
        kb.fw.emit()
    return nc


def build_gla(S, x_bf16):
    nc = bass.Bass("TRN2", target_bir_lowering=False)
    xT = nc.dram_tensor("xT", [1024, S], BF16 if x_bf16 else F32, kind="ExternalInput").ap()
    wq = nc.dram_tensor("wq", [1024, 512], F32, kind="ExternalInput").ap()
    wlr = nc.dram_tensor("wlr", [1024, 16], F32, kind="ExternalInput").ap()
    wtm = nc.dram_tensor("wtm", [1024, 1280], F32, kind="ExternalInput").ap()
    wgu = nc.dram_tensor("wgu", [16, 256], F32, kind="ExternalInput").ap()
    bg = nc.dram_tensor("bg", [1, 256], F32, kind="ExternalInput").ap()
    gn = nc.dram_tensor("gn128", [128, 256], F32, kind="ExternalInput").ap()
    hoT = nc.dram_tensor("hoT", [4, 128, S], BF16, kind="ExternalOutput").ap()
    with contextlib.ExitStack() as st:
        kb = KB(nc, st)
        kb.consts()
        outs = stage_gla(kb, S, lambda t0: (xT.rearrange("(k p) t -> p k t", p=128)[:, :, t0:t0 + 512], x_bf16), wq, wlr, wtm, wgu, bg, gn, lambda t0: hoT.rearrange("c p t -> p c t")[:, :, t0:t0 + 512])
        kb.fw.wait_all("sp", outs)
        kb.fw.emit()
    return nc


def build_row(T):
    nc = bass.Bass("TRN2", target_bir_lowering=False)
    ho = nc.dram_tensor("ho", [8, 128, T], BF16, kind="ExternalInput").ap()
    xres = nc.dram_tensor("xres", [T, 1024], F32, kind="ExternalInput").ap()
    wout = nc.dram_tensor("wout", [1024, 1024], F32, kind="ExternalInput").ap()
    w1 = nc.dram_tensor("w1", [1024, 4096], F32, kind="ExternalInput").ap()
    w2 = nc.dram_tensor("w2", [4096, 1024], F32, kind="ExternalInput").ap()
    lnp = nc.dram_tensor("lnp", [128, 4, 1024], F32, kind="ExternalInput").ap()
    xout = nc.dram_tensor("xout", [T, 1024], F32, kind="ExternalOutput").ap()
    xTout = nc.dram_tensor("xTout", [1024, T], BF16, kind="ExternalOutput").ap()
    with contextlib.ExitStack() as st:
        kb = KB(nc, st)
        kb.consts()
        outs = stage_row(kb, T, ho, xres, wout, w1, w2, lnp, xout, lambda t0: xTout.rearrange("(k p) t -> p k t", p=128)[:, :, t0:t0 + 512])
        kb.fw.wait_all("sp", outs)
        kb.fw.emit()
    return nc


def kernel_unfused(**inputs):
    inp = {k: np.asarray(v) for k, v in inputs.items()}
    x = inp["x"]
    Bn, S, _ = x.shape
    T = S // 2
    depth = inp["ln_mix_g"].shape[0]
    ncore = 2 * Bn
    cores = list(range(ncore))
    xT = [np.ascontiguousarray(x[b].T) for b in range(Bn)]
    xres = [x[c // 2, (c % 2) * T:(c % 2 + 1) * T] for c in cores]
    row_nc = build_row(T)
    gla_nc = None
    for l in range(depth):
        x_bf16 = l > 0
        if l % 2 == 0:
            lam_init = 0.8 - 0.6 * math.exp(-0.3 * l)
            nc = build_even(S, x_bf16, lam_init)
            maps = [even_inputs(inp, l // 2, c % 2, xT[c // 2], S) for c in cores]
        else:
            if gla_nc is None:
                gla_nc = build_gla(S, x_bf16)
            nc = gla_nc
            maps = [gla_inputs(inp, l // 2, c % 2, xT[c // 2], S) for c in cores]
        res = run_bass_kernel_spmd(nc, maps, core_ids=cores)
        hoT = [res.results[c]["hoT"] for c in cores]
        maps = []
        for c in cores:
            b, h = c // 2, c % 2
            ho = np.concatenate([hoT[2 * b][:, :, h * T:(h + 1) * T], hoT[2 * b + 1][:, :, h * T:(h + 1) * T]], axis=0)
            maps.append(row_inputs(inp, l, ho, xres[c]))
        res = run_bass_kernel_spmd(row_nc, maps, core_ids=cores)
        xres = [res.results[c]["xout"] for c in cores]
        xT = [np.concatenate([res.results[2 * b]["xTout"], res.results[2 * b + 1]["xTout"]], axis=1) for b in range(Bn)]
    out = np.stack([np.concatenate([xres[2 * b], xres[2 * b + 1]], axis=0) for b in range(Bn)])
    return out.astype(np.float32)


import os
CC_COLS = int(os.environ.get("CC_COLS", "0"))


def cc_chunked(fw, name, src, dst, groups, rows, cols):
    if os.environ.get("NOCC"):
        return
    step = CC_COLS if CC_COLS else cols
    for i, c0 in enumerate(range(0, cols, step)):
        fw.cc(Buf(f"{name}_{i}"), "AllGather", src[:, c0:c0 + step], dst[:, c0:c0 + step], groups)


def build_fused(S, depth, ncore):
    T = S // 2
    nc = bass.Bass("TRN2", target_bir_lowering=False)

    def ext(name, shape, dt=F32):
        return nc.dram_tensor(name, list(shape), dt, kind="ExternalInput").ap()

    xT0 = ext("xT0", [1024, S])
    xres0 = ext("xres0", [T, 1024])
    sel = ext("sel", [128, 2])
    ropes = ext("ropes", [2, 2, 128, S])
    W = []
    for l in range(depth):
        d = {}
        if l % 2 == 0:
            d["wfm"] = ext(f"wfm{l}", [2, 1024, 1024])
            d["wv"] = ext(f"wv{l}", [2, 1024, 256])
            d["lam"] = ext(f"lam{l}", [128, 256])
            d["gsub"] = ext(f"gsub{l}", [128, 1])
        else:
            d["wq"] = ext(f"wq{l}", [1024, 512])
            d["wlr"] = ext(f"wlr{l}", [1024, 16])
            d["wtm"] = ext(f"wtm{l}", [1024, 1280])
            d["wgu"] = ext(f"wgu{l}", [16, 256])
            d["bg"] = ext(f"bg{l}", [1, 256])
            d["gn"] = ext(f"gn{l}", [128, 256])
        d["wout"] = ext(f"wout{l}", [1024, 1024])
        d["w1"] = ext(f"w1_{l}", [1024, 4096])
        d["w2"] = ext(f"w2_{l}", [4096, 1024])
        d["lnp"] = ext(f"lnp{l}", [128, 4, 1024])
        W.append(d)
    xout = nc.dram_tensor("xout", [T, 1024], F32, kind="ExternalOutput").ap()
    HC = 1024
    XC_ = 512
    hoT_own = [nc.dram_tensor(f"hoT_own{k}", [512, HC], BF16).ap() for k in range(S // HC)]
    ho_all = [nc.dram_tensor(f"ho_all{k}", [1024, HC], BF16).ap() for k in range(S // HC)]
    xT_own = [nc.dram_tensor(f"xT_own{k}", [1024, XC_], BF16).ap() for k in range(T // XC_)]
    xT_all = [nc.dram_tensor(f"xT_all{k}", [2048, XC_], BF16).ap() for k in range(T // XC_)]
    xres_i = [nc.dram_tensor(f"xres_i{i}", [T, 1024], F32).ap() for i in range(2)]
    groups = [[2 * i, 2 * i + 1] for i in range(ncore // 2)]
    WBF = [dict(wout=nc.dram_tensor(f"woutb{l}", [1024, 1024], BF16).ap(),
                w1=nc.dram_tensor(f"w1b_{l}", [1024, 4096], BF16).ap(),
                w2=nc.dram_tensor(f"w2b_{l}", [4096, 1024], BF16).ap()) for l in range(depth)]

    with contextlib.ExitStack() as st:
        kb = KB(nc, st)
        fw = kb.fw
        kb.consts(moba=True)
        for l in range(depth):
            d = W[l]
            if l == 0:
                def xsrc(t0):
                    return xT0.rearrange("(k p) t -> p k t", p=128)[:, :, t0:t0 + 512], False
            else:
                def xsrc(t0):
                    r, tl = t0 // T, t0 % T
                    return xT_all[tl // XC_][r * 1024:(r + 1) * 1024, :].rearrange("(k p) t -> p k t", p=128), True

            def hodst(och, Q0):
                return hoT_own[Q0 // HC][och * 128:(och + 1) * 128, Q0 % HC:Q0 % HC + 512]

            def hodst4(t0):
                return hoT_own[t0 // HC].rearrange("(c p) t -> p c t", p=128)[:, :, t0 % HC:t0 % HC + 512]

            def hosrc(tok0):
                return ho_all[tok0 // HC].rearrange("(c p) t -> p c t", p=128)[:, :, tok0 % HC:tok0 % HC + 512]

            def xTdst(t0):
                return xT_own[t0 // XC_].rearrange("(k p) t -> p k t", p=128)
            def hook(l=l, d=d):
                cb = Buf(f"wconv{l}")
                for nm in ("wout", "w1", "w2"):
                    src, dst = d[nm], WBF[l][nm]
                    for r0 in range(0, src.shape[0], 256):
                        fw.dma(cb, dst[r0:r0 + 256, :], src[r0:r0 + 256, :], queue="pool")

            with fw.stage():
                if l % 2 == 0:
                    lam_init = 0.8 - 0.6 * math.exp(-0.3 * l)
                    stage_even(kb, S, xsrc, d["wfm"], d["wv"], ropes, d["lam"], d["gsub"], hodst, lam_init, hook=hook)
                else:
                    stage_gla(kb, S, xsrc, d["wq"], d["wlr"], d["wtm"], d["wgu"], d["bg"], d["gn"], hodst4, hook=hook)
            with fw.stage():
                for k in range(S // HC):
                    fw.cc(Buf(f"ccA{l}_{k}"), "AllGather", hoT_own[k], ho_all[k], groups)
            xin = xres0 if l == 0 else xres_i[(l - 1) % 2]
            xo = xout if l == depth - 1 else xres_i[l % 2]
            with fw.stage():
                outs = stage_row2(kb, T, xin, WBF[l]["wout"], WBF[l]["w1"], WBF[l]["w2"], d["lnp"], xo, xTdst,
                                  hosrc, sel, T)
                if l == depth - 1:
                    fw.wait_all("sp", outs)
            if l < depth - 1:
                with fw.stage():
                    for k in range(T // XC_):
                        fw.cc(Buf(f"ccB{l}_{k}"), "AllGather", xT_own[k], xT_all[k], groups)
        fw.emit()
        print("instr counts", {k: len(v) for k, v in fw.q.items()}, "sems", fw.nsem, flush=True)
    return nc


def fused_inputs(inp, c, S, depth):
    b, h = c // 2, c % 2
    T = S // 2
    x = inp["x"]
    m = dict(xT0=np.ascontiguousarray(x[b, :S].T), xres0=np.ascontiguousarray(x[b, h * T:(h + 1) * T]),
             sel=np.ascontiguousarray(np.broadcast_to(np.eye(2, dtype=np.float32)[h][None], (128, 2))))
    for l in range(depth):
        if l % 2 == 0:
            e = even_inputs(inp, l // 2, h, None, S)
            m["ropes"] = e["ropes"]
            m[f"wfm{l}"], m[f"wv{l}"], m[f"lam{l}"], m[f"gsub{l}"] = e["wfm"], e["wv"], e["lam128"], e["gsub"]
        else:
            g = gla_inputs(inp, l // 2, h, None, S)
            for k in ("wq", "wlr", "wtm", "wgu", "bg"):
                m[f"{k}{l}"] = g[k]
            m[f"gn{l}"] = g["gn128"]
        r = row_inputs(inp, l, None, None)
        m[f"wout{l}"], m[f"w1_{l}"], m[f"w2_{l}"], m[f"lnp{l}"] = r["wout"], r["w1"], r["w2"], r["lnp"]
    return m


def kernel(**inputs):
    inp = {k: np.asarray(v) for k, v in inputs.items()}
    x = inp["x"]
    Bn, S, _ = x.shape
    depth = inp["ln_mix_g"].shape[0]
    ncore = 2 * Bn
    nc = build_fused(S, depth, ncore)
    maps = [fused_inputs(inp, c, S, depth) for c in range(ncore)]
    res = run_bass_kernel_spmd(nc, maps, core_ids=list(range(ncore)))
    out = np.stack([np.concatenate([res.results[2 * b]["xout"], res.results[2 * b + 1]["xout"]], axis=0)
                    for b in range(Bn)])
    return out.astype(np.float32)
```

```python
import contextlib
import numpy as np
import concourse.bass as bass
import concourse.mybir as mybir
from concourse.bass_utils import run_bass_kernel_spmd

F32 = mybir.dt.float32
BF16 = mybir.dt.bfloat16
AF = mybir.ActivationFunctionType
ALU = mybir.AluOpType
AX = mybir.AxisListType

SEM_LIMIT = 30000


class Ent:
    __slots__ = ("stream", "seq", "flag", "hw", "val", "n", "item")

    def __init__(self, stream, seq, n):
        self.stream = stream
        self.seq = seq
        self.flag = False
        self.hw = None
        self.val = 0
        self.n = n
        self.item = None


class Stream:
    def __init__(self, name, inorder):
        self.name = name
        self.inorder = inorder
        self.ents = []

    def new(self, n):
        e = Ent(self, len(self.ents) + 1, n)
        self.ents.append(e)
        return e


class Item:
    __slots__ = ("waits", "fn", "ent")

    def __init__(self, waits, fn, ent):
        self.waits = waits
        self.fn = fn
        self.ent = ent


class Buf:
    def __init__(self, name=""):
        self.name = name
        self.w = {}
        self.r = {}
        self.ds = None


def _merge(dst, src):
    for k, e in src.items():
        o = dst.get(k)
        if o is None or o.seq < e.seq:
            dst[k] = e


class FW:
    def __init__(self, nc, stack):
        self.nc = nc
        self.stack = stack
        self.engs = ["sp", "pe", "act", "dve", "pool"]
        self.q = {k: [] for k in self.engs}
        self.es = {k: Stream(k, True) for k in self.engs}
        self.seen = {k: {} for k in self.engs}
        self.dstreams = []
        self.free_streams = []
        self.live_streams = []
        self.nsem = 0
        self.alloc_stack = stack

    def sb(self, name, shape, dtype):
        self.ntens = getattr(self, "ntens", 0) + 1
        name = f"{name}_{self.ntens}"
        return self.alloc_stack.enter_context(self.nc.sbuf_tensor(name, list(shape), dtype))

    def ps(self, name, shape, dtype=F32):
        self.ntens = getattr(self, "ntens", 0) + 1
        name = f"{name}_{self.ntens}"
        return self.alloc_stack.enter_context(self.nc.psum_tensor(name, list(shape), dtype))

    def dstream(self, name):
        s = Stream(name, False)
        self.dstreams.append(s)
        return s

    def _hw(self, name):
        self.nsem += 1
        return self.stack.enter_context(self.nc.semaphore(f"s{self.nsem}_{name}"))

    def _waits(self, eng, reads, writes):
        raw = {}
        for b in reads:
            _merge(raw, b.w)
        oth = {}
        for b in writes:
            _merge(oth, b.w)
            _merge(oth, b.r)
        own = self.es[eng]
        need = dict(raw)
        for k, e in oth.items():
            if e.stream is own and eng == "pe":
                continue
            o = need.get(k)
            if o is None or o.seq < e.seq:
                need[k] = e
        waits = []
        seen = self.seen[eng]
        for k, e in need.items():
            if seen.get(k, 0) >= e.seq:
                continue
            seen[k] = e.seq
            e.flag = True
            waits.append(e)
        return waits

    def _commit(self, ent, reads, writes):
        k = id(ent.stream)
        for b in writes:
            b.w = {k: ent}
            b.r = {}
        for b in reads:
            o = b.r.get(k)
            if o is None or o.seq < ent.seq:
                b.r[k] = ent

    def op(self, eng, fn, reads=(), writes=()):
        waits = self._waits(eng, reads, writes)
        ent = self.es[eng].new(1)
        it = Item(waits, fn, ent)
        ent.item = it
        self.q[eng].append(it)
        self._commit(ent, reads, writes)
        return ent

    def dma(self, sbuf, out, in_, reads=(), writes=(), queue="sp", **kw):
        stream = getattr(sbuf, "ds", None)
        if stream is None:
            stream = sbuf.ds = self._take_stream("d" + sbuf.name)
        waits = self._waits(queue, reads, writes)
        ent = stream.new(16)
        ent.flag = True
        it = Item(waits, lambda e: e.dma_start(out=out, in_=in_, **kw), ent)
        ent.item = it
        self.q[queue].append(it)
        self._commit(ent, reads, writes)
        return ent

    def _take_stream(self, name):
        if self.free_streams:
            st = self.free_streams.pop()
        else:
            st = self.dstream(name)
        self.live_streams.append(st)
        return st

    def cc(self, buf, kind, in_ap, out_ap, groups, reads=(), writes=()):
        stream = getattr(buf, "ds", None)
        if stream is None:
            stream = buf.ds = self._take_stream("cc" + buf.name)
        waits = self._waits("pool", reads, writes)
        ent = stream.new(1)
        ent.flag = True
        it = Item(waits, lambda e: e.collective_compute(kind, op=ALU.bypass, replica_groups=groups,
                                                        ins=[in_ap.opt()], outs=[out_ap.opt()]), ent)
        ent.item = it
        self.q["pool"].append(it)
        self._commit(ent, reads, writes)
        return ent

    def barrier(self):
        toks = []
        for k in self.engs:
            if self.es[k].ents:
                toks.append(self.es[k].ents[-1])
        for st in self.live_streams:
            if st.ents:
                toks.append(st.ents[-1])
        for eng in self.engs:
            waits = []
            seen = self.seen[eng]
            for e in toks:
                if e.stream is self.es[eng]:
                    continue
                k = id(e.stream)
                if seen.get(k, 0) >= e.seq:
                    continue
                seen[k] = e.seq
                e.flag = True
                waits.append(e)
            self.q[eng].append(Item(waits, None, None))
        self.free_streams.extend(self.live_streams)
        self.live_streams = []

    @contextlib.contextmanager
    def stage(self):
        outer = self.alloc_stack
        with contextlib.ExitStack() as sub:
            self.alloc_stack = sub
            try:
                yield
            finally:
                self.alloc_stack = outer
            self.barrier()

    def wait_all(self, eng, bufs):
        waits = self._waits(eng, bufs, ())
        self.q[eng].append(Item(waits, None, None))

    def emit(self):
        for s in list(self.es.values()) + self.dstreams:
            hw = None
            val = 0
            prev = None
            for e in s.ents:
                if not e.flag:
                    continue
                if hw is None or val + e.n > SEM_LIMIT:
                    if hw is not None and not s.inorder:
                        e.item.waits.append(prev)
                    hw = self._hw(s.name)
                    val = 0
                val += e.n
                e.hw = hw
                e.val = val
                prev = e
        nc = self.nc

        def mk(name):
            items = self.q[name]

            def body(e):
                for it in items:
                    for w in it.waits:
                        e.wait_ge(w.hw, w.val)
                    if it.fn is not None:
                        ins = it.fn(e)
                        if it.ent.flag:
                            ins.then_inc(it.ent.hw, it.ent.n)

            return body

        with nc.Block() as block:
            block.sync(mk("sp"))
            block.tensor(mk("pe"))
            block.scalar(mk("act"))
            block.vector(mk("dve"))
            block.gpsimd(mk("pool"))


import math


D = 1024
ALPHA = 8.0 ** 0.25
NEG = -30000.0


class KB:
    def __init__(self, nc, stack):
        self.nc = nc
        self.fw = FW(nc, stack)
        self.nps = 0

    def sb(self, name, shape, dt):
        return self.fw.sb(name, shape, dt)

    def bank(self, dt=F32):
        self.nps += 1
        n = 512 if dt == F32 else 1024
        return self.fw.ps(f"ps{self.nps}", [128, n], dt), Buf(f"ps{self.nps}")

    def mm(self, out, lhsT, rhs, start, stop, reads, writes):
        return self.fw.op("pe", lambda e: e.matmul(out, lhsT=lhsT, rhs=rhs, start=start, stop=stop,
                                                   skip_group_check=True), reads, writes)

    def tr(self, out, in_, ident, reads, writes):
        return self.fw.op("pe", lambda e: e.transpose(out=out, in_=in_, identity=ident), reads, writes)

    def act(self, out, in_, func, reads, writes, **kw):
        return self.fw.op("act", lambda e: e.activation(out=out, in_=in_, func=func, **kw), reads, writes)

    def tt(self, eng, out, in0, in1, op, reads, writes):
        return self.fw.op(eng, lambda e: e.tensor_tensor(out=out, in0=in0, in1=in1, op=op), reads, writes)

    def ts(self, eng, out, in0, s1, s2, op0, op1, reads, writes):
        if op1 is None:
            return self.fw.op(eng, lambda e: e.tensor_scalar(out=out, in0=in0, scalar1=s1, scalar2=None, op0=op0),
                              reads, writes)
        return self.fw.op(eng, lambda e: e.tensor_scalar(out=out, in0=in0, scalar1=s1, scalar2=s2, op0=op0, op1=op1),
                          reads, writes)

    def stt(self, eng, out, in0, scalar, in1, op0, op1, reads, writes):
        return self.fw.op(eng, lambda e: e.scalar_tensor_tensor(out=out, in0=in0, scalar=scalar, in1=in1,
                                                                op0=op0, op1=op1), reads, writes)

    def cp(self, eng, out, in_, reads, writes):
        if eng == "act":
            return self.fw.op("act", lambda e: e.copy(out=out, in_=in_), reads, writes)
        return self.fw.op(eng, lambda e: e.tensor_copy(out=out, in_=in_), reads, writes)

    def memset(self, eng, ap, val, writes):
        return self.fw.op(eng, lambda e: e.memset(ap, val), (), writes)

    def asel(self, out, in_, pattern, cmp, fill, base, cm, bufs):
        return self.fw.op("pool", lambda e: e.affine_select(out=out, in_=in_, pattern=pattern, compare_op=cmp,
                                                            fill=fill, base=base, channel_multiplier=cm), bufs, bufs)

    def consts(self, moba=False):
        c = {}
        B = Buf("consts")
        c["buf"] = B
        ident = self.sb("ident", [128, 128], BF16)
        self.memset("pool", ident[:], 1.0, [B])
        self.asel(ident[:], ident[:], [[-1, 128]], ALU.is_equal, 0.0, 0, 1, [B])
        c["ident"] = ident
        negm = self.sb("negm", [128, 128], BF16)
        self.memset("pool", negm[:], 0.0, [B])
        self.asel(negm[:], negm[:], [[1, 128]], ALU.is_ge, NEG, 0, -1, [B])
        c["negm"] = negm
        tri = self.sb("tri", [128, 128], F32)
        self.memset("pool", tri[:], 1.0, [B])
        self.asel(tri[:], tri[:], [[1, 128]], ALU.is_ge, 0.0, 0, -1, [B])
        c["tri"] = tri
        ui = self.sb("uincl", [128, 128], F32)
        self.memset("pool", ui[:], -1.0 / 16.0, [B])
        self.asel(ui[:], ui[:], [[1, 128]], ALU.is_ge, 0.0, 0, -1, [B])
        c["uincl"] = ui
        us = self.sb("ustr", [128, 128], F32)
        self.memset("pool", us[:], -1.0 / 16.0, [B])
        self.asel(us[:], us[:], [[-1, 128]], ALU.is_gt, 0.0, 0, 1, [B])
        c["ustr"] = us
        ones_f = self.sb("ones_f", [128, 128], F32)
        self.memset("pool", ones_f[:], 1.0, [B])
        c["ones_f"] = ones_f
        ones_b = self.sb("ones_b", [128, 128], BF16)
        self.memset("pool", ones_b[:], 1.0, [B])
        c["ones_b"] = ones_b
        e01 = self.sb("e01", [128, 4], BF16)
        self.memset("pool", e01[:], 0.0, [B])
        self.memset("pool", e01[:, 0:1], 1.0, [B])
        self.memset("pool", e01[:, 3:4], 1.0, [B])
        c["e01"] = e01
        e01f = self.sb("e01f", [128, 4], F32)
        self.memset("pool", e01f[:], 0.0, [B])
        self.memset("pool", e01f[:, 0:1], 1.0, [B])
        self.memset("pool", e01f[:, 3:4], 1.0, [B])
        c["e01f"] = e01f
        sel2 = self.sb("sel2", [2, 256], F32)
        self.memset("pool", sel2[:], 1.0, [B])
        self.asel(sel2[:, 0:128], sel2[:, 0:128], [[0, 128]], ALU.is_equal, 0.0, 0, 1, [B])
        self.asel(sel2[:, 128:256], sel2[:, 128:256], [[0, 128]], ALU.is_equal, 0.0, -1, 1, [B])
        c["sel2"] = sel2
        if not moba:
            self.c = c
            return c
        esel = self.sb("esel", [32, 32 * 128], BF16)
        self.memset("pool", esel[:], NEG, [B])
        ev = esel[:].rearrange("p (n k) -> p n k", k=128)
        self.asel(ev, ev, [[-1, 32], [0, 128]], ALU.is_equal, 0.0, 0, 1, [B])
        c["esel"] = esel
        self.c = c
        return c


def stage_row(kb, T, ho, xres, wout, w1, w2, lnp, xout, xTout, dbg=None, ho_sel=None, w_bf16=False):
    fw = kb.fw
    c = kb.c
    CB = c["buf"]
    wo_sb = kb.sb("wo_sb", [128, 8, 1024], BF16)
    WO = Buf("wo")
    ln_sb = kb.sb("ln_sb", [128, 4, 1024], F32)
    LNB = Buf("ln")
    wq_ = "sp" if w_bf16 else "pool"
    fw.dma(WO, wo_sb[:], wout.rearrange("(c p) n -> p c n", p=128), writes=[WO], queue=wq_)
    fw.dma(LNB, ln_sb[:], lnp, writes=[LNB])
    NW = 2
    w1b = [kb.sb(f"w1b{i}", [128, 8, 1024], BF16) for i in range(NW)]
    w2b = [kb.sb(f"w2b{i}", [128, 8, 1024], BF16) for i in range(NW)]
    W1B = [Buf() for _ in range(NW)]
    W2B = [Buf() for _ in range(NW)]
    hoc = kb.sb("hoc", [128, 8, 512], BF16)
    HOC = Buf("hoc")
    if ho_sel is not None:
        hoa = hoc
        hob = kb.sb("hob", [128, 8, 512], BF16)
        sel_sb = kb.sb("sel_sb", [128, 2], F32)
        HOA, HOBB, SELB = HOC, Buf("hob"), Buf("selb")
        fw.dma(SELB, sel_sb[:], ho_sel[1], writes=[SELB])
    y = kb.sb("y", [128, 4, 1024], F32)
    Y = [Buf(f"y{i}") for i in range(4)]
    acc = kb.sb("acc", [128, 4, 1024], F32)
    ACC = [Buf(f"acc{i}") for i in range(4)]
    xb = kb.sb("xb", [128, 1024], BF16)
    XB = Buf()
    x1T = kb.sb("x1T", [128, 8, 512], BF16)
    X1T = Buf("x1T")
    xTo, XTO = x1T, X1T
    hsq = [kb.sb(f"hsq{i}", [128, 8, 512], BF16) for i in range(2)]
    HSQ = [[Buf() for _ in range(8)] for _ in range(2)]
    rl = [kb.sb(f"rl{i}", [128, 512], F32) for i in range(2)]
    RL = [Buf() for _ in range(2)]
    st6 = kb.sb("st6", [128, 2, 6], F32)
    mv = kb.sb("mv", [128, 2], F32)
    rstd = kb.sb("rstd", [128, 2], F32)
    STB = Buf()
    G = [kb.bank() for _ in range(4)]
    TR = [kb.bank(BF16) for _ in range(2)]
    OUTS = []

    def nout():
        b = Buf("o")
        OUTS.append(b)
        return b
    gi = [0]

    def nextG():
        g = G[gi[0] % 4]
        gi[0] += 1
        return g

    ti = [0]

    def layer_norm(buf_ap, BUFS, j, gidx):
        v = buf_ap[:, j, :]
        for hh in range(2):
            fw.op("dve", lambda e, hh=hh: e.bn_stats(out=st6[:, hh, :], in_=buf_ap[:, j, hh * 512:(hh + 1) * 512]),
                  [BUFS[j]], [STB])
        fw.op("dve", lambda e: e.bn_aggr(out=mv[:], in_=st6[:].rearrange("p a b -> p (a b)")), [STB], [STB])
        kb.act(rstd[:, 0:1], mv[:, 1:2], AF.Sqrt, [STB], [STB], bias=1e-5, scale=1.0)
        fw.op("dve", lambda e: e.reciprocal(out=rstd[:, 1:2], in_=rstd[:, 0:1]), [STB], [STB])
        kb.ts("dve", v, v, mv[:, 0:1], rstd[:, 1:2], ALU.subtract, ALU.mult, [STB, BUFS[j]], [BUFS[j]])
        kb.tt("pool", v, v, ln_sb[:, gidx, :], ALU.mult, [BUFS[j], LNB], [BUFS[j]])
        kb.tt("pool", v, v, ln_sb[:, gidx + 1, :], ALU.add, [BUFS[j], LNB], [BUFS[j]])

    def to_T(src_ap, SRC, j, dstT, DST):
        kb.cp("act", xb[:], src_ap[:, j, :], [SRC[j]], [XB])
        trp, TRB = TR[ti[0] % 2]
        ti[0] += 1
        for k in range(8):
            kb.tr(trp[:, k * 128:(k + 1) * 128], xb[:, k * 128:(k + 1) * 128], c["ident"][:], [XB, CB], [TRB])
        kb.cp("dve", dstT[:, :, j * 128:(j + 1) * 128], trp[:].rearrange("p (k t) -> p k t", t=128), [TRB], [DST])

    nst = T // 512
    for st in range(nst):
        t0 = st * 512
        if ho_sel is None:
            fw.dma(HOC, hoc[:], ho.rearrange("c p t -> p c t")[:, :, t0:t0 + 512], writes=[HOC])
        else:
            hosrc, Thalf = ho_sel[0], ho_sel[2]
            fw.dma(HOA, hoa[:], hosrc(t0), writes=[HOA])
            fw.dma(HOBB, hob[:], hosrc(Thalf + t0), writes=[HOBB])
            kb.act(hoa[:], hoa[:], AF.Copy, [HOA, SELB], [HOA], scale=sel_sb[:, 0:1])
            kb.stt("dve", hoa[:], hob[:], sel_sb[:, 1:2], hoa[:], ALU.mult, ALU.add, [HOBB, HOA, SELB], [HOA])
        fw.dma(Y[0], y[:], xres[t0:t0 + 512, :].rearrange("(j p) d -> p j d", p=128), writes=Y)
        for j in range(4):
            for nb in range(2):
                g, GB = nextG()
                for cc in range(8):
                    kb.mm(g[:], hoc[:, cc, j * 128:(j + 1) * 128], wo_sb[:, cc, nb * 512:(nb + 1) * 512],
                          cc == 0, cc == 7, [HOC, WO], [GB])
                kb.stt("dve", y[:, j, nb * 512:(nb + 1) * 512], y[:, j, nb * 512:(nb + 1) * 512], ALPHA, g[:],
                       ALU.mult, ALU.add, [Y[j], GB], [Y[j]])
            layer_norm(y, Y, j, 0)
            to_T(y, Y, j, x1T, X1T)
        if dbg is not None:
            fw.dma(Y[0], dbg[0], y[:], reads=Y, writes=[nout()])
            fw.dma(X1T, dbg[1], x1T[:], reads=[X1T], writes=[nout()])
        for fb in range(4):
            wi = (st * 4 + fb) % NW
            fw.dma(W1B[wi], w1b[wi][:], w1.rearrange("(k p) f -> p k f", p=128)[:, :, fb * 1024:(fb + 1) * 1024],
                   writes=[W1B[wi]], queue=wq_)
            fw.dma(W2B[wi], w2b[wi][:], w2[fb * 1024:(fb + 1) * 1024, :].rearrange("(c p) d -> p c d", p=128),
                   writes=[W2B[wi]], queue=wq_)
            hi = fb % 2
            for fc in range(8):
                g, GB = nextG()
                for k in range(8):
                    kb.mm(g[:], w1b[wi][:, k, fc * 128:(fc + 1) * 128], x1T[:, k, :], k == 0, k == 7,
                          [W1B[wi], X1T], [GB])
                ri = fc % 2
                kb.act(rl[ri][:], g[:], AF.Relu, [GB], [RL[ri]])
                kb.tt("pool", hsq[hi][:, fc, :], rl[ri][:], rl[ri][:], ALU.mult, [RL[ri]], [HSQ[hi][fc]])
            for j in range(4):
                for nb in range(2):
                    g, GB = nextG()
                    for fc in range(8):
                        kb.mm(g[:], hsq[hi][:, fc, j * 128:(j + 1) * 128], w2b[wi][:, fc, nb * 512:(nb + 1) * 512],
                              fc == 0, fc == 7, [HSQ[hi][fc], W2B[wi]], [GB])
                    a = acc[:, j, nb * 512:(nb + 1) * 512]
                    if fb == 0:
                        kb.cp("dve", a, g[:], [GB], [ACC[j]])
                    else:
                        kb.tt("dve", a, a, g[:], ALU.add, [GB, ACC[j]], [ACC[j]])
        if dbg is not None:
            fw.dma(ACC[0], dbg[2], acc[:], reads=ACC, writes=[nout()])
            fw.dma(HSQ[1][0], dbg[3], hsq[1][:], reads=HSQ[1], writes=[nout()])
        for j in range(4):
            kb.stt("dve", acc[:, j, :], y[:, j, :], ALPHA, acc[:, j, :], ALU.mult, ALU.add, [Y[j], ACC[j]], [ACC[j]])
            layer_norm(acc, ACC, j, 2)
            to_T(acc, ACC, j, xTo, XTO)
        fw.dma(ACC[0], xout[t0:t0 + 512, :].rearrange("(j p) d -> p j d", p=128), acc[:], reads=ACC, writes=[nout()])
        fw.dma(XTO, xTout(t0), xTo[:], reads=[XTO], writes=[nout()])
    return OUTS


def stage_even(kb, S, xsrc, wfm, wv, ropes, lam128, gsub, hodst, lambda_init, hook=None):
    fw = kb.fw
    c = kb.c
    CB = c["buf"]
    nkt = S // 128
    nqb = S // 512
    nblk = S // 256
    QT = [kb.sb(f"QT{i}", [128, S], BF16) for i in range(2)]
    KT = [kb.sb(f"KT{i}", [128, S], BF16) for i in range(2)]
    V = [kb.sb(f"V{i}", [128, nkt, 128], BF16) for i in range(2)]
    QTB = [Buf() for _ in range(2)]
    KTB = [Buf() for _ in range(2)]
    VB = [Buf() for _ in range(2)]
    wfm_sb = kb.sb("wfm_sb", [128, 8, 1024], BF16)
    WFM = Buf("wfm")
    wv_sb = kb.sb("wv_sb", [128, 8, 256], BF16)
    WV = Buf("wv")
    xc = [kb.sb(f"xc{i}", [128, 8, 512], BF16) for i in range(2)]
    XC = [Buf(f"xc{i}") for i in range(2)]
    rp = [kb.sb(f"rp{i}", [128, 2, 512], F32) for i in range(2)]
    RP = [Buf(f"rp{i}") for i in range(2)]
    F = [kb.sb(f"F{i}", [128, 512], F32) for i in range(4)]
    FBUF = [Buf() for _ in range(4)]
    PT = [kb.sb(f"PT{i}", [128, 512], BF16) for i in range(4)]
    PTB = [Buf() for _ in range(4)]
    obf = [kb.sb(f"obf{i}", [128, 512], BF16) for i in range(2)]
    OBF = [Buf(f"obf{i}") for i in range(2)]
    rr = kb.sb("rr", [2, 512], F32)
    RR = Buf()
    accs = [[kb.sb(f"accs{p}_{i}", [128, 512], F32) for i in range(4)] for p in range(2)]
    ACCB = [[Buf() for _ in range(4)] for _ in range(2)]
    Os = [[kb.sb(f"Os{p}_{i}", [128, 512], F32) for i in range(2)] for p in range(2)]
    OSB = [[Buf() for _ in range(2)] for _ in range(2)]
    lam_sb = kb.sb("lam_sb", [128, 256], F32)
    gs_sb = kb.sb("gs_sb", [128, 1], F32)
    sm = kb.sb("sm_e", [128, 8], F32)
    LAM = Buf("lam")
    kmf = kb.sb("kmf", [128, 32], F32)
    kmb = [kb.sb(f"kmb{i}", [128, 32], BF16) for i in range(2)]
    KMB = [Buf() for _ in range(2)]
    Gs = kb.sb("Gs", [128, 32], F32)
    top8 = kb.sb("top8", [128, 8], F32)
    nots = kb.sb("nots", [128, 32], BF16)
    GSB = Buf()
    biasT = kb.sb("biasT", [32, 512], BF16)
    BIAS = Buf()
    SBK = [kb.bank() for _ in range(4)]
    O1, O1B = kb.bank()
    O2, O2B = kb.bank()
    SUMP, SUMB = kb.bank()
    FBK, FBB = kb.bank()
    OUTS = []

    fw.dma(LAM, lam_sb[:], lam128, writes=[LAM])
    fw.dma(LAM, gs_sb[:], gsub, writes=[LAM])
    kb.tt("dve", F[0][:, 0:64], lam_sb[:, 0:64], lam_sb[:, 64:128], ALU.mult, [LAM], [FBUF[0]])
    kb.tt("dve", F[0][:, 64:128], lam_sb[:, 128:192], lam_sb[:, 192:256], ALU.mult, [LAM], [FBUF[0]])
    fw.op("dve", lambda e: e.reduce_sum(out=sm[:, 0:1], in_=F[0][:, 0:64], axis=AX.X), [FBUF[0]], [LAM])
    fw.op("dve", lambda e: e.reduce_sum(out=sm[:, 1:2], in_=F[0][:, 64:128], axis=AX.X), [FBUF[0]], [LAM])
    kb.act(sm[:, 2:4], sm[:, 0:2], AF.Exp, [LAM], [LAM])
    kb.stt("dve", sm[:, 4:5], sm[:, 3:4], -float(lambda_init), sm[:, 2:3], ALU.add, ALU.subtract, [LAM], [LAM])

    def inproj(typ):
        fw.dma(WFM, wfm_sb[:], wfm[typ].rearrange("(k p) n -> p k n", p=128), writes=[WFM], queue="pool")
        fw.dma(WV, wv_sb[:], wv[typ].rearrange("(k p) n -> p k n", p=128), writes=[WV], queue="pool")
        gi = 0
        for cch in range(S // 512):
            t0 = cch * 512
            xi = cch % 2
            src, x_bf16 = xsrc(t0)
            fw.dma(XC[xi], xc[xi][:], src, writes=[XC[xi]], queue=("sp" if x_bf16 else "pool"))
            fw.dma(RP[xi], rp[xi][:], ropes[typ].rearrange("a p t -> p a t")[:, :, t0:t0 + 512], writes=[RP[xi]])
            for g in range(2):
                for hd in range(2):
                    dst, DB = (QT[hd], QTB[hd]) if g == 0 else (KT[hd], KTB[hd])
                    po, POB = SBK[gi % 4]
                    pp, PPB = SBK[(gi + 1) % 4]
                    gi += 2
                    fo = g * 4 + hd
                    fp = g * 4 + 2 + hd
                    for k in range(8):
                        kb.mm(po[:], wfm_sb[:, k, fo * 128:(fo + 1) * 128], xc[xi][:, k, :], k == 0, k == 7,
                              [WFM, XC[xi]], [POB])
                    for k in range(8):
                        kb.mm(pp[:], wfm_sb[:, k, fp * 128:(fp + 1) * 128], xc[xi][:, k, :], k == 0, k == 7,
                              [WFM, XC[xi]], [PPB])
                    fa = (g * 2 + hd) % 2 * 2
                    kb.tt("dve", F[fa][:], po[:], rp[xi][:, 0, :], ALU.mult, [POB, RP[xi]], [FBUF[fa]])
                    kb.tt("dve", F[fa + 1][:], pp[:], rp[xi][:, 1, :], ALU.mult, [PPB, RP[xi]], [FBUF[fa + 1]])
                    kb.tt("pool", dst[:, t0:t0 + 512], F[fa][:], F[fa + 1][:], ALU.add, [FBUF[fa], FBUF[fa + 1]], [DB])
            for sub in range(4):
                pv, PVB = SBK[gi % 4]
                gi += 1
                for k in range(8):
                    kb.mm(pv[:, 0:256], xc[xi][:, k, sub * 128:(sub + 1) * 128], wv_sb[:, k, :], k == 0, k == 7,
                          [WV, XC[xi]], [PVB])
                for hd in range(2):
                    kb.cp("act", V[hd][:, cch * 4 + sub, :], pv[:, hd * 128:(hd + 1) * 128], [PVB], [VB[hd]])

    def attention(typ, hd):
        nmap = 2 if typ == 0 else 1
        scale = 64.0 ** -0.5 if typ == 0 else 128.0 ** -0.5
        och = typ * 2 + hd
        for qb in range(nqb):
            Q0 = qb * 512
            if typ == 1:
                for j in range(4):
                    q0 = Q0 + j * 128
                    ob = q0 // 256
                    kb.memset("pool", nots[:], 0.0, [GSB])
                    if ob > 0:
                        kb.memset("pool", Gs[:], -1e30, [GSB])
                        gp, GPB = SBK[j % 4]
                        kb.mm(gp[:, 0:32], QT[hd][:, q0:q0 + 128], kmb[hd][:, 0:32], True, True,
                              [QTB[hd], KMB[hd]], [GPB])
                        kb.cp("dve", Gs[:, 0:ob], gp[:, 0:ob], [GPB, GSB], [GSB])
                        fw.op("dve", lambda e: e.max(out=top8[:], in_=Gs[:]), [GSB], [GSB])
                        kb.ts("dve", nots[:, 0:ob], Gs[:, 0:ob], top8[:, 2:3], None, ALU.is_lt, None, [GSB], [GSB])
                    kb.mm(FBK[0:32, j * 128:(j + 1) * 128], nots[:, 0:32], c["ident"][:], True, True, [GSB, CB], [FBB])
                kb.cp("act", biasT[:], FBK[0:32, 0:512], [FBB], [BIAS])
            items = [(kt, m) for kt in range((Q0 + 512) // 128) for m in range(nmap)]
            n = len(items)
            OB_ = [(O1, O1B), (O2, O2B)]
            last_kt = (Q0 + 512) // 128 - 1

            def issueS(i):
                kt, m = items[i]
                K0 = kt * 128
                o = max(0, K0 - Q0)
                diag = K0 >= Q0
                sp_, SPB = SBK[i % 4]
                if typ == 0:
                    kb.mm(sp_[:, o:512], KT[hd][m * 64:(m + 1) * 64, K0:K0 + 128],
                          QT[hd][m * 64:(m + 1) * 64, Q0 + o:Q0 + 512], True, not diag, [KTB[hd], QTB[hd]], [SPB])
                else:
                    kb.mm(sp_[:, o:512], KT[hd][:, K0:K0 + 128], QT[hd][:, Q0 + o:Q0 + 512], True, False,
                          [KTB[hd], QTB[hd]], [SPB])
                    nb_ = K0 // 256
                    kb.mm(sp_[:, o:512], c["esel"][0:32, nb_ * 128:(nb_ + 1) * 128], biasT[0:32, o:512], False,
                          not diag, [CB, BIAS], [SPB])
                if diag:
                    kb.mm(sp_[:, o:o + 128], c["ident"][:], c["negm"][:], False, True, [CB], [SPB])
                kb.act(PT[i % 4][:, o:512], sp_[:, o:512], AF.Exp, [SPB], [PTB[i % 4]], scale=scale)

            def issuePV(i):
                kt, m = items[i]
                K0 = kt * 128
                o = max(0, K0 - Q0)
                Op, OpB = OB_[m]
                kb.mm(Op[:, o:512], V[hd][:, kt, :], PT[i % 4][:, o:512], kt == 0, kt == last_kt,
                      [VB[hd], PTB[i % 4]], [OpB])
                eng = "pool" if i % 3 == 2 else "dve"
                ai = (1 if eng == "pool" else 0) * 2 + m
                A_, AB_ = accs[qcount[0] % 2], ACCB[qcount[0] % 2]
                if not acc_used[ai]:
                    acc_used[ai] = True
                    if o > 0:
                        kb.memset(eng, A_[ai][:, 0:o], 0.0, [AB_[ai]])
                    kb.cp(eng, A_[ai][:, o:512], PT[i % 4][:, o:512], [PTB[i % 4]], [AB_[ai]])
                else:
                    kb.tt(eng, A_[ai][:, o:512], A_[ai][:, o:512], PT[i % 4][:, o:512], ALU.add,
                          [PTB[i % 4], AB_[ai]], [AB_[ai]])

            acc_used = [False] * 4
            if typ == 0:
                for g in range(n // 2 + 1):
                    if g < n // 2:
                        issueS(2 * g)
                        issueS(2 * g + 1)
                    if g >= 1:
                        issuePV(2 * g - 2)
                        issuePV(2 * g - 1)
                    if g % 3 == 2:
                        defer_tick()
            else:
                LA = 2
                for i in range(n + LA):
                    if i < n:
                        issueS(i)
                    if i - LA >= 0:
                        issuePV(i - LA)
                    if i % 4 == 3:
                        defer_tick()
            flush()
            par = qcount[0] % 2
            qcount[0] += 1
            nr = 2 if typ == 0 else 1
            kb.cp("act", Os[par][0][:], O1[:], [O1B], [OSB[par][0]])
            if typ == 0:
                kb.cp("dve", Os[par][1][:], O2[:], [O2B], [OSB[par][1]])
            used = [ai for ai in range(4) if acc_used[ai]]
            pending.extend(make_steps(typ, och, Q0, par, nr, used, qb % 2))

    def make_steps(typ, och, Q0, par, nr, used, oi):
        A = accs[par]
        AB = ACCB[par]
        O1s, O2s = Os[par][0], Os[par][1]
        O1sB, O2sB = OSB[par][0], OSB[par][1]
        st = []

        def s_sum():
            for ui, ai in enumerate(used):
                m_ = ai % 2
                lhs = c["e01f"][:, 2 * m_:2 * m_ + 2] if typ == 0 else c["ones_f"][:, 0:1]
                kb.mm(SUMP[0:nr, :], lhs, A[ai][:], ui == 0, ui == len(used) - 1, [CB, AB[ai]], [SUMB])
        st.append(s_sum)

        def s_rcp():
            kb.act(rr[0:nr, :], SUMP[0:nr, :], AF.Ln, [SUMB], [RR])
            kb.act(rr[0:nr, :], rr[0:nr, :], AF.Exp, [RR], [RR], scale=-1.0)
        st.append(s_rcp)

        def s_out():
            ob_ = Buf("o")
            OUTS.append(ob_)
            fw.dma(OBF[oi], hodst(och, Q0), obf[oi][:], reads=[OBF[oi]], writes=[ob_])

        if typ == 1:
            def s_b():
                kb.mm(FBK[:], c["ones_f"][0:1, :], rr[0:1, :], True, True, [CB, RR], [FBB])
                kb.cp("act", F[0][:], FBK[:], [FBB], [FBUF[0]])
            st.append(s_b)

            def s_m():
                kb.tt("dve", obf[oi][:], O1s[:], F[0][:], ALU.mult, [O1sB, FBUF[0]], [OBF[oi]])
                s_out()
            st.append(s_m)
            return st

        def s1():
            kb.mm(FBK[:], c["sel2"][0:2, 0:128], rr[0:2, :], True, True, [CB, RR], [FBB])
            kb.cp("act", F[0][:], FBK[:], [FBB], [FBUF[0]])
        st.append(s1)

        def s2():
            kb.tt("dve", F[1][:], O1s[:], F[0][:], ALU.mult, [O1sB, FBUF[0]], [FBUF[1]])
            kb.mm(FBK[:], c["sel2"][0:2, 128:256], rr[0:2, :], True, True, [CB, RR], [FBB])
            kb.cp("act", F[0][:], FBK[:], [FBB], [FBUF[0]])
        st.append(s2)

        def s3():
            kb.tt("dve", F[2][:], O2s[:], F[0][:], ALU.mult, [O2sB, FBUF[0]], [FBUF[2]])
            kb.stt("dve", F[3][:], F[2][:], sm[:, 4:5], F[1][:], ALU.mult, ALU.add, [FBUF[2], FBUF[1], LAM],
                   [FBUF[3]])
            kb.tt("dve", F[1][:], F[3][:], F[3][:], ALU.mult, [FBUF[3]], [FBUF[1]])
        st.append(s3)

        def s4():
            kb.mm(FBK[0:1, :], c["ones_f"][:, 0:1], F[1][:], True, True, [CB, FBUF[1]], [FBB])
            kb.act(rr[0:1, :], FBK[0:1, :], AF.Ln, [FBB], [RR], bias=1e-5, scale=1.0 / 128.0)
            kb.act(rr[0:1, :], rr[0:1, :], AF.Exp, [RR], [RR], scale=-0.5)
        st.append(s4)

        def s5():
            kb.mm(FBK[:], c["ones_f"][0:1, :], rr[0:1, :], True, True, [CB, RR], [FBB])
            kb.stt("dve", F[2][:], F[3][:], gs_sb[:, 0:1], FBK[:], ALU.mult, ALU.mult, [FBUF[3], LAM, FBB],
                   [FBUF[2]])
            kb.act(obf[oi][:], F[2][:], AF.Copy, [FBUF[2]], [OBF[oi]], scale=float(1.0 - lambda_init))
            s_out()
        st.append(s5)
        return st

    pending = []
    qcount = [0]

    def flush():
        while pending:
            pending.pop(0)()

    def defer_tick():
        if pending:
            pending.pop(0)()

    KMF = Buf()
    for typ in range(2):
        inproj(typ)
        if typ == 0 and hook is not None:
            hook()
        if typ == 1:
            for hd in range(2):
                kb.memset("pool", kmf[:], 0.0, [KMF])
                fw.op("dve", lambda e, hd=hd: e.reduce_sum(out=kmf[:, 0:nblk],
                                                           in_=KT[hd][:].rearrange("p (n l) -> p n l", l=256),
                                                           axis=AX.X), [KTB[hd]], [KMF])
                kb.ts("dve", kmb[hd][:], kmf[:], 1.0 / 256.0, None, ALU.mult, None, [KMF], [KMB[hd]])
        for hd in range(2):
            attention(typ, hd)
            flush()
    return OUTS


def stage_gla(kb, S, xsrc, wq, wlr, wtm, wgu, bg, gn128, hodst4, hook=None):
    fw = kb.fw
    c = kb.c
    CB = c["buf"]
    wq_sb = kb.sb("wq_sb", [128, 8, 512], BF16)
    wlr_sb = kb.sb("wlr_sb", [128, 8, 16], BF16)
    wtm_sb = kb.sb("wtm_sb", [128, 8, 1280], BF16)
    wgu_sb = kb.sb("wgu_sb", [16, 256], BF16)
    bg_sb = kb.sb("bg_sb", [1, 256], BF16)
    gn_sb = kb.sb("gn_sb", [128, 256], F32)
    WB = Buf("glaw")
    fw.dma(WB, wq_sb[:], wq.rearrange("(k p) n -> p k n", p=128), writes=[WB], queue="pool")
    fw.dma(WB, wlr_sb[:], wlr.rearrange("(k p) n -> p k n", p=128), writes=[WB], queue="pool")
    fw.dma(WB, wtm_sb[:], wtm.rearrange("(k p) n -> p k n", p=128), writes=[WB], queue="pool")
    fw.dma(WB, wgu_sb[:], wgu, writes=[WB], queue="pool")
    fw.dma(WB, bg_sb[:], bg, writes=[WB], queue="pool")
    fw.dma(WB, gn_sb[:], gn128, writes=[WB])
    if hook is not None:
        hook()
    xc = [kb.sb(f"gxc{i}", [128, 8, 512], BF16) for i in range(2)]
    XC = [Buf(f"gxc{i}") for i in range(2)]
    qk = kb.sb("qk", [128, 4, 512], F32)
    QK = Buf()
    lrT = kb.sb("lrT", [16, 512], BF16)
    LRT = Buf()
    vb = kb.sb("vb", [128, 512], BF16)
    VBB = Buf()
    sr = kb.sb("sr", [128, 512], F32)
    SRB = Buf()
    ee = kb.sb("ee", [128, 256], F32)
    sp_ = kb.sb("spl", [128, 256], F32)
    SPB = Buf()
    E3 = kb.sb("E3", [128, 256], F32)
    E3B = Buf()
    khat = kb.sb("khat", [128, 256], BF16)
    KHB = Buf()
    E1 = kb.sb("E1", [128, 128], F32)
    E2 = kb.sb("E2", [128, 128], F32)
    EB = Buf()
    dec = kb.sb("dec", [128, 2], F32)
    DECB = [Buf() for _ in range(2)]
    qtl = [kb.sb(f"qtl{i}", [128, 128], BF16) for i in range(2)]
    ktl = [kb.sb(f"ktl{i}", [128, 128], BF16) for i in range(2)]
    QTL = [Buf() for _ in range(2)]
    KTL = [Buf() for _ in range(2)]
    attm = [kb.sb(f"attm{i}", [128, 128], BF16) for i in range(2)]
    ATM = [Buf() for _ in range(2)]
    Sst = [kb.sb(f"Sst{i}", [128, 256], F32) for i in range(2)]
    Sbf = [kb.sb(f"Sbf{i}", [128, 256], BF16) for i in range(2)]
    SST = [Buf() for _ in range(2)]
    SBF = [Buf() for _ in range(2)]
    junk = kb.sb("junk", [128, 256], F32)
    ssq = kb.sb("ssq", [128, 4], F32)
    SSQ = Buf()
    og = kb.sb("og", [128, 256], F32)
    OGB = Buf()
    ogb = kb.sb("ogb", [128, 256], BF16)
    OGBB = Buf()
    hoc = [kb.sb(f"ghoc{i}", [128, 4, 512], BF16) for i in range(2)]
    HOCB = [Buf(f"ghoc{i}") for i in range(2)]
    G = [kb.bank() for _ in range(3)]
    BBK, BBB = kb.bank()
    ATK, ATB = kb.bank()
    OK_, OKB = kb.bank()
    DSK, DSB = kb.bank()
    TRK, TRB = kb.bank(BF16)
    OUTS = []
    for hd in range(2):
        kb.memset("pool", Sst[hd][:], 0.0, [SST[hd]])
        kb.memset("pool", Sbf[hd][:], 0.0, [SBF[hd]])
    gi = [0]

    def nextG():
        g = G[gi[0] % 3]
        gi[0] += 1
        return g

    lnscale = math.log(128.0 ** -0.5)
    qk2 = [qk, kb.sb("qk_b", [128, 4, 512], F32)]
    QK2 = [QK, Buf()]
    lrT2 = [lrT, kb.sb("lrT_b", [16, 512], BF16)]
    LRT2 = [LRT, Buf()]
    vb2 = [vb, kb.sb("vb_b", [128, 512], BF16)]
    VB2 = [VBB, Buf()]
    sr2 = [sr, kb.sb("sr_b", [128, 512], F32)]
    SR2 = [SRB, Buf()]
    khat2 = [khat, kb.sb("khat_b", [128, 256], BF16)]
    KH2 = [KHB, Buf()]
    qtl2 = [qtl, [kb.sb(f"qtl_b{i}", [128, 128], BF16) for i in range(2)]]
    QTL2 = [QTL, [Buf() for _ in range(2)]]
    attm2 = [attm, [kb.sb(f"attm_b{i}", [128, 128], BF16) for i in range(2)]]
    ATM2 = [ATM, [Buf() for _ in range(2)]]
    dec2 = [dec, kb.sb("dec_b", [128, 2], F32)]
    DEC2 = [DECB, [Buf() for _ in range(2)]]

    E1h = [kb.sb(f"E1h{i}", [128, 128], F32) for i in range(2)]
    E2h = [kb.sb(f"E2h{i}", [128, 128], F32) for i in range(2)]
    EBh = [Buf() for _ in range(2)]
    ATBh = [Buf() for _ in range(2)]
    OKBh = [Buf() for _ in range(2)]
    DSBh = [Buf() for _ in range(2)]
    junkh = [kb.sb(f"junkh{i}", [128, 256], F32) for i in range(2)]
    ssqh = [kb.sb(f"ssqh{i}", [128, 4], F32) for i in range(2)]
    SSQh = [Buf() for _ in range(2)]
    ogh = [kb.sb(f"ogh{i}", [128, 256], F32) for i in range(2)]
    OGBh = [Buf() for _ in range(2)]
    ogbh = [kb.sb(f"ogbh{i}", [128, 256], BF16) for i in range(2)]
    OGBBh = [Buf() for _ in range(2)]

    def prologue(cch):
        t0 = cch * 512
        xi = cch % 2
        src, x_bf16 = xsrc(t0)
        fw.dma(XC[xi], xc[xi][:], src, writes=[XC[xi]], queue=("sp" if x_bf16 else "pool"))
        for ft in range(4):
            g, GB = nextG()
            for k in range(8):
                kb.mm(g[:], wq_sb[:, k, ft * 128:(ft + 1) * 128], xc[xi][:, k, :], k == 0, k == 7, [WB, XC[xi]], [GB])
            kb.cp("act", qk2[xi][:, ft, :], g[:], [GB], [QK2[xi]])
        g, GB = nextG()
        for k in range(8):
            kb.mm(g[0:16, :], wlr_sb[:, k, :], xc[xi][:, k, :], k == 0, k == 7, [WB, XC[xi]], [GB])
        kb.cp("act", lrT2[xi][:], g[0:16, :], [GB], [LRT2[xi]])

    def front(cch, j):
        xi = cch % 2
        pj = (cch * 4 + j) % 2
        ts_ = slice(j * 128, (j + 1) * 128)
        gk, GKB = nextG()
        for k in range(8):
            kb.mm(gk[:, 0:256], xc[xi][:, k, ts_], wtm_sb[:, k, 0:256], k == 0, k == 7, [WB, XC[xi]], [GKB])
        kb.mm(gk[:, 256:512], lrT2[xi][0:16, ts_], wgu_sb[0:16, :], True, False, [LRT2[xi], WB], [GKB])
        kb.mm(gk[:, 256:512], c["ones_b"][0:1, 0:128], bg_sb[0:1, :], False, True, [CB, WB], [GKB])
        gv, GVB = nextG()
        for k in range(8):
            kb.mm(gv[:], xc[xi][:, k, ts_], wtm_sb[:, k, 256:768], k == 0, k == 7, [WB, XC[xi]], [GVB])
        kb.cp("act", vb2[pj][:], gv[:], [GVB], [VB2[pj]])
        gr, GRB = nextG()
        for k in range(8):
            kb.mm(gr[:], xc[xi][:, k, ts_], wtm_sb[:, k, 768:1280], k == 0, k == 7, [WB, XC[xi]], [GRB])
        kb.act(sr2[pj][:], gr[:], AF.Silu, [GRB], [SR2[pj]])
        kb.act(ee[:], gk[:, 256:512], AF.Exp, [GKB], [SPB], scale=-1.0)
        kb.act(sp_[:], ee[:], AF.Ln, [SPB], [SPB], bias=1.0, scale=1.0)
        for hd in range(2):
            kb.mm(BBK[:, hd * 128:(hd + 1) * 128], sp_[:, hd * 128:(hd + 1) * 128], c["uincl"][:], True, True,
                  [SPB, CB], [BBB])
        kb.mm(BBK[:, 256:512], c["ustr"][:], sp_[:], True, True, [SPB, CB], [BBB])
        kb.act(E3[:], BBK[:, 256:512], AF.Exp, [BBB], [E3B])
        kb.tt("dve", khat2[pj][:], gk[:, 0:256], E3[:], ALU.mult, [GKB, E3B], [KH2[pj]])
        for hd in range(2):
            bt = BBK[:, hd * 128:(hd + 1) * 128]
            kb.act(E1h[hd][:], bt, AF.Exp, [BBB], [EBh[hd]], bias=lnscale, scale=1.0)
            kb.act(E2h[hd][:], bt, AF.Exp, [BBB], [EBh[hd]], scale=-1.0)
            kb.act(dec2[pj][:, hd:hd + 1], BBK[:, hd * 128 + 127:hd * 128 + 128], AF.Exp, [BBB], [DEC2[pj][hd]])
        for hd in range(2):
            kb.tt("dve", qtl2[pj][hd][:], qk2[xi][:, hd, ts_], E1h[hd][:], ALU.mult, [QK2[xi], EBh[hd]], [QTL2[pj][hd]])
            kb.tt("dve", ktl[hd][:], qk2[xi][:, 2 + hd, ts_], E2h[hd][:], ALU.mult, [QK2[xi], EBh[hd]], [KTL[hd]])
        for hd in range(2):
            kb.mm(ATK[:, hd * 128:(hd + 1) * 128], ktl[hd][:], qtl2[pj][hd][:], True, True,
                  [KTL[hd], QTL2[pj][hd]], [ATB])
        for hd in range(2):
            kb.tt("dve", attm2[pj][hd][:], ATK[:, hd * 128:(hd + 1) * 128], c["tri"][:], ALU.mult, [ATB, CB],
                  [ATM2[pj][hd]])

    def back(cch, j):
        pj = (cch * 4 + j) % 2
        hi = cch % 2
        ts_ = slice(j * 128, (j + 1) * 128)
        H = range(2)
        ov = [OK_[:, hd * 256:(hd + 1) * 256] for hd in H]
        vh = [vb2[pj][:, hd * 256:(hd + 1) * 256] for hd in H]
        dv = [DSK[:, hd * 256:(hd + 1) * 256] for hd in H]
        for hd in H:
            kb.mm(ov[hd], attm2[pj][hd][:], vh[hd], True, False, [ATM2[pj][hd], VB2[pj]], [OKB])
            kb.mm(ov[hd], qtl2[pj][hd][:], Sbf[hd][:], False, True, [QTL2[pj][hd], SBF[hd]], [OKB])
        for hd in H:
            kb.mm(dv[hd], khat2[pj][:, hd * 128:(hd + 1) * 128], vh[hd], True, True, [KH2[pj], VB2[pj]], [DSB])
        for hd in H:
            kb.stt("dve", Sst[hd][:], Sst[hd][:], dec2[pj][:, hd:hd + 1], dv[hd], ALU.mult, ALU.add,
                   [SST[hd], DEC2[pj][hd], DSB], [SST[hd]])
        for hd in H:
            kb.cp("pool", Sbf[hd][:], Sst[hd][:], [SST[hd]], [SBF[hd]])
        for hd in H:
            kb.act(junkh[hd][:], ov[hd], AF.Square, [OKB], [SSQh[hd]], accum_out=ssqh[hd][:, 0:1])
        for hd in H:
            kb.act(ssqh[hd][:, 1:2], ssqh[hd][:, 0:1], AF.Ln, [SSQh[hd]], [SSQh[hd]], bias=1e-5, scale=1.0 / 256.0)
        for hd in H:
            kb.act(ssqh[hd][:, 2:3], ssqh[hd][:, 1:2], AF.Exp, [SSQh[hd]], [SSQh[hd]], scale=-0.5)
        for hd in H:
            kb.stt("dve", ogh[hd][:], ov[hd], ssqh[hd][:, 2:3], gn_sb[:], ALU.mult, ALU.mult,
                   [OKB, SSQh[hd], WB], [OGBh[hd]])
        for hd in H:
            kb.tt("pool", ogbh[hd][:], ogh[hd][:], sr2[pj][:, hd * 256:(hd + 1) * 256], ALU.mult,
                  [OGBh[hd], SR2[pj]], [OGBBh[hd]])
        for hd in H:
            for cc in range(2):
                kb.tr(TRK[:, (hd * 2 + cc) * 128:(hd * 2 + cc + 1) * 128], ogbh[hd][:, cc * 128:(cc + 1) * 128],
                      c["ident"][:], [OGBBh[hd], CB], [TRB])
        kb.cp("act", hoc[hi][:, :, ts_], TRK[:, 0:512].rearrange("p (c t) -> p c t", t=128), [TRB], [HOCB[hi]])
        if j == 3:
            ob_ = Buf("o")
            OUTS.append(ob_)
            fw.dma(HOCB[hi], hodst4(cch * 512), hoc[hi][:], reads=[HOCB[hi]], writes=[ob_])

    subs = [(cch, j) for cch in range(S // 512) for j in range(4)]
    prologue(0)
    front(0, 0)
    for idx, (cch, j) in enumerate(subs):
        if idx + 1 < len(subs):
            nc_, nj = subs[idx + 1]
            if nj == 0:
                prologue(nc_)
            front(nc_, nj)
        back(cch, j)
    return OUTS


def stage_row2(kb, T, xres, wout, w1, w2, lnp, xout, xTout, hosrc, sel, Thalf):
    fw = kb.fw
    c = kb.c
    CB = c["buf"]
    wo_sb = kb.sb("wo_sb", [128, 8, 1024], BF16)
    WO = Buf("wo")
    ln_sb = kb.sb("ln_sb", [128, 4, 1024], F32)
    LNB = Buf("ln")
    sel_sb = kb.sb("sel_sb", [128, 2], F32)
    SELB = Buf("selb")
    fw.dma(WO, wo_sb[:], wout.rearrange("(c p) n -> p c n", p=128), writes=[WO])
    fw.dma(LNB, ln_sb[:], lnp, writes=[LNB])
    fw.dma(SELB, sel_sb[:], sel, writes=[SELB])
    w1b = [kb.sb(f"w1b{i}", [128, 8, 512], BF16) for i in range(2)]
    w2b = [kb.sb(f"w2b{i}", [128, 4, 1024], BF16) for i in range(2)]
    W1B = [Buf(f"w1b{i}") for i in range(2)]
    W2B = [Buf(f"w2b{i}") for i in range(2)]
    hoa = kb.sb("hoa", [128, 8, 512], BF16)
    hob = kb.sb("hob", [128, 8, 512], BF16)
    HOA, HOBB = Buf("hoa"), Buf("hob")
    y = [kb.sb(f"y{i}", [128, 4, 1024], F32) for i in range(2)]
    Y = [[Buf(f"y{i}_{j}") for j in range(4)] for i in range(2)]
    acc = kb.sb("acc", [128, 4, 1024], F32)
    ACC = [Buf(f"acc{j}") for j in range(4)]
    xb = kb.sb("xb", [128, 1024], BF16)
    XB = Buf()
    x1T = [kb.sb(f"x1T{i}", [128, 8, 512], BF16) for i in range(2)]
    X1T = [Buf(f"x1T{i}") for i in range(2)]
    xTo = kb.sb("xTo", [128, 8, 512], BF16)
    XTO = Buf("xTo")
    hsq = [kb.sb(f"hsq{i}", [128, 4, 512], BF16) for i in range(2)]
    HSQ = [[Buf() for _ in range(4)] for _ in range(2)]
    rl = [kb.sb(f"rl{i}", [128, 512], F32) for i in range(2)]
    RL = [Buf() for _ in range(2)]
    st6 = kb.sb("st6", [128, 2, 6], F32)
    mv = kb.sb("mv", [128, 2], F32)
    rstd = kb.sb("rstd", [128, 2], F32)
    STB = Buf()
    G = [kb.bank() for _ in range(4)]
    TR = [kb.bank(BF16) for _ in range(2)]
    OUTS = []
    gi = [0]
    ti = [0]
    wi_ = [0]

    def nout():
        b = Buf("o")
        OUTS.append(b)
        return b

    def nextG():
        g = G[gi[0] % 4]
        gi[0] += 1
        return g

    def layer_norm(buf_ap, BUFS, j, gidx):
        v = buf_ap[:, j, :]
        for hh in range(2):
            fw.op("dve", lambda e, hh=hh: e.bn_stats(out=st6[:, hh, :], in_=buf_ap[:, j, hh * 512:(hh + 1) * 512]),
                  [BUFS[j]], [STB])
        fw.op("dve", lambda e: e.bn_aggr(out=mv[:], in_=st6[:].rearrange("p a b -> p (a b)")), [STB], [STB])
        kb.act(rstd[:, 0:1], mv[:, 1:2], AF.Sqrt, [STB], [STB], bias=1e-5, scale=1.0)
        fw.op("dve", lambda e: e.reciprocal(out=rstd[:, 1:2], in_=rstd[:, 0:1]), [STB], [STB])
        kb.ts("dve", v, v, mv[:, 0:1], rstd[:, 1:2], ALU.subtract, ALU.mult, [STB, BUFS[j]], [BUFS[j]])
        kb.tt("pool", v, v, ln_sb[:, gidx, :], ALU.mult, [BUFS[j], LNB], [BUFS[j]])
        kb.tt("pool", v, v, ln_sb[:, gidx + 1, :], ALU.add, [BUFS[j], LNB], [BUFS[j]])

    def to_T(src_ap, SRC, j, dstT, DST):
        kb.cp("act", xb[:], src_ap[:, j, :], [SRC[j]], [XB])
        trp, TRB = TR[ti[0] % 2]
        ti[0] += 1
        for k in range(8):
            kb.tr(trp[:, k * 128:(k + 1) * 128], xb[:, k * 128:(k + 1) * 128], c["ident"][:], [XB, CB], [TRB])
        kb.cp("dve", dstT[:, :, j * 128:(j + 1) * 128], trp[:].rearrange("p (k t) -> p k t", t=128), [TRB], [DST])

    def A_load(st):
        p = st % 2
        t0 = st * 512
        fw.dma(HOA, hoa[:], hosrc(t0), writes=[HOA])
        fw.dma(HOBB, hob[:], hosrc(Thalf + t0), writes=[HOBB])
        fw.dma(Y[p][0], y[p][:], xres[t0:t0 + 512, :].rearrange("(j p) d -> p j d", p=128), writes=Y[p])

    def A_front(st):
        p = st % 2
        kb.act(hoa[:], hoa[:], AF.Copy, [HOA, SELB], [HOA], scale=sel_sb[:, 0:1])
        kb.stt("dve", hoa[:], hob[:], sel_sb[:, 1:2], hoa[:], ALU.mult, ALU.add, [HOBB, HOA, SELB], [HOA])
        for j in range(4):
            for nb in range(2):
                g, GB = nextG()
                for cc in range(8):
                    kb.mm(g[:], hoa[:, cc, j * 128:(j + 1) * 128], wo_sb[:, cc, nb * 512:(nb + 1) * 512],
                          cc == 0, cc == 7, [HOA, WO], [GB])
                ys = y[p][:, j, nb * 512:(nb + 1) * 512]
                kb.stt("dve", ys, ys, ALPHA, g[:], ALU.mult, ALU.add, [Y[p][j], GB], [Y[p][j]])
            layer_norm(y[p], Y[p], j, 0)

    def A_T(st, j):
        p = st % 2
        to_T(y[p], Y[p], j, x1T[p], X1T[p])

    def F1(st, fb):
        p = st % 2
        wi = fb % 2
        fw.dma(W1B[wi], w1b[wi][:], w1.rearrange("(k p) f -> p k f", p=128)[:, :, fb * 512:(fb + 1) * 512],
               writes=[W1B[wi]])
        fw.dma(W2B[wi], w2b[wi][:], w2[fb * 512:(fb + 1) * 512, :].rearrange("(c p) d -> p c d", p=128),
               writes=[W2B[wi]])
        hi = fb % 2
        for fc in range(4):
            g, GB = nextG()
            for k in range(8):
                kb.mm(g[:], w1b[wi][:, k, fc * 128:(fc + 1) * 128], x1T[p][:, k, :], k == 0, k == 7,
                      [W1B[wi], X1T[p]], [GB])
            ri = fc % 2
            kb.act(rl[ri][:], g[:], AF.Relu, [GB], [RL[ri]])
            kb.tt("pool", hsq[hi][:, fc, :], rl[ri][:], rl[ri][:], ALU.mult, [RL[ri]], [HSQ[hi][fc]])

    def F2(st, fb):
        wi = fb % 2
        hi = fb % 2
        for j in range(4):
            for nb in range(2):
                g, GB = nextG()
                for fc in range(4):
                    kb.mm(g[:], hsq[hi][:, fc, j * 128:(j + 1) * 128], w2b[wi][:, fc, nb * 512:(nb + 1) * 512],
                          fc == 0, fc == 3, [HSQ[hi][fc], W2B[wi]], [GB])
                a = acc[:, j, nb * 512:(nb + 1) * 512]
                if fb == 0:
                    kb.cp("dve", a, g[:], [GB], [ACC[j]])
                else:
                    kb.tt("dve", a, a, g[:], ALU.add, [GB, ACC[j]], [ACC[j]])

    def phaseB1(st):
        p = st % 2
        for j in range(4):
            kb.stt("dve", y[p][:, j, :], y[p][:, j, :], ALPHA, acc[:, j, :], ALU.mult, ALU.add,
                   [Y[p][j], ACC[j]], [Y[p][j]])

    def B_ln(st):
        p = st % 2
        for j in range(4):
            layer_norm(y[p], Y[p], j, 2)

    def B_T(st, j):
        p = st % 2
        to_T(y[p], Y[p], j, xTo, XTO)

    def B_store(st):
        p = st % 2
        t0 = st * 512
        fw.dma(Y[p][0], xout[t0:t0 + 512, :].rearrange("(j p) d -> p j d", p=128), y[p][:], reads=Y[p], writes=[nout()])
        fw.dma(XTO, xTout(t0), xTo[:], reads=[XTO], writes=[nout()])

    nst = T // 512
    A_load(0)
    A_front(0)
    for j in range(4):
        A_T(0, j)
    F1(0, 0)
    for st in range(nst):
        for fb in range(8):
            if fb + 1 < 8:
                F1(st, fb + 1)
            F2(st, fb)
            if st > 0:
                if fb == 0:
                    B_ln(st - 1)
                elif fb == 1:
                    B_T(st - 1, 0)
                    B_T(st - 1, 1)
                elif fb == 2:
                    B_T(st - 1, 2)
                    B_T(st - 1, 3)
                    B_store(st - 1)
            if st + 1 < nst:
                if fb == 2:
                    A_load(st + 1)
                elif fb == 3:
                    A_front(st + 1)
                elif fb >= 4:
                    A_T(st + 1, fb - 4)
                if fb == 7:
                    F1(st + 1, 0)
        phaseB1(st)
    B_ln(nst - 1)
    for j in range(4):
        B_T(nst - 1, j)
    B_store(nst - 1)
    return OUTS


ROPE_THETA = 10000.0


def rope_table(S, dim, nrows):
    half = dim // 2
    inv = (1.0 / (ROPE_THETA ** (np.arange(0, dim, 2, dtype=np.float32) / np.float32(dim)))).astype(np.float32)
    ang = np.arange(S, dtype=np.float32)[None, :] * inv[:, None]
    cs = np.cos(ang).astype(np.float32)
    sn = np.sin(ang).astype(np.float32)
    out = np.zeros((2, nrows, S), np.float32)
    for r in range(nrows):
        i = (r % dim) % half
        out[0, r] = cs[i]
        out[1, r] = -sn[i] if (r % dim) < half else sn[i]
    return out


def perm_cols(w, dim):
    n = w.shape[-1] // dim
    w4 = w.reshape(w.shape[0], n, 2, dim // 2)
    return np.ascontiguousarray(w4[:, :, ::-1, :]).reshape(w.shape)


def even_inputs(inp, e, h, xT, S):
    w = inp["hy_w_in"][e]
    hs = slice(2 * h * 128, (2 * h + 2) * 128)
    def fm(q, k, dim):
        qc = q[:, hs]; kc = k[:, hs]
        return np.concatenate([qc, perm_cols(qc, dim), kc, perm_cols(kc, dim)], axis=1)
    wfm = np.stack([fm(w[:, 0:512], w[:, 512:1024], 64), fm(w[:, 1536:2048], w[:, 2048:2560], 128)])
    wv = np.stack([w[:, 1024:1536][:, hs], w[:, 2560:3072][:, hs]])
    ropes = np.stack([rope_table(S, 64, 128), rope_table(S, 128, 128)])
    lam128 = np.broadcast_to(inp["diff_lambda"][e].reshape(1, 256), (128, 256))
    gsub = inp["diff_subln"][e].reshape(128, 1)
    return dict(xT=(None if xT is None else np.ascontiguousarray(xT)), wfm=np.ascontiguousarray(wfm, dtype=np.float32),
                wv=np.ascontiguousarray(wv, dtype=np.float32), ropes=ropes,
                lam128=np.ascontiguousarray(lam128, dtype=np.float32), gsub=np.ascontiguousarray(gsub, dtype=np.float32))


def gla_inputs(inp, o, h, xT, S):
    w = inp["gla_w_in"][o]
    hk = slice(2 * h * 128, (2 * h + 2) * 128)
    hv = slice(2 * h * 256, (2 * h + 2) * 256)
    q = w[:, 0:512][:, hk]; k = w[:, 512:1024][:, hk]
    v = w[:, 1024:2048][:, hv]; r = w[:, 2048:3072][:, hv]
    wq = np.concatenate([q, k], axis=1)
    wtm = np.concatenate([k, v, r], axis=1)
    f = lambda a: np.ascontiguousarray(a, dtype=np.float32)
    return dict(xT=(None if xT is None else np.ascontiguousarray(xT)), wq=f(wq), wlr=f(w[:, 3072:3088]), wtm=f(wtm),
                wgu=f(inp["gla_w_gate_up"][o][:, hk]), bg=f(inp["gla_b_gate"][o][hk].reshape(1, 256)),
                gn128=f(np.broadcast_to(inp["gla_norm"][o].reshape(1, 256), (128, 256))))


def row_inputs(inp, l, ho, xres):
    if l % 2 == 0:
        wo = inp["hy_w_out"][l // 2]
        wo = np.concatenate([wo[0:256], wo[512:768], wo[256:512], wo[768:1024]], axis=0)
    else:
        wo = inp["gla_w_out"][l // 2]
    ln = np.stack([inp["ln_mix_g"][l], inp["ln_mix_b"][l], inp["ln_ffn_g"][l], inp["ln_ffn_b"][l]])
    f = lambda a: np.ascontiguousarray(a, dtype=np.float32)
    return dict(ho=(None if ho is None else np.ascontiguousarray(ho)), xres=(None if xres is None else f(xres)), wout=f(wo), w1=f(inp["ffn_w1"][l]), w2=f(inp["ffn_w2"][l]),
                lnp=f(np.broadcast_to(ln[None], (128, 4, 1024))))


def build_even(S, x_bf16, lambda_init):
    nc = bass.Bass("TRN2", target_bir_lowering=False)
    xT = nc.dram_tensor("xT", [1024, S], BF16 if x_bf16 else F32, kind="ExternalInput").ap()
    wfm = nc.dram_tensor("wfm", [2, 1024, 1024], F32, kind="ExternalInput").ap()
    wv = nc.dram_tensor("wv", [2, 1024, 256], F32, kind="ExternalInput").ap()
    ropes = nc.dram_tensor("ropes", [2, 2, 128, S], F32, kind="ExternalInput").ap()
    lam = nc.dram_tensor("lam128", [128, 256], F32, kind="ExternalInput").ap()
    gsub = nc.dram_tensor("gsub", [128, 1], F32, kind="ExternalInput").ap()
    hoT = nc.dram_tensor("hoT", [4, 128, S], BF16, kind="ExternalOutput").ap()
    with contextlib.ExitStack() as st:
        kb = KB(nc, st)
        kb.consts(moba=True)
        outs = stage_even(kb, S, lambda t0: (xT.rearrange("(k p) t -> p k t", p=128)[:, :, t0:t0 + 512], x_bf16), wfm, wv, ropes, lam, gsub, lambda och, Q0: hoT[och][:, Q0:Q0 + 512], lambda_init)
        kb.fw.wait_all("sp", outs)
        kb.fw.emit()
    return nc


def build_gla(S, x_bf16):
    nc = bass.Bass("TRN2", target_bir_lowering=False)
    xT = nc.dram_tensor("xT", [1024, S], BF16 if x_bf16 else F32, kind="ExternalInput").ap()
    wq = nc.dram_tensor("wq", [1024, 512], F32, kind="ExternalInput").ap()
    wlr = nc.dram_tensor("wlr", [1024, 16], F32, kind="ExternalInput").ap()
    wtm = nc.dram_tensor("wtm", [1024, 1280], F32, kind="ExternalInput").ap()
    wgu = nc.dram_tensor("wgu", [16, 256], F32, kind="ExternalInput").ap()
    bg = nc.dram_tensor("bg", [1, 256], F32, kind="ExternalInput").ap()
    gn = nc.dram_tensor("gn128", [128, 256], F32, kind="ExternalInput").ap()
    hoT = nc.dram_tensor("hoT", [4, 128, S], BF16, kind="ExternalOutput").ap()
    with contextlib.ExitStack() as st:
        kb = KB(nc, st)
        kb.consts()
        outs = stage_gla(kb, S, lambda t0: (xT.rearrange("(k p) t -> p k t", p=128)[:, :, t0:t0 + 512], x_bf16), wq, wlr, wtm, wgu, bg, gn, lambda t0: hoT.rearrange("c p t -> p c t")[:, :, t0:t0 + 512])
        kb.fw.wait_all("sp", outs)
        kb.fw.emit()
    return nc


def build_row(T):
    nc = bass.Bass("TRN2", target_bir_lowering=False)
    ho = nc.dram_tensor("ho", [8, 128, T], BF16, kind="ExternalInput").ap()
    xres = nc.dram_tensor("xres", [T, 1024], F32, kind="ExternalInput").ap()
    wout = nc.dram_tensor("wout", [1024, 1024], F32, kind="ExternalInput").ap()
    w1 = nc.dram_tensor("w1", [1024, 4096], F32, kind="ExternalInput").ap()
    w2 = nc.dram_tensor("w2", [4096, 1024], F32, kind="ExternalInput").ap()
    lnp = nc.dram_tensor("lnp", [128, 4, 1024], F32, kind="ExternalInput").ap()
    xout = nc.dram_tensor("xout", [T, 1024], F32, kind="ExternalOutput").ap()
    xTout = nc.dram_tensor("xTout", [1024, T], BF16, kind="ExternalOutput").ap()
    with contextlib.ExitStack() as st:
        kb = KB(nc, st)
        kb.consts()
        outs = stage_row(kb, T, ho, xres, wout, w1, w2, lnp, xout, lambda t0: xTout.rearrange("(k p) t -> p k t", p=128)[:, :, t0:t0 + 512])
        kb.fw.wait_all("sp", outs)
        kb.fw.emit()
    return nc


def kernel_unfused(**inputs):
    inp = {k: np.asarray(v) for k, v in inputs.items()}
    x = inp["x"]
    Bn, S, _ = x.shape
    T = S // 2
    depth = inp["ln_mix_g"].shape[0]
    ncore = 2 * Bn
    cores = list(range(ncore))
    xT = [np.ascontiguousarray(x[b].T) for b in range(Bn)]
    xres = [x[c // 2, (c % 2) * T:(c % 2 + 1) * T] for c in cores]
    row_nc = build_row(T)
    gla_nc = None
    for l in range(depth):
        x_bf16 = l > 0
        if l % 2 == 0:
            lam_init = 0.8 - 0.6 * math.exp(-0.3 * l)
            nc = build_even(S, x_bf16, lam_init)
            maps = [even_inputs(inp, l // 2, c % 2, xT[c // 2], S) for c in cores]
        else:
            if gla_nc is None:
                gla_nc = build_gla(S, x_bf16)
            nc = gla_nc
            maps = [gla_inputs(inp, l // 2, c % 2, xT[c // 2], S) for c in cores]
        res = run_bass_kernel_spmd(nc, maps, core_ids=cores)
        hoT = [res.results[c]["hoT"] for c in cores]
        maps = []
        for c in cores:
            b, h = c // 2, c % 2
            ho = np.concatenate([hoT[2 * b][:, :, h * T:(h + 1) * T], hoT[2 * b + 1][:, :, h * T:(h + 1) * T]], axis=0)
            maps.append(row_inputs(inp, l, ho, xres[c]))
        res = run_bass_kernel_spmd(row_nc, maps, core_ids=cores)
        xres = [res.results[c]["xout"] for c in cores]
        xT = [np.concatenate([res.results[2 * b]["xTout"], res.results[2 * b + 1]["xTout"]], axis=1) for b in range(Bn)]
    out = np.stack([np.concatenate([xres[2 * b], xres[2 * b + 1]], axis=0) for b in range(Bn)])
    return out.astype(np.float32)


import os
CC_COLS = int(os.environ.get("CC_COLS", "0"))


def cc_chunked(fw, name, src, dst, groups, rows, cols):
    if os.environ.get("NOCC"):
        return
    step = CC_COLS if CC_COLS else cols
    for i, c0 in enumerate(range(0, cols, step)):
        fw.cc(Buf(f"{name}_{i}"), "AllGather", src[:, c0:c0 + step], dst[:, c0:c0 + step], groups)


def build_fused(S, depth, ncore):
    T = S // 2
    nc = bass.Bass("TRN2", target_bir_lowering=False)

    def ext(name, shape, dt=F32):
        return nc.dram_tensor(name, list(shape), dt, kind="ExternalInput").ap()

    xT0 = ext("xT0", [1024, S])
    xres0 = ext("xres0", [T, 1024])
    sel = ext("sel", [128, 2])
    ropes = ext("ropes", [2, 2, 128, S])
    W = []
    for l in range(depth):
        d = {}
        if l % 2 == 0:
            d["wfm"] = ext(f"wfm{l}", [2, 1024, 1024])
            d["wv"] = ext(f"wv{l}", [2, 1024, 256])
            d["lam"] = ext(f"lam{l}", [128, 256])
            d["gsub"] = ext(f"gsub{l}", [128, 1])
        else:
            d["wq"] = ext(f"wq{l}", [1024, 512])
            d["wlr"] = ext(f"wlr{l}", [1024, 16])
            d["wtm"] = ext(f"wtm{l}", [1024, 1280])
            d["wgu"] = ext(f"wgu{l}", [16, 256])
            d["bg"] = ext(f"bg{l}", [1, 256])
            d["gn"] = ext(f"gn{l}", [128, 256])
        d["wout"] = ext(f"wout{l}", [1024, 1024])
        d["w1"] = ext(f"w1_{l}", [1024, 4096])
        d["w2"] = ext(f"w2_{l}", [4096, 1024])
        d["lnp"] = ext(f"lnp{l}", [128, 4, 1024])
        W.append(d)
    xout = nc.dram_tensor("xout", [T, 1024], F32, kind="ExternalOutput").ap()
    HC = int(os.environ.get("HC", "2048"))
    XC_ = int(os.environ.get("XCC", "1024"))
    HC = min(HC, S)
    XC_ = min(XC_, T)
    hoT_own = [nc.dram_tensor(f"hoT_own{k}", [512, HC], BF16).ap() for k in range(S // HC)]
    ho_all = [nc.dram_tensor(f"ho_all{k}", [1024, HC], BF16).ap() for k in range(S // HC)]
    xT_own = [nc.dram_tensor(f"xT_own{k}", [1024, XC_], BF16).ap() for k in range(T // XC_)]
    xT_all = [nc.dram_tensor(f"xT_all{k}", [2048, XC_], BF16).ap() for k in range(T // XC_)]
    xres_i = [nc.dram_tensor(f"xres_i{i}", [T, 1024], F32).ap() for i in range(2)]
    groups = [[2 * i, 2 * i + 1] for i in range(ncore // 2)]
    WBF = [dict(wout=nc.dram_tensor(f"woutb{l}", [1024, 1024], BF16).ap(),
                w1=nc.dram_tensor(f"w1b_{l}", [1024, 4096], BF16).ap(),
                w2=nc.dram_tensor(f"w2b_{l}", [4096, 1024], BF16).ap()) for l in range(depth)]

    with contextlib.ExitStack() as st:
        kb = KB(nc, st)
        fw = kb.fw
        kb.consts(moba=True)
        for l in range(depth):
            d = W[l]
            if l == 0:
                def xsrc(t0):
                    return xT0.rearrange("(k p) t -> p k t", p=128)[:, :, t0:t0 + 512], False
            else:
                def xsrc(t0):
                    r, tl = t0 // T, t0 % T
                    return (xT_all[tl // XC_][r * 1024:(r + 1) * 1024, tl % XC_:tl % XC_ + 512]
                            .rearrange("(k p) t -> p k t", p=128), True)

            def hodst(och, Q0):
                return hoT_own[Q0 // HC][och * 128:(och + 1) * 128, Q0 % HC:Q0 % HC + 512]

            def hodst4(t0):
                return hoT_own[t0 // HC].rearrange("(c p) t -> p c t", p=128)[:, :, t0 % HC:t0 % HC + 512]

            def hosrc(tok0):
                return ho_all[tok0 // HC].rearrange("(c p) t -> p c t", p=128)[:, :, tok0 % HC:tok0 % HC + 512]

            def xTdst(t0):
                return xT_own[t0 // XC_].rearrange("(k p) t -> p k t", p=128)[:, :, t0 % XC_:t0 % XC_ + 512]
            def hook(l=l, d=d):
                cb = Buf(f"wconv{l}")
                for nm in ("wout", "w1", "w2"):
                    src, dst = d[nm], WBF[l][nm]
                    for r0 in range(0, src.shape[0], 256):
                        fw.dma(cb, dst[r0:r0 + 256, :], src[r0:r0 + 256, :], queue="pool")

            with fw.stage():
                if l % 2 == 0:
                    lam_init = 0.8 - 0.6 * math.exp(-0.3 * l)
                    stage_even(kb, S, xsrc, d["wfm"], d["wv"], ropes, d["lam"], d["gsub"], hodst, lam_init, hook=hook)
                else:
                    stage_gla(kb, S, xsrc, d["wq"], d["wlr"], d["wtm"], d["wgu"], d["bg"], d["gn"], hodst4, hook=hook)
            with fw.stage():
                for k in range(S // HC):
                    fw.cc(Buf(f"ccA{l}_{k}"), "AllGather", hoT_own[k], ho_all[k], groups)
            xin = xres0 if l == 0 else xres_i[(l - 1) % 2]
            xo = xout if l == depth - 1 else xres_i[l % 2]
            with fw.stage():
                outs = stage_row2(kb, T, xin, WBF[l]["wout"], WBF[l]["w1"], WBF[l]["w2"], d["lnp"], xo, xTdst,
                                  hosrc, sel, T)
                if l == depth - 1:
                    fw.wait_all("sp", outs)
            if l < depth - 1:
                with fw.stage():
                    for k in range(T // XC_):
                        fw.cc(Buf(f"ccB{l}_{k}"), "AllGather", xT_own[k], xT_all[k], groups)
        fw.emit()
        print("instr counts", {k: len(v) for k, v in fw.q.items()}, "sems", fw.nsem, flush=True)
    return nc


def fused_inputs(inp, c, S, depth):
    b, h = c // 2, c % 2
    T = S // 2
    x = inp["x"]
    m = dict(xT0=np.ascontiguousarray(x[b, :S].T), xres0=np.ascontiguousarray(x[b, h * T:(h + 1) * T]),
             sel=np.ascontiguousarray(np.broadcast_to(np.eye(2, dtype=np.float32)[h][None], (128, 2))))
    for l in range(depth):
        if l % 2 == 0:
            e = even_inputs(inp, l // 2, h, None, S)
            m["ropes"] = e["ropes"]
            m[f"wfm{l}"], m[f"wv{l}"], m[f"lam{l}"], m[f"gsub{l}"] = e["wfm"], e["wv"], e["lam128"], e["gsub"]
        else:
            g = gla_inputs(inp, l // 2, h, None, S)
            for k in ("wq", "wlr", "wtm", "wgu", "bg"):
                m[f"{k}{l}"] = g[k]
            m[f"gn{l}"] = g["gn128"]
        r = row_inputs(inp, l, None, None)
        m[f"wout{l}"], m[f"w1_{l}"], m[f"w2_{l}"], m[f"lnp{l}"] = r["wout"], r["w1"], r["w2"], r["lnp"]
    return m


def kernel(**inputs):
    inp = {k: np.asarray(v) for k, v in inputs.items()}
    x = inp["x"]
    Bn, S, _ = x.shape
    depth = inp["ln_mix_g"].shape[0]
    ncore = 2 * Bn
    nc = build_fused(S, depth, ncore)
    maps = [fused_inputs(inp, c, S, depth) for c in range(ncore)]
    res = run_bass_kernel_spmd(nc, maps, core_ids=list(range(ncore)))
    out = np.stack([np.concatenate([res.results[2 * b]["xout"], res.results[2 * b + 1]["xout"]], axis=0)
                    for b in range(Bn)])
    return out.astype(np.float32)
```

```python
import contextlib
import numpy as np
import concourse.bass as bass
import concourse.mybir as mybir
from concourse.bass_utils import run_bass_kernel_spmd

F32 = mybir.dt.float32
BF16 = mybir.dt.bfloat16
AF = mybir.ActivationFunctionType
ALU = mybir.AluOpType
AX = mybir.AxisListType

SEM_LIMIT = 30000


class Ent:
    __slots__ = ("stream", "seq", "flag", "hw", "val", "n", "item")

    def __init__(self, stream, seq, n):
        self.stream = stream
        self.seq = seq
        self.flag = False
        self.hw = None
        self.val = 0
        self.n = n
        self.item = None


class Stream:
    def __init__(self, name, inorder):
        self.name = name
        self.inorder = inorder
        self.ents = []

    def new(self, n):
        e = Ent(self, len(self.ents) + 1, n)
        self.ents.append(e)
        return e


class Item:
    __slots__ = ("waits", "fn", "ent")

    def __init__(self, waits, fn, ent):
        self.waits = waits
        self.fn = fn
        self.ent = ent


class Buf:
    def __init__(self, name=""):
        self.name = name
        self.w = {}
        self.r = {}
        self.ds = None


def _merge(dst, src):
    for k, e in src.items():
        o = dst.get(k)
        if o is None or o.seq < e.seq:
            dst[k] = e


class FW:
    def __init__(self, nc, stack):
        self.nc = nc
        self.stack = stack
        self.engs = ["sp", "pe", "act", "dve", "pool"]
        self.q = {k: [] for k in self.engs}
        self.es = {k: Stream(k, True) for k in self.engs}
        self.seen = {k: {} for k in self.engs}
        self.dstreams = []
        self.free_streams = []
        self.live_streams = []
        self.nsem = 0
        self.alloc_stack = stack

    def sb(self, name, shape, dtype):
        self.ntens = getattr(self, "ntens", 0) + 1
        name = f"{name}_{self.ntens}"
        return self.alloc_stack.enter_context(self.nc.sbuf_tensor(name, list(shape), dtype))

    def ps(self, name, shape, dtype=F32):
        self.ntens = getattr(self, "ntens", 0) + 1
        name = f"{name}_{self.ntens}"
        return self.alloc_stack.enter_context(self.nc.psum_tensor(name, list(shape), dtype))

    def dstream(self, name):
        s = Stream(name, False)
        self.dstreams.append(s)
        return s

    def _hw(self, name):
        self.nsem += 1
        return self.stack.enter_context(self.nc.semaphore(f"s{self.nsem}_{name}"))

    def _waits(self, eng, reads, writes):
        raw = {}
        for b in reads:
            _merge(raw, b.w)
        oth = {}
        for b in writes:
            _merge(oth, b.w)
            _merge(oth, b.r)
        own = self.es[eng]
        need = dict(raw)
        for k, e in oth.items():
            if e.stream is own and eng == "pe":
                continue
            o = need.get(k)
            if o is None or o.seq < e.seq:
                need[k] = e
        waits = []
        seen = self.seen[eng]
        for k, e in need.items():
            if seen.get(k, 0) >= e.seq:
                continue
            seen[k] = e.seq
            e.flag = True
            waits.append(e)
        return waits

    def _commit(self, ent, reads, writes):
        k = id(ent.stream)
        for b in writes:
            b.w = {k: ent}
            b.r = {}
        for b in reads:
            o = b.r.get(k)
            if o is None or o.seq < ent.seq:
                b.r[k] = ent

    def op(self, eng, fn, reads=(), writes=()):
        waits = self._waits(eng, reads, writes)
        ent = self.es[eng].new(1)
        it = Item(waits, fn, ent)
        ent.item = it
        self.q[eng].append(it)
        self._commit(ent, reads, writes)
        return ent

    def dma(self, sbuf, out, in_, reads=(), writes=(), queue="sp", **kw):
        stream = getattr(sbuf, "ds", None)
        if stream is None:
            stream = sbuf.ds = self._take_stream("d" + sbuf.name)
        waits = self._waits(queue, reads, writes)
        ent = stream.new(16)
        ent.flag = True
        it = Item(waits, lambda e: e.dma_start(out=out, in_=in_, **kw), ent)
        ent.item = it
        self.q[queue].append(it)
        self._commit(ent, reads, writes)
        return ent

    def _take_stream(self, name):
        if self.free_streams:
            st = self.free_streams.pop()
        else:
            st = self.dstream(name)
        self.live_streams.append(st)
        return st

    def cc(self, buf, kind, in_ap, out_ap, groups, reads=(), writes=()):
        stream = getattr(buf, "ds", None)
        if stream is None:
            stream = buf.ds = self._take_stream("cc" + buf.name)
        waits = self._waits("pool", reads, writes)
        ent = stream.new(1)
        ent.flag = True
        it = Item(waits, lambda e: e.collective_compute(kind, op=ALU.bypass, replica_groups=groups,
                                                        ins=[in_ap.opt()], outs=[out_ap.opt()]), ent)
        ent.item = it
        self.q["pool"].append(it)
        self._commit(ent, reads, writes)
        return ent

    def barrier(self):
        toks = []
        for k in self.engs:
            if self.es[k].ents:
                toks.append(self.es[k].ents[-1])
        for st in self.live_streams:
            if st.ents:
                toks.append(st.ents[-1])
        for eng in self.engs:
            waits = []
            seen = self.seen[eng]
            for e in toks:
                if e.stream is self.es[eng]:
                    continue
                k = id(e.stream)
                if seen.get(k, 0) >= e.seq:
                    continue
                seen[k] = e.seq
                e.flag = True
                waits.append(e)
            self.q[eng].append(Item(waits, None, None))
        self.free_streams.extend(self.live_streams)
        self.live_streams = []

    @contextlib.contextmanager
    def stage(self):
        outer = self.alloc_stack
        with contextlib.ExitStack() as sub:
            self.alloc_stack = sub
            try:
                yield
            finally:
                self.alloc_stack = outer
            self.barrier()

    def wait_all(self, eng, bufs):
        waits = self._waits(eng, bufs, ())
        self.q[eng].append(Item(waits, None, None))

    def emit(self):
        for s in list(self.es.values()) + self.dstreams:
            hw = None
            val = 0
            prev = None
            for e in s.ents:
                if not e.flag:
                    continue
                if hw is None or val + e.n > SEM_LIMIT:
                    if hw is not None and not s.inorder:
                        e.item.waits.append(prev)
                    hw = self._hw(s.name)
                    val = 0
                val += e.n
                e.hw = hw
                e.val = val
                prev = e
        nc = self.nc

        def mk(name):
            items = self.q[name]

            def body(e):
                for it in items:
                    for w in it.waits:
                        e.wait_ge(w.hw, w.val)
                    if it.fn is not None:
                        ins = it.fn(e)
                        if it.ent.flag:
                            ins.then_inc(it.ent.hw, it.ent.n)

            return body

        with nc.Block() as block:
            block.sync(mk("sp"))
            block.tensor(mk("pe"))
            block.scalar(mk("act"))
            block.vector(mk("dve"))
            block.gpsimd(mk("pool"))


import math


D = 1024
ALPHA = 8.0 ** 0.25
NEG = -30000.0


class KB:
    def __init__(self, nc, stack):
        self.nc = nc
        self.fw = FW(nc, stack)
        self.nps = 0

    def sb(self, name, shape, dt):
        return self.fw.sb(name, shape, dt)

    def bank(self, dt=F32):
        self.nps += 1
        n = 512 if dt == F32 else 1024
        return self.fw.ps(f"ps{self.nps}", [128, n], dt), Buf(f"ps{self.nps}")

    def mm(self, out, lhsT, rhs, start, stop, reads, writes):
        return self.fw.op("pe", lambda e: e.matmul(out, lhsT=lhsT, rhs=rhs, start=start, stop=stop,
                                                   skip_group_check=True), reads, writes)

    def tr(self, out, in_, ident, reads, writes):
        return self.fw.op("pe", lambda e: e.transpose(out=out, in_=in_, identity=ident), reads, writes)

    def act(self, out, in_, func, reads, writes, **kw):
        return self.fw.op("act", lambda e: e.activation(out=out, in_=in_, func=func, **kw), reads, writes)

    def tt(self, eng, out, in0, in1, op, reads, writes):
        return self.fw.op(eng, lambda e: e.tensor_tensor(out=out, in0=in0, in1=in1, op=op), reads, writes)

    def ts(self, eng, out, in0, s1, s2, op0, op1, reads, writes):
        if op1 is None:
            return self.fw.op(eng, lambda e: e.tensor_scalar(out=out, in0=in0, scalar1=s1, scalar2=None, op0=op0),
                              reads, writes)
        return self.fw.op(eng, lambda e: e.tensor_scalar(out=out, in0=in0, scalar1=s1, scalar2=s2, op0=op0, op1=op1),
                          reads, writes)

    def stt(self, eng, out, in0, scalar, in1, op0, op1, reads, writes):
        return self.fw.op(eng, lambda e: e.scalar_tensor_tensor(out=out, in0=in0, scalar=scalar, in1=in1,
                                                                op0=op0, op1=op1), reads, writes)

    def cp(self, eng, out, in_, reads, writes):
        if eng == "act":
            return self.fw.op("act", lambda e: e.copy(out=out, in_=in_), reads, writes)
        return self.fw.op(eng, lambda e: e.tensor_copy(out=out, in_=in_), reads, writes)

    def memset(self, eng, ap, val, writes):
        return self.fw.op(eng, lambda e: e.memset(ap, val), (), writes)

    def asel(self, out, in_, pattern, cmp, fill, base, cm, bufs):
        return self.fw.op("pool", lambda e: e.affine_select(out=out, in_=in_, pattern=pattern, compare_op=cmp,
                                                            fill=fill, base=base, channel_multiplier=cm), bufs, bufs)

    def consts(self, moba=False):
        c = {}
        B = Buf("consts")
        c["buf"] = B
        ident = self.sb("ident", [128, 128], BF16)
        self.memset("pool", ident[:], 1.0, [B])
        self.asel(ident[:], ident[:], [[-1, 128]], ALU.is_equal, 0.0, 0, 1, [B])
        c["ident"] = ident
        negm = self.sb("negm", [128, 128], BF16)
        self.memset("pool", negm[:], 0.0, [B])
        self.asel(negm[:], negm[:], [[1, 128]], ALU.is_ge, NEG, 0, -1, [B])
        c["negm"] = negm
        tri = self.sb("tri", [128, 128], F32)
        self.memset("pool", tri[:], 1.0, [B])
        self.asel(tri[:], tri[:], [[1, 128]], ALU.is_ge, 0.0, 0, -1, [B])
        c["tri"] = tri
        ui = self.sb("uincl", [128, 128], F32)
        self.memset("pool", ui[:], -1.0 / 16.0, [B])
        self.asel(ui[:], ui[:], [[1, 128]], ALU.is_ge, 0.0, 0, -1, [B])
        c["uincl"] = ui
        us = self.sb("ustr", [128, 128], F32)
        self.memset("pool", us[:], -1.0 / 16.0, [B])
        self.asel(us[:], us[:], [[-1, 128]], ALU.is_gt, 0.0, 0, 1, [B])
        c["ustr"] = us
        ones_f = self.sb("ones_f", [128, 128], F32)
        self.memset("pool", ones_f[:], 1.0, [B])
        c["ones_f"] = ones_f
        ones_b = self.sb("ones_b", [128, 128], BF16)
        self.memset("pool", ones_b[:], 1.0, [B])
        c["ones_b"] = ones_b
        e01 = self.sb("e01", [128, 4], BF16)
        self.memset("pool", e01[:], 0.0, [B])
        self.memset("pool", e01[:, 0:1], 1.0, [B])
        self.memset("pool", e01[:, 3:4], 1.0, [B])
        c["e01"] = e01
        e01f = self.sb("e01f", [128, 4], F32)
        self.memset("pool", e01f[:], 0.0, [B])
        self.memset("pool", e01f[:, 0:1], 1.0, [B])
        self.memset("pool", e01f[:, 3:4], 1.0, [B])
        c["e01f"] = e01f
        sel2 = self.sb("sel2", [2, 256], F32)
        self.memset("pool", sel2[:], 1.0, [B])
        self.asel(sel2[:, 0:128], sel2[:, 0:128], [[0, 128]], ALU.is_equal, 0.0, 0, 1, [B])
        self.asel(sel2[:, 128:256], sel2[:, 128:256], [[0, 128]], ALU.is_equal, 0.0, -1, 1, [B])
        c["sel2"] = sel2
        if not moba:
            self.c = c
            return c
        esel = self.sb("esel", [32, 32 * 128], BF16)
        self.memset("pool", esel[:], NEG, [B])
        ev = esel[:].rearrange("p (n k) -> p n k", k=128)
        self.asel(ev, ev, [[-1, 32], [0, 128]], ALU.is_equal, 0.0, 0, 1, [B])
        c["esel"] = esel
        self.c = c
        return c


def stage_row(kb, T, ho, xres, wout, w1, w2, lnp, xout, xTout, dbg=None, ho_sel=None, w_bf16=False):
    fw = kb.fw
    c = kb.c
    CB = c["buf"]
    wo_sb = kb.sb("wo_sb", [128, 8, 1024], BF16)
    WO = Buf("wo")
    ln_sb = kb.sb("ln_sb", [128, 4, 1024], F32)
    LNB = Buf("ln")
    wq_ = "sp" if w_bf16 else "pool"
    fw.dma(WO, wo_sb[:], wout.rearrange("(c p) n -> p c n", p=128), writes=[WO], queue=wq_)
    fw.dma(LNB, ln_sb[:], lnp, writes=[LNB])
    NW = 2
    w1b = [kb.sb(f"w1b{i}", [128, 8, 1024], BF16) for i in range(NW)]
    w2b = [kb.sb(f"w2b{i}", [128, 8, 1024], BF16) for i in range(NW)]
    W1B = [Buf() for _ in range(NW)]
    W2B = [Buf() for _ in range(NW)]
    hoc = kb.sb("hoc", [128, 8, 512], BF16)
    HOC = Buf("hoc")
    if ho_sel is not None:
        hoa = hoc
        hob = kb.sb("hob", [128, 8, 512], BF16)
        sel_sb = kb.sb("sel_sb", [128, 2], F32)
        HOA, HOBB, SELB = HOC, Buf("hob"), Buf("selb")
        fw.dma(SELB, sel_sb[:], ho_sel[1], writes=[SELB])
    y = kb.sb("y", [128, 4, 1024], F32)
    Y = [Buf(f"y{i}") for i in range(4)]
    acc = kb.sb("acc", [128, 4, 1024], F32)
    ACC = [Buf(f"acc{i}") for i in range(4)]
    xb = kb.sb("xb", [128, 1024], BF16)
    XB = Buf()
    x1T = kb.sb("x1T", [128, 8, 512], BF16)
    X1T = Buf("x1T")
    xTo, XTO = x1T, X1T
    hsq = [kb.sb(f"hsq{i}", [128, 8, 512], BF16) for i in range(2)]
    HSQ = [[Buf() for _ in range(8)] for _ in range(2)]
    rl = [kb.sb(f"rl{i}", [128, 512], F32) for i in range(2)]
    RL = [Buf() for _ in range(2)]
    st6 = kb.sb("st6", [128, 2, 6], F32)
    mv = kb.sb("mv", [128, 2], F32)
    rstd = kb.sb("rstd", [128, 2], F32)
    STB = Buf()
    G = [kb.bank() for _ in range(4)]
    TR = [kb.bank(BF16) for _ in range(2)]
    OUTS = []

    def nout():
        b = Buf("o")
        OUTS.append(b)
        return b
    gi = [0]

    def nextG():
        g = G[gi[0] % 4]
        gi[0] += 1
        return g

    ti = [0]

    def layer_norm(buf_ap, BUFS, j, gidx):
        v = buf_ap[:, j, :]
        for hh in range(2):
            fw.op("dve", lambda e, hh=hh: e.bn_stats(out=st6[:, hh, :], in_=buf_ap[:, j, hh * 512:(hh + 1) * 512]),
                  [BUFS[j]], [STB])
        fw.op("dve", lambda e: e.bn_aggr(out=mv[:], in_=st6[:].rearrange("p a b -> p (a b)")), [STB], [STB])
        kb.act(rstd[:, 0:1], mv[:, 1:2], AF.Sqrt, [STB], [STB], bias=1e-5, scale=1.0)
        fw.op("dve", lambda e: e.reciprocal(out=rstd[:, 1:2], in_=rstd[:, 0:1]), [STB], [STB])
        kb.ts("dve", v, v, mv[:, 0:1], rstd[:, 1:2], ALU.subtract, ALU.mult, [STB, BUFS[j]], [BUFS[j]])
        kb.tt("pool", v, v, ln_sb[:, gidx, :], ALU.mult, [BUFS[j], LNB], [BUFS[j]])
        kb.tt("pool", v, v, ln_sb[:, gidx + 1, :], ALU.add, [BUFS[j], LNB], [BUFS[j]])

    def to_T(src_ap, SRC, j, dstT, DST):
        kb.cp("act", xb[:], src_ap[:, j, :], [SRC[j]], [XB])
        trp, TRB = TR[ti[0] % 2]
        ti[0] += 1
        for k in range(8):
            kb.tr(trp[:, k * 128:(k + 1) * 128], xb[:, k * 128:(k + 1) * 128], c["ident"][:], [XB, CB], [TRB])
        kb.cp("dve", dstT[:, :, j * 128:(j + 1) * 128], trp[:].rearrange("p (k t) -> p k t", t=128), [TRB], [DST])

    nst = T // 512
    for st in range(nst):
        t0 = st * 512
        if ho_sel is None:
            fw.dma(HOC, hoc[:], ho.rearrange("c p t -> p c t")[:, :, t0:t0 + 512], writes=[HOC])
        else:
            hosrc, Thalf = ho_sel[0], ho_sel[2]
            fw.dma(HOA, hoa[:], hosrc(t0), writes=[HOA])
            fw.dma(HOBB, hob[:], hosrc(Thalf + t0), writes=[HOBB])
            kb.act(hoa[:], hoa[:], AF.Copy, [HOA, SELB], [HOA], scale=sel_sb[:, 0:1])
            kb.stt("dve", hoa[:], hob[:], sel_sb[:, 1:2], hoa[:], ALU.mult, ALU.add, [HOBB, HOA, SELB], [HOA])
        fw.dma(Y[0], y[:], xres[t0:t0 + 512, :].rearrange("(j p) d -> p j d", p=128), writes=Y)
        for j in range(4):
            for nb in range(2):
                g, GB = nextG()
                for cc in range(8):
                    kb.mm(g[:], hoc[:, cc, j * 128:(j + 1) * 128], wo_sb[:, cc, nb * 512:(nb + 1) * 512],
                          cc == 0, cc == 7, [HOC, WO], [GB])
                kb.stt("dve", y[:, j, nb * 512:(nb + 1) * 512], y[:, j, nb * 512:(nb + 1) * 512], ALPHA, g[:],
                       ALU.mult, ALU.add, [Y[j], GB], [Y[j]])
            layer_norm(y, Y, j, 0)
            to_T(y, Y, j, x1T, X1T)
        if dbg is not None:
            fw.dma(Y[0], dbg[0], y[:], reads=Y, writes=[nout()])
            fw.dma(X1T, dbg[1], x1T[:], reads=[X1T], writes=[nout()])
        for fb in range(4):
            wi = (st * 4 + fb) % NW
            fw.dma(W1B[wi], w1b[wi][:], w1.rearrange("(k p) f -> p k f", p=128)[:, :, fb * 1024:(fb + 1) * 1024],
                   writes=[W1B[wi]], queue=wq_)
            fw.dma(W2B[wi], w2b[wi][:], w2[fb * 1024:(fb + 1) * 1024, :].rearrange("(c p) d -> p c d", p=128),
                   writes=[W2B[wi]], queue=wq_)
            hi = fb % 2
            for fc in range(8):
                g, GB = nextG()
                for k in range(8):
                    kb.mm(g[:], w1b[wi][:, k, fc * 128:(fc + 1) * 128], x1T[:, k, :], k == 0, k == 7,
                          [W1B[wi], X1T], [GB])
                ri = fc % 2
                kb.act(rl[ri][:], g[:], AF.Relu, [GB], [RL[ri]])
                kb.tt("pool", hsq[hi][:, fc, :], rl[ri][:], rl[ri][:], ALU.mult, [RL[ri]], [HSQ[hi][fc]])
            for j in range(4):
                for nb in range(2):
                    g, GB = nextG()
                    for fc in range(8):
                        kb.mm(g[:], hsq[hi][:, fc, j * 128:(j + 1) * 128], w2b[wi][:, fc, nb * 512:(nb + 1) * 512],
                              fc == 0, fc == 7, [HSQ[hi][fc], W2B[wi]], [GB])
                    a = acc[:, j, nb * 512:(nb + 1) * 512]
                    if fb == 0:
                        kb.cp("dve", a, g[:], [GB], [ACC[j]])
                    else:
                        kb.tt("dve", a, a, g[:], ALU.add, [GB, ACC[j]], [ACC[j]])
        if dbg is not None:
            fw.dma(ACC[0], dbg[2], acc[:], reads=ACC, writes=[nout()])
            fw.dma(HSQ[1][0], dbg[3], hsq[1][:], reads=HSQ[1], writes=[nout()])
        for j in range(4):
            kb.stt("dve", acc[:, j, :], y[:, j, :], ALPHA, acc[:, j, :], ALU.mult, ALU.add, [Y[j], ACC[j]], [ACC[j]])
            layer_norm(acc, ACC, j, 2)
            to_T(acc, ACC, j, xTo, XTO)
        fw.dma(ACC[0], xout[t0:t0 + 512, :].rearrange("(j p) d -> p j d", p=128), acc[:], reads=ACC, writes=[nout()])
        fw.dma(XTO, xTout(t0), xTo[:], reads=[XTO], writes=[nout()])
    return OUTS


def stage_even(kb, S, xsrc, wfm, wv, ropes, lam128, gsub, hodst, lambda_init, hook=None):
    fw = kb.fw
    c = kb.c
    CB = c["buf"]
    nkt = S // 128
    nqb = S // 512
    nblk = S // 256
    QT = [kb.sb(f"QT{i}", [128, S], BF16) for i in range(2)]
    KT = [kb.sb(f"KT{i}", [128, S], BF16) for i in range(2)]
    V = [kb.sb(f"V{i}", [128, nkt, 128], BF16) for i in range(2)]
    QTB = [Buf() for _ in range(2)]
    KTB = [Buf() for _ in range(2)]
    VB = [Buf() for _ in range(2)]
    wfm_sb = kb.sb("wfm_sb", [128, 8, 1024], BF16)
    WFM = Buf("wfm")
    wv_sb = kb.sb("wv_sb", [128, 8, 256], BF16)
    WV = Buf("wv")
    xc = [kb.sb(f"xc{i}", [128, 8, 512], BF16) for i in range(2)]
    XC = [Buf(f"xc{i}") for i in range(2)]
    rp = [kb.sb(f"rp{i}", [128, 2, 512], F32) for i in range(2)]
    RP = [Buf(f"rp{i}") for i in range(2)]
    F = [kb.sb(f"F{i}", [128, 512], F32) for i in range(4)]
    FBUF = [Buf() for _ in range(4)]
    PT = [kb.sb(f"PT{i}", [128, 512], BF16) for i in range(4)]
    PTB = [Buf() for _ in range(4)]
    obf = [kb.sb(f"obf{i}", [128, 512], BF16) for i in range(2)]
    OBF = [Buf(f"obf{i}") for i in range(2)]
    rr = kb.sb("rr", [2, 512], F32)
    RR = Buf()
    accs = [[kb.sb(f"accs{p}_{i}", [128, 512], F32) for i in range(4)] for p in range(2)]
    ACCB = [[Buf() for _ in range(4)] for _ in range(2)]
    Os = [[kb.sb(f"Os{p}_{i}", [128, 512], F32) for i in range(2)] for p in range(2)]
    OSB = [[Buf() for _ in range(2)] for _ in range(2)]
    lam_sb = kb.sb("lam_sb", [128, 256], F32)
    gs_sb = kb.sb("gs_sb", [128, 1], F32)
    sm = kb.sb("sm_e", [128, 8], F32)
    LAM = Buf("lam")
    kmf = kb.sb("kmf", [128, 32], F32)
    kmb = [kb.sb(f"kmb{i}", [128, 32], BF16) for i in range(2)]
    KMB = [Buf() for _ in range(2)]
    Gs = kb.sb("Gs", [128, 32], F32)
    top8 = kb.sb("top8", [128, 8], F32)
    nots = kb.sb("nots", [128, 32], BF16)
    GSB = Buf()
    biasT = kb.sb("biasT", [32, 512], BF16)
    BIAS = Buf()
    SBK = [kb.bank() for _ in range(4)]
    O1, O1B = kb.bank()
    O2, O2B = kb.bank()
    SUMP, SUMB = kb.bank()
    FBK, FBB = kb.bank()
    OUTS = []

    fw.dma(LAM, lam_sb[:], lam128, writes=[LAM])
    fw.dma(LAM, gs_sb[:], gsub, writes=[LAM])
    kb.tt("dve", F[0][:, 0:64], lam_sb[:, 0:64], lam_sb[:, 64:128], ALU.mult, [LAM], [FBUF[0]])
    kb.tt("dve", F[0][:, 64:128], lam_sb[:, 128:192], lam_sb[:, 192:256], ALU.mult, [LAM], [FBUF[0]])
    fw.op("dve", lambda e: e.reduce_sum(out=sm[:, 0:1], in_=F[0][:, 0:64], axis=AX.X), [FBUF[0]], [LAM])
    fw.op("dve", lambda e: e.reduce_sum(out=sm[:, 1:2], in_=F[0][:, 64:128], axis=AX.X), [FBUF[0]], [LAM])
    kb.act(sm[:, 2:4], sm[:, 0:2], AF.Exp, [LAM], [LAM])
    kb.stt("dve", sm[:, 4:5], sm[:, 3:4], -float(lambda_init), sm[:, 2:3], ALU.add, ALU.subtract, [LAM], [LAM])

    def inproj(typ):
        fw.dma(WFM, wfm_sb[:], wfm[typ].rearrange("(k p) n -> p k n", p=128), writes=[WFM], queue="pool")
        fw.dma(WV, wv_sb[:], wv[typ].rearrange("(k p) n -> p k n", p=128), writes=[WV], queue="pool")
        gi = 0
        for cch in range(S // 512):
            t0 = cch * 512
            xi = cch % 2
            src, x_bf16 = xsrc(t0)
            fw.dma(XC[xi], xc[xi][:], src, writes=[XC[xi]], queue=("sp" if x_bf16 else "pool"))
            fw.dma(RP[xi], rp[xi][:], ropes[typ].rearrange("a p t -> p a t")[:, :, t0:t0 + 512], writes=[RP[xi]])
            for g in range(2):
                for hd in range(2):
                    dst, DB = (QT[hd], QTB[hd]) if g == 0 else (KT[hd], KTB[hd])
                    po, POB = SBK[gi % 4]
                    pp, PPB = SBK[(gi + 1) % 4]
                    gi += 2
                    fo = g * 4 + hd
                    fp = g * 4 + 2 + hd
                    for k in range(8):
                        kb.mm(po[:], wfm_sb[:, k, fo * 128:(fo + 1) * 128], xc[xi][:, k, :], k == 0, k == 7,
                              [WFM, XC[xi]], [POB])
                    for k in range(8):
                        kb.mm(pp[:], wfm_sb[:, k, fp * 128:(fp + 1) * 128], xc[xi][:, k, :], k == 0, k == 7,
                              [WFM, XC[xi]], [PPB])
                    fa = (g * 2 + hd) % 2 * 2
                    kb.tt("dve", F[fa][:], po[:], rp[xi][:, 0, :], ALU.mult, [POB, RP[xi]], [FBUF[fa]])
                    kb.tt("dve", F[fa + 1][:], pp[:], rp[xi][:, 1, :], ALU.mult, [PPB, RP[xi]], [FBUF[fa + 1]])
                    kb.tt("pool", dst[:, t0:t0 + 512], F[fa][:], F[fa + 1][:], ALU.add, [FBUF[fa], FBUF[fa + 1]], [DB])
            for sub in range(4):
                pv, PVB = SBK[gi % 4]
                gi += 1
                for k in range(8):
                    kb.mm(pv[:, 0:256], xc[xi][:, k, sub * 128:(sub + 1) * 128], wv_sb[:, k, :], k == 0, k == 7,
                          [WV, XC[xi]], [PVB])
                for hd in range(2):
                    kb.cp("act", V[hd][:, cch * 4 + sub, :], pv[:, hd * 128:(hd + 1) * 128], [PVB], [VB[hd]])

    def attention(typ, hd):
        nmap = 2 if typ == 0 else 1
        scale = 64.0 ** -0.5 if typ == 0 else 128.0 ** -0.5
        och = typ * 2 + hd
        for qb in range(nqb):
            Q0 = qb * 512
            if typ == 1:
                for j in range(4):
                    q0 = Q0 + j * 128
                    ob = q0 // 256
                    kb.memset("pool", nots[:], 0.0, [GSB])
                    if ob > 0:
                        kb.memset("pool", Gs[:], -1e30, [GSB])
                        gp, GPB = SBK[j % 4]
                        kb.mm(gp[:, 0:32], QT[hd][:, q0:q0 + 128], kmb[hd][:, 0:32], True, True,
                              [QTB[hd], KMB[hd]], [GPB])
                        kb.cp("dve", Gs[:, 0:ob], gp[:, 0:ob], [GPB, GSB], [GSB])
                        fw.op("dve", lambda e: e.max(out=top8[:], in_=Gs[:]), [GSB], [GSB])
                        kb.ts("dve", nots[:, 0:ob], Gs[:, 0:ob], top8[:, 2:3], None, ALU.is_lt, None, [GSB], [GSB])
                    kb.mm(FBK[0:32, j * 128:(j + 1) * 128], nots[:, 0:32], c["ident"][:], True, True, [GSB, CB], [FBB])
                kb.cp("act", biasT[:], FBK[0:32, 0:512], [FBB], [BIAS])
            items = [(kt, m) for kt in range((Q0 + 512) // 128) for m in range(nmap)]
            n = len(items)
            OB_ = [(O1, O1B), (O2, O2B)]
            last_kt = (Q0 + 512) // 128 - 1

            def issueS(i):
                kt, m = items[i]
                K0 = kt * 128
                o = max(0, K0 - Q0)
                diag = K0 >= Q0
                sp_, SPB = SBK[i % 4]
                if typ == 0:
                    kb.mm(sp_[:, o:512], KT[hd][m * 64:(m + 1) * 64, K0:K0 + 128],
                          QT[hd][m * 64:(m + 1) * 64, Q0 + o:Q0 + 512], True, not diag, [KTB[hd], QTB[hd]], [SPB])
                else:
                    kb.mm(sp_[:, o:512], KT[hd][:, K0:K0 + 128], QT[hd][:, Q0 + o:Q0 + 512], True, False,
                          [KTB[hd], QTB[hd]], [SPB])
                    nb_ = K0 // 256
                    kb.mm(sp_[:, o:512], c["esel"][0:32, nb_ * 128:(nb_ + 1) * 128], biasT[0:32, o:512], False,
                          not diag, [CB, BIAS], [SPB])
                if diag:
                    kb.mm(sp_[:, o:o + 128], c["ident"][:], c["negm"][:], False, True, [CB], [SPB])
                kb.act(PT[i % 4][:, o:512], sp_[:, o:512], AF.Exp, [SPB], [PTB[i % 4]], scale=scale)

            def issuePV(i):
                kt, m = items[i]
                K0 = kt * 128
                o = max(0, K0 - Q0)
                Op, OpB = OB_[m]
                kb.mm(Op[:, o:512], V[hd][:, kt, :], PT[i % 4][:, o:512], kt == 0, kt == last_kt,
                      [VB[hd], PTB[i % 4]], [OpB])
                eng = "pool" if i % 3 == 2 else "dve"
                ai = (1 if eng == "pool" else 0) * 2 + m
                A_, AB_ = accs[qcount[0] % 2], ACCB[qcount[0] % 2]
                if not acc_used[ai]:
                    acc_used[ai] = True
                    if o > 0:
                        kb.memset(eng, A_[ai][:, 0:o], 0.0, [AB_[ai]])
                    kb.cp(eng, A_[ai][:, o:512], PT[i % 4][:, o:512], [PTB[i % 4]], [AB_[ai]])
                else:
                    kb.tt(eng, A_[ai][:, o:512], A_[ai][:, o:512], PT[i % 4][:, o:512], ALU.add,
                          [PTB[i % 4], AB_[ai]], [AB_[ai]])

            acc_used = [False] * 4
            if typ == 0:
                for g in range(n // 2 + 1):
                    if g < n // 2:
                        issueS(2 * g)
                        issueS(2 * g + 1)
                    if g >= 1:
                        issuePV(2 * g - 2)
                        issuePV(2 * g - 1)
                    if g % 3 == 2:
                        defer_tick()
            else:
                LA = 2
                for i in range(n + LA):
                    if i < n:
                        issueS(i)
                    if i - LA >= 0:
                        issuePV(i - LA)
                    if i % 4 == 3:
                        defer_tick()
            flush()
            par = qcount[0] % 2
            qcount[0] += 1
            nr = 2 if typ == 0 else 1
            kb.cp("act", Os[par][0][:], O1[:], [O1B], [OSB[par][0]])
            if typ == 0:
                kb.cp("dve", Os[par][1][:], O2[:], [O2B], [OSB[par][1]])
            used = [ai for ai in range(4) if acc_used[ai]]
            pending.extend(make_steps(typ, och, Q0, par, nr, used, qb % 2))

    def make_steps(typ, och, Q0, par, nr, used, oi):
        A = accs[par]
        AB = ACCB[par]
        O1s, O2s = Os[par][0], Os[par][1]
        O1sB, O2sB = OSB[par][0], OSB[par][1]
        st = []

        def s_sum():
            for ui, ai in enumerate(used):
                m_ = ai % 2
                lhs = c["e01f"][:, 2 * m_:2 * m_ + 2] if typ == 0 else c["ones_f"][:, 0:1]
                kb.mm(SUMP[0:nr, :], lhs, A[ai][:], ui == 0, ui == len(used) - 1, [CB, AB[ai]], [SUMB])
        st.append(s_sum)

        def s_rcp():
            kb.act(rr[0:nr, :], SUMP[0:nr, :], AF.Ln, [SUMB], [RR])
            kb.act(rr[0:nr, :], rr[0:nr, :], AF.Exp, [RR], [RR], scale=-1.0)
        st.append(s_rcp)

        def s_out():
            ob_ = Buf("o")
            OUTS.append(ob_)
            fw.dma(OBF[oi], hodst(och, Q0), obf[oi][:], reads=[OBF[oi]], writes=[ob_])

        if typ == 1:
            def s_b():
                kb.mm(FBK[:], c["ones_f"][0:1, :], rr[0:1, :], True, True, [CB, RR], [FBB])
                kb.cp("act", F[0][:], FBK[:], [FBB], [FBUF[0]])
            st.append(s_b)

            def s_m():
                kb.tt("dve", obf[oi][:], O1s[:], F[0][:], ALU.mult, [O1sB, FBUF[0]], [OBF[oi]])
                s_out()
            st.append(s_m)
            return st

        def s1():
            kb.mm(FBK[:], c["sel2"][0:2, 0:128], rr[0:2, :], True, True, [CB, RR], [FBB])
            kb.cp("act", F[0][:], FBK[:], [FBB], [FBUF[0]])
        st.append(s1)

        def s2():
            kb.tt("dve", F[1][:], O1s[:], F[0][:], ALU.mult, [O1sB, FBUF[0]], [FBUF[1]])
            kb.mm(FBK[:], c["sel2"][0:2, 128:256], rr[0:2, :], True, True, [CB, RR], [FBB])
            kb.cp("act", F[0][:], FBK[:], [FBB], [FBUF[0]])
        st.append(s2)

        def s3():
            kb.tt("dve", F[2][:], O2s[:], F[0][:], ALU.mult, [O2sB, FBUF[0]], [FBUF[2]])
            kb.stt("dve", F[3][:], F[2][:], sm[:, 4:5], F[1][:], ALU.mult, ALU.add, [FBUF[2], FBUF[1], LAM],
                   [FBUF[3]])
            kb.tt("dve", F[1][:], F[3][:], F[3][:], ALU.mult, [FBUF[3]], [FBUF[1]])
        st.append(s3)

        def s4():
            kb.mm(FBK[0:1, :], c["ones_f"][:, 0:1], F[1][:], True, True, [CB, FBUF[1]], [FBB])
            kb.act(rr[0:1, :], FBK[0:1, :], AF.Ln, [FBB], [RR], bias=1e-5, scale=1.0 / 128.0)
            kb.act(rr[0:1, :], rr[0:1, :], AF.Exp, [RR], [RR], scale=-0.5)
        st.append(s4)

        def s5():
            kb.mm(FBK[:], c["ones_f"][0:1, :], rr[0:1, :], True, True, [CB, RR], [FBB])
            kb.stt("dve", F[2][:], F[3][:], gs_sb[:, 0:1], FBK[:], ALU.mult, ALU.mult, [FBUF[3], LAM, FBB],
                   [FBUF[2]])
            kb.act(obf[oi][:], F[2][:], AF.Copy, [FBUF[2]], [OBF[oi]], scale=float(1.0 - lambda_init))
            s_out()
        st.append(s5)
        return st

    pending = []
    qcount = [0]

    def flush():
        while pending:
            pending.pop(0)()

    def defer_tick():
        if pending:
            pending.pop(0)()

    KMF = Buf()
    for typ in range(2):
        inproj(typ)
        if typ == 0 and hook is not None:
            hook()
        if typ == 1:
            for hd in range(2):
                kb.memset("pool", kmf[:], 0.0, [KMF])
                fw.op("dve", lambda e, hd=hd: e.reduce_sum(out=kmf[:, 0:nblk],
                                                           in_=KT[hd][:].rearrange("p (n l) -> p n l", l=256),
                                                           axis=AX.X), [KTB[hd]], [KMF])
                kb.ts("dve", kmb[hd][:], kmf[:], 1.0 / 256.0, None, ALU.mult, None, [KMF], [KMB[hd]])
        for hd in range(2):
            attention(typ, hd)
            flush()
    return OUTS


def stage_gla(kb, S, xsrc, wq, wlr, wtm, wgu, bg, gn128, hodst4, hook=None, on_store=None):
    fw = kb.fw
    c = kb.c
    CB = c["buf"]
    wq_sb = kb.sb("wq_sb", [128, 8, 512], BF16)
    wlr_sb = kb.sb("wlr_sb", [128, 8, 16], BF16)
    wtm_sb = kb.sb("wtm_sb", [128, 8, 1280], BF16)
    wgu_sb = kb.sb("wgu_sb", [16, 256], BF16)
    bg_sb = kb.sb("bg_sb", [1, 256], BF16)
    gn_sb = kb.sb("gn_sb", [128, 256], F32)
    WB = Buf("glaw")
    fw.dma(WB, wq_sb[:], wq.rearrange("(k p) n -> p k n", p=128), writes=[WB], queue="pool")
    fw.dma(WB, wlr_sb[:], wlr.rearrange("(k p) n -> p k n", p=128), writes=[WB], queue="pool")
    fw.dma(WB, wtm_sb[:], wtm.rearrange("(k p) n -> p k n", p=128), writes=[WB], queue="pool")
    fw.dma(WB, wgu_sb[:], wgu, writes=[WB], queue="pool")
    fw.dma(WB, bg_sb[:], bg, writes=[WB], queue="pool")
    fw.dma(WB, gn_sb[:], gn128, writes=[WB])
    if hook is not None:
        hook()
    xc = [kb.sb(f"gxc{i}", [128, 8, 512], BF16) for i in range(2)]
    XC = [Buf(f"gxc{i}") for i in range(2)]
    qk = kb.sb("qk", [128, 4, 512], F32)
    QK = Buf()
    lrT = kb.sb("lrT", [16, 512], BF16)
    LRT = Buf()
    vb = kb.sb("vb", [128, 512], BF16)
    VBB = Buf()
    sr = kb.sb("sr", [128, 512], F32)
    SRB = Buf()
    ee = kb.sb("ee", [128, 256], F32)
    sp_ = kb.sb("spl", [128, 256], F32)
    SPB = Buf()
    E3 = kb.sb("E3", [128, 256], F32)
    E3B = Buf()
    khat = kb.sb("khat", [128, 256], BF16)
    KHB = Buf()
    E1 = kb.sb("E1", [128, 128], F32)
    E2 = kb.sb("E2", [128, 128], F32)
    EB = Buf()
    dec = kb.sb("dec", [128, 2], F32)
    DECB = [Buf() for _ in range(2)]
    qtl = [kb.sb(f"qtl{i}", [128, 128], BF16) for i in range(2)]
    ktl = [kb.sb(f"ktl{i}", [128, 128], BF16) for i in range(2)]
    QTL = [Buf() for _ in range(2)]
    KTL = [Buf() for _ in range(2)]
    attm = [kb.sb(f"attm{i}", [128, 128], BF16) for i in range(2)]
    ATM = [Buf() for _ in range(2)]
    Sst = [kb.sb(f"Sst{i}", [128, 256], F32) for i in range(2)]
    Sbf = [kb.sb(f"Sbf{i}", [128, 256], BF16) for i in range(2)]
    SST = [Buf() for _ in range(2)]
    SBF = [Buf() for _ in range(2)]
    junk = kb.sb("junk", [128, 256], F32)
    ssq = kb.sb("ssq", [128, 4], F32)
    SSQ = Buf()
    og = kb.sb("og", [128, 256], F32)
    OGB = Buf()
    ogb = kb.sb("ogb", [128, 256], BF16)
    OGBB = Buf()
    hoc = [kb.sb(f"ghoc{i}", [128, 4, 512], BF16) for i in range(2)]
    HOCB = [Buf(f"ghoc{i}") for i in range(2)]
    G = [kb.bank() for _ in range(3)]
    BBK, BBB = kb.bank()
    ATK, ATB = kb.bank()
    OK_, OKB = kb.bank()
    DSK, DSB = kb.bank()
    TRK, TRB = kb.bank(BF16)
    OUTS = []
    for hd in range(2):
        kb.memset("pool", Sst[hd][:], 0.0, [SST[hd]])
        kb.memset("pool", Sbf[hd][:], 0.0, [SBF[hd]])
    gi = [0]

    def nextG():
        g = G[gi[0] % 3]
        gi[0] += 1
        return g

    lnscale = math.log(128.0 ** -0.5)
    qk2 = [qk, kb.sb("qk_b", [128, 4, 512], F32)]
    QK2 = [QK, Buf()]
    lrT2 = [lrT, kb.sb("lrT_b", [16, 512], BF16)]
    LRT2 = [LRT, Buf()]
    vb2 = [vb, kb.sb("vb_b", [128, 512], BF16)]
    VB2 = [VBB, Buf()]
    sr2 = [sr, kb.sb("sr_b", [128, 512], F32)]
    SR2 = [SRB, Buf()]
    khat2 = [khat, kb.sb("khat_b", [128, 256], BF16)]
    KH2 = [KHB, Buf()]
    qtl2 = [qtl, [kb.sb(f"qtl_b{i}", [128, 128], BF16) for i in range(2)]]
    QTL2 = [QTL, [Buf() for _ in range(2)]]
    attm2 = [attm, [kb.sb(f"attm_b{i}", [128, 128], BF16) for i in range(2)]]
    ATM2 = [ATM, [Buf() for _ in range(2)]]
    dec2 = [dec, kb.sb("dec_b", [128, 2], F32)]
    DEC2 = [DECB, [Buf() for _ in range(2)]]

    E1h = [kb.sb(f"E1h{i}", [128, 128], F32) for i in range(2)]
    E2h = [kb.sb(f"E2h{i}", [128, 128], F32) for i in range(2)]
    EBh = [Buf() for _ in range(2)]
    ATBh = [Buf() for _ in range(2)]
    OKBh = [Buf() for _ in range(2)]
    DSBh = [Buf() for _ in range(2)]
    junkh = [kb.sb(f"junkh{i}", [128, 256], F32) for i in range(2)]
    ssqh = [kb.sb(f"ssqh{i}", [128, 4], F32) for i in range(2)]
    SSQh = [Buf() for _ in range(2)]
    ogh = [kb.sb(f"ogh{i}", [128, 256], F32) for i in range(2)]
    OGBh = [Buf() for _ in range(2)]
    ogbh = [kb.sb(f"ogbh{i}", [128, 256], BF16) for i in range(2)]
    OGBBh = [Buf() for _ in range(2)]

    def prologue(cch):
        t0 = cch * 512
        xi = cch % 2
        src, x_bf16 = xsrc(t0)
        fw.dma(XC[xi], xc[xi][:], src, writes=[XC[xi]], queue=("sp" if x_bf16 else "pool"))
        for ft in range(4):
            g, GB = nextG()
            for k in range(8):
                kb.mm(g[:], wq_sb[:, k, ft * 128:(ft + 1) * 128], xc[xi][:, k, :], k == 0, k == 7, [WB, XC[xi]], [GB])
            kb.cp("act", qk2[xi][:, ft, :], g[:], [GB], [QK2[xi]])
        g, GB = nextG()
        for k in range(8):
            kb.mm(g[0:16, :], wlr_sb[:, k, :], xc[xi][:, k, :], k == 0, k == 7, [WB, XC[xi]], [GB])
        kb.cp("act", lrT2[xi][:], g[0:16, :], [GB], [LRT2[xi]])

    def front(cch, j):
        xi = cch % 2
        pj = (cch * 4 + j) % 2
        ts_ = slice(j * 128, (j + 1) * 128)
        gk, GKB = nextG()
        for k in range(8):
            kb.mm(gk[:, 0:256], xc[xi][:, k, ts_], wtm_sb[:, k, 0:256], k == 0, k == 7, [WB, XC[xi]], [GKB])
        kb.mm(gk[:, 256:512], lrT2[xi][0:16, ts_], wgu_sb[0:16, :], True, False, [LRT2[xi], WB], [GKB])
        kb.mm(gk[:, 256:512], c["ones_b"][0:1, 0:128], bg_sb[0:1, :], False, True, [CB, WB], [GKB])
        gv, GVB = nextG()
        for k in range(8):
            kb.mm(gv[:], xc[xi][:, k, ts_], wtm_sb[:, k, 256:768], k == 0, k == 7, [WB, XC[xi]], [GVB])
        kb.cp("act", vb2[pj][:], gv[:], [GVB], [VB2[pj]])
        gr, GRB = nextG()
        for k in range(8):
            kb.mm(gr[:], xc[xi][:, k, ts_], wtm_sb[:, k, 768:1280], k == 0, k == 7, [WB, XC[xi]], [GRB])
        kb.act(sr2[pj][:], gr[:], AF.Silu, [GRB], [SR2[pj]])
        kb.act(ee[:], gk[:, 256:512], AF.Exp, [GKB], [SPB], scale=-1.0)
        kb.act(sp_[:], ee[:], AF.Ln, [SPB], [SPB], bias=1.0, scale=1.0)
        for hd in range(2):
            kb.mm(BBK[:, hd * 128:(hd + 1) * 128], sp_[:, hd * 128:(hd + 1) * 128], c["uincl"][:], True, True,
                  [SPB, CB], [BBB])
        kb.mm(BBK[:, 256:512], c["ustr"][:], sp_[:], True, True, [SPB, CB], [BBB])
        kb.act(E3[:], BBK[:, 256:512], AF.Exp, [BBB], [E3B])
        kb.tt("dve", khat2[pj][:], gk[:, 0:256], E3[:], ALU.mult, [GKB, E3B], [KH2[pj]])
        for hd in range(2):
            bt = BBK[:, hd * 128:(hd + 1) * 128]
            kb.act(E1h[hd][:], bt, AF.Exp, [BBB], [EBh[hd]], bias=lnscale, scale=1.0)
            kb.act(E2h[hd][:], bt, AF.Exp, [BBB], [EBh[hd]], scale=-1.0)
            kb.act(dec2[pj][:, hd:hd + 1], BBK[:, hd * 128 + 127:hd * 128 + 128], AF.Exp, [BBB], [DEC2[pj][hd]])
        for hd in range(2):
            kb.tt("dve", qtl2[pj][hd][:], qk2[xi][:, hd, ts_], E1h[hd][:], ALU.mult, [QK2[xi], EBh[hd]], [QTL2[pj][hd]])
            kb.tt("dve", ktl[hd][:], qk2[xi][:, 2 + hd, ts_], E2h[hd][:], ALU.mult, [QK2[xi], EBh[hd]], [KTL[hd]])
        for hd in range(2):
            kb.mm(ATK[:, hd * 128:(hd + 1) * 128], ktl[hd][:], qtl2[pj][hd][:], True, True,
                  [KTL[hd], QTL2[pj][hd]], [ATB])
        for hd in range(2):
            kb.tt("dve", attm2[pj][hd][:], ATK[:, hd * 128:(hd + 1) * 128], c["tri"][:], ALU.mult, [ATB, CB],
                  [ATM2[pj][hd]])

    def back(cch, j):
        pj = (cch * 4 + j) % 2
        hi = cch % 2
        ts_ = slice(j * 128, (j + 1) * 128)
        H = range(2)
        ov = [OK_[:, hd * 256:(hd + 1) * 256] for hd in H]
        vh = [vb2[pj][:, hd * 256:(hd + 1) * 256] for hd in H]
        dv = [DSK[:, hd * 256:(hd + 1) * 256] for hd in H]
        for hd in H:
            kb.mm(ov[hd], attm2[pj][hd][:], vh[hd], True, False, [ATM2[pj][hd], VB2[pj]], [OKB])
            kb.mm(ov[hd], qtl2[pj][hd][:], Sbf[hd][:], False, True, [QTL2[pj][hd], SBF[hd]], [OKB])
        for hd in H:
            kb.mm(dv[hd], khat2[pj][:, hd * 128:(hd + 1) * 128], vh[hd], True, True, [KH2[pj], VB2[pj]], [DSB])
        for hd in H:
            kb.stt("dve", Sst[hd][:], Sst[hd][:], dec2[pj][:, hd:hd + 1], dv[hd], ALU.mult, ALU.add,
                   [SST[hd], DEC2[pj][hd], DSB], [SST[hd]])
        for hd in H:
            kb.cp("pool", Sbf[hd][:], Sst[hd][:], [SST[hd]], [SBF[hd]])
        for hd in H:
            kb.act(junkh[hd][:], ov[hd], AF.Square, [OKB], [SSQh[hd]], accum_out=ssqh[hd][:, 0:1])
        for hd in H:
            kb.act(ssqh[hd][:, 1:2], ssqh[hd][:, 0:1], AF.Ln, [SSQh[hd]], [SSQh[hd]], bias=1e-5, scale=1.0 / 256.0)
        for hd in H:
            kb.act(ssqh[hd][:, 2:3], ssqh[hd][:, 1:2], AF.Exp, [SSQh[hd]], [SSQh[hd]], scale=-0.5)
        for hd in H:
            kb.stt("dve", ogh[hd][:], ov[hd], ssqh[hd][:, 2:3], gn_sb[:], ALU.mult, ALU.mult,
                   [OKB, SSQh[hd], WB], [OGBh[hd]])
        for hd in H:
            kb.tt("pool", ogbh[hd][:], ogh[hd][:], sr2[pj][:, hd * 256:(hd + 1) * 256], ALU.mult,
                  [OGBh[hd], SR2[pj]], [OGBBh[hd]])
        for hd in H:
            for cc in range(2):
                kb.tr(TRK[:, (hd * 2 + cc) * 128:(hd * 2 + cc + 1) * 128], ogbh[hd][:, cc * 128:(cc + 1) * 128],
                      c["ident"][:], [OGBBh[hd], CB], [TRB])
        kb.cp("act", hoc[hi][:, :, ts_], TRK[:, 0:512].rearrange("p (c t) -> p c t", t=128), [TRB], [HOCB[hi]])
        if j == 3:
            ob_ = Buf("o")
            OUTS.append(ob_)
            fw.dma(HOCB[hi], hodst4(cch * 512), hoc[hi][:], reads=[HOCB[hi]], writes=[ob_])
            if on_store is not None:
                on_store(cch, ob_)

    subs = [(cch, j) for cch in range(S // 512) for j in range(4)]
    prologue(0)
    front(0, 0)
    for idx, (cch, j) in enumerate(subs):
        if idx + 1 < len(subs):
            nc_, nj = subs[idx + 1]
            if nj == 0:
                prologue(nc_)
            front(nc_, nj)
        back(cch, j)
    return OUTS


def stage_row2(kb, T, xres, wout, w1, w2, lnp, xout, xTout, hosrc, sel, Thalf, on_store=None):
    fw = kb.fw
    c = kb.c
    CB = c["buf"]
    wo_sb = kb.sb("wo_sb", [128, 8, 1024], BF16)
    WO = Buf("wo")
    ln_sb = kb.sb("ln_sb", [128, 4, 1024], F32)
    LNB = Buf("ln")
    sel_sb = kb.sb("sel_sb", [128, 2], F32)
    SELB = Buf("selb")
    fw.dma(WO, wo_sb[:], wout.rearrange("(c p) n -> p c n", p=128), writes=[WO])
    fw.dma(LNB, ln_sb[:], lnp, writes=[LNB])
    fw.dma(SELB, sel_sb[:], sel, writes=[SELB])
    w1b = [kb.sb(f"w1b{i}", [128, 8, 512], BF16) for i in range(2)]
    w2b = [kb.sb(f"w2b{i}", [128, 4, 1024], BF16) for i in range(2)]
    W1B = [Buf(f"w1b{i}") for i in range(2)]
    W2B = [Buf(f"w2b{i}") for i in range(2)]
    hoa = kb.sb("hoa", [128, 8, 512], BF16)
    hob = kb.sb("hob", [128, 8, 512], BF16)
    HOA, HOBB = Buf("hoa"), Buf("hob")
    y = [kb.sb(f"y{i}", [128, 4, 1024], F32) for i in range(2)]
    Y = [[Buf(f"y{i}_{j}") for j in range(4)] for i in range(2)]
    acc = kb.sb("acc", [128, 4, 1024], F32)
    ACC = [Buf(f"acc{j}") for j in range(4)]
    xb = kb.sb("xb", [128, 1024], BF16)
    XB = Buf()
    x1T = [kb.sb(f"x1T{i}", [128, 8, 512], BF16) for i in range(2)]
    X1T = [Buf(f"x1T{i}") for i in range(2)]
    xTo = kb.sb("xTo", [128, 8, 512], BF16)
    XTO = Buf("xTo")
    hsq = [kb.sb(f"hsq{i}", [128, 4, 512], BF16) for i in range(2)]
    HSQ = [[Buf() for _ in range(4)] for _ in range(2)]
    rl = [kb.sb(f"rl{i}", [128, 512], F32) for i in range(2)]
    RL = [Buf() for _ in range(2)]
    st6 = kb.sb("st6", [128, 2, 6], F32)
    mv = kb.sb("mv", [128, 2], F32)
    rstd = kb.sb("rstd", [128, 2], F32)
    STB = Buf()
    G = [kb.bank() for _ in range(4)]
    TR = [kb.bank(BF16) for _ in range(2)]
    OUTS = []
    gi = [0]
    ti = [0]
    wi_ = [0]

    def nout():
        b = Buf("o")
        OUTS.append(b)
        return b

    def nextG():
        g = G[gi[0] % 4]
        gi[0] += 1
        return g

    def layer_norm(buf_ap, BUFS, j, gidx):
        v = buf_ap[:, j, :]
        for hh in range(2):
            fw.op("dve", lambda e, hh=hh: e.bn_stats(out=st6[:, hh, :], in_=buf_ap[:, j, hh * 512:(hh + 1) * 512]),
                  [BUFS[j]], [STB])
        fw.op("dve", lambda e: e.bn_aggr(out=mv[:], in_=st6[:].rearrange("p a b -> p (a b)")), [STB], [STB])
        kb.act(rstd[:, 0:1], mv[:, 1:2], AF.Sqrt, [STB], [STB], bias=1e-5, scale=1.0)
        fw.op("dve", lambda e: e.reciprocal(out=rstd[:, 1:2], in_=rstd[:, 0:1]), [STB], [STB])
        kb.ts("dve", v, v, mv[:, 0:1], rstd[:, 1:2], ALU.subtract, ALU.mult, [STB, BUFS[j]], [BUFS[j]])
        kb.tt("pool", v, v, ln_sb[:, gidx, :], ALU.mult, [BUFS[j], LNB], [BUFS[j]])
        kb.tt("pool", v, v, ln_sb[:, gidx + 1, :], ALU.add, [BUFS[j], LNB], [BUFS[j]])

    def to_T(src_ap, SRC, j, dstT, DST):
        kb.cp("act", xb[:], src_ap[:, j, :], [SRC[j]], [XB])
        trp, TRB = TR[ti[0] % 2]
        ti[0] += 1
        for k in range(8):
            kb.tr(trp[:, k * 128:(k + 1) * 128], xb[:, k * 128:(k + 1) * 128], c["ident"][:], [XB, CB], [TRB])
        kb.cp("dve", dstT[:, :, j * 128:(j + 1) * 128], trp[:].rearrange("p (k t) -> p k t", t=128), [TRB], [DST])

    def A_load(st):
        p = st % 2
        t0 = st * 512
        fw.dma(HOA, hoa[:], hosrc(t0), writes=[HOA])
        fw.dma(HOBB, hob[:], hosrc(Thalf + t0), writes=[HOBB])
        fw.dma(Y[p][0], y[p][:], xres[t0:t0 + 512, :].rearrange("(j p) d -> p j d", p=128), writes=Y[p])

    def A_front(st):
        p = st % 2
        kb.act(hoa[:], hoa[:], AF.Copy, [HOA, SELB], [HOA], scale=sel_sb[:, 0:1])
        kb.stt("dve", hoa[:], hob[:], sel_sb[:, 1:2], hoa[:], ALU.mult, ALU.add, [HOBB, HOA, SELB], [HOA])
        for j in range(4):
            for nb in range(2):
                g, GB = nextG()
                for cc in range(8):
                    kb.mm(g[:], hoa[:, cc, j * 128:(j + 1) * 128], wo_sb[:, cc, nb * 512:(nb + 1) * 512],
                          cc == 0, cc == 7, [HOA, WO], [GB])
                ys = y[p][:, j, nb * 512:(nb + 1) * 512]
                kb.stt("dve", ys, ys, ALPHA, g[:], ALU.mult, ALU.add, [Y[p][j], GB], [Y[p][j]])
            layer_norm(y[p], Y[p], j, 0)

    def A_T(st, j):
        p = st % 2
        to_T(y[p], Y[p], j, x1T[p], X1T[p])

    def F1(st, fb):
        p = st % 2
        wi = fb % 2
        fw.dma(W1B[wi], w1b[wi][:], w1.rearrange("(k p) f -> p k f", p=128)[:, :, fb * 512:(fb + 1) * 512],
               writes=[W1B[wi]])
        fw.dma(W2B[wi], w2b[wi][:], w2[fb * 512:(fb + 1) * 512, :].rearrange("(c p) d -> p c d", p=128),
               writes=[W2B[wi]])
        hi = fb % 2
        for fc in range(4):
            g, GB = nextG()
            for k in range(8):
                kb.mm(g[:], w1b[wi][:, k, fc * 128:(fc + 1) * 128], x1T[p][:, k, :], k == 0, k == 7,
                      [W1B[wi], X1T[p]], [GB])
            ri = fc % 2
            kb.act(rl[ri][:], g[:], AF.Relu, [GB], [RL[ri]])
            kb.tt("pool", hsq[hi][:, fc, :], rl[ri][:], rl[ri][:], ALU.mult, [RL[ri]], [HSQ[hi][fc]])

    def F2(st, fb):
        wi = fb % 2
        hi = fb % 2
        for j in range(4):
            for nb in range(2):
                g, GB = nextG()
                for fc in range(4):
                    kb.mm(g[:], hsq[hi][:, fc, j * 128:(j + 1) * 128], w2b[wi][:, fc, nb * 512:(nb + 1) * 512],
                          fc == 0, fc == 3, [HSQ[hi][fc], W2B[wi]], [GB])
                a = acc[:, j, nb * 512:(nb + 1) * 512]
                if fb == 0:
                    kb.cp("dve", a, g[:], [GB], [ACC[j]])
                else:
                    kb.tt("dve", a, a, g[:], ALU.add, [GB, ACC[j]], [ACC[j]])

    def phaseB1(st):
        p = st % 2
        for j in range(4):
            kb.stt("dve", y[p][:, j, :], y[p][:, j, :], ALPHA, acc[:, j, :], ALU.mult, ALU.add,
                   [Y[p][j], ACC[j]], [Y[p][j]])

    def B_ln(st):
        p = st % 2
        for j in range(4):
            layer_norm(y[p], Y[p], j, 2)

    def B_T(st, j):
        p = st % 2
        to_T(y[p], Y[p], j, xTo, XTO)

    def B_store(st):
        p = st % 2
        t0 = st * 512
        fw.dma(Y[p][0], xout[t0:t0 + 512, :].rearrange("(j p) d -> p j d", p=128), y[p][:], reads=Y[p], writes=[nout()])
        xb_ = nout()
        fw.dma(XTO, xTout(t0), xTo[:], reads=[XTO], writes=[xb_])
        if on_store is not None:
            on_store(st, xb_)

    nst = T // 512
    A_load(0)
    A_front(0)
    for j in range(4):
        A_T(0, j)
    F1(0, 0)
    for st in range(nst):
        for fb in range(8):
            if fb + 1 < 8:
                F1(st, fb + 1)
            F2(st, fb)
            if st > 0:
                if fb == 0:
                    B_ln(st - 1)
                elif fb == 1:
                    B_T(st - 1, 0)
                    B_T(st - 1, 1)
                elif fb == 2:
                    B_T(st - 1, 2)
                    B_T(st - 1, 3)
                    B_store(st - 1)
            if st + 1 < nst:
                if fb == 2:
                    A_load(st + 1)
                elif fb == 3:
                    A_front(st + 1)
                elif fb >= 4:
                    A_T(st + 1, fb - 4)
                if fb == 7:
                    F1(st + 1, 0)
        phaseB1(st)
    B_ln(nst - 1)
    for j in range(4):
        B_T(nst - 1, j)
    B_store(nst - 1)
    return OUTS


ROPE_THETA = 10000.0


def rope_table(S, dim, nrows):
    half = dim // 2
    inv = (1.0 / (ROPE_THETA ** (np.arange(0, dim, 2, dtype=np.float32) / np.float32(dim)))).astype(np.float32)
    ang = np.arange(S, dtype=np.float32)[None, :] * inv[:, None]
    cs = np.cos(ang).astype(np.float32)
    sn = np.sin(ang).astype(np.float32)
    out = np.zeros((2, nrows, S), np.float32)
    for r in range(nrows):
        i = (r % dim) % half
        out[0, r] = cs[i]
        out[1, r] = -sn[i] if (r % dim) < half else sn[i]
    return out


def perm_cols(w, dim):
    n = w.shape[-1] // dim
    w4 = w.reshape(w.shape[0], n, 2, dim // 2)
    return np.ascontiguousarray(w4[:, :, ::-1, :]).reshape(w.shape)


def even_inputs(inp, e, h, xT, S):
    w = inp["hy_w_in"][e]
    hs = slice(2 * h * 128, (2 * h + 2) * 128)
    def fm(q, k, dim):
        qc = q[:, hs]; kc = k[:, hs]
        return np.concatenate([qc, perm_cols(qc, dim), kc, perm_cols(kc, dim)], axis=1)
    wfm = np.stack([fm(w[:, 0:512], w[:, 512:1024], 64), fm(w[:, 1536:2048], w[:, 2048:2560], 128)])
    wv = np.stack([w[:, 1024:1536][:, hs], w[:, 2560:3072][:, hs]])
    ropes = np.stack([rope_table(S, 64, 128), rope_table(S, 128, 128)])
    lam128 = np.broadcast_to(inp["diff_lambda"][e].reshape(1, 256), (128, 256))
    gsub = inp["diff_subln"][e].reshape(128, 1)
    return dict(xT=(None if xT is None else np.ascontiguousarray(xT)), wfm=np.ascontiguousarray(wfm, dtype=np.float32),
                wv=np.ascontiguousarray(wv, dtype=np.float32), ropes=ropes,
                lam128=np.ascontiguousarray(lam128, dtype=np.float32), gsub=np.ascontiguousarray(gsub, dtype=np.float32))


def gla_inputs(inp, o, h, xT, S):
    w = inp["gla_w_in"][o]
    hk = slice(2 * h * 128, (2 * h + 2) * 128)
    hv = slice(2 * h * 256, (2 * h + 2) * 256)
    q = w[:, 0:512][:, hk]; k = w[:, 512:1024][:, hk]
    v = w[:, 1024:2048][:, hv]; r = w[:, 2048:3072][:, hv]
    wq = np.concatenate([q, k], axis=1)
    wtm = np.concatenate([k, v, r], axis=1)
    f = lambda a: np.ascontiguousarray(a, dtype=np.float32)
    return dict(xT=(None if xT is None else np.ascontiguousarray(xT)), wq=f(wq), wlr=f(w[:, 3072:3088]), wtm=f(wtm),
                wgu=f(inp["gla_w_gate_up"][o][:, hk]), bg=f(inp["gla_b_gate"][o][hk].reshape(1, 256)),
                gn128=f(np.broadcast_to(inp["gla_norm"][o].reshape(1, 256), (128, 256))))


def row_inputs(inp, l, ho, xres):
    if l % 2 == 0:
        wo = inp["hy_w_out"][l // 2]
        wo = np.concatenate([wo[0:256], wo[512:768], wo[256:512], wo[768:1024]], axis=0)
    else:
        wo = inp["gla_w_out"][l // 2]
    ln = np.stack([inp["ln_mix_g"][l], inp["ln_mix_b"][l], inp["ln_ffn_g"][l], inp["ln_ffn_b"][l]])
    f = lambda a: np.ascontiguousarray(a, dtype=np.float32)
    return dict(ho=(None if ho is None else np.ascontiguousarray(ho)), xres=(None if xres is None else f(xres)), wout=f(wo), w1=f(inp["ffn_w1"][l]), w2=f(inp["ffn_w2"][l]),
                lnp=f(np.broadcast_to(ln[None], (128, 4, 1024))))


def build_even(S, x_bf16, lambda_init):
    nc = bass.Bass("TRN2", target_bir_lowering=False)
    xT = nc.dram_tensor("xT", [1024, S], BF16 if x_bf16 else F32, kind="ExternalInput").ap()
    wfm = nc.dram_tensor("wfm", [2, 1024, 1024], F32, kind="ExternalInput").ap()
    wv = nc.dram_tensor("wv", [2, 1024, 256], F32, kind="ExternalInput").ap()
    ropes = nc.dram_tensor("ropes", [2, 2, 128, S], F32, kind="ExternalInput").ap()
    lam = nc.dram_tensor("lam128", [128, 256], F32, kind="ExternalInput").ap()
    gsub = nc.dram_tensor("gsub", [128, 1], F32, kind="ExternalInput").ap()
    hoT = nc.dram_tensor("hoT", [4, 128, S], BF16, kind="ExternalOutput").ap()
    with contextlib.ExitStack() as st:
        kb = KB(nc, st)
        kb.consts(moba=True)
        outs = stage_even(kb, S, lambda t0: (xT.rearrange("(k p) t -> p k t", p=128)[:, :, t0:t0 + 512], x_bf16), wfm, wv, ropes, lam, gsub, lambda och, Q0: hoT[och][:, Q0:Q0 + 512], lambda_init)
        kb.fw.wait_all("sp", outs)
        kb.fw.emit()
    return nc


def build_gla(S, x_bf16):
    nc = bass.Bass("TRN2", target_bir_lowering=False)
    xT = nc.dram_tensor("xT", [1024, S], BF16 if x_bf16 else F32, kind="ExternalInput").ap()
    wq = nc.dram_tensor("wq", [1024, 512], F32, kind="ExternalInput").ap()
    wlr = nc.dram_tensor("wlr", [1024, 16], F32, kind="ExternalInput").ap()
    wtm = nc.dram_tensor("wtm", [1024, 1280], F32, kind="ExternalInput").ap()
    wgu = nc.dram_tensor("wgu", [16, 256], F32, kind="ExternalInput").ap()
    bg = nc.dram_tensor("bg", [1, 256], F32, kind="ExternalInput").ap()
    gn = nc.dram_tensor("gn128", [128, 256], F32, kind="ExternalInput").ap()
    hoT = nc.dram_tensor("hoT", [4, 128, S], BF16, kind="ExternalOutput").ap()
    with contextlib.ExitStack() as st:
        kb = KB(nc, st)
        kb.consts()
        outs = stage_gla(kb, S, lambda t0: (xT.rearrange("(k p) t -> p k t", p=128)[:, :, t0:t0 + 512], x_bf16), wq, wlr, wtm, wgu, bg, gn, lambda t0: hoT.rearrange("c p t -> p c t")[:, :, t0:t0 + 512])
        kb.fw.wait_all("sp", outs)
        kb.fw.emit()
    return nc


def build_row(T):
    nc = bass.Bass("TRN2", target_bir_lowering=False)
    ho = nc.dram_tensor("ho", [8, 128, T], BF16, kind="ExternalInput").ap()
    xres = nc.dram_tensor("xres", [T, 1024], F32, kind="ExternalInput").ap()
    wout = nc.dram_tensor("wout", [1024, 1024], F32, kind="ExternalInput").ap()
    w1 = nc.dram_tensor("w1", [1024, 4096], F32, kind="ExternalInput").ap()
    w2 = nc.dram_tensor("w2", [4096, 1024], F32, kind="ExternalInput").ap()
    lnp = nc.dram_tensor("lnp", [128, 4, 1024], F32, kind="ExternalInput").ap()
    xout = nc.dram_tensor("xout", [T, 1024], F32, kind="ExternalOutput").ap()
    xTout = nc.dram_tensor("xTout", [1024, T], BF16, kind="ExternalOutput").ap()
    with contextlib.ExitStack() as st:
        kb = KB(nc, st)
        kb.consts()
        outs = stage_row(kb, T, ho, xres, wout, w1, w2, lnp, xout, lambda t0: xTout.rearrange("(k p) t -> p k t", p=128)[:, :, t0:t0 + 512])
        kb.fw.wait_all("sp", outs)
        kb.fw.emit()
    return nc


def kernel_unfused(**inputs):
    inp = {k: np.asarray(v) for k, v in inputs.items()}
    x = inp["x"]
    Bn, S, _ = x.shape
    T = S // 2
    depth = inp["ln_mix_g"].shape[0]
    ncore = 2 * Bn
    cores = list(range(ncore))
    xT = [np.ascontiguousarray(x[b].T) for b in range(Bn)]
    xres = [x[c // 2, (c % 2) * T:(c % 2 + 1) * T] for c in cores]
    row_nc = build_row(T)
    gla_nc = None
    for l in range(depth):
        x_bf16 = l > 0
        if l % 2 == 0:
            lam_init = 0.8 - 0.6 * math.exp(-0.3 * l)
            nc = build_even(S, x_bf16, lam_init)
            maps = [even_inputs(inp, l // 2, c % 2, xT[c // 2], S) for c in cores]
        else:
            if gla_nc is None:
                gla_nc = build_gla(S, x_bf16)
            nc = gla_nc
            maps = [gla_inputs(inp, l // 2, c % 2, xT[c // 2], S) for c in cores]
        res = run_bass_kernel_spmd(nc, maps, core_ids=cores)
        hoT = [res.results[c]["hoT"] for c in cores]
        maps = []
        for c in cores:
            b, h = c // 2, c % 2
            ho = np.concatenate([hoT[2 * b][:, :, h * T:(h + 1) * T], hoT[2 * b + 1][:, :, h * T:(h + 1) * T]], axis=0)
            maps.append(row_inputs(inp, l, ho, xres[c]))
        res = run_bass_kernel_spmd(row_nc, maps, core_ids=cores)
        xres = [res.results[c]["xout"] for c in cores]
        xT = [np.concatenate([res.results[2 * b]["xTout"], res.results[2 * b + 1]["xTout"]], axis=1) for b in range(Bn)]
    out = np.stack([np.concatenate([xres[2 * b], xres[2 * b + 1]], axis=0) for b in range(Bn)])
    return out.astype(np.float32)


import os
CC_COLS = int(os.environ.get("CC_COLS", "0"))


def cc_chunked(fw, name, src, dst, groups, rows, cols):
    if os.environ.get("NOCC"):
        return
    step = CC_COLS if CC_COLS else cols
    for i, c0 in enumerate(range(0, cols, step)):
        fw.cc(Buf(f"{name}_{i}"), "AllGather", src[:, c0:c0 + step], dst[:, c0:c0 + step], groups)


def build_fused(S, depth, ncore):
    T = S // 2
    nc = bass.Bass("TRN2", target_bir_lowering=False)

    def ext(name, shape, dt=F32):
        return nc.dram_tensor(name, list(shape), dt, kind="ExternalInput").ap()

    xT0 = ext("xT0", [1024, S])
    xres0 = ext("xres0", [T, 1024])
    sel = ext("sel", [128, 2])
    ropes = ext("ropes", [2, 2, 128, S])
    W = []
    for l in range(depth):
        d = {}
        if l % 2 == 0:
            d["wfm"] = ext(f"wfm{l}", [2, 1024, 1024])
            d["wv"] = ext(f"wv{l}", [2, 1024, 256])
            d["lam"] = ext(f"lam{l}", [128, 256])
            d["gsub"] = ext(f"gsub{l}", [128, 1])
        else:
            d["wq"] = ext(f"wq{l}", [1024, 512])
            d["wlr"] = ext(f"wlr{l}", [1024, 16])
            d["wtm"] = ext(f"wtm{l}", [1024, 1280])
            d["wgu"] = ext(f"wgu{l}", [16, 256])
            d["bg"] = ext(f"bg{l}", [1, 256])
            d["gn"] = ext(f"gn{l}", [128, 256])
        d["wout"] = ext(f"wout{l}", [1024, 1024])
        d["w1"] = ext(f"w1_{l}", [1024, 4096])
        d["w2"] = ext(f"w2_{l}", [4096, 1024])
        d["lnp"] = ext(f"lnp{l}", [128, 4, 1024])
        W.append(d)
    xout = nc.dram_tensor("xout", [T, 1024], F32, kind="ExternalOutput").ap()
    HC = int(os.environ.get("HC", "2048"))
    XC_ = int(os.environ.get("XCC", "1024"))
    HC = min(HC, S)
    XC_ = min(XC_, T)
    hoT_own = [nc.dram_tensor(f"hoT_own{k}", [512, HC], BF16).ap() for k in range(S // HC)]
    ho_all = [nc.dram_tensor(f"ho_all{k}", [1024, HC], BF16).ap() for k in range(S // HC)]
    xT_own = [nc.dram_tensor(f"xT_own{k}", [1024, XC_], BF16).ap() for k in range(T // XC_)]
    xT_all = [nc.dram_tensor(f"xT_all{k}", [2048, XC_], BF16).ap() for k in range(T // XC_)]
    xres_i = [nc.dram_tensor(f"xres_i{i}", [T, 1024], F32).ap() for i in range(2)]
    groups = [[2 * i, 2 * i + 1] for i in range(ncore // 2)]
    WBF = [dict(wout=nc.dram_tensor(f"woutb{l}", [1024, 1024], BF16).ap(),
                w1=nc.dram_tensor(f"w1b_{l}", [1024, 4096], BF16).ap(),
                w2=nc.dram_tensor(f"w2b_{l}", [4096, 1024], BF16).ap()) for l in range(depth)]

    with contextlib.ExitStack() as st:
        kb = KB(nc, st)
        fw = kb.fw
        kb.consts(moba=True)
        for l in range(depth):
            d = W[l]
            if l == 0:
                def xsrc(t0):
                    return xT0.rearrange("(k p) t -> p k t", p=128)[:, :, t0:t0 + 512], False
            else:
                def xsrc(t0):
                    r, tl = t0 // T, t0 % T
                    return (xT_all[tl // XC_][r * 1024:(r + 1) * 1024, tl % XC_:tl % XC_ + 512]
                            .rearrange("(k p) t -> p k t", p=128), True)

            def hodst(och, Q0):
                return hoT_own[Q0 // HC][och * 128:(och + 1) * 128, Q0 % HC:Q0 % HC + 512]

            def hodst4(t0):
                return hoT_own[t0 // HC].rearrange("(c p) t -> p c t", p=128)[:, :, t0 % HC:t0 % HC + 512]

            def hosrc(tok0):
                return ho_all[tok0 // HC].rearrange("(c p) t -> p c t", p=128)[:, :, tok0 % HC:tok0 % HC + 512]

            def xTdst(t0):
                return xT_own[t0 // XC_].rearrange("(k p) t -> p k t", p=128)[:, :, t0 % XC_:t0 % XC_ + 512]
            def hook(l=l, d=d):
                cb = Buf(f"wconv{l}")
                for nm in ("wout", "w1", "w2"):
                    src, dst = d[nm], WBF[l][nm]
                    for r0 in range(0, src.shape[0], 256):
                        fw.dma(cb, dst[r0:r0 + 256, :], src[r0:r0 + 256, :], queue="pool")

            def overlapped_cc(name, per_chunk, src, dst):
                state = dict(bufs=[], done=0, n=0)

                def emit_upto(kmax):
                    while state["done"] < kmax:
                        k = state["done"]
                        fw.cc(Buf(f"{name}_{k}"), "AllGather", src[k], dst[k], groups,
                              reads=state["bufs"][k * per_chunk:(k + 1) * per_chunk])
                        state["done"] += 1

                def on_store(i, buf):
                    emit_upto(len(state["bufs"]) // per_chunk)
                    state["bufs"].append(buf)

                def finish():
                    emit_upto(len(state["bufs"]) // per_chunk)
                return on_store, finish

            with fw.stage():
                if l % 2 == 0:
                    lam_init = 0.8 - 0.6 * math.exp(-0.3 * l)
                    stage_even(kb, S, xsrc, d["wfm"], d["wv"], ropes, d["lam"], d["gsub"], hodst, lam_init, hook=hook)
                else:
                    on_s, fin = overlapped_cc(f"ccA{l}", HC // 512, hoT_own, ho_all)
                    stage_gla(kb, S, xsrc, d["wq"], d["wlr"], d["wtm"], d["wgu"], d["bg"], d["gn"], hodst4, hook=hook,
                              on_store=on_s)
                    fin()
            if l % 2 == 0:
                with fw.stage():
                    for k in range(S // HC):
                        fw.cc(Buf(f"ccA{l}_{k}"), "AllGather", hoT_own[k], ho_all[k], groups)
            xin = xres0 if l == 0 else xres_i[(l - 1) % 2]
            xo = xout if l == depth - 1 else xres_i[l % 2]
            with fw.stage():
                if l < depth - 1:
                    on_s, fin = overlapped_cc(f"ccB{l}", XC_ // 512, xT_own, xT_all)
                else:
                    on_s, fin = None, None
                outs = stage_row2(kb, T, xin, WBF[l]["wout"], WBF[l]["w1"], WBF[l]["w2"], d["lnp"], xo, xTdst,
                                  hosrc, sel, T, on_store=on_s)
                if fin is not None:
                    fin()
                if l == depth - 1:
                    fw.wait_all("sp", outs)
        fw.emit()
        print("instr counts", {k: len(v) for k, v in fw.q.items()}, "sems", fw.nsem, flush=True)
    return nc


def fused_inputs(inp, c, S, depth):
    b, h = c // 2, c % 2
    T = S // 2
    x = inp["x"]
    m = dict(xT0=np.ascontiguousarray(x[b, :S].T), xres0=np.ascontiguousarray(x[b, h * T:(h + 1) * T]),
             sel=np.ascontiguousarray(np.broadcast_to(np.eye(2, dtype=np.float32)[h][None], (128, 2))))
    for l in range(depth):
        if l % 2 == 0:
            e = even_inputs(inp, l // 2, h, None, S)
            m["ropes"] = e["ropes"]
            m[f"wfm{l}"], m[f"wv{l}"], m[f"lam{l}"], m[f"gsub{l}"] = e["wfm"], e["wv"], e["lam128"], e["gsub"]
        else:
            g = gla_inputs(inp, l // 2, h, None, S)
            for k in ("wq", "wlr", "wtm", "wgu", "bg"):
                m[f"{k}{l}"] = g[k]
            m[f"gn{l}"] = g["gn128"]
        r = row_inputs(inp, l, None, None)
        m[f"wout{l}"], m[f"w1_{l}"], m[f"w2_{l}"], m[f"lnp{l}"] = r["wout"], r["w1"], r["w2"], r["lnp"]
    return m


def kernel(**inputs):
    inp = {k: np.asarray(v) for k, v in inputs.items()}
    x = inp["x"]
    Bn, S, _ = x.shape
    depth = inp["ln_mix_g"].shape[0]
    ncore = 2 * Bn
    nc = build_fused(S, depth, ncore)
    maps = [fused_inputs(inp, c, S, depth) for c in range(ncore)]
    res = run_bass_kernel_spmd(nc, maps, core_ids=list(range(ncore)))
    out = np.stack([np.concatenate([res.results[2 * b]["xout"], res.results[2 * b + 1]["xout"]], axis=0)
                    for b in range(Bn)])
    return out.astype(np.float32)
```
